# Optimizing a Trainium2 kernel written in Bass

```python
import math
import jax
import jax.numpy as jnp
from jax import lax
import numpy as np

D_MODEL = 2048
BATCH = 16
SEQ = 256
DEPTH = 2
DEC_BATCH = 8
DEC_SEQ = 2048
PAST_LEN = 256

GRID_W = 64
HEAD_DIM = 128
ROPE_QUARTER = HEAD_DIM // 4
ROPE_THETA = 10000.0
NORM_EPS = 1e-6
Q_BLOCK = 128

A_HEADS = 8
A_DK = 128
A_DV = 128
A_CHUNK = 32
A_W = A_HEADS * A_DV
B_HEADS = 8
B_KV_HEADS = 2
B_GROUP = B_HEADS // B_KV_HEADS
B_W = B_HEADS * HEAD_DIM
C_HEADS = 16
C_KV_HEADS = 4
C_GROUP = C_HEADS // C_KV_HEADS
C_W = C_HEADS * HEAD_DIM
WINDOW = 128

L0_SIZES = (A_HEADS * A_DK, A_HEADS * A_DK, A_HEADS * A_DK, A_W, A_W,
            B_W, B_KV_HEADS * HEAD_DIM, B_KV_HEADS * HEAD_DIM, B_W)
L0_IN = 3 * A_HEADS * A_DK + 2 * A_W + 2 * B_W + 2 * B_KV_HEADS * HEAD_DIM
L1_SIZES = (C_W, C_KV_HEADS * HEAD_DIM, C_KV_HEADS * HEAD_DIM, C_W)
L1_IN = 2 * C_W + 2 * C_KV_HEADS * HEAD_DIM

kernel_name = "hybrid_hgrn2_axialgqa_swa_prefix_dit_step"


def _split(u, sizes):
    out, start = [], 0
    for s in sizes:
        out.append(u[..., start:start + s])
        start += s
    return out


def _rmsnorm(x, g):
    xf = x.astype(jnp.float32)
    y = xf * lax.rsqrt(jnp.mean(xf * xf, axis=-1, keepdims=True) + NORM_EPS)
    return (y * g.astype(jnp.float32)).astype(x.dtype)


def _adaln(cond, w_mod, b_mod):
    m = jax.nn.silu(cond) @ w_mod + b_mod
    return jnp.split(m, 3, axis=-1)


def _modulate(x, norm_g, shift, scale):
    return _rmsnorm(x, norm_g) * (1.0 + scale) + shift


def _axial_rope_tables(n_tokens):
    rows = n_tokens // GRID_W
    row = jnp.repeat(jnp.arange(rows), GRID_W).astype(jnp.float32)
    col = (jnp.arange(rows * GRID_W) % GRID_W).astype(jnp.float32)
    inv = ROPE_THETA ** (-jnp.arange(ROPE_QUARTER, dtype=jnp.float32) / ROPE_QUARTER)
    ar = row[:, None] * inv
    ac = col[:, None] * inv
    ang = jnp.concatenate([ar, ar, ac, ac], axis=-1)
    return jnp.cos(ang), jnp.sin(ang)


def _apply_rope(x, cos, sin):
    shp = (x.shape[1],) + (1,) * (x.ndim - 3) + (HEAD_DIM,)
    xr = x.reshape(x.shape[:-1] + (2, 2, ROPE_QUARTER))
    rot = jnp.stack([-xr[..., 1, :], xr[..., 0, :]], axis=-2).reshape(x.shape)
    return x * cos.reshape(shp).astype(x.dtype) + rot * sin.reshape(shp).astype(x.dtype)


def _attend(q, k, v, mask=None, sink=None):
    s = jnp.einsum('bqhgd,bkhd->bhgqk', q.astype(jnp.float32), k.astype(jnp.float32))
    s = s / math.sqrt(HEAD_DIM)
    if mask is not None:
        s = jnp.where(mask, s, -jnp.inf)
    if sink is not None:
        sk = jnp.broadcast_to(sink.astype(jnp.float32)[None, :, :, None, None], s.shape[:-1] + (1,))
        p = jax.nn.softmax(jnp.concatenate([s, sk], axis=-1), axis=-1)[..., :-1]
    else:
        p = jax.nn.softmax(s, axis=-1)
    return jnp.einsum('bhgqk,bkhd->bqhgd', p.astype(v.dtype), v)


def _blocked_dense(q, k, v, sink=None):
    bsz, lq = q.shape[:2]
    nb = lq // Q_BLOCK
    qb = jnp.moveaxis(q.reshape((bsz, nb, Q_BLOCK) + q.shape[2:]), 1, 0)
    out = lax.map(lambda qi: _attend(qi, k, v, None, sink), qb)
    return jnp.moveaxis(out, 0, 1).reshape(q.shape)


def _banded_window(q, k, v, k_ctx, v_ctx, sink):
    bsz, L = q.shape[:2]
    nb = L // Q_BLOCK
    pad = ((0, 0), (Q_BLOCK, Q_BLOCK), (0, 0), (0, 0))
    kp = jnp.pad(k, pad)
    vp = jnp.pad(v, pad)
    qb = jnp.moveaxis(q.reshape((bsz, nb, Q_BLOCK) + q.shape[2:]), 1, 0)
    r = jnp.arange(Q_BLOCK)
    cidx = jnp.arange(3 * Q_BLOCK)
    ctx_mask = jnp.ones((Q_BLOCK, k_ctx.shape[1]), dtype=bool)

    def blk(args):
        b, qi = args
        kw = lax.dynamic_slice_in_dim(kp, b * Q_BLOCK, 3 * Q_BLOCK, axis=1)
        vw = lax.dynamic_slice_in_dim(vp, b * Q_BLOCK, 3 * Q_BLOCK, axis=1)
        qpos = b * Q_BLOCK + r
        kpos = (b - 1) * Q_BLOCK + cidx
        band = (jnp.abs(qpos[:, None] - kpos[None, :]) <= WINDOW) & (kpos >= 0)[None, :] & (kpos < L)[None, :]
        mask = jnp.concatenate([band, ctx_mask], axis=1)
        return _attend(qi, jnp.concatenate([kw, k_ctx], axis=1),
                       jnp.concatenate([vw, v_ctx], axis=1), mask, sink)

    out = lax.map(blk, (jnp.arange(nb), qb))
    return jnp.moveaxis(out, 0, 1).reshape(q.shape)


def _hgrn2_chunk_scan(q, k, v, logf, s0):
    bsz, L, H, _ = q.shape
    dv = v.shape[-1]
    n = L // A_CHUNK

    def chunks(t):
        return t.reshape(bsz, n, A_CHUNK, H, t.shape[-1])

    q, k, v, logf = chunks(q), chunks(k), chunks(v), chunks(logf)
    b = jnp.cumsum(logf, axis=2)
    b_last = b[:, :, -1:]
    q_dec = q * jnp.exp(b)
    k_inv = k * jnp.exp(-b)
    k_end = k * jnp.exp(b_last - b)
    causal = jnp.tril(jnp.ones((A_CHUNK, A_CHUNK), dtype=bool))
    att = jnp.where(causal, jnp.einsum('bnthd,bnshd->bnhts', q_dec, k_inv), 0.0)
    o_intra = jnp.einsum('bnhts,bnshv->bnthv', att, v)

    def step(S, xs):
        qd, ke, vc, dec = xs
        o = jnp.einsum('bthd,bhdv->bthv', qd, S)
        S = dec[..., None] * S + jnp.einsum('bshd,bshv->bhdv', ke, vc)
        return S, o

    xs = (jnp.moveaxis(q_dec, 1, 0), jnp.moveaxis(k_end, 1, 0), jnp.moveaxis(v, 1, 0),
          jnp.moveaxis(jnp.exp(b_last[:, :, 0]), 1, 0))
    s_final, o_inter = lax.scan(step, s0, xs)
    o = o_intra + jnp.moveaxis(o_inter, 0, 1)
    return o.reshape(bsz, L, H, dv), s_final


def _layer0_mixer(h, w_in, w_out, lb, a_onorm, b_qnorm, b_knorm, ctx, rope):
    bsz, L, _ = h.shape
    aq, af_f, af_b, ai, ag, bq, bk, bv, bg = _split(h @ w_in, L0_SIZES)
    q = jax.nn.silu(aq.astype(jnp.float32)).reshape(bsz, L, A_HEADS, A_DK)
    v = ai.astype(jnp.float32).reshape(bsz, L, A_HEADS, A_DV)

    def gates(fr, lbd):
        f = lbd + (1.0 - lbd) * jax.nn.sigmoid(fr.astype(jnp.float32).reshape(bsz, L, A_HEADS, A_DK))
        return 1.0 - f, jnp.log(f)

    k_f, lf_f = gates(af_f, lb[0].reshape(A_HEADS, A_DK))
    k_b, lf_b = gates(af_b, lb[1].reshape(A_HEADS, A_DK))
    if ctx is None:
        s0f = jnp.zeros((bsz, A_HEADS, A_DK, A_DV), jnp.float32)
        s0b = s0f
    else:
        s0f = ctx[0][:, 0].astype(jnp.float32)
        s0b = ctx[0][:, 1].astype(jnp.float32)
    o_f, s_f = _hgrn2_chunk_scan(q, k_f, v, lf_f, s0f)
    flip = lambda t: jnp.flip(t, axis=1)
    o_bw, s_b = _hgrn2_chunk_scan(flip(q), flip(k_b), flip(v), flip(lf_b), s0b)
    o_a = _rmsnorm(o_f + flip(o_bw), a_onorm).reshape(bsz, L, A_W).astype(h.dtype) * jax.nn.silu(ag)
    qb = _rmsnorm(bq.reshape(bsz, L, B_KV_HEADS, B_GROUP, HEAD_DIM), b_qnorm)
    kb = _rmsnorm(bk.reshape(bsz, L, B_KV_HEADS, HEAD_DIM), b_knorm)
    vb = bv.reshape(bsz, L, B_KV_HEADS, HEAD_DIM)
    if ctx is None:
        o_b = _blocked_dense(qb, kb, vb)
    else:
        cos, sin = rope
        o_b = _blocked_dense(_apply_rope(qb, cos, sin),
                             jnp.concatenate([_apply_rope(kb, cos, sin), ctx[1]], axis=1),
                             jnp.concatenate([vb, ctx[2]], axis=1))
    o_b = o_b.reshape(bsz, L, B_W) * jax.nn.silu(bg)
    y = jnp.concatenate([o_a, o_b], axis=-1) @ w_out
    if ctx is None:
        return y, (jnp.stack([s_f, s_b], axis=1).astype(h.dtype), kb, vb)
    return y, None


def _layer1_mixer(h, w_in, w_out, qnorm, knorm, sink, ctx, rope):
    bsz, L, _ = h.shape
    cq, ck, cv, cg = _split(h @ w_in, L1_SIZES)
    q = _rmsnorm(cq.reshape(bsz, L, C_KV_HEADS, C_GROUP, HEAD_DIM), qnorm)
    k = _rmsnorm(ck.reshape(bsz, L, C_KV_HEADS, HEAD_DIM), knorm)
    v = cv.reshape(bsz, L, C_KV_HEADS, HEAD_DIM)
    sink_g = sink.reshape(C_KV_HEADS, C_GROUP)
    if ctx is None:
        o = _blocked_dense(q, k, v, sink_g)
    else:
        cos, sin = rope
        o = _banded_window(_apply_rope(q, cos, sin), _apply_rope(k, cos, sin), v, ctx[0], ctx[1], sink_g)
    y = (o.reshape(bsz, L, C_W) * jax.nn.silu(cg)) @ w_out
    if ctx is None:
        return y, (k, v)
    return y, None


def setup_inputs(seed: int = 0) -> dict:
    key = jax.random.key(seed)
    ks = jax.random.split(key, 32)
    D = D_MODEL

    def nrm(k, shape, s):
        return jax.random.normal(k, shape, jnp.float32) * s

    return {
        "x_prompt": nrm(ks[0], (BATCH, SEQ, D), 1.0),
        "x_sample": nrm(ks[1], (DEC_BATCH, DEC_SEQ, D), 1.0),
        "state_l0_hgrn": nrm(ks[2], (DEC_BATCH, 2, A_HEADS, A_DK, A_DV), 0.5),
        "cache_l0_k": nrm(ks[3], (DEC_BATCH, PAST_LEN, B_KV_HEADS, HEAD_DIM), 1.0),
        "cache_l0_v": nrm(ks[4], (DEC_BATCH, PAST_LEN, B_KV_HEADS, HEAD_DIM), 1.0),
        "cache_l1_k": nrm(ks[5], (DEC_BATCH, PAST_LEN, C_KV_HEADS, HEAD_DIM), 1.0),
        "cache_l1_v": nrm(ks[6], (DEC_BATCH, PAST_LEN, C_KV_HEADS, HEAD_DIM), 1.0),
        "c": nrm(ks[7], (DEC_BATCH, D), 1.0),
        "c_ctx": nrm(ks[8], (D,), 1.0),
        "lb_gamma": nrm(ks[9], (DEPTH + 1, 2, A_HEADS * A_DK), 0.1),
        "l0_norm": 1.0 + nrm(ks[10], (D,), 0.1),
        "l0_w_mod": nrm(ks[11], (D, 3 * D), D ** -0.5),
        "l0_b_mod": nrm(ks[12], (3 * D,), 0.02),
        "l0_w_in": nrm(ks[13], (D, L0_IN), D ** -0.5),
        "l0_w_out": nrm(ks[14], (A_W + B_W, D), (A_W + B_W) ** -0.5),
        "l0_a_onorm": 1.0 + nrm(ks[15], (A_DV,), 0.1),
        "l0_b_qnorm": 1.0 + nrm(ks[16], (HEAD_DIM,), 0.1),
        "l0_b_knorm": 1.0 + nrm(ks[17], (HEAD_DIM,), 0.1),
        "l1_norm": 1.0 + nrm(ks[18], (D,), 0.1),
        "l1_w_mod": nrm(ks[19], (D, 3 * D), D ** -0.5),
        "l1_b_mod": nrm(ks[20], (3 * D,), 0.02),
        "l1_w_in": nrm(ks[21], (D, L1_IN), D ** -0.5),
        "l1_w_out": nrm(ks[22], (C_W, D), C_W ** -0.5),
        "l1_c_qnorm": 1.0 + nrm(ks[23], (HEAD_DIM,), 0.1),
        "l1_c_knorm": 1.0 + nrm(ks[24], (HEAD_DIM,), 0.1),
        "l1_c_sink": nrm(ks[25], (C_HEADS,), 0.5),
    }


def reference(x_prompt, x_sample, state_l0_hgrn, cache_l0_k, cache_l0_v, cache_l1_k, cache_l1_v,
              c, c_ctx, lb_gamma,
              l0_norm, l0_w_mod, l0_b_mod, l0_w_in, l0_w_out, l0_a_onorm, l0_b_qnorm, l0_b_knorm,
              l1_norm, l1_w_mod, l1_b_mod, l1_w_in, l1_w_out, l1_c_qnorm, l1_c_knorm, l1_c_sink):
    lb_all = jnp.cumsum(jax.nn.softmax(lb_gamma.astype(jnp.float32), axis=0), axis=0)
    rope = _axial_rope_tables(x_sample.shape[1])
    mod_params = [(l0_norm, l0_w_mod, l0_b_mod), (l1_norm, l1_w_mod, l1_b_mod)]
    y_prompt, y_sample = x_prompt, x_sample
    for layer in range(DEPTH):
        norm_g, w_mod, b_mod = mod_params[layer]
        sh_p, sc_p, gt_p = _adaln(c_ctx, w_mod, b_mod)
        sh_s, sc_s, gt_s = _adaln(c[:, None, :], w_mod, b_mod)
        h_p = _modulate(y_prompt, norm_g, sh_p, sc_p)
        h_s = _modulate(y_sample, norm_g, sh_s, sc_s)
        if layer % 2 == 0:
            lb = lb_all[layer]
            out_p, (new_state_l0_hgrn, new_cache_l0_k, new_cache_l0_v) = _layer0_mixer(
                h_p, l0_w_in, l0_w_out, lb, l0_a_onorm, l0_b_qnorm, l0_b_knorm, None, None)
            out_s, _ = _layer0_mixer(
                h_s, l0_w_in, l0_w_out, lb, l0_a_onorm, l0_b_qnorm, l0_b_knorm,
                (state_l0_hgrn, cache_l0_k, cache_l0_v), rope)
        else:
            out_p, (new_cache_l1_k, new_cache_l1_v) = _layer1_mixer(
                h_p, l1_w_in, l1_w_out, l1_c_qnorm, l1_c_knorm, l1_c_sink, None, None)
            out_s, _ = _layer1_mixer(
                h_s, l1_w_in, l1_w_out, l1_c_qnorm, l1_c_knorm, l1_c_sink,
                (cache_l1_k, cache_l1_v), rope)
        y_prompt = y_prompt + gt_p * out_p
        y_sample = y_sample + gt_s * out_s
    return (y_prompt, y_sample, new_state_l0_hgrn, new_cache_l0_k, new_cache_l0_v, new_cache_l1_k, new_cache_l1_v)
```

```python
import math
import os
from contextlib import ExitStack

import numpy as np
import concourse.bass as bass
import concourse.mybir as mybir
from concourse.bass_utils import run_bass_kernel_spmd

F32 = mybir.dt.float32
BF16 = mybir.dt.bfloat16
AF = mybir.ActivationFunctionType
ALU = mybir.AluOpType

D = 2048
T = 2560
NT = 20
NTB = 5
EPS = 1e-6
SCALE = 1.0 / math.sqrt(128.0)
SEQS = [(0, 2), (2, 2), (4, 16)]


class Ctx:
    NDMA = 24

    def __init__(self, nc):
        self.nc = nc
        self.engs = {"pe": nc.tensor, "dve": nc.vector, "act": nc.scalar,
                     "pool": nc.gpsimd, "sp": nc.sync}
        self.sem = {k: nc.alloc_semaphore(name="s_" + k) for k in self.engs}
        self.cnt = {k: 0 for k in self.engs}
        self.waited = {k: {} for k in self.engs}
        self.last_w = {}
        self.readers = {}
        self.dma_sems = [nc.alloc_semaphore(name=f"s_dma{i}") for i in range(self.NDMA)]
        self.dma_val = [0] * self.NDMA
        self.dma_pool = {"sp": list(range(0, 16)), "pool": list(range(16, self.NDMA))}
        self.dma_rr = {"sp": 0, "pool": 0}

    def _deps(self, reads, writes):
        evs = []
        for r in reads:
            e = self.last_w.get(r)
            if e is not None:
                evs.append(e)
        for w in writes:
            e = self.last_w.get(w)
            if e is not None:
                evs.append(e)
            evs.extend(self.readers.get(w, ()))
        return evs

    def _wait(self, eng, evs, skip_self=False):
        best = {}
        for (name, sem, val) in evs:
            if skip_self and name == eng:
                continue
            if best.get(name, (None, 0))[1] < val:
                best[name] = (sem, val)
        wd = self.waited[eng]
        for name, (sem, val) in best.items():
            if wd.get(name, 0) < val:
                self.engs[eng].wait_ge(sem, val)
                wd[name] = val

    def _commit(self, ev, reads, writes):
        ws = set(writes)
        for r in reads:
            if r in ws:
                continue
            self.readers.setdefault(r, []).append(ev)
        for w in writes:
            self.last_w[w] = ev
            self.readers[w] = []

    EXCL = {"pj", "pms", "prot", "pv", "pS", "pO", "pL", "pm", "pT", "bA", "ptr", "po"}

    def op(self, eng, fn, reads=(), writes=()):
        reads = list(reads)
        writes = list(writes)
        ex = [r for r in reads if (r if isinstance(r, str) else r[0]) in self.EXCL]
        if ex:
            reads = [r for r in reads if r not in ex]
            writes = writes + [r for r in ex if r not in writes]
        evs = self._deps(reads, writes)
        self._wait(eng, evs, skip_self=(eng == "pe"))
        inst = fn()
        inst.then_inc(self.sem[eng], 1)
        self.cnt[eng] += 1
        ev = (eng, self.sem[eng], self.cnt[eng])
        self._commit(ev, reads, writes)
        return ev

    def dma(self, q, out, in_, reads=(), writes=()):
        reads = list(reads)
        writes = list(writes)
        evs = self._deps(reads, writes)
        lst = self.dma_pool[q]
        k = lst[self.dma_rr[q] % len(lst)]
        self.dma_rr[q] += 1
        name = f"dma{k}"
        if self.dma_val[k] > 0:
            evs.append((name, self.dma_sems[k], self.dma_val[k]))
        self._wait(q, evs)
        self.engs[q].dma_start(out=out, in_=in_).then_inc(self.dma_sems[k], 16)
        self.dma_val[k] += 16
        ev = (name, self.dma_sems[k], self.dma_val[k])
        self._commit(ev, reads, writes)
        return ev

    def all_events(self):
        evs = [(k, self.sem[k], self.cnt[k]) for k in self.engs if self.cnt[k] > 0]
        for i in range(self.NDMA):
            if self.dma_val[i] > 0:
                evs.append((f"dma{i}", self.dma_sems[i], self.dma_val[i]))
        return evs

    def barrier_all(self):
        evs = self.all_events()
        for e in self.engs:
            self._wait(e, evs, skip_self=True)

    def finish(self):
        self._wait("sp", self.all_events(), skip_self=True)


def build_program(phases=None):
    nc = bass.Bass("TRN2", target_bir_lowering=False)

    def din(name, shape, dt=F32):
        return nc.dram_tensor(name, list(shape), dt, kind="ExternalInput").ap()

    def dout(name, shape, dt=F32):
        return nc.dram_tensor(name, list(shape), dt, kind="ExternalOutput").ap()

    def dscr(name, shape, dt=F32):
        return nc.dram_tensor(name, list(shape), dt, kind="Internal").ap()

    x = din("x", [T, D])
    crows = din("crows", [128, 2, 16])
    lbg = din("lbg", [128, 3, 16])
    st0 = din("st0", [2, 8, 128, 128])
    ckT = [din("ck0T", [128, 2, 256]), din("ck1T", [128, 4, 256])]
    cv = [din("cv0", [256, 2, 128]), din("cv1", [256, 4, 128])]
    w_mod = [din("w_mod0", [D, 3 * D]), din("w_mod1", [D, 3 * D])]
    b_mod = [din("b_mod0", [1, 3 * D]), din("b_mod1", [1, 3 * D])]
    norm_g = [din("norm0", [1, D]), din("norm1", [1, D])]
    w_in = [din("w_in0", [D, 7680]), din("w_in1", [D, 5120])]
    w_out = [din("w_out0", [D, D]), din("w_out1", [D, D])]
    onorm_d = din("onorm", [128, 1])
    qn_d = [din("qn0", [128, 1]), din("qn1", [128, 1])]
    kn_d = [din("kn0", [128, 1]), din("kn1", [128, 1])]
    sink_d = din("sink", [1, 16])
    ident_d = din("ident", [128, 128])
    maskF_d = din("maskF", [128, 512])
    mfb_d = din("mfb", [128, 2, 128])
    cm4_d = din("cm4", [128, 4, 128])
    RT_d = din("RT", [128, 128])
    cosT_d = din("cosT", [128, 2048])
    sinT_d = din("sinT", [128, 2048])
    wbias_d = din("wbias", [128, 2, 128])

    y = dout("y", [T, D])
    nstate = dout("nstate", [2, 2, 8, 128, 128])
    nk = [dout("nk0", [512, 2, 128]), dout("nk1", [512, 4, 128])]
    nv = [dout("nv0", [512, 2, 128]), dout("nv1", [512, 4, 128])]

    gts = dscr("gts", [4, D])
    oTs = [dscr("oT0", [D, T], BF16), dscr("oT1", [D, T], BF16)]
    y1 = dscr("y1", [T, D])

    c = Ctx(nc)

    uid = [0]

    def sb(es, name, shape, dt):
        uid[0] += 1
        return es.enter_context(nc.sbuf_tensor(f"{name}_{uid[0]}", list(shape), dt))

    def ps(es, name, shape, dt=F32):
        uid[0] += 1
        return es.enter_context(nc.psum_tensor(f"{name}_{uid[0]}", list(shape), dt))

    def mm_group(out, pairs, reads, writes):
        def f():
            n = len(pairs)
            inst = None
            for i, (l, r) in enumerate(pairs):
                inst = nc.tensor.matmul(out, lhsT=l, rhs=r, start=(i == 0), stop=(i == n - 1))
            return inst
        return c.op("pe", f, reads, writes)

    def wview(w, c0, ncols):
        return w[:, c0:c0 + ncols].rearrange("(k p) n -> p k n", p=128)

    with ExitStack() as top:
        identb = sb(top, "identb", [128, 128], BF16)
        identf = sb(top, "identf", [128, 128], F32)
        onesb = sb(top, "onesb", [128, 128], BF16)
        onesf = sb(top, "onesf", [128, 128], F32)
        epsc = sb(top, "epsc", [128, 1], F32)
        hT = sb(top, "hT", [128, 16, T], BF16)
        wbuf = [sb(top, f"wbuf{i}", [128, 16, 640], BF16) for i in range(2)]
        wslot = [0]

        c.dma("sp", identf[:], ident_d, writes=["identf"])
        c.dma("pool", identb[:], ident_d, writes=["identb"])
        c.op("dve", lambda: nc.vector.memset(onesb[:], 1.0), writes=["onesb"])
        c.op("dve", lambda: nc.vector.memset(onesf[:], 1.0 / 128.0), writes=["onesf"])
        c.op("dve", lambda: nc.vector.memset(epsc[:], EPS), writes=["epsc"])

        def load_w(w, c0, ncols):
            s = wslot[0]
            wslot[0] = 1 - s
            c.dma("pool", wbuf[s][:, :, 0:ncols], wview(w, c0, ncols), writes=[("wb", s)])
            return wbuf[s], ("wb", s)

        hkeys = lambda tb: [("hT", tt) for tt in range(tb * 4, tb * 4 + 4)]
        allh = [("hT", tt) for tt in range(NT)]

        def phase_mod_h(L):
            with ExitStack() as es:
                G = sb(es, "G", [128, 2, D], F32)
                SH = sb(es, "SH", [128, 2, D], F32)
                with ExitStack() as es2:
                    crs = sb(es2, "crs", [128, 2, 16], F32)
                    scl = sb(es2, "scl", [128, 2, 16], F32)
                    srep = sb(es2, "srep", [128, 2, 16, 128], BF16)
                    bb = [sb(es2, f"bb{i}", [128, 512], F32) for i in range(2)]
                    gb = [sb(es2, f"gb{i}", [128, 512], F32) for i in range(2)]
                    tmp = [sb(es2, f"mtmp{i}", [128, 512], F32) for i in range(2)]
                    pm = [ps(es2, f"pm{i}", [128, 512]) for i in range(2)]
                    c.dma("sp", crs[:], crows, writes=["crs"])
                    c.op("act", lambda: nc.scalar.activation(scl[:], crs[:], AF.Silu), ["crs"], ["scl"])
                    c.op("dve", lambda: nc.vector.tensor_copy(
                        srep[:], scl[:].unsqueeze(3).to_broadcast([128, 2, 16, 128])), ["scl"], ["srep"])
                    for blk in range(12):
                        kind, cb = divmod(blk, 4)
                        b = blk % 2
                        wb, wk = load_w(w_mod[L], blk * 512, 512)
                        c.dma("sp", bb[b][:], b_mod[L][:, blk * 512:(blk + 1) * 512].partition_broadcast(128),
                              writes=[("bb", b)])
                        if kind == 1:
                            c.dma("sp", gb[b][:], norm_g[L][:, cb * 512:(cb + 1) * 512].partition_broadcast(128),
                                  writes=[("gb", b)])
                        cols = slice(cb * 512, (cb + 1) * 512)
                        for g in range(2):
                            mm_group(pm[g][:], [(srep[:, g, k, :], wb[:, k, 0:512]) for k in range(16)],
                                     [wk, "srep"], [("pm", g)])
                            if kind == 0:
                                c.op("dve", lambda g=g, b=b, cols=cols: nc.vector.tensor_tensor(
                                    out=SH[:, g, cols], in0=pm[g][:], in1=bb[b][:], op=ALU.add),
                                    [("pm", g), ("bb", b)], [("SH", g, cb)])
                            elif kind == 1:
                                c.op("dve", lambda g=g, b=b: nc.vector.tensor_tensor(
                                    out=tmp[g][:], in0=pm[g][:], in1=bb[b][:], op=ALU.add),
                                    [("pm", g), ("bb", b)], [("mtmp", g)])
                                c.op("dve", lambda g=g, b=b, cols=cols: nc.vector.scalar_tensor_tensor(
                                    out=G[:, g, cols], in0=tmp[g][:], scalar=1.0, in1=gb[b][:],
                                    op0=ALU.add, op1=ALU.mult),
                                    [("mtmp", g), ("gb", b)], [("G", g, cb)])
                            else:
                                c.op("dve", lambda g=g, b=b: nc.vector.tensor_tensor(
                                    out=tmp[g][:], in0=pm[g][:], in1=bb[b][:], op=ALU.add),
                                    [("pm", g), ("bb", b)], [("mtmp", g)])
                                c.dma("sp", gts[L * 2 + g:L * 2 + g + 1, cols], tmp[g][0:1, :],
                                      reads=[("mtmp", g)], writes=[("gts", L, g, cb)])
                    c.barrier_all()
                with ExitStack() as es2:
                    xt = [sb(es2, f"xt{i}", [128, D], F32) for i in range(2)]
                    junk = sb(es2, "junk", [128, D], BF16)
                    st = sb(es2, "st", [128, 8], F32)
                    t1 = sb(es2, "t1", [128, D], F32)
                    hb = [sb(es2, f"hb{i}", [128, D], BF16) for i in range(2)]
                    pT = [ps(es2, f"pT{i}", [128, 8, 128], BF16) for i in range(4)]
                    GK = [[("G", g, cb) for cb in range(4)] for g in range(2)]
                    SK = [[("SH", g, cb) for cb in range(4)] for g in range(2)]
                    for tt in range(NT):
                        b = tt % 2
                        g = 0 if tt < 4 else 1
                        rows = slice(tt * 128, (tt + 1) * 128)
                        if L == 0:
                            c.dma("sp", xt[b][:], x[rows, :], writes=[("xt", b)])
                        else:
                            c.dma("sp", xt[b][:], y1[rows, :], reads=[("y1", tt, cb) for cb in range(4)],
                                  writes=[("xt", b)])
                        c.op("act", lambda b=b: nc.scalar.activation(junk[:], xt[b][:], AF.Square,
                                                                     accum_out=st[:, b:b + 1]),
                             [("xt", b)], ["junk", ("ssq", b)])
                        c.op("act", lambda b=b: nc.scalar.activation(st[:, 2 + b:3 + b], st[:, b:b + 1], AF.Sqrt,
                                                                     scale=1.0 / D, bias=epsc[:]),
                             [("ssq", b), "epsc"], [("std", b)])
                        c.op("dve", lambda b=b: nc.vector.reciprocal(st[:, 4 + b:5 + b], st[:, 2 + b:3 + b]),
                             [("std", b)], [("rstd", b)])
                        c.op("dve", lambda b=b, g=g: nc.vector.scalar_tensor_tensor(
                            out=t1[:], in0=xt[b][:], scalar=st[:, 4 + b:5 + b], in1=G[:, g, :],
                            op0=ALU.mult, op1=ALU.mult),
                            [("xt", b), ("rstd", b)] + GK[g], ["t1"])
                        c.op("pool", lambda b=b, g=g: nc.gpsimd.tensor_tensor(
                            out=hb[b][:], in0=t1[:], in1=SH[:, g, :], op=ALU.add),
                            ["t1"] + SK[g], [("hb", b)])
                        for half in range(2):
                            pp = pT[b * 2 + half]

                            def tr(pp=pp, b=b, half=half):
                                inst = None
                                for kk in range(8):
                                    k = half * 8 + kk
                                    inst = nc.tensor.transpose(pp[:, kk, :], hb[b][:, k * 128:(k + 1) * 128], identb[:])
                                return inst
                            c.op("pe", tr, [("hb", b), "identb"], [("pT", b, half)])
                            eng = "act" if half == 0 else "dve"
                            dst = hT[:, half * 8:(half + 1) * 8, tt * 128:(tt + 1) * 128]
                            if eng == "act":
                                c.op("act", lambda pp=pp, dst=dst: nc.scalar.copy(dst, pp[:]),
                                     [("pT", b, half)], [("hT", tt, half)])
                            else:
                                c.op("dve", lambda pp=pp, dst=dst: nc.vector.tensor_copy(dst, pp[:]),
                                     [("pT", b, half)], [("hT", tt, half)])
                    c.barrier_all()
                c.barrier_all()

        def featnorm(es_bufs, pj, gcol, rope_cols, out_bf, out_f32=None, pjkey=None, outkey=None):
            (sq32, std32, rstd32, kn32, knb, tt_, uu_, pms, prot, cosT, sinT, RTb) = es_bufs
            n = pj.shape[-1]
            c.op("act", lambda: nc.scalar.activation(sq32[:, :n], pj, AF.Square), [pjkey], ["sq32"])
            c.op("pe", lambda: nc.tensor.matmul(pms[:, :n], lhsT=onesf[:], rhs=sq32[:, :n], start=True, stop=True),
                 ["sq32", "onesf"], ["pms"])
            c.op("act", lambda: nc.scalar.activation(std32[:, :n], pms[:, :n], AF.Sqrt, bias=epsc[:]),
                 ["pms", "epsc"], ["std32"])
            c.op("dve", lambda: nc.vector.reciprocal(rstd32[:, :n], std32[:, :n]), ["std32"], ["rstd32"])
            dst = kn32[:, :n] if out_f32 is None else out_f32
            dkey = "kn32" if out_f32 is None else outkey + ("f32",)
            c.op("dve", lambda: nc.vector.scalar_tensor_tensor(
                out=dst, in0=pj, scalar=gcol, in1=rstd32[:, :n], op0=ALU.mult, op1=ALU.mult),
                [pjkey, "rstd32", "consts"], [dkey])
            if rope_cols is None:
                c.op("pool", lambda: nc.gpsimd.tensor_copy(out_bf, dst), [dkey], [outkey])
            else:
                c.op("pool", lambda: nc.gpsimd.tensor_copy(knb[:, :n], dst), [dkey], ["knb"])
                c.op("pe", lambda: nc.tensor.matmul(prot[:, :n], lhsT=RTb[:], rhs=knb[:, :n], start=True, stop=True),
                     ["knb", "consts"], ["prot"])
                c.op("pool", lambda: nc.gpsimd.tensor_tensor(out=tt_[:, :n], in0=dst, in1=cosT[:, rope_cols],
                                                             op=ALU.mult), [dkey, "consts"], ["ropet"])
                c.op("dve", lambda: nc.vector.tensor_tensor(out=uu_[:, :n], in0=prot[:, :n], in1=sinT[:, rope_cols],
                                                            op=ALU.mult), ["prot", "consts"], ["ropeu"])
                c.op("pool", lambda: nc.gpsimd.tensor_tensor(out=out_bf, in0=tt_[:, :n], in1=uu_[:, :n], op=ALU.add),
                     ["ropet", "ropeu"], [outkey])

        def phase_attn(L):
            nkv = 2 if L == 0 else 4
            if L == 0:
                kvc0 = 5120
                qc0 = 5120 + 512
                orow0 = 8
            else:
                kvc0 = 0
                qc0 = 1024
                orow0 = 0
            with ExitStack() as es:
                sq32 = sb(es, "sq32", [128, 512], F32)
                std32 = sb(es, "std32", [128, 512], F32)
                rstd32 = sb(es, "rstd32", [128, 512], F32)
                kn32 = sb(es, "kn32", [128, 512], F32)
                knb = sb(es, "knb", [128, 512], BF16)
                tt_ = sb(es, "ropet", [128, 512], F32)
                uu_ = sb(es, "ropeu", [128, 512], F32)
                cosT = sb(es, "cosT", [128, 2048], F32)
                sinT = sb(es, "sinT", [128, 2048], F32)
                RTb = sb(es, "RTb", [128, 128], BF16)
                gq = sb(es, "gq", [128, 1], F32)
                gk = sb(es, "gk", [128, 1], F32)
                esk = sb(es, "esk", [128, 16], F32)
                wbias = sb(es, "wbias", [128, 2, 128], BF16)
                KT = sb(es, "KT", [128, T], BF16)
                KcT = sb(es, "KcT", [128, 256], BF16)
                V = sb(es, "V", [128, NT, 128], BF16)
                Vc = sb(es, "Vc", [128, 2, 128], BF16)
                QT = sb(es, "QT", [128, T], BF16)
                GTh = sb(es, "GTh", [128, T], BF16)
                oTh = sb(es, "oTh", [128, T], BF16)
                kf32 = sb(es, "kf32", [128, 512], F32)
                kout = [sb(es, f"kout{i}", [128, 128], F32) for i in range(2)]
                vout = [sb(es, f"vout{i}", [128, 128], F32) for i in range(2)]
                PT = [sb(es, f"PT{i}", [128, 512], BF16) for i in range(2)]
                rl = sb(es, "rl", [128, 512], F32)
                o32 = sb(es, "o32", [128, 512], F32)
                pj = ps(es, "pj", [128, 512])
                pms = ps(es, "pms", [128, 512])
                prot = ps(es, "prot", [128, 512])
                pv = ps(es, "pv", [128, 512])
                pS = [ps(es, f"pS{i}", [128, 512]) for i in range(2)]
                pO = ps(es, "pO", [128, 512])
                pL = ps(es, "pL", [128, 512])
                nb = (sq32, std32, rstd32, kn32, knb, tt_, uu_, pms, prot, cosT, sinT, RTb)

                c.dma("sp", cosT[:], cosT_d, writes=["consts"])
                c.dma("sp", sinT[:], sinT_d, writes=["consts"])
                c.dma("pool", RTb[:], RT_d, writes=["consts"])
                c.dma("sp", gq[:], qn_d[L], writes=["consts"])
                c.dma("sp", gk[:], kn_d[L], writes=["consts"])
                c.dma("pool", wbias[:], wbias_d, writes=["consts"])
                c.dma("sp", esk[:], sink_d.partition_broadcast(128), writes=["esk0"])
                c.op("act", lambda: nc.scalar.activation(esk[:], esk[:], AF.Exp), ["esk0"], ["esk0", "consts"])

                CUT = int(os.environ.get("MK_CUT", "99"))
                for j in range(nkv):
                    if CUT <= 1:
                        break
                    wb, wk = load_w(w_in[L], kvc0 + j * 256, 256)
                    SUB = os.environ.get("MK_SUB", "").split(",")
                    if "novc" not in SUB:
                        c.dma("pool", KcT[:], ckT[L][:, j, :], writes=["KcT"])
                        c.dma("pool", Vc[:], cv[L][:, j, :].rearrange("(t p) d -> p t d", p=128), writes=["Vc"])
                    for tb in range(NTB):
                        cols = slice(tb * 512, (tb + 1) * 512)
                        mm_group(pj[:], [(wb[:, k, 0:128], hT[:, k, cols]) for k in range(16)],
                                 [wk] + hkeys(tb), ["pj"])
                        if tb == 0:
                            featnorm(nb, pj[:], gk[:, 0:1], None, KT[:, cols], out_f32=kf32[:],
                                     pjkey="pj", outkey=("KT", tb))
                            for t4 in range(4):
                                if "notr" in SUB:
                                    break
                                b = t4 % 2
                                c.op("pe", lambda t4=t4: nc.tensor.transpose(
                                    pv[:, 0:128], kf32[:, t4 * 128:(t4 + 1) * 128], identf[:]),
                                    [("KT", tb, "f32"), "identf"], ["pv"])
                                c.op("act", lambda b=b: nc.scalar.copy(kout[b][:], pv[:, 0:128]), ["pv"], [("kout", b)])
                                c.dma("sp", nk[L][t4 * 128:(t4 + 1) * 128, j, :], kout[b][:],
                                      reads=[("kout", b)], writes=[("nk", t4, j)])
                        else:
                            featnorm(nb, pj[:], gk[:, 0:1], None if "norope" in SUB else slice((tb - 1) * 512, tb * 512),
                                     KT[:, cols], pjkey="pj", outkey=("KT", tb))
                    if CUT <= 2:
                        break
                    for tt in range(NT):
                        tcols = slice(tt * 128, (tt + 1) * 128)
                        mm_group(pv[:, 0:128], [(hT[:, k, tcols], wb[:, k, 128:256]) for k in range(16)],
                                 [wk, ("hT", tt)], ["pv"])
                        c.op("act", lambda tt=tt: nc.scalar.copy(V[:, tt, :], pv[:, 0:128]), ["pv"], [("V", tt)])
                        if tt < 4:
                            b = tt % 2
                            c.op("dve", lambda b=b: nc.vector.tensor_copy(vout[b][:], pv[:, 0:128]),
                                 ["pv"], [("vout", b)])
                            c.dma("sp", nv[L][tt * 128:(tt + 1) * 128, j, :], vout[b][:],
                                  reads=[("vout", b)], writes=[("nv", tt, j)])
                    if CUT <= 3:
                        break
                    for h in range(4 * j, 4 * j + 4):
                        wb, wk = load_w(w_in[L], qc0 + h * 256, 256)
                        for tb in range(NTB):
                            cols = slice(tb * 512, (tb + 1) * 512)
                            mm_group(pj[:], [(wb[:, k, 0:128], hT[:, k, cols]) for k in range(16)],
                                     [wk] + hkeys(tb), ["pj"])
                            featnorm(nb, pj[:], gq[:, 0:1], None if tb == 0 else slice((tb - 1) * 512, tb * 512),
                                     QT[:, cols], pjkey="pj", outkey=("QT", tb))
                            mm_group(pj[:], [(wb[:, k, 128:256], hT[:, k, cols]) for k in range(16)],
                                     [wk] + hkeys(tb), ["pj"])
                            c.op("act", lambda cols=cols: nc.scalar.activation(GTh[:, cols], pj[:], AF.Silu),
                                 ["pj"], [("GTh", tb)])
                        if CUT <= 4:
                            break
                        blocks = []
                        for (t0, n) in SEQS[:2]:
                            blocks.append((t0 * 128, 256, [("l", t0, None), ("l", t0 + 1, None)], t0 // 4))
                        if L == 0:
                            for tb in range(1, NTB):
                                keys = [("l", kt, None) for kt in range(4, NT)] + [("c", 0, None), ("c", 1, None)]
                                blocks.append((tb * 512, 512, keys, tb))
                        else:
                            for i in range(16):
                                keys = []
                                if i > 0:
                                    keys.append(("l", 4 + i - 1, 0))
                                keys.append(("l", 4 + i, None))
                                if i < 15:
                                    keys.append(("l", 4 + i + 1, 1))
                                keys += [("c", 0, None), ("c", 1, None)]
                                blocks.append(((4 + i) * 128, 128, keys, (4 + i) // 4))
                        if CUT <= 5:
                            blocks = blocks[:1]
                        if CUT <= 6:
                            blocks = blocks[:3]
                        for (q0, nq, keys, tbq) in blocks:
                            qcols = slice(q0, q0 + nq)
                            nk_ = len(keys)
                            for ki, (kind, idx, mi) in enumerate(keys):
                                p = ki % 2
                                if kind == "l":
                                    Kl = KT[:, idx * 128:(idx + 1) * 128]
                                    Vl = V[:, idx, :]
                                    kr = [("KT", idx // 4), ("V", idx)]
                                else:
                                    Kl = KcT[:, idx * 128:(idx + 1) * 128]
                                    Vl = Vc[:, idx, :]
                                    kr = ["KcT", "Vc"]

                                def smm(Kl=Kl, p=p, mi=mi, qcols=qcols, nq=nq):
                                    inst = nc.tensor.matmul(pS[p][:, :nq], lhsT=Kl, rhs=QT[:, qcols],
                                                            start=True, stop=(mi is None))
                                    if mi is not None:
                                        inst = nc.tensor.matmul(pS[p][:, :nq], lhsT=identb[:], rhs=wbias[:, mi, :],
                                                                start=False, stop=True)
                                    return inst
                                c.op("pe", smm, kr + [("QT", tbq), "consts", "identb"], [("pS", p)])
                                c.op("act", lambda p=p, nq=nq: nc.scalar.activation(
                                    PT[p][:, :nq], pS[p][:, :nq], AF.Exp, scale=SCALE), [("pS", p)], [("PT", p)])

                                def pvmm(Vl=Vl, p=p, ki=ki, nq=nq, nk_=nk_):
                                    nc.tensor.matmul(pO[:, :nq], lhsT=Vl, rhs=PT[p][:, :nq],
                                                     start=(ki == 0), stop=(ki == nk_ - 1))
                                    return nc.tensor.matmul(pL[:, :nq], lhsT=onesb[:], rhs=PT[p][:, :nq],
                                                            start=(ki == 0), stop=(ki == nk_ - 1))
                                c.op("pe", pvmm, kr + [("PT", p), "onesb"], ["pO", "pL"])
                            if L == 1:
                                c.op("dve", lambda nq=nq, h=h: nc.vector.tensor_scalar(
                                    out=rl[:, :nq], in0=pL[:, :nq], scalar1=esk[:, h:h + 1], scalar2=None,
                                    op0=ALU.add), ["pL", "consts"], ["rl0"])
                                c.op("dve", lambda nq=nq: nc.vector.reciprocal(rl[:, :nq], rl[:, :nq]), ["rl0"], ["rl", "rl0"])
                            else:
                                c.op("dve", lambda nq=nq: nc.vector.reciprocal(rl[:, :nq], pL[:, :nq]), ["pL"], ["rl"])
                            c.op("dve", lambda nq=nq: nc.vector.tensor_tensor(
                                out=o32[:, :nq], in0=pO[:, :nq], in1=rl[:, :nq], op=ALU.mult), ["pO", "rl"], ["o32"])
                            c.op("pool", lambda nq=nq, qcols=qcols: nc.gpsimd.tensor_tensor(
                                out=oTh[:, qcols], in0=o32[:, :nq], in1=GTh[:, qcols], op=ALU.mult),
                                ["o32", ("GTh", tbq)], ["oTh"])
                        r0 = (orow0 + h) * 128
                        c.dma("sp", oTs[L][r0:r0 + 128, :], oTh[:], reads=["oTh"], writes=[("oTs", L, orow0 + h)])
                c.barrier_all()

        def phase_hgrn():
            with ExitStack() as es:
                maskF = sb(es, "maskF", [128, 512], F32)
                mfb = sb(es, "mfb", [128, 2, 128], F32)
                cm4 = sb(es, "cm4", [128, 4, 128], BF16)
                onc = sb(es, "onc", [128, 1], F32)
                lbe = sb(es, "lbe", [128, 3, 16], F32)
                lbs = sb(es, "lbs", [128, 16], F32)
                lbv = sb(es, "lbv", [128, 16], F32)
                oml = sb(es, "oml", [128, 16], F32)
                noml = sb(es, "noml", [128, 16], F32)
                q32 = sb(es, "q32", [128, 512], F32)
                sg = sb(es, "sg", [128, 512], F32)
                lg = sb(es, "lg", [128, 512], F32)
                k32 = sb(es, "k32", [128, 512], F32)
                bF = sb(es, "bF", [128, 512], F32)
                bB = sb(es, "bB", [128, 512], F32)
                eb = sb(es, "eb", [128, 512], F32)
                enb = sb(es, "enb", [128, 512], F32)
                ki32 = sb(es, "ki32", [128, 512], F32)
                dec = sb(es, "dec", [128, 2, 80], F32)
                qd = [sb(es, f"qd{d}", [128, T], BF16) for d in range(2)]
                ki = [sb(es, f"ki{d}", [128, T], BF16) for d in range(2)]
                keT = [sb(es, f"keT{d}", [128, T], BF16) for d in range(2)]
                sgT = sb(es, "sgT", [128, T], BF16)
                V = sb(es, "Va", [128, NT, 128], BF16)
                OT = sb(es, "OT", [128, T], F32)
                kend = [sb(es, f"kend{d}", [128, 128], BF16) for d in range(2)]
                Vm = [sb(es, f"Vm{d}", [128, 4, 128], BF16) for d in range(2)]
                Am = [sb(es, f"Am{d}", [128, 128], BF16) for d in range(2)]
                S32 = [sb(es, f"S32_{d}", [128, 128], F32) for d in range(2)]
                Sbf = [[sb(es, f"Sbf{d}{p}", [128, 128], BF16) for p in range(2)] for d in range(2)]
                sq32 = q32
                std32 = sg
                oTh = sb(es, "hoTh", [128, T], BF16)
                pj = [ps(es, f"hpj{i}", [128, 512]) for i in range(2)]
                pv = ps(es, "hpv", [128, 512])
                bA = [ps(es, f"hbA{d}", [128, 4, 128]) for d in range(2)]
                pO = [ps(es, f"hpO{d}", [128, 512]) for d in range(2)]
                ptr = ps(es, "hptr", [128, 8, 128], BF16)

                c.dma("sp", maskF[:], maskF_d, writes=["hc"])
                c.dma("sp", mfb[:], mfb_d, writes=["hc"])
                c.dma("pool", cm4[:], cm4_d, writes=["hc"])
                c.dma("sp", onc[:], onorm_d, writes=["hc"])
                c.dma("sp", lbe[:], lbg, writes=["lbe"])
                c.op("act", lambda: nc.scalar.activation(lbe[:], lbe[:], AF.Exp), ["lbe"], ["lbe"])
                c.op("dve", lambda: nc.vector.tensor_tensor(out=lbs[:], in0=lbe[:, 0, :], in1=lbe[:, 1, :], op=ALU.add),
                     ["lbe"], ["lbs"])
                c.op("dve", lambda: nc.vector.tensor_tensor(out=lbs[:], in0=lbs[:], in1=lbe[:, 2, :], op=ALU.add),
                     ["lbe", "lbs"], ["lbs"])
                c.op("dve", lambda: nc.vector.reciprocal(lbs[:], lbs[:]), ["lbs"], ["lbs"])
                c.op("dve", lambda: nc.vector.tensor_tensor(out=lbv[:], in0=lbe[:, 0, :], in1=lbs[:], op=ALU.mult),
                     ["lbe", "lbs"], ["lbv"])
                c.op("dve", lambda: nc.vector.tensor_scalar(out=oml[:], in0=lbv[:], scalar1=-1.0, scalar2=1.0,
                                                            op0=ALU.mult, op1=ALU.add), ["lbv"], ["oml"])
                c.op("dve", lambda: nc.vector.tensor_scalar(out=noml[:], in0=oml[:], scalar1=-1.0, scalar2=None,
                                                            op0=ALU.mult), ["oml"], ["noml", "hc"])

                def bc32(t, ncol=16):
                    return t.unsqueeze(2).to_broadcast([128, ncol, 32])

                for h in range(8):
                    wb, wk = load_w(w_in[0], h * 640, 640)
                    for tb in range(NTB):
                        cols = slice(tb * 512, (tb + 1) * 512)
                        hk = [wk] + hkeys(tb)
                        mm_group(pj[0][:], [(wb[:, k, 0:128], hT[:, k, cols]) for k in range(16)], hk, [("pj", 0)])
                        c.op("act", lambda: nc.scalar.activation(q32[:], pj[0][:], AF.Silu), [("pj", 0)], ["q32"])
                        for d in range(2):
                            i = d * 8 + h
                            mm_group(pj[1][:], [(wb[:, k, 128 * (1 + d):128 * (2 + d)], hT[:, k, cols]) for k in range(16)],
                                     hk, [("pj", 1)])
                            c.op("act", lambda: nc.scalar.activation(sg[:], pj[1][:], AF.Sigmoid), [("pj", 1)], ["sg"])
                            c.op("act", lambda i=i: nc.scalar.activation(lg[:], sg[:], AF.Ln, scale=oml[:, i:i + 1],
                                                                         bias=lbv[:, i:i + 1]), ["sg", "hc"], ["lg"])
                            c.op("dve", lambda i=i: nc.vector.tensor_scalar(
                                out=k32[:], in0=sg[:], scalar1=noml[:, i:i + 1], scalar2=oml[:, i:i + 1],
                                op0=ALU.mult, op1=ALU.add), ["sg", "hc"], ["k32"])
                            c.op("dve", lambda: nc.vector.tensor_tensor_scan(bF[:], maskF[:], lg[:], 0.0, ALU.mult, ALU.add),
                                 ["lg", "hc"], ["bF"])
                            tot = bF[:].rearrange("p (c t) -> p c t", t=32)[:, :, 31]
                            dslice = dec[:, d, tb * 16:(tb + 1) * 16]
                            c.op("act", lambda tot=tot, dslice=dslice: nc.scalar.activation(dslice, tot, AF.Exp),
                                 ["bF"], [("dec", d, tb)])
                            if d == 0:
                                bsrc, bkey = bF, "bF"
                            else:
                                b3 = bB[:].rearrange("p (c t) -> p c t", t=32)
                                f3 = bF[:].rearrange("p (c t) -> p c t", t=32)
                                c.op("dve", lambda b3=b3, f3=f3, tot=tot: nc.vector.tensor_tensor(
                                    out=b3, in0=bc32(tot), in1=f3, op=ALU.subtract), ["bF"], ["bB"])
                                c.op("pool", lambda: nc.gpsimd.tensor_tensor(out=bB[:], in0=bB[:], in1=lg[:], op=ALU.add),
                                     ["bB", "lg"], ["bB"])
                                bsrc, bkey = bB, "bB"
                            c.op("act", lambda bsrc=bsrc: nc.scalar.activation(eb[:], bsrc[:], AF.Exp), [bkey], ["eb"])
                            c.op("act", lambda bsrc=bsrc: nc.scalar.activation(enb[:], bsrc[:], AF.Exp, scale=-1.0),
                                 [bkey], ["enb"])
                            c.op("dve", lambda d=d, cols=cols: nc.vector.tensor_tensor(
                                out=qd[d][:, cols], in0=q32[:], in1=eb[:], op=ALU.mult), ["q32", "eb"], [("qd", d, tb)])
                            c.op("dve", lambda: nc.vector.tensor_tensor(out=ki32[:], in0=k32[:], in1=enb[:], op=ALU.mult),
                                 ["k32", "enb"], ["ki32"])
                            c.op("pool", lambda d=d, cols=cols: nc.gpsimd.tensor_copy(ki[d][:, cols], ki32[:]),
                                 ["ki32"], [("ki", d, tb)])
                            k3 = ki32[:].rearrange("p (c t) -> p c t", t=32)
                            o3 = keT[d][:, cols].rearrange("p (c t) -> p c t", t=32)
                            c.op("pool", lambda k3=k3, o3=o3, dslice=dslice: nc.gpsimd.tensor_tensor(
                                out=o3, in0=k3, in1=bc32(dslice), op=ALU.mult),
                                ["ki32", ("dec", d, tb)], [("keT", d, tb)])
                        mm_group(pj[0][:], [(wb[:, k, 512:640], hT[:, k, cols]) for k in range(16)], hk, [("pj", 0)])
                        c.op("act", lambda cols=cols: nc.scalar.activation(sgT[:, cols], pj[0][:], AF.Silu),
                             [("pj", 0)], [("sgT", tb)])
                        for tt in range(tb * 4, tb * 4 + 4):
                            tcols = slice(tt * 128, (tt + 1) * 128)
                            mm_group(pv[:, 0:128], [(hT[:, k, tcols], wb[:, k, 384:512]) for k in range(16)],
                                     [wk, ("hT", tt)], ["pv"])
                            c.op("act", lambda tt=tt: nc.scalar.copy(V[:, tt, :], pv[:, 0:128]), ["pv"], [("V", tt)])

                    nchunk = [0, 0]

                    def tile_step(d, tt, first_visit):
                        tb = tt // 4
                        tcols = slice(tt * 128, (tt + 1) * 128)
                        c.op("pe", lambda: nc.tensor.transpose(ptr[:, d, :], keT[d][:, tcols], identb[:]),
                             [("keT", d, tb), "identb"], ["ptr"])
                        c.op("act", lambda: nc.scalar.copy(kend[d][:], ptr[:, d, :]), ["ptr"], [("kend", d)])
                        c.op("pool", lambda: nc.gpsimd.tensor_tensor(
                            out=Vm[d][:], in0=V[:, tt, :].unsqueeze(1).to_broadcast([128, 4, 128]), in1=cm4[:],
                            op=ALU.mult), [("V", tt), "hc"], [("Vm", d)])
                        c.op("pe", lambda: nc.tensor.matmul(bA[d][:, 0, :], lhsT=ki[d][:, tcols], rhs=qd[d][:, tcols],
                                                            start=True, stop=True),
                             [("ki", d, tb), ("qd", d, tb)], [("bA", d)])
                        c.op("dve", lambda: nc.vector.tensor_tensor(out=Am[d][:], in0=bA[d][:, 0, :], in1=mfb[:, d, :],
                                                                    op=ALU.mult), [("bA", d), "hc"], [("Am", d)])
                        c.op("pe", lambda: nc.tensor.matmul(pO[d][:, 0:128], lhsT=V[:, tt, :], rhs=Am[d][:],
                                                            start=True, stop=False),
                             [("V", tt), ("Am", d)], [("pO", d)])
                        order = range(4) if d == 0 else range(3, -1, -1)
                        for n_, j in enumerate(order):
                            p = nchunk[d] % 2
                            ccols = slice(tt * 128 + 32 * j, tt * 128 + 32 * j + 32)
                            c.op("pe", lambda p=p, j=j, ccols=ccols, n_=n_: nc.tensor.matmul(
                                pO[d][:, 32 * j:32 * j + 32], lhsT=Sbf[d][p][:], rhs=qd[d][:, ccols],
                                start=False, stop=(n_ == 3)),
                                [("Sbf", d, p), ("qd", d, tb)], [("pO", d)])
                            c.op("pe", lambda j=j: nc.tensor.matmul(
                                bA[d][:, 1 + (j % 2), :], lhsT=kend[d][:], rhs=Vm[d][:, j, :], start=True, stop=True),
                                [("kend", d), ("Vm", d)], [("bA", d)])
                            gch = tt * 4 + j
                            c.op("dve", lambda j=j, gch=gch: nc.vector.scalar_tensor_tensor(
                                out=S32[d][:], in0=S32[d][:], scalar=dec[:, d, gch:gch + 1], in1=bA[d][:, 1 + (j % 2), :],
                                op0=ALU.mult, op1=ALU.add),
                                [("S32", d), ("dec", d, tb), ("bA", d)], [("S32", d)])
                            c.op("act", lambda p=p: nc.scalar.copy(Sbf[d][1 - p][:], S32[d][:]),
                                 [("S32", d)], [("Sbf", d, 1 - p)])
                            nchunk[d] += 1
                        if first_visit:
                            c.op("act", lambda: nc.scalar.copy(OT[:, tcols], pO[d][:, 0:128]), [("pO", d)], [("OT", tt)])
                        else:
                            c.op("dve", lambda: nc.vector.tensor_tensor(out=OT[:, tcols], in0=pO[d][:, 0:128], in1=OT[:, tcols],
                                                                        op=ALU.add), [("pO", d), ("OT", tt)], [("OT", tt)])

                    for si, (t0, n) in enumerate(SEQS):
                        for d in range(2):
                            p = nchunk[d] % 2
                            if si < 2:
                                c.op("dve", lambda d=d: nc.vector.memset(S32[d][:], 0.0), [], [("S32", d)])
                            else:
                                c.dma("sp", S32[d][:], st0[d, h], writes=[("S32", d)])
                            c.op("act", lambda d=d, p=p: nc.scalar.copy(Sbf[d][p][:], S32[d][:]),
                                 [("S32", d)], [("Sbf", d, p)])
                        for idx in range(n):
                            tf = t0 + idx
                            tbk = t0 + n - 1 - idx
                            fv = idx < n - 1 - idx
                            tile_step(0, tf, first_visit=fv)
                            tile_step(1, tbk, first_visit=fv)
                        if si < 2:
                            for d in range(2):
                                c.dma("sp", nstate[si, d, h], S32[d][:], reads=[("S32", d)], writes=[("nstate", si, d, h)])

                    for tb in range(NTB):
                        cols = slice(tb * 512, (tb + 1) * 512)
                        ok = [("OT", tt) for tt in range(tb * 4, tb * 4 + 4)]
                        c.op("act", lambda cols=cols: nc.scalar.activation(sq32[:], OT[:, cols], AF.Square), ok, ["q32"])
                        c.op("pe", lambda: nc.tensor.matmul(pj[1][:], lhsT=onesf[:], rhs=sq32[:], start=True, stop=True),
                             ["q32", "onesf"], [("pj", 1)])
                        c.op("act", lambda: nc.scalar.activation(std32[:], pj[1][:], AF.Sqrt, bias=epsc[:]),
                             [("pj", 1), "epsc"], ["sg"])
                        c.op("dve", lambda: nc.vector.reciprocal(std32[:], std32[:]), ["sg"], ["sg"])
                        c.op("dve", lambda cols=cols: nc.vector.scalar_tensor_tensor(
                            out=sq32[:], in0=OT[:, cols], scalar=onc[:, 0:1], in1=std32[:], op0=ALU.mult, op1=ALU.mult),
                            ok + ["sg", "hc"], ["q32"])
                        c.op("pool", lambda cols=cols: nc.gpsimd.tensor_tensor(
                            out=oTh[:, cols], in0=sq32[:], in1=sgT[:, cols], op=ALU.mult),
                            ["q32", ("sgT", tb)], ["hoTh"])
                    c.dma("sp", oTs[0][h * 128:(h + 1) * 128, :], oTh[:], reads=["hoTh"], writes=[("oTs", 0, h)])
                c.barrier_all()

        def phase_out(L):
            with ExitStack() as es:
                gtb = sb(es, "gtb", [128, 2, D], F32)
                xr = [sb(es, f"xr{i}", [128, 512], F32) for i in range(2)]
                tm = [sb(es, f"otm{i}", [128, 512], F32) for i in range(2)]
                yt = [sb(es, f"yt{i}", [128, 512], F32) for i in range(2)]
                po = [ps(es, f"po{i}", [128, 512]) for i in range(2)]
                for g in range(2):
                    c.dma("sp", gtb[:, g, :], gts[L * 2 + g:L * 2 + g + 1, :].partition_broadcast(128),
                          reads=[("gts", L, g, cb) for cb in range(4)], writes=[("gtb", g)])
                for k in range(16):
                    c.dma("sp", hT[:, k, :], oTs[L][k * 128:(k + 1) * 128, :], reads=[("oTs", L, k)],
                          writes=[("hTk", k)])
                for cb in range(4):
                    ccols = slice(cb * 512, (cb + 1) * 512)
                    wb, wk = load_w(w_out[L], cb * 512, 512)
                    for tt in range(NT):
                        b = tt % 2
                        g = 0 if tt < 4 else 1
                        rows = slice(tt * 128, (tt + 1) * 128)
                        tcols = slice(tt * 128, (tt + 1) * 128)
                        mm_group(po[b][:], [(hT[:, k, tcols], wb[:, k, 0:512]) for k in range(16)],
                                 [wk] + [("hTk", k) for k in range(16)], [("po", b)])
                        if L == 0:
                            c.dma("sp", xr[b][:], x[rows, ccols], writes=[("xr", b)])
                        else:
                            c.dma("sp", xr[b][:], y1[rows, ccols], reads=[("y1", tt, cb)], writes=[("xr", b)])
                        c.op("dve", lambda b=b, g=g, ccols=ccols: nc.vector.tensor_tensor(
                            out=tm[b][:], in0=po[b][:], in1=gtb[:, g, ccols], op=ALU.mult),
                            [("po", b), ("gtb", g)], [("otm", b)])
                        c.op("pool", lambda b=b: nc.gpsimd.tensor_tensor(out=yt[b][:], in0=tm[b][:], in1=xr[b][:],
                                                                         op=ALU.add),
                             [("otm", b), ("xr", b)], [("yt", b)])
                        if L == 0:
                            c.dma("sp", y1[rows, ccols], yt[b][:], reads=[("yt", b)], writes=[("y1", tt, cb)])
                        else:
                            c.dma("sp", y[rows, ccols], yt[b][:], reads=[("yt", b)], writes=[("y", tt, cb)])
                c.barrier_all()

        plist = [("mod0", lambda: phase_mod_h(0)), ("hgrn", phase_hgrn), ("attn0", lambda: phase_attn(0)),
                 ("out0", lambda: phase_out(0)), ("mod1", lambda: phase_mod_h(1)), ("attn1", lambda: phase_attn(1)),
                 ("out1", lambda: phase_out(1))]
        for nm, fn in plist:
            if phases is None or nm in phases:
                fn()
        c.finish()
    return nc


def _consts():
    ident = np.eye(128, dtype=np.float32)
    maskF = np.ones((128, 512), np.float32)
    maskF[:, ::32] = 0.0
    s = np.arange(128)[:, None]
    t = np.arange(128)[None, :]
    same = (s // 32) == (t // 32)
    mfb = np.stack([(same & (s <= t)), (same & (s >= t))], axis=1).astype(np.float32)
    cm4 = np.zeros((128, 4, 128), np.float32)
    for j in range(4):
        cm4[32 * j:32 * j + 32, j, :] = 1.0
    R = np.zeros((128, 128), np.float32)
    for m in range(128):
        q = m // 32
        if q in (0, 2):
            R[m, m + 32] = -1.0
        else:
            R[m, m - 32] = 1.0
    RT = np.ascontiguousarray(R.T)
    n_tok = 2048
    row = (np.arange(n_tok) // 64).astype(np.float32)
    col = (np.arange(n_tok) % 64).astype(np.float32)
    inv = (10000.0 ** (-np.arange(32, dtype=np.float32) / 32)).astype(np.float32)
    ar = row[:, None] * inv
    ac = col[:, None] * inv
    ang = np.concatenate([ar, ar, ac, ac], axis=-1).astype(np.float32)
    cosT = np.ascontiguousarray(np.cos(ang).T.astype(np.float32))
    sinT = np.ascontiguousarray(np.sin(ang).T.astype(np.float32))
    b = np.arange(128)[:, None]
    a = np.arange(128)[None, :]
    NEG = -30000.0
    wbias = np.stack([np.where(b >= a, 0.0, NEG), np.where(b <= a, 0.0, NEG)], axis=1).astype(np.float32)
    return dict(ident=ident, maskF=maskF, mfb=mfb, cm4=cm4, RT=RT, cosT=cosT, sinT=sinT, wbias=wbias)


def _perm_w0(w):
    cols = []
    for h in range(8):
        for base in (0, 1024, 2048, 3072, 4096):
            cols.append(np.arange(base + h * 128, base + (h + 1) * 128))
    for j in range(2):
        cols.append(np.arange(6144 + j * 128, 6144 + (j + 1) * 128))
        cols.append(np.arange(6400 + j * 128, 6400 + (j + 1) * 128))
    for h in range(8):
        cols.append(np.arange(5120 + h * 128, 5120 + (h + 1) * 128))
        cols.append(np.arange(6656 + h * 128, 6656 + (h + 1) * 128))
    return np.ascontiguousarray(w[:, np.concatenate(cols)])


def _perm_w1(w):
    cols = []
    for j in range(4):
        cols.append(np.arange(2048 + j * 128, 2048 + (j + 1) * 128))
        cols.append(np.arange(2560 + j * 128, 2560 + (j + 1) * 128))
    for h in range(16):
        cols.append(np.arange(h * 128, (h + 1) * 128))
        cols.append(np.arange(3072 + h * 128, 3072 + (h + 1) * 128))
    return np.ascontiguousarray(w[:, np.concatenate(cols)])


def _prep(x_prompt, x_sample, state_l0_hgrn, cache_l0_k, cache_l0_v, cache_l1_k, cache_l1_v,
           c, c_ctx, lb_gamma,
           l0_norm, l0_w_mod, l0_b_mod, l0_w_in, l0_w_out, l0_a_onorm, l0_b_qnorm, l0_b_knorm,
           l1_norm, l1_w_mod, l1_b_mod, l1_w_in, l1_w_out, l1_c_qnorm, l1_c_knorm, l1_c_sink):
    f = lambda a: np.ascontiguousarray(np.asarray(a, dtype=np.float32))
    x_prompt, x_sample = f(x_prompt), f(x_sample)
    consts = _consts()
    shared = dict(
        w_mod0=f(l0_w_mod), w_mod1=f(l1_w_mod),
        b_mod0=f(l0_b_mod).reshape(1, -1), b_mod1=f(l1_b_mod).reshape(1, -1),
        norm0=f(l0_norm).reshape(1, -1), norm1=f(l1_norm).reshape(1, -1),
        w_in0=_perm_w0(f(l0_w_in)), w_in1=_perm_w1(f(l1_w_in)),
        w_out0=f(l0_w_out), w_out1=f(l1_w_out),
        onorm=f(l0_a_onorm).reshape(128, 1),
        qn0=f(l0_b_qnorm).reshape(128, 1), kn0=f(l0_b_knorm).reshape(128, 1),
        qn1=f(l1_c_qnorm).reshape(128, 1), kn1=f(l1_c_knorm).reshape(128, 1),
        sink=f(l1_c_sink).reshape(1, 16),
        lbg=np.ascontiguousarray(f(lb_gamma).reshape(3, 2, 8, 128).transpose(3, 0, 1, 2).reshape(128, 3, 16)),
        **consts,
    )
    c = f(c)
    c_ctx = f(c_ctx)
    in_maps = []
    for i in range(8):
        m = dict(shared)
        m["x"] = np.ascontiguousarray(np.concatenate(
            [x_prompt[2 * i], x_prompt[2 * i + 1], x_sample[i]], axis=0))
        cr = np.stack([c_ctx, c[i]], axis=0)
        m["crows"] = np.ascontiguousarray(cr.reshape(2, 16, 128).transpose(2, 0, 1))
        m["st0"] = f(state_l0_hgrn[i])
        m["ck0T"] = np.ascontiguousarray(f(cache_l0_k[i]).transpose(2, 1, 0))
        m["cv0"] = f(cache_l0_v[i])
        m["ck1T"] = np.ascontiguousarray(f(cache_l1_k[i]).transpose(2, 1, 0))
        m["cv1"] = f(cache_l1_v[i])
        in_maps.append(m)
    return in_maps


def kernel(**inputs):
    in_maps = _prep(**inputs)
    nc = build_program()
    res = run_bass_kernel_spmd(nc, in_maps, core_ids=list(range(8)))
    r = res.results
    y_prompt = np.stack([r[i // 2]["y"][(i % 2) * 256:(i % 2 + 1) * 256] for i in range(16)], axis=0)
    y_sample = np.stack([r[i]["y"][512:] for i in range(8)], axis=0)
    nstate = np.concatenate([r[i]["nstate"] for i in range(8)], axis=0)
    outs = [y_prompt.astype(np.float32), y_sample.astype(np.float32), nstate.astype(np.float32)]
    for nm in ("nk0", "nv0", "nk1", "nv1"):
        a = np.concatenate([r[i][nm].reshape(2, 256, r[i][nm].shape[1], 128) for i in range(8)], axis=0)
        outs.append(a.astype(np.float32))
    return tuple(outs)
```

```python
import math
import os
from contextlib import ExitStack

import numpy as np
import concourse.bass as bass
import concourse.mybir as mybir
from concourse.bass_utils import run_bass_kernel_spmd

F32 = mybir.dt.float32
BF16 = mybir.dt.bfloat16
AF = mybir.ActivationFunctionType
ALU = mybir.AluOpType

D = 2048
T = 2560
NT = 20
NTB = 5
EPS = 1e-6
SCALE = 1.0 / math.sqrt(128.0)
SEQS = [(0, 2), (2, 2), (4, 16)]


class Ctx:
    NDMA = 24

    def __init__(self, nc):
        self.nc = nc
        self.engs = {"pe": nc.tensor, "dve": nc.vector, "act": nc.scalar,
                     "pool": nc.gpsimd, "sp": nc.sync}
        self.sem = {k: nc.alloc_semaphore(name="s_" + k) for k in self.engs}
        self.cnt = {k: 0 for k in self.engs}
        self.waited = {k: {} for k in self.engs}
        self.last_w = {}
        self.readers = {}
        self.dma_sems = [nc.alloc_semaphore(name=f"s_dma{i}") for i in range(self.NDMA)]
        self.dma_val = [0] * self.NDMA
        self.dma_pool = {"sp": list(range(0, 16)), "pool": list(range(16, self.NDMA))}
        self.dma_rr = {"sp": 0, "pool": 0}

    def _deps(self, reads, writes):
        evs = []
        for r in reads:
            e = self.last_w.get(r)
            if e is not None:
                evs.append(e)
        for w in writes:
            e = self.last_w.get(w)
            if e is not None:
                evs.append(e)
            evs.extend(self.readers.get(w, ()))
        return evs

    def _wait(self, eng, evs, skip_self=False):
        best = {}
        for (name, sem, val) in evs:
            if skip_self and name == eng:
                continue
            if best.get(name, (None, 0))[1] < val:
                best[name] = (sem, val)
        wd = self.waited[eng]
        for name, (sem, val) in best.items():
            if wd.get(name, 0) < val:
                self.engs[eng].wait_ge(sem, val)
                wd[name] = val

    def _commit(self, ev, reads, writes):
        ws = set(writes)
        for r in reads:
            if r in ws:
                continue
            self.readers.setdefault(r, []).append(ev)
        for w in writes:
            self.last_w[w] = ev
            self.readers[w] = []

    EXCL = {"pj", "pms", "prot", "pv", "pS", "pO", "pL", "pm", "pT", "bA", "ptr", "po"}

    def op(self, eng, fn, reads=(), writes=()):
        reads = list(reads)
        writes = list(writes)
        ex = [r for r in reads if (r if isinstance(r, str) else r[0]) in self.EXCL]
        if ex:
            reads = [r for r in reads if r not in ex]
            writes = writes + [r for r in ex if r not in writes]
        evs = self._deps(reads, writes)
        self._wait(eng, evs, skip_self=(eng == "pe"))
        inst = fn()
        inst.then_inc(self.sem[eng], 1)
        self.cnt[eng] += 1
        ev = (eng, self.sem[eng], self.cnt[eng])
        self._commit(ev, reads, writes)
        return ev

    def dma(self, q, out, in_, reads=(), writes=()):
        reads = list(reads)
        writes = list(writes)
        evs = self._deps(reads, writes)
        lst = self.dma_pool[q]
        k = lst[self.dma_rr[q] % len(lst)]
        self.dma_rr[q] += 1
        name = f"dma{k}"
        if self.dma_val[k] > 0:
            evs.append((name, self.dma_sems[k], self.dma_val[k]))
        self._wait(q, evs)
        self.engs[q].dma_start(out=out, in_=in_).then_inc(self.dma_sems[k], 16)
        self.dma_val[k] += 16
        ev = (name, self.dma_sems[k], self.dma_val[k])
        self._commit(ev, reads, writes)
        return ev

    def all_events(self):
        evs = [(k, self.sem[k], self.cnt[k]) for k in self.engs if self.cnt[k] > 0]
        for i in range(self.NDMA):
            if self.dma_val[i] > 0:
                evs.append((f"dma{i}", self.dma_sems[i], self.dma_val[i]))
        return evs

    def barrier_all(self):
        evs = self.all_events()
        for e in self.engs:
            self._wait(e, evs, skip_self=True)

    def finish(self):
        self._wait("sp", self.all_events(), skip_self=True)


def build_program(phases=None):
    nc = bass.Bass("TRN2", target_bir_lowering=False)

    def din(name, shape, dt=F32):
        return nc.dram_tensor(name, list(shape), dt, kind="ExternalInput").ap()

    def dout(name, shape, dt=F32):
        return nc.dram_tensor(name, list(shape), dt, kind="ExternalOutput").ap()

    def dscr(name, shape, dt=F32):
        return nc.dram_tensor(name, list(shape), dt, kind="Internal").ap()

    x = din("x", [T, D])
    crows = din("crows", [128, 2, 16])
    lbg = din("lbg", [128, 3, 16])
    st0 = din("st0", [2, 8, 128, 128])
    ckT = [din("ck0T", [128, 2, 256]), din("ck1T", [128, 4, 256])]
    cv = [din("cv0", [256, 2, 128]), din("cv1", [256, 4, 128])]
    w_mod = [din("w_mod0", [D, 3 * D]), din("w_mod1", [D, 3 * D])]
    b_mod = [din("b_mod0", [1, 3 * D]), din("b_mod1", [1, 3 * D])]
    norm_g = [din("norm0", [1, D]), din("norm1", [1, D])]
    w_in = [din("w_in0", [D, 7680]), din("w_in1", [D, 5120])]
    w_out = [din("w_out0", [D, D]), din("w_out1", [D, D])]
    onorm_d = din("onorm", [128, 1])
    qn_d = [din("qn0", [128, 1]), din("qn1", [128, 1])]
    kn_d = [din("kn0", [128, 1]), din("kn1", [128, 1])]
    sink_d = din("sink", [1, 16])
    ident_d = din("ident", [128, 128])
    maskF_d = din("maskF", [128, 512])
    mfb_d = din("mfb", [128, 2, 128])
    cm4_d = din("cm4", [128, 4, 128])
    RT_d = din("RT", [128, 128])
    cosT_d = din("cosT", [128, 2048])
    sinT_d = din("sinT", [128, 2048])
    wbias4_d = din("wbias4", [128, 2, 4, 128])

    y = dout("y", [T, D])
    nstate = dout("nstate", [2, 2, 8, 128, 128])
    nk = [dout("nk0", [512, 2, 128]), dout("nk1", [512, 4, 128])]
    nv = [dout("nv0", [512, 2, 128]), dout("nv1", [512, 4, 128])]

    gts = dscr("gts", [4, D])
    oTs = [dscr("oT0", [D, T], BF16), dscr("oT1", [D, T], BF16)]
    y1 = dscr("y1", [T, D])

    c = Ctx(nc)

    uid = [0]

    def sb(es, name, shape, dt):
        uid[0] += 1
        return es.enter_context(nc.sbuf_tensor(f"{name}_{uid[0]}", list(shape), dt))

    def ps(es, name, shape, dt=F32):
        uid[0] += 1
        return es.enter_context(nc.psum_tensor(f"{name}_{uid[0]}", list(shape), dt))

    def mm_group(out, pairs, reads, writes):
        def f():
            n = len(pairs)
            inst = None
            for i, (l, r) in enumerate(pairs):
                inst = nc.tensor.matmul(out, lhsT=l, rhs=r, start=(i == 0), stop=(i == n - 1))
            return inst
        return c.op("pe", f, reads, writes)

    def wview(w, c0, ncols):
        return w[:, c0:c0 + ncols].rearrange("(k p) n -> p k n", p=128)

    with ExitStack() as top:
        identb = sb(top, "identb", [128, 128], BF16)
        identf = sb(top, "identf", [128, 128], F32)
        onesb = sb(top, "onesb", [128, 128], BF16)
        onesf = sb(top, "onesf", [128, 128], F32)
        epsc = sb(top, "epsc", [128, 1], F32)
        hT = sb(top, "hT", [128, 16, T], BF16)

        c.dma("sp", identf[:], ident_d, writes=["identf"])
        c.dma("pool", identb[:], ident_d, writes=["identb"])
        c.op("dve", lambda: nc.vector.memset(onesb[:], 1.0), writes=["onesb"])
        c.op("dve", lambda: nc.vector.memset(onesf[:], 1.0 / 128.0), writes=["onesf"])
        c.op("dve", lambda: nc.vector.memset(epsc[:], EPS), writes=["epsc"])

        def make_loader(es, ncols_max, nslots=2):
            bufs = [sb(es, f"wbuf{i}", [128, 16, ncols_max], BF16) for i in range(nslots)]
            ctr = [0]

            def load_w(w, c0, ncols):
                s = ctr[0] % nslots
                ctr[0] += 1
                c.dma("pool", bufs[s][:, :, 0:ncols], wview(w, c0, ncols), writes=[("wb", s)])
                return bufs[s], ("wb", s)
            return load_w

        hkeys = lambda tb: [("hT", tt) for tt in range(tb * 4, tb * 4 + 4)]
        allh = [("hT", tt) for tt in range(NT)]

        def phase_mod_h(L):
            with ExitStack() as es:
                G = sb(es, "G", [128, 2, D], F32)
                SH = sb(es, "SH", [128, 2, D], F32)
                with ExitStack() as es2:
                    load_w = make_loader(es2, 512)
                    crs = sb(es2, "crs", [128, 2, 16], F32)
                    scl = sb(es2, "scl", [128, 2, 16], F32)
                    srep = sb(es2, "srep", [128, 2, 16, 128], BF16)
                    bb = [sb(es2, f"bb{i}", [128, 512], F32) for i in range(2)]
                    gb = [sb(es2, f"gb{i}", [128, 512], F32) for i in range(2)]
                    tmp = [sb(es2, f"mtmp{i}", [128, 512], F32) for i in range(2)]
                    pm = [ps(es2, f"pm{i}", [128, 512]) for i in range(2)]
                    c.dma("sp", crs[:], crows, writes=["crs"])
                    c.op("act", lambda: nc.scalar.activation(scl[:], crs[:], AF.Silu), ["crs"], ["scl"])
                    c.op("dve", lambda: nc.vector.tensor_copy(
                        srep[:], scl[:].unsqueeze(3).to_broadcast([128, 2, 16, 128])), ["scl"], ["srep"])
                    for blk in range(12):
                        kind, cb = divmod(blk, 4)
                        b = blk % 2
                        wb, wk = load_w(w_mod[L], blk * 512, 512)
                        c.dma("sp", bb[b][:], b_mod[L][:, blk * 512:(blk + 1) * 512].partition_broadcast(128),
                              writes=[("bb", b)])
                        if kind == 1:
                            c.dma("sp", gb[b][:], norm_g[L][:, cb * 512:(cb + 1) * 512].partition_broadcast(128),
                                  writes=[("gb", b)])
                        cols = slice(cb * 512, (cb + 1) * 512)
                        for g in range(2):
                            mm_group(pm[g][:], [(srep[:, g, k, :], wb[:, k, 0:512]) for k in range(16)],
                                     [wk, "srep"], [("pm", g)])
                            if kind == 0:
                                c.op("dve", lambda g=g, b=b, cols=cols: nc.vector.tensor_tensor(
                                    out=SH[:, g, cols], in0=pm[g][:], in1=bb[b][:], op=ALU.add),
                                    [("pm", g), ("bb", b)], [("SH", g, cb)])
                            elif kind == 1:
                                c.op("dve", lambda g=g, b=b: nc.vector.tensor_tensor(
                                    out=tmp[g][:], in0=pm[g][:], in1=bb[b][:], op=ALU.add),
                                    [("pm", g), ("bb", b)], [("mtmp", g)])
                                c.op("dve", lambda g=g, b=b, cols=cols: nc.vector.scalar_tensor_tensor(
                                    out=G[:, g, cols], in0=tmp[g][:], scalar=1.0, in1=gb[b][:],
                                    op0=ALU.add, op1=ALU.mult),
                                    [("mtmp", g), ("gb", b)], [("G", g, cb)])
                            else:
                                c.op("dve", lambda g=g, b=b: nc.vector.tensor_tensor(
                                    out=tmp[g][:], in0=pm[g][:], in1=bb[b][:], op=ALU.add),
                                    [("pm", g), ("bb", b)], [("mtmp", g)])
                                c.dma("sp", gts[L * 2 + g:L * 2 + g + 1, cols], tmp[g][0:1, :],
                                      reads=[("mtmp", g)], writes=[("gts", L, g, cb)])
                    c.barrier_all()
                with ExitStack() as es2:
                    xt = [sb(es2, f"xt{i}", [128, D], F32) for i in range(2)]
                    junk = sb(es2, "junk", [128, D], BF16)
                    st = sb(es2, "st", [128, 8], F32)
                    t1 = sb(es2, "t1", [128, D], F32)
                    hb = [sb(es2, f"hb{i}", [128, D], BF16) for i in range(2)]
                    pT = [ps(es2, f"pT{i}", [128, 8, 128], BF16) for i in range(4)]
                    GK = [[("G", g, cb) for cb in range(4)] for g in range(2)]
                    SK = [[("SH", g, cb) for cb in range(4)] for g in range(2)]
                    for tt in range(NT):
                        b = tt % 2
                        g = 0 if tt < 4 else 1
                        rows = slice(tt * 128, (tt + 1) * 128)
                        if L == 0:
                            c.dma("sp", xt[b][:], x[rows, :], writes=[("xt", b)])
                        else:
                            c.dma("sp", xt[b][:], y1[rows, :], reads=[("y1", tt, cb) for cb in range(4)],
                                  writes=[("xt", b)])
                        c.op("act", lambda b=b: nc.scalar.activation(junk[:], xt[b][:], AF.Square,
                                                                     accum_out=st[:, b:b + 1]),
                             [("xt", b)], ["junk", ("ssq", b)])
                        c.op("act", lambda b=b: nc.scalar.activation(st[:, 2 + b:3 + b], st[:, b:b + 1], AF.Sqrt,
                                                                     scale=1.0 / D, bias=epsc[:]),
                             [("ssq", b), "epsc"], [("std", b)])
                        c.op("dve", lambda b=b: nc.vector.reciprocal(st[:, 4 + b:5 + b], st[:, 2 + b:3 + b]),
                             [("std", b)], [("rstd", b)])
                        c.op("dve", lambda b=b, g=g: nc.vector.scalar_tensor_tensor(
                            out=t1[:], in0=xt[b][:], scalar=st[:, 4 + b:5 + b], in1=G[:, g, :],
                            op0=ALU.mult, op1=ALU.mult),
                            [("xt", b), ("rstd", b)] + GK[g], ["t1"])
                        c.op("pool", lambda b=b, g=g: nc.gpsimd.tensor_tensor(
                            out=hb[b][:], in0=t1[:], in1=SH[:, g, :], op=ALU.add),
                            ["t1"] + SK[g], [("hb", b)])
                        for half in range(2):
                            pp = pT[b * 2 + half]

                            def tr(pp=pp, b=b, half=half):
                                inst = None
                                for kk in range(8):
                                    k = half * 8 + kk
                                    inst = nc.tensor.transpose(pp[:, kk, :], hb[b][:, k * 128:(k + 1) * 128], identb[:])
                                return inst
                            c.op("pe", tr, [("hb", b), "identb"], [("pT", b, half)])
                            eng = "act" if half == 0 else "dve"
                            dst = hT[:, half * 8:(half + 1) * 8, tt * 128:(tt + 1) * 128]
                            if eng == "act":
                                c.op("act", lambda pp=pp, dst=dst: nc.scalar.copy(dst, pp[:]),
                                     [("pT", b, half)], [("hT", tt, half)])
                            else:
                                c.op("dve", lambda pp=pp, dst=dst: nc.vector.tensor_copy(dst, pp[:]),
                                     [("pT", b, half)], [("hT", tt, half)])
                    c.barrier_all()
                c.barrier_all()

        def run_tasks(factories, width):
            it = iter(factories)
            active = {}
            free = list(range(width))
            while True:
                while free:
                    f = next(it, None)
                    if f is None:
                        break
                    sl = free.pop(0)
                    active[sl] = f(sl)
                if not active:
                    break
                for sl in sorted(active):
                    try:
                        next(active[sl])
                    except StopIteration:
                        del active[sl]
                        free.append(sl)

        def phase_attn(L):
            nkv = 2 if L == 0 else 4
            G = 1 if L == 0 else 4
            if L == 0:
                kvc0, qc0, orow0 = 5120, 5120 + 512, 8
            else:
                kvc0, qc0, orow0 = 0, 1024, 0
            with ExitStack() as es:
                wsl = [sb(es, f"aw{i}", [128, 16, 256], BF16) for i in range(3)]
                wctr = [0]

                def load_w(c0):
                    s = wctr[0] % 3
                    wctr[0] += 1
                    c.dma("pool", wsl[s][:], wview(w_in[L], c0, 256), writes=[("aw", s)])
                    return wsl[s], ("aw", s)

                T_sq = [sb(es, f"sq{i}", [128, 512], F32) for i in range(2)]
                T_sd = [sb(es, f"sd{i}", [128, 512], F32) for i in range(2)]
                T_kb = [sb(es, f"kb{i}", [128, 512], BF16) for i in range(2)]
                T_t = [sb(es, f"rt{i}", [128, 512], F32) for i in range(2)]
                T_u = [sb(es, f"ru{i}", [128, 512], F32) for i in range(2)]
                cosT = sb(es, "cosT", [128, 2048], F32)
                sinT = sb(es, "sinT", [128, 2048], F32)
                RTb = sb(es, "RTb", [128, 128], BF16)
                gq = sb(es, "gq", [128, 1], F32)
                gk = sb(es, "gk", [128, 1], F32)
                esk = sb(es, "esk", [128, 16], F32)
                wb4 = sb(es, "wb4", [128, 2, 4, 128], BF16)
                KT = sb(es, "KT", [128, T], BF16)
                KcT = sb(es, "KcT", [128, 256], BF16)
                V = sb(es, "V", [128, NT, 128], BF16)
                Vc = sb(es, "Vc", [128, 2, 128], BF16)
                QTg = sb(es, "QTg", [128, G, T], BF16)
                oThg = sb(es, "oThg", [128, G, T], BF16)
                kf32 = sb(es, "kf32", [128, 512], F32)
                kout = [sb(es, f"kout{i}", [128, 128], F32) for i in range(2)]
                vout = [sb(es, f"vout{i}", [128, 128], F32) for i in range(2)]
                PT = [sb(es, f"PT{i}", [128, 512], BF16) for i in range(2)]
                rl = [sb(es, f"rl{i}", [128, 512], F32) for i in range(2)]
                pj = ps(es, "pj", [128, 512])
                pms = ps(es, "pms", [128, 512])
                pS = [ps(es, f"pS{i}", [128, 512]) for i in range(2)]
                pO = [ps(es, f"pO{i}", [128, 512]) for i in range(2)]
                pL = [ps(es, f"pL{i}", [128, 512]) for i in range(2)]
                pbank = [(pj, "pj"), (pS[0], ("pS", 0))]
                rbank = [(pO[0], ("pO", 0)), (pO[1], ("pO", 1))]
                tbank = (pS[1], ("pS", 1))
                msbank = [(pms, "pms"), (pL[0], ("pL", 0))]

                c.dma("sp", cosT[:], cosT_d, writes=["consts"])
                c.dma("sp", sinT[:], sinT_d, writes=["consts"])
                c.dma("pool", RTb[:], RT_d, writes=["consts"])
                c.dma("sp", gq[:], qn_d[L], writes=["consts"])
                c.dma("sp", gk[:], kn_d[L], writes=["consts"])
                c.dma("pool", wb4[:], wbias4_d, writes=["consts"])
                c.dma("sp", esk[:], sink_d.partition_broadcast(128), writes=["esk0"])
                c.op("act", lambda: nc.scalar.activation(esk[:], esk[:], AF.Exp), ["esk0"], ["esk0", "consts"])

                def fn_task(s, pjt, pjk, gcol, rope_cols, out_bf, outkey, out_f32=None):
                    sq, sd, kb, tt_, uu_ = T_sq[s], T_sd[s], T_kb[s], T_t[s], T_u[s]
                    pmt, pmk = msbank[s]
                    c.op("act", lambda: nc.scalar.activation(sq[:], pjt, AF.Square), [pjk], [("sq", s)])
                    yield
                    c.op("pe", lambda: nc.tensor.matmul(pmt[:], lhsT=onesf[:], rhs=sq[:], start=True, stop=True),
                         [("sq", s), "onesf"], [pmk])
                    yield
                    c.op("act", lambda: nc.scalar.activation(sd[:], pmt[:], AF.Sqrt, bias=epsc[:]),
                         [pmk, "epsc"], [("sd", s)])
                    yield
                    c.op("dve", lambda: nc.vector.reciprocal(sd[:], sd[:]), [("sd", s)], [("sd", s)])
                    yield
                    if out_f32 is None:
                        dst, dkey = sq[:], ("sq", s)
                    else:
                        dst, dkey = out_f32, outkey + ("f32",)
                    c.op("dve", lambda: nc.vector.scalar_tensor_tensor(
                        out=dst, in0=pjt, scalar=gcol, in1=sd[:], op0=ALU.mult, op1=ALU.mult),
                        [pjk, ("sd", s), "consts"], [dkey])
                    yield
                    if rope_cols is None:
                        c.op("pool", lambda: nc.gpsimd.tensor_copy(out_bf, dst), [dkey], [outkey])
                        yield
                    else:
                        c.op("pool", lambda: nc.gpsimd.tensor_copy(kb[:], dst), [dkey], [("kb", s)])
                        yield
                        prt, prk = rbank[s]
                        c.op("pe", lambda: nc.tensor.matmul(prt[:], lhsT=RTb[:], rhs=kb[:], start=True, stop=True),
                             [("kb", s), "consts"], [prk])
                        yield
                        c.op("pool", lambda: nc.gpsimd.tensor_tensor(out=tt_[:], in0=dst, in1=cosT[:, rope_cols],
                                                                     op=ALU.mult), [dkey, "consts"], [("rt", s)])
                        yield
                        c.op("dve", lambda: nc.vector.tensor_tensor(out=uu_[:], in0=prt[:], in1=sinT[:, rope_cols],
                                                                    op=ALU.mult), [prk, "consts"], [("ru", s)])
                        yield
                        c.op("pool", lambda: nc.gpsimd.tensor_tensor(out=out_bf, in0=tt_[:], in1=uu_[:], op=ALU.add),
                             [("rt", s), ("ru", s)], [outkey])
                        yield

                def rope_of(tb):
                    return None if tb == 0 else slice((tb - 1) * 512, tb * 512)

                def k_task(sl, wb, wk, j, tb):
                    cols = slice(tb * 512, (tb + 1) * 512)
                    pjt, pjk = pbank[sl]
                    mm_group(pjt[:], [(wb[:, k, 0:128], hT[:, k, cols]) for k in range(16)], [wk], [pjk])
                    yield
                    if tb == 0:
                        yield from fn_task(sl, pjt[:], pjk, gk[:, 0:1], None, KT[:, cols], ("KT", tb), out_f32=kf32[:])
                        for t4 in range(4):
                            b = t4 % 2
                            pvt, pvk = tbank
                            c.op("pe", lambda: nc.tensor.transpose(
                                pvt[:, 0:128], kf32[:, t4 * 128:(t4 + 1) * 128], identf[:]),
                                [("KT", tb, "f32"), "identf"], [pvk])
                            yield
                            c.op("act", lambda: nc.scalar.copy(kout[b][:], pvt[:, 0:128]), [pvk], [("kout", b)])
                            yield
                            c.dma("sp", nk[L][t4 * 128:(t4 + 1) * 128, j, :], kout[b][:],
                                  reads=[("kout", b)], writes=[("nk", t4, j)])
                    else:
                        yield from fn_task(sl, pjt[:], pjk, gk[:, 0:1], rope_of(tb), KT[:, cols], ("KT", tb))

                def v_task(sl, wb, wk, j, tt):
                    tcols = slice(tt * 128, (tt + 1) * 128)
                    pvt, pvk = pbank[sl]
                    mm_group(pvt[:, 0:128], [(hT[:, k, tcols], wb[:, k, 128:256]) for k in range(16)], [wk], [pvk])
                    yield
                    if tt < 4:
                        b = sl
                        c.op("act", lambda: nc.scalar.copy(vout[b][:], pvt[:, 0:128]), [pvk], [("vout", b)])
                        yield
                        c.op("pool", lambda: nc.gpsimd.tensor_copy(V[:, tt, :], vout[b][:]), [("vout", b)], [("V", tt)])
                        c.dma("sp", nv[L][tt * 128:(tt + 1) * 128, j, :], vout[b][:],
                              reads=[("vout", b)], writes=[("nv", tt, j)])
                        yield
                    else:
                        c.op("act", lambda: nc.scalar.copy(V[:, tt, :], pvt[:, 0:128]), [pvk], [("V", tt)])
                        yield

                def q_task(sl, wb, wk, hh, tb):
                    cols = slice(tb * 512, (tb + 1) * 512)
                    pjt, pjk = pbank[sl]
                    mm_group(pjt[:], [(wb[:, k, 0:128], hT[:, k, cols]) for k in range(16)], [wk], [pjk])
                    yield
                    yield from fn_task(sl, pjt[:], pjk, gq[:, 0:1], rope_of(tb), QTg[:, hh, cols], ("QT", hh, tb))

                def g_task(sl, wb, wk, hh, tb):
                    cols = slice(tb * 512, (tb + 1) * 512)
                    pjt, pjk = pbank[sl]
                    mm_group(pjt[:], [(wb[:, k, 128:256], hT[:, k, cols]) for k in range(16)], [wk], [pjk])
                    yield
                    c.op("act", lambda: nc.scalar.activation(oThg[:, hh, cols], pjt[:], AF.Silu),
                         [pjk], [("oTh", hh, tb)])
                    yield

                sctr = [0]
                bctr2 = [0]

                def attn_block(j, q0, nq, keys, tbq):
                    qcols = slice(q0, q0 + nq)
                    N = G * nq
                    ob = bctr2[0] % 2
                    bctr2[0] += 1
                    nk_ = len(keys)
                    rhsQ = QTg[:, :, qcols] if G > 1 else QTg[:, 0, qcols]
                    qkeys = [("QT", hh, tbq) for hh in range(G)]
                    slots = []

                    def smm(ki):
                        kind, idx, mi = keys[ki]
                        p = sctr[0] % 2
                        sctr[0] += 1
                        slots.append(p)
                        if kind == "l":
                            Kl = KT[:, idx * 128:(idx + 1) * 128]
                            kr = [("KT", idx // 4)]
                        else:
                            Kl = KcT[:, idx * 128:(idx + 1) * 128]
                            kr = ["KcT"]

                        def f():
                            out = pS[p][:, :N] if G == 1 else pS[p][:, :N].rearrange("p (g q) -> p g q", g=G)
                            inst = nc.tensor.matmul(out, lhsT=Kl, rhs=rhsQ, start=True, stop=(mi is None))
                            if mi is not None:
                                inst = nc.tensor.matmul(out, lhsT=identb[:], rhs=wb4[:, mi, :, 0:nq],
                                                        start=False, stop=True)
                            return inst
                        c.op("pe", f, kr + qkeys + ["consts", "identb"], [("pS", p)])

                    def pv(ki):
                        kind, idx, mi = keys[ki]
                        p = slots[ki]
                        if kind == "l":
                            Vl = V[:, idx, :]
                            kr = [("V", idx)]
                        else:
                            Vl = Vc[:, idx, :]
                            kr = ["Vc"]
                        c.op("act", lambda: nc.scalar.activation(PT[p][:, :N], pS[p][:, :N], AF.Exp, scale=SCALE),
                             [("pS", p)], [("PT", p)])

                        def f():
                            nc.tensor.matmul(pO[ob][:, :N], lhsT=Vl, rhs=PT[p][:, :N],
                                             start=(ki == 0), stop=(ki == nk_ - 1))
                            return nc.tensor.matmul(pL[ob][:, :N], lhsT=onesb[:], rhs=PT[p][:, :N],
                                                    start=(ki == 0), stop=(ki == nk_ - 1))
                        c.op("pe", f, kr + [("PT", p), "onesb"], [("pO", ob), ("pL", ob)])

                    smm(0)
                    if nk_ > 1:
                        smm(1)
                    for ki in range(nk_):
                        pv(ki)
                        if ki + 2 < nk_:
                            smm(ki + 2)
                    r = rl[ob]
                    if L == 1:
                        r3 = r[:, :N].rearrange("p (g q) -> p g q", g=G)
                        l3 = pL[ob][:, :N].rearrange("p (g q) -> p g q", g=G)
                        c.op("dve", lambda: nc.vector.tensor_tensor(
                            out=r3, in0=l3, in1=esk[:, 4 * j:4 * j + 4].unsqueeze(2).to_broadcast([128, G, nq]),
                            op=ALU.add), [("pL", ob), "consts"], [("rl", ob)])
                        c.op("dve", lambda: nc.vector.reciprocal(r[:, :N], r[:, :N]), [("rl", ob)], [("rl", ob)])
                    else:
                        c.op("dve", lambda: nc.vector.reciprocal(r[:, :N], pL[ob][:, :N]), [("pL", ob)], [("rl", ob)])
                    c.op("dve", lambda: nc.vector.tensor_tensor(out=r[:, :N], in0=pO[ob][:, :N], in1=r[:, :N],
                                                                op=ALU.mult), [("pO", ob), ("rl", ob)], [("rl", ob)])
                    okeys = [("oTh", hh, tbq) for hh in range(G)]
                    if G > 1:
                        o3 = oThg[:, :, qcols]
                        r3 = r[:, :N].rearrange("p (g q) -> p g q", g=G)
                    else:
                        o3 = oThg[:, 0, qcols]
                        r3 = r[:, :N]
                    c.op("pool", lambda: nc.gpsimd.tensor_tensor(out=o3, in0=r3, in1=o3, op=ALU.mult),
                         [("rl", ob)] + okeys, okeys)

                for j in range(nkv):
                    wb, wk = load_w(kvc0 + j * 256)
                    c.dma("pool", KcT[:], ckT[L][:, j, :], writes=["KcT"])
                    c.dma("pool", Vc[:], cv[L][:, j, :].rearrange("(t p) d -> p t d", p=128), writes=["Vc"])
                    gens = [(lambda sl, tb=tb: k_task(sl, wb, wk, j, tb)) for tb in range(NTB)] + \
                           [(lambda sl, tt=tt: v_task(sl, wb, wk, j, tt)) for tt in range(NT)]
                    run_tasks(gens, 2)
                    for h0 in range(4 * j, 4 * j + 4, G):
                        gens = []
                        for hh in range(G):
                            wbq, wkq = load_w(qc0 + (h0 + hh) * 256)
                            for tb in range(NTB):
                                gens.append(lambda sl, wbq=wbq, wkq=wkq, hh=hh, tb=tb: q_task(sl, wbq, wkq, hh, tb))
                                gens.append(lambda sl, wbq=wbq, wkq=wkq, hh=hh, tb=tb: g_task(sl, wbq, wkq, hh, tb))
                            if hh % 2 == 1 or G == 1:
                                run_tasks(gens, 2)
                                gens = []
                        blocks = []
                        if G == 1:
                            for (t0, n) in SEQS[:2]:
                                blocks.append((t0 * 128, 256, [("l", t0, None), ("l", t0 + 1, None)], 0))
                            for tb in range(1, NTB):
                                keys = [("l", kt, None) for kt in range(4, NT)] + [("c", 0, None), ("c", 1, None)]
                                blocks.append((tb * 512, 512, keys, tb))
                        else:
                            for (t0, n) in SEQS[:2]:
                                for tq in range(t0, t0 + n):
                                    blocks.append((tq * 128, 128, [("l", t0, None), ("l", t0 + 1, None)], 0))
                            for i in range(16):
                                keys = []
                                if i > 0:
                                    keys.append(("l", 4 + i - 1, 0))
                                keys.append(("l", 4 + i, None))
                                if i < 15:
                                    keys.append(("l", 4 + i + 1, 1))
                                keys += [("c", 0, None), ("c", 1, None)]
                                blocks.append(((4 + i) * 128, 128, keys, (4 + i) // 4))
                        for (q0, nq, keys, tbq) in blocks:
                            attn_block(j, q0, nq, keys, tbq)
                        for hh in range(G):
                            r0 = (orow0 + h0 + hh) * 128
                            c.dma("sp", oTs[L][r0:r0 + 128, :], oThg[:, hh, :],
                                  reads=[("oTh", hh, tb) for tb in range(NTB)], writes=[("oTs", L, orow0 + h0 + hh)])
                c.barrier_all()

        def phase_hgrn():
            with ExitStack() as es:
                load_w = make_loader(es, 640)
                maskF = sb(es, "maskF", [128, 512], F32)
                mfb = sb(es, "mfb", [128, 2, 128], F32)
                cm4 = sb(es, "cm4", [128, 4, 128], BF16)
                onc = sb(es, "onc", [128, 1], F32)
                lbe = sb(es, "lbe", [128, 3, 16], F32)
                lbs = sb(es, "lbs", [128, 16], F32)
                lbv = sb(es, "lbv", [128, 16], F32)
                oml = sb(es, "oml", [128, 16], F32)
                noml = sb(es, "noml", [128, 16], F32)
                q32 = sb(es, "q32", [128, 512], F32)
                sg = sb(es, "sg", [128, 512], F32)
                lg = sb(es, "lg", [128, 512], F32)
                k32 = sb(es, "k32", [128, 512], F32)
                bF = sb(es, "bF", [128, 512], F32)
                bB = sb(es, "bB", [128, 512], F32)
                eb = sb(es, "eb", [128, 512], F32)
                enb = sb(es, "enb", [128, 512], F32)
                ki32 = sb(es, "ki32", [128, 512], F32)
                dec = sb(es, "dec", [128, 2, 80], F32)
                qd = [sb(es, f"qd{d}", [128, T], BF16) for d in range(2)]
                ki = [sb(es, f"ki{d}", [128, T], BF16) for d in range(2)]
                keT = [sb(es, f"keT{d}", [128, T], BF16) for d in range(2)]
                sgT = sb(es, "sgT", [128, T], BF16)
                V = sb(es, "Va", [128, NT, 128], BF16)
                OT = sb(es, "OT", [128, T], F32)
                kend = [sb(es, f"kend{d}", [128, 128], BF16) for d in range(2)]
                Vm = [sb(es, f"Vm{d}", [128, 4, 128], BF16) for d in range(2)]
                Am = [sb(es, f"Am{d}", [128, 128], BF16) for d in range(2)]
                S32 = [sb(es, f"S32_{d}", [128, 128], F32) for d in range(2)]
                Sbf = [[sb(es, f"Sbf{d}{p}", [128, 128], BF16) for p in range(2)] for d in range(2)]
                sq32 = q32
                std32 = sg
                oTh = sb(es, "hoTh", [128, T], BF16)
                pj = [ps(es, f"hpj{i}", [128, 512]) for i in range(2)]
                pv = ps(es, "hpv", [128, 512])
                bA = [ps(es, f"hbA{d}", [128, 4, 128]) for d in range(2)]
                pO = [ps(es, f"hpO{d}", [128, 512]) for d in range(2)]
                ptr = ps(es, "hptr", [128, 8, 128], BF16)

                c.dma("sp", maskF[:], maskF_d, writes=["hc"])
                c.dma("sp", mfb[:], mfb_d, writes=["hc"])
                c.dma("pool", cm4[:], cm4_d, writes=["hc"])
                c.dma("sp", onc[:], onorm_d, writes=["hc"])
                c.dma("sp", lbe[:], lbg, writes=["lbe"])
                c.op("act", lambda: nc.scalar.activation(lbe[:], lbe[:], AF.Exp), ["lbe"], ["lbe"])
                c.op("dve", lambda: nc.vector.tensor_tensor(out=lbs[:], in0=lbe[:, 0, :], in1=lbe[:, 1, :], op=ALU.add),
                     ["lbe"], ["lbs"])
                c.op("dve", lambda: nc.vector.tensor_tensor(out=lbs[:], in0=lbs[:], in1=lbe[:, 2, :], op=ALU.add),
                     ["lbe", "lbs"], ["lbs"])
                c.op("dve", lambda: nc.vector.reciprocal(lbs[:], lbs[:]), ["lbs"], ["lbs"])
                c.op("dve", lambda: nc.vector.tensor_tensor(out=lbv[:], in0=lbe[:, 0, :], in1=lbs[:], op=ALU.mult),
                     ["lbe", "lbs"], ["lbv"])
                c.op("dve", lambda: nc.vector.tensor_scalar(out=oml[:], in0=lbv[:], scalar1=-1.0, scalar2=1.0,
                                                            op0=ALU.mult, op1=ALU.add), ["lbv"], ["oml"])
                c.op("dve", lambda: nc.vector.tensor_scalar(out=noml[:], in0=oml[:], scalar1=-1.0, scalar2=None,
                                                            op0=ALU.mult), ["oml"], ["noml", "hc"])

                def bc32(t, ncol=16):
                    return t.unsqueeze(2).to_broadcast([128, ncol, 32])

                for h in range(8):
                    wb, wk = load_w(w_in[0], h * 640, 640)
                    for tb in range(NTB):
                        cols = slice(tb * 512, (tb + 1) * 512)
                        hk = [wk] + hkeys(tb)
                        mm_group(pj[0][:], [(wb[:, k, 0:128], hT[:, k, cols]) for k in range(16)], hk, [("pj", 0)])
                        c.op("act", lambda: nc.scalar.activation(q32[:], pj[0][:], AF.Silu), [("pj", 0)], ["q32"])
                        for d in range(2):
                            i = d * 8 + h
                            mm_group(pj[1][:], [(wb[:, k, 128 * (1 + d):128 * (2 + d)], hT[:, k, cols]) for k in range(16)],
                                     hk, [("pj", 1)])
                            c.op("act", lambda: nc.scalar.activation(sg[:], pj[1][:], AF.Sigmoid), [("pj", 1)], ["sg"])
                            c.op("act", lambda i=i: nc.scalar.activation(lg[:], sg[:], AF.Ln, scale=oml[:, i:i + 1],
                                                                         bias=lbv[:, i:i + 1]), ["sg", "hc"], ["lg"])
                            c.op("dve", lambda i=i: nc.vector.tensor_scalar(
                                out=k32[:], in0=sg[:], scalar1=noml[:, i:i + 1], scalar2=oml[:, i:i + 1],
                                op0=ALU.mult, op1=ALU.add), ["sg", "hc"], ["k32"])
                            c.op("dve", lambda: nc.vector.tensor_tensor_scan(bF[:], maskF[:], lg[:], 0.0, ALU.mult, ALU.add),
                                 ["lg", "hc"], ["bF"])
                            tot = bF[:].rearrange("p (c t) -> p c t", t=32)[:, :, 31]
                            dslice = dec[:, d, tb * 16:(tb + 1) * 16]
                            c.op("act", lambda tot=tot, dslice=dslice: nc.scalar.activation(dslice, tot, AF.Exp),
                                 ["bF"], [("dec", d, tb)])
                            if d == 0:
                                bsrc, bkey = bF, "bF"
                            else:
                                b3 = bB[:].rearrange("p (c t) -> p c t", t=32)
                                f3 = bF[:].rearrange("p (c t) -> p c t", t=32)
                                c.op("dve", lambda b3=b3, f3=f3, tot=tot: nc.vector.tensor_tensor(
                                    out=b3, in0=bc32(tot), in1=f3, op=ALU.subtract), ["bF"], ["bB"])
                                c.op("pool", lambda: nc.gpsimd.tensor_tensor(out=bB[:], in0=bB[:], in1=lg[:], op=ALU.add),
                                     ["bB", "lg"], ["bB"])
                                bsrc, bkey = bB, "bB"
                            c.op("act", lambda bsrc=bsrc: nc.scalar.activation(eb[:], bsrc[:], AF.Exp), [bkey], ["eb"])
                            c.op("act", lambda bsrc=bsrc: nc.scalar.activation(enb[:], bsrc[:], AF.Exp, scale=-1.0),
                                 [bkey], ["enb"])
                            c.op("dve", lambda d=d, cols=cols: nc.vector.tensor_tensor(
                                out=qd[d][:, cols], in0=q32[:], in1=eb[:], op=ALU.mult), ["q32", "eb"], [("qd", d, tb)])
                            c.op("dve", lambda: nc.vector.tensor_tensor(out=ki32[:], in0=k32[:], in1=enb[:], op=ALU.mult),
                                 ["k32", "enb"], ["ki32"])
                            c.op("pool", lambda d=d, cols=cols: nc.gpsimd.tensor_copy(ki[d][:, cols], ki32[:]),
                                 ["ki32"], [("ki", d, tb)])
                            k3 = ki32[:].rearrange("p (c t) -> p c t", t=32)
                            o3 = keT[d][:, cols].rearrange("p (c t) -> p c t", t=32)
                            c.op("pool", lambda k3=k3, o3=o3, dslice=dslice: nc.gpsimd.tensor_tensor(
                                out=o3, in0=k3, in1=bc32(dslice), op=ALU.mult),
                                ["ki32", ("dec", d, tb)], [("keT", d, tb)])
                        mm_group(pj[0][:], [(wb[:, k, 512:640], hT[:, k, cols]) for k in range(16)], hk, [("pj", 0)])
                        c.op("act", lambda cols=cols: nc.scalar.activation(sgT[:, cols], pj[0][:], AF.Silu),
                             [("pj", 0)], [("sgT", tb)])
                        for tt in range(tb * 4, tb * 4 + 4):
                            tcols = slice(tt * 128, (tt + 1) * 128)
                            mm_group(pv[:, 0:128], [(hT[:, k, tcols], wb[:, k, 384:512]) for k in range(16)],
                                     [wk, ("hT", tt)], ["pv"])
                            c.op("act", lambda tt=tt: nc.scalar.copy(V[:, tt, :], pv[:, 0:128]), ["pv"], [("V", tt)])

                    nchunk = [0, 0]

                    def tile_step(d, tt, first_visit):
                        tb = tt // 4
                        tcols = slice(tt * 128, (tt + 1) * 128)
                        c.op("pe", lambda: nc.tensor.transpose(ptr[:, d, :], keT[d][:, tcols], identb[:]),
                             [("keT", d, tb), "identb"], ["ptr"])
                        c.op("act", lambda: nc.scalar.copy(kend[d][:], ptr[:, d, :]), ["ptr"], [("kend", d)])
                        c.op("pool", lambda: nc.gpsimd.tensor_tensor(
                            out=Vm[d][:], in0=V[:, tt, :].unsqueeze(1).to_broadcast([128, 4, 128]), in1=cm4[:],
                            op=ALU.mult), [("V", tt), "hc"], [("Vm", d)])
                        c.op("pe", lambda: nc.tensor.matmul(bA[d][:, 0, :], lhsT=ki[d][:, tcols], rhs=qd[d][:, tcols],
                                                            start=True, stop=True),
                             [("ki", d, tb), ("qd", d, tb)], [("bA", d)])
                        c.op("dve", lambda: nc.vector.tensor_tensor(out=Am[d][:], in0=bA[d][:, 0, :], in1=mfb[:, d, :],
                                                                    op=ALU.mult), [("bA", d), "hc"], [("Am", d)])
                        c.op("pe", lambda: nc.tensor.matmul(pO[d][:, 0:128], lhsT=V[:, tt, :], rhs=Am[d][:],
                                                            start=True, stop=False),
                             [("V", tt), ("Am", d)], [("pO", d)])
                        order = range(4) if d == 0 else range(3, -1, -1)
                        for n_, j in enumerate(order):
                            p = nchunk[d] % 2
                            ccols = slice(tt * 128 + 32 * j, tt * 128 + 32 * j + 32)
                            c.op("pe", lambda p=p, j=j, ccols=ccols, n_=n_: nc.tensor.matmul(
                                pO[d][:, 32 * j:32 * j + 32], lhsT=Sbf[d][p][:], rhs=qd[d][:, ccols],
                                start=False, stop=(n_ == 3)),
                                [("Sbf", d, p), ("qd", d, tb)], [("pO", d)])
                            c.op("pe", lambda j=j: nc.tensor.matmul(
                                bA[d][:, 1 + (j % 2), :], lhsT=kend[d][:], rhs=Vm[d][:, j, :], start=True, stop=True),
                                [("kend", d), ("Vm", d)], [("bA", d)])
                            gch = tt * 4 + j
                            c.op("dve", lambda j=j, gch=gch: nc.vector.scalar_tensor_tensor(
                                out=S32[d][:], in0=S32[d][:], scalar=dec[:, d, gch:gch + 1], in1=bA[d][:, 1 + (j % 2), :],
                                op0=ALU.mult, op1=ALU.add),
                                [("S32", d), ("dec", d, tb), ("bA", d)], [("S32", d)])
                            c.op("act", lambda p=p: nc.scalar.copy(Sbf[d][1 - p][:], S32[d][:]),
                                 [("S32", d)], [("Sbf", d, 1 - p)])
                            nchunk[d] += 1
                        if first_visit:
                            c.op("act", lambda: nc.scalar.copy(OT[:, tcols], pO[d][:, 0:128]), [("pO", d)], [("OT", tt)])
                        else:
                            c.op("dve", lambda: nc.vector.tensor_tensor(out=OT[:, tcols], in0=pO[d][:, 0:128], in1=OT[:, tcols],
                                                                        op=ALU.add), [("pO", d), ("OT", tt)], [("OT", tt)])

                    for si, (t0, n) in enumerate(SEQS):
                        for d in range(2):
                            p = nchunk[d] % 2
                            if si < 2:
                                c.op("dve", lambda d=d: nc.vector.memset(S32[d][:], 0.0), [], [("S32", d)])
                            else:
                                c.dma("sp", S32[d][:], st0[d, h], writes=[("S32", d)])
                            c.op("act", lambda d=d, p=p: nc.scalar.copy(Sbf[d][p][:], S32[d][:]),
                                 [("S32", d)], [("Sbf", d, p)])
                        for idx in range(n):
                            tf = t0 + idx
                            tbk = t0 + n - 1 - idx
                            fv = idx < n - 1 - idx
                            tile_step(0, tf, first_visit=fv)
                            tile_step(1, tbk, first_visit=fv)
                        if si < 2:
                            for d in range(2):
                                c.dma("sp", nstate[si, d, h], S32[d][:], reads=[("S32", d)], writes=[("nstate", si, d, h)])

                    for tb in range(NTB):
                        cols = slice(tb * 512, (tb + 1) * 512)
                        ok = [("OT", tt) for tt in range(tb * 4, tb * 4 + 4)]
                        c.op("act", lambda cols=cols: nc.scalar.activation(sq32[:], OT[:, cols], AF.Square), ok, ["q32"])
                        c.op("pe", lambda: nc.tensor.matmul(pj[1][:], lhsT=onesf[:], rhs=sq32[:], start=True, stop=True),
                             ["q32", "onesf"], [("pj", 1)])
                        c.op("act", lambda: nc.scalar.activation(std32[:], pj[1][:], AF.Sqrt, bias=epsc[:]),
                             [("pj", 1), "epsc"], ["sg"])
                        c.op("dve", lambda: nc.vector.reciprocal(std32[:], std32[:]), ["sg"], ["sg"])
                        c.op("dve", lambda cols=cols: nc.vector.scalar_tensor_tensor(
                            out=sq32[:], in0=OT[:, cols], scalar=onc[:, 0:1], in1=std32[:], op0=ALU.mult, op1=ALU.mult),
                            ok + ["sg", "hc"], ["q32"])
                        c.op("pool", lambda cols=cols: nc.gpsimd.tensor_tensor(
                            out=oTh[:, cols], in0=sq32[:], in1=sgT[:, cols], op=ALU.mult),
                            ["q32", ("sgT", tb)], ["hoTh"])
                    c.dma("sp", oTs[0][h * 128:(h + 1) * 128, :], oTh[:], reads=["hoTh"], writes=[("oTs", 0, h)])
                c.barrier_all()

        def phase_out(L):
            with ExitStack() as es:
                load_w = make_loader(es, 512)
                gtb = sb(es, "gtb", [128, 2, D], F32)
                xr = [sb(es, f"xr{i}", [128, 512], F32) for i in range(2)]
                tm = [sb(es, f"otm{i}", [128, 512], F32) for i in range(2)]
                yt = [sb(es, f"yt{i}", [128, 512], F32) for i in range(2)]
                po = [ps(es, f"po{i}", [128, 512]) for i in range(2)]
                for g in range(2):
                    c.dma("sp", gtb[:, g, :], gts[L * 2 + g:L * 2 + g + 1, :].partition_broadcast(128),
                          reads=[("gts", L, g, cb) for cb in range(4)], writes=[("gtb", g)])
                for k in range(16):
                    c.dma("sp", hT[:, k, :], oTs[L][k * 128:(k + 1) * 128, :], reads=[("oTs", L, k)],
                          writes=[("hTk", k)])
                for cb in range(4):
                    ccols = slice(cb * 512, (cb + 1) * 512)
                    wb, wk = load_w(w_out[L], cb * 512, 512)
                    for tt in range(NT):
                        b = tt % 2
                        g = 0 if tt < 4 else 1
                        rows = slice(tt * 128, (tt + 1) * 128)
                        tcols = slice(tt * 128, (tt + 1) * 128)
                        mm_group(po[b][:], [(hT[:, k, tcols], wb[:, k, 0:512]) for k in range(16)],
                                 [wk] + [("hTk", k) for k in range(16)], [("po", b)])
                        if L == 0:
                            c.dma("sp", xr[b][:], x[rows, ccols], writes=[("xr", b)])
                        else:
                            c.dma("sp", xr[b][:], y1[rows, ccols], reads=[("y1", tt, cb)], writes=[("xr", b)])
                        c.op("dve", lambda b=b, g=g, ccols=ccols: nc.vector.tensor_tensor(
                            out=tm[b][:], in0=po[b][:], in1=gtb[:, g, ccols], op=ALU.mult),
                            [("po", b), ("gtb", g)], [("otm", b)])
                        c.op("pool", lambda b=b: nc.gpsimd.tensor_tensor(out=yt[b][:], in0=tm[b][:], in1=xr[b][:],
                                                                         op=ALU.add),
                             [("otm", b), ("xr", b)], [("yt", b)])
                        if L == 0:
                            c.dma("sp", y1[rows, ccols], yt[b][:], reads=[("yt", b)], writes=[("y1", tt, cb)])
                        else:
                            c.dma("sp", y[rows, ccols], yt[b][:], reads=[("yt", b)], writes=[("y", tt, cb)])
                c.barrier_all()

        plist = [("mod0", lambda: phase_mod_h(0)), ("hgrn", phase_hgrn), ("attn0", lambda: phase_attn(0)),
                 ("out0", lambda: phase_out(0)), ("mod1", lambda: phase_mod_h(1)), ("attn1", lambda: phase_attn(1)),
                 ("out1", lambda: phase_out(1))]
        for nm, fn in plist:
            if phases is None or nm in phases:
                fn()
        c.finish()
    return nc


def _consts():
    ident = np.eye(128, dtype=np.float32)
    maskF = np.ones((128, 512), np.float32)
    maskF[:, ::32] = 0.0
    s = np.arange(128)[:, None]
    t = np.arange(128)[None, :]
    same = (s // 32) == (t // 32)
    mfb = np.stack([(same & (s <= t)), (same & (s >= t))], axis=1).astype(np.float32)
    cm4 = np.zeros((128, 4, 128), np.float32)
    for j in range(4):
        cm4[32 * j:32 * j + 32, j, :] = 1.0
    R = np.zeros((128, 128), np.float32)
    for m in range(128):
        q = m // 32
        if q in (0, 2):
            R[m, m + 32] = -1.0
        else:
            R[m, m - 32] = 1.0
    RT = np.ascontiguousarray(R.T)
    n_tok = 2048
    row = (np.arange(n_tok) // 64).astype(np.float32)
    col = (np.arange(n_tok) % 64).astype(np.float32)
    inv = (10000.0 ** (-np.arange(32, dtype=np.float32) / 32)).astype(np.float32)
    ar = row[:, None] * inv
    ac = col[:, None] * inv
    ang = np.concatenate([ar, ar, ac, ac], axis=-1).astype(np.float32)
    cosT = np.ascontiguousarray(np.cos(ang).T.astype(np.float32))
    sinT = np.ascontiguousarray(np.sin(ang).T.astype(np.float32))
    b = np.arange(128)[:, None]
    a = np.arange(128)[None, :]
    NEG = -30000.0
    wbias = np.stack([np.where(b >= a, 0.0, NEG), np.where(b <= a, 0.0, NEG)], axis=1).astype(np.float32)
    wbias4 = np.ascontiguousarray(np.broadcast_to(wbias[:, :, None, :], (128, 2, 4, 128))).astype(np.float32)
    return dict(ident=ident, maskF=maskF, mfb=mfb, cm4=cm4, RT=RT, cosT=cosT, sinT=sinT, wbias4=wbias4)


def _perm_w0(w):
    cols = []
    for h in range(8):
        for base in (0, 1024, 2048, 3072, 4096):
            cols.append(np.arange(base + h * 128, base + (h + 1) * 128))
    for j in range(2):
        cols.append(np.arange(6144 + j * 128, 6144 + (j + 1) * 128))
        cols.append(np.arange(6400 + j * 128, 6400 + (j + 1) * 128))
    for h in range(8):
        cols.append(np.arange(5120 + h * 128, 5120 + (h + 1) * 128))
        cols.append(np.arange(6656 + h * 128, 6656 + (h + 1) * 128))
    return np.ascontiguousarray(w[:, np.concatenate(cols)])


def _perm_w1(w):
    cols = []
    for j in range(4):
        cols.append(np.arange(2048 + j * 128, 2048 + (j + 1) * 128))
        cols.append(np.arange(2560 + j * 128, 2560 + (j + 1) * 128))
    for h in range(16):
        cols.append(np.arange(h * 128, (h + 1) * 128))
        cols.append(np.arange(3072 + h * 128, 3072 + (h + 1) * 128))
    return np.ascontiguousarray(w[:, np.concatenate(cols)])


def _prep(x_prompt, x_sample, state_l0_hgrn, cache_l0_k, cache_l0_v, cache_l1_k, cache_l1_v,
           c, c_ctx, lb_gamma,
           l0_norm, l0_w_mod, l0_b_mod, l0_w_in, l0_w_out, l0_a_onorm, l0_b_qnorm, l0_b_knorm,
           l1_norm, l1_w_mod, l1_b_mod, l1_w_in, l1_w_out, l1_c_qnorm, l1_c_knorm, l1_c_sink):
    f = lambda a: np.ascontiguousarray(np.asarray(a, dtype=np.float32))
    x_prompt, x_sample = f(x_prompt), f(x_sample)
    consts = _consts()
    shared = dict(
        w_mod0=f(l0_w_mod), w_mod1=f(l1_w_mod),
        b_mod0=f(l0_b_mod).reshape(1, -1), b_mod1=f(l1_b_mod).reshape(1, -1),
        norm0=f(l0_norm).reshape(1, -1), norm1=f(l1_norm).reshape(1, -1),
        w_in0=_perm_w0(f(l0_w_in)), w_in1=_perm_w1(f(l1_w_in)),
        w_out0=f(l0_w_out), w_out1=f(l1_w_out),
        onorm=f(l0_a_onorm).reshape(128, 1),
        qn0=f(l0_b_qnorm).reshape(128, 1), kn0=f(l0_b_knorm).reshape(128, 1),
        qn1=f(l1_c_qnorm).reshape(128, 1), kn1=f(l1_c_knorm).reshape(128, 1),
        sink=f(l1_c_sink).reshape(1, 16),
        lbg=np.ascontiguousarray(f(lb_gamma).reshape(3, 2, 8, 128).transpose(3, 0, 1, 2).reshape(128, 3, 16)),
        **consts,
    )
    c = f(c)
    c_ctx = f(c_ctx)
    in_maps = []
    for i in range(8):
        m = dict(shared)
        m["x"] = np.ascontiguousarray(np.concatenate(
            [x_prompt[2 * i], x_prompt[2 * i + 1], x_sample[i]], axis=0))
        cr = np.stack([c_ctx, c[i]], axis=0)
        m["crows"] = np.ascontiguousarray(cr.reshape(2, 16, 128).transpose(2, 0, 1))
        m["st0"] = f(state_l0_hgrn[i])
        m["ck0T"] = np.ascontiguousarray(f(cache_l0_k[i]).transpose(2, 1, 0))
        m["cv0"] = f(cache_l0_v[i])
        m["ck1T"] = np.ascontiguousarray(f(cache_l1_k[i]).transpose(2, 1, 0))
        m["cv1"] = f(cache_l1_v[i])
        in_maps.append(m)
    return in_maps


def kernel(**inputs):
    in_maps = _prep(**inputs)
    nc = build_program()
    res = run_bass_kernel_spmd(nc, in_maps, core_ids=list(range(8)))
    r = res.results
    y_prompt = np.stack([r[i // 2]["y"][(i % 2) * 256:(i % 2 + 1) * 256] for i in range(16)], axis=0)
    y_sample = np.stack([r[i]["y"][512:] for i in range(8)], axis=0)
    nstate = np.concatenate([r[i]["nstate"] for i in range(8)], axis=0)
    outs = [y_prompt.astype(np.float32), y_sample.astype(np.float32), nstate.astype(np.float32)]
    for nm in ("nk0", "nv0", "nk1", "nv1"):
        a = np.concatenate([r[i][nm].reshape(2, 256, r[i][nm].shape[1], 128) for i in range(8)], axis=0)
        outs.append(a.astype(np.float32))
    return tuple(outs)
```

```python
import math
import os
from contextlib import ExitStack

import numpy as np
import concourse.bass as bass
import concourse.mybir as mybir
from concourse.bass_utils import run_bass_kernel_spmd

F32 = mybir.dt.float32
BF16 = mybir.dt.bfloat16
AF = mybir.ActivationFunctionType
ALU = mybir.AluOpType

D = 2048
T = 2560
NT = 20
NTB = 5
EPS = 1e-6
SCALE = 1.0 / math.sqrt(128.0)
SEQS = [(0, 2), (2, 2), (4, 16)]


class Ctx:
    NDMA = 24

    def __init__(self, nc):
        self.nc = nc
        self.engs = {"pe": nc.tensor, "dve": nc.vector, "act": nc.scalar,
                     "pool": nc.gpsimd, "sp": nc.sync}
        self.sem = {k: nc.alloc_semaphore(name="s_" + k) for k in self.engs}
        self.cnt = {k: 0 for k in self.engs}
        self.waited = {k: {} for k in self.engs}
        self.last_w = {}
        self.readers = {}
        self.dma_sems = [nc.alloc_semaphore(name=f"s_dma{i}") for i in range(self.NDMA)]
        self.dma_val = [0] * self.NDMA
        self.dma_pool = {"sp": list(range(0, 16)), "pool": list(range(16, self.NDMA))}
        self.dma_rr = {"sp": 0, "pool": 0}

    def _deps(self, reads, writes):
        evs = []
        for r in reads:
            e = self.last_w.get(r)
            if e is not None:
                evs.append(e)
        for w in writes:
            e = self.last_w.get(w)
            if e is not None:
                evs.append(e)
            evs.extend(self.readers.get(w, ()))
        return evs

    def _wait(self, eng, evs, skip_self=False):
        best = {}
        for (name, sem, val) in evs:
            if skip_self and name == eng:
                continue
            if best.get(name, (None, 0))[1] < val:
                best[name] = (sem, val)
        wd = self.waited[eng]
        for name, (sem, val) in best.items():
            if wd.get(name, 0) < val:
                self.engs[eng].wait_ge(sem, val)
                wd[name] = val

    def _commit(self, ev, reads, writes):
        ws = set(writes)
        for r in reads:
            if r in ws:
                continue
            self.readers.setdefault(r, []).append(ev)
        for w in writes:
            self.last_w[w] = ev
            self.readers[w] = []

    EXCL = {"pj", "pms", "prot", "pv", "pS", "pO", "pL", "pm", "pT", "bA", "ptr", "po", "fb"}

    def op(self, eng, fn, reads=(), writes=()):
        reads = list(reads)
        writes = list(writes)
        ex = [r for r in reads if (r if isinstance(r, str) else r[0]) in self.EXCL]
        if ex:
            reads = [r for r in reads if r not in ex]
            writes = writes + [r for r in ex if r not in writes]
        evs = self._deps(reads, writes)
        self._wait(eng, evs, skip_self=(eng == "pe"))
        inst = fn()
        inst.then_inc(self.sem[eng], 1)
        self.cnt[eng] += 1
        ev = (eng, self.sem[eng], self.cnt[eng])
        self._commit(ev, reads, writes)
        return ev

    def dma(self, q, out, in_, reads=(), writes=()):
        reads = list(reads)
        writes = list(writes)
        evs = self._deps(reads, writes)
        lst = self.dma_pool[q]
        k = lst[self.dma_rr[q] % len(lst)]
        self.dma_rr[q] += 1
        name = f"dma{k}"
        if self.dma_val[k] > 0:
            evs.append((name, self.dma_sems[k], self.dma_val[k]))
        self._wait(q, evs)
        self.engs[q].dma_start(out=out, in_=in_).then_inc(self.dma_sems[k], 16)
        self.dma_val[k] += 16
        ev = (name, self.dma_sems[k], self.dma_val[k])
        self._commit(ev, reads, writes)
        return ev

    def all_events(self):
        evs = [(k, self.sem[k], self.cnt[k]) for k in self.engs if self.cnt[k] > 0]
        for i in range(self.NDMA):
            if self.dma_val[i] > 0:
                evs.append((f"dma{i}", self.dma_sems[i], self.dma_val[i]))
        return evs

    def barrier_all(self):
        evs = self.all_events()
        for e in self.engs:
            self._wait(e, evs, skip_self=True)

    def finish(self):
        self._wait("sp", self.all_events(), skip_self=True)


def build_program(phases=None):
    nc = bass.Bass("TRN2", target_bir_lowering=False)

    def din(name, shape, dt=F32):
        return nc.dram_tensor(name, list(shape), dt, kind="ExternalInput").ap()

    def dout(name, shape, dt=F32):
        return nc.dram_tensor(name, list(shape), dt, kind="ExternalOutput").ap()

    def dscr(name, shape, dt=F32):
        return nc.dram_tensor(name, list(shape), dt, kind="Internal").ap()

    x = din("x", [T, D])
    crows = din("crows", [128, 2, 16])
    lbg = din("lbg", [128, 3, 16])
    st0 = din("st0", [2, 8, 128, 128])
    ckT = [din("ck0T", [128, 2, 256]), din("ck1T", [128, 4, 256])]
    cv = [din("cv0", [256, 2, 128]), din("cv1", [256, 4, 128])]
    w_mod = [din("w_mod0", [D, 3 * D]), din("w_mod1", [D, 3 * D])]
    b_mod = [din("b_mod0", [1, 3 * D]), din("b_mod1", [1, 3 * D])]
    norm_g = [din("norm0", [1, D]), din("norm1", [1, D])]
    w_in = [din("w_in0", [D, 7680]), din("w_in1", [D, 5120])]
    w_out = [din("w_out0", [D, D]), din("w_out1", [D, D])]
    onorm_d = din("onorm", [128, 1])
    qn_d = [din("qn0", [128, 1]), din("qn1", [128, 1])]
    kn_d = [din("kn0", [128, 1]), din("kn1", [128, 1])]
    sink_d = din("sink", [1, 16])
    ident_d = din("ident", [128, 128])
    maskF_d = din("maskF", [128, 512])
    mfb_d = din("mfb", [128, 2, 128])
    cm4_d = din("cm4", [128, 4, 128])
    RT_d = din("RT", [128, 128])
    cosT_d = din("cosT", [128, 2048])
    sinT_d = din("sinT", [128, 2048])
    wbias4_d = din("wbias4", [128, 2, 4, 128])

    y = dout("y", [T, D])
    nstate = dout("nstate", [2, 2, 8, 128, 128])
    nk = [dout("nk0", [512, 2, 128]), dout("nk1", [512, 4, 128])]
    nv = [dout("nv0", [512, 2, 128]), dout("nv1", [512, 4, 128])]

    gts = dscr("gts", [4, D])
    oTs = [dscr("oT0", [D, T], BF16), dscr("oT1", [D, T], BF16)]
    y1 = dscr("y1", [T, D])

    c = Ctx(nc)

    uid = [0]

    def sb(es, name, shape, dt):
        uid[0] += 1
        return es.enter_context(nc.sbuf_tensor(f"{name}_{uid[0]}", list(shape), dt))

    def ps(es, name, shape, dt=F32):
        uid[0] += 1
        return es.enter_context(nc.psum_tensor(f"{name}_{uid[0]}", list(shape), dt))

    def mm_group(out, pairs, reads, writes):
        def f():
            n = len(pairs)
            inst = None
            for i, (l, r) in enumerate(pairs):
                inst = nc.tensor.matmul(out, lhsT=l, rhs=r, start=(i == 0), stop=(i == n - 1))
            return inst
        return c.op("pe", f, reads, writes)

    def wview(w, c0, ncols):
        return w[:, c0:c0 + ncols].rearrange("(k p) n -> p k n", p=128)

    with ExitStack() as top:
        identb = sb(top, "identb", [128, 128], BF16)
        identf = sb(top, "identf", [128, 128], F32)
        onesb = sb(top, "onesb", [128, 128], BF16)
        onesf = sb(top, "onesf", [128, 128], F32)
        epsc = sb(top, "epsc", [128, 1], F32)
        hT = sb(top, "hT", [128, 16, T], BF16)

        c.dma("sp", identf[:], ident_d, writes=["identf"])
        c.dma("pool", identb[:], ident_d, writes=["identb"])
        c.op("dve", lambda: nc.vector.memset(onesb[:], 1.0), writes=["onesb"])
        c.op("dve", lambda: nc.vector.memset(onesf[:], 1.0 / 128.0), writes=["onesf"])
        c.op("dve", lambda: nc.vector.memset(epsc[:], EPS), writes=["epsc"])

        def make_stream(es, blocks, ncols_max, nslots=2):
            bufs = [sb(es, f"wbuf{i}", [128, 16, ncols_max], BF16) for i in range(nslots)]
            st = {"issued": 0, "got": 0}

            def issue():
                i = st["issued"]
                if i >= len(blocks):
                    return
                w, c0, ncols = blocks[i]
                s = i % nslots
                c.dma("pool", bufs[s][:, :, 0:ncols], wview(w, c0, ncols), writes=[("wb", s)])
                st["issued"] += 1

            def get():
                i = st["got"]
                while st["issued"] <= i:
                    issue()
                st["got"] += 1
                if st["issued"] <= i + 1:
                    issue()
                s = i % nslots
                return bufs[s], ("wb", s)
            return get

        hkeys = lambda tb: [("hT", tt) for tt in range(tb * 4, tb * 4 + 4)]
        allh = [("hT", tt) for tt in range(NT)]

        def phase_mod_h(L):
            with ExitStack() as es:
                G = sb(es, "G", [128, 2, D], F32)
                SH = sb(es, "SH", [128, 2, D], F32)
                with ExitStack() as es2:
                    getw = make_stream(es2, [(w_mod[L], blk * 512, 512) for blk in range(12)], 512)
                    crs = sb(es2, "crs", [128, 2, 16], F32)
                    scl = sb(es2, "scl", [128, 2, 16], F32)
                    srep = sb(es2, "srep", [128, 2, 16, 128], BF16)
                    bb = [sb(es2, f"bb{i}", [128, 512], F32) for i in range(2)]
                    gb = [sb(es2, f"gb{i}", [128, 512], F32) for i in range(2)]
                    tmp = [sb(es2, f"mtmp{i}", [128, 512], F32) for i in range(2)]
                    pm = [ps(es2, f"pm{i}", [128, 512]) for i in range(2)]
                    c.dma("sp", crs[:], crows, writes=["crs"])
                    c.op("act", lambda: nc.scalar.activation(scl[:], crs[:], AF.Silu), ["crs"], ["scl"])
                    c.op("dve", lambda: nc.vector.tensor_copy(
                        srep[:], scl[:].unsqueeze(3).to_broadcast([128, 2, 16, 128])), ["scl"], ["srep"])
                    for blk in range(12):
                        kind, cb = divmod(blk, 4)
                        b = blk % 2
                        wb, wk = getw()
                        c.dma("sp", bb[b][:], b_mod[L][:, blk * 512:(blk + 1) * 512].partition_broadcast(128),
                              writes=[("bb", b)])
                        if kind == 1:
                            c.dma("sp", gb[b][:], norm_g[L][:, cb * 512:(cb + 1) * 512].partition_broadcast(128),
                                  writes=[("gb", b)])
                        cols = slice(cb * 512, (cb + 1) * 512)
                        for g in range(2):
                            mm_group(pm[g][:], [(srep[:, g, k, :], wb[:, k, 0:512]) for k in range(16)],
                                     [wk, "srep"], [("pm", g)])
                            if kind == 0:
                                c.op("dve", lambda g=g, b=b, cols=cols: nc.vector.tensor_tensor(
                                    out=SH[:, g, cols], in0=pm[g][:], in1=bb[b][:], op=ALU.add),
                                    [("pm", g), ("bb", b)], [("SH", g, cb)])
                            elif kind == 1:
                                c.op("dve", lambda g=g, b=b: nc.vector.tensor_tensor(
                                    out=tmp[g][:], in0=pm[g][:], in1=bb[b][:], op=ALU.add),
                                    [("pm", g), ("bb", b)], [("mtmp", g)])
                                c.op("dve", lambda g=g, b=b, cols=cols: nc.vector.scalar_tensor_tensor(
                                    out=G[:, g, cols], in0=tmp[g][:], scalar=1.0, in1=gb[b][:],
                                    op0=ALU.add, op1=ALU.mult),
                                    [("mtmp", g), ("gb", b)], [("G", g, cb)])
                            else:
                                c.op("dve", lambda g=g, b=b: nc.vector.tensor_tensor(
                                    out=tmp[g][:], in0=pm[g][:], in1=bb[b][:], op=ALU.add),
                                    [("pm", g), ("bb", b)], [("mtmp", g)])
                                c.dma("sp", gts[L * 2 + g:L * 2 + g + 1, cols], tmp[g][0:1, :],
                                      reads=[("mtmp", g)], writes=[("gts", L, g, cb)])
                    c.barrier_all()
                with ExitStack() as es2:
                    xt = [sb(es2, f"xt{i}", [128, D], F32) for i in range(2)]
                    junk = sb(es2, "junk", [128, D], BF16)
                    st = sb(es2, "st", [128, 8], F32)
                    t1 = sb(es2, "t1", [128, D], F32)
                    hb = [sb(es2, f"hb{i}", [128, D], BF16) for i in range(2)]
                    pT = [ps(es2, f"pT{i}", [128, 8, 128], BF16) for i in range(4)]
                    GK = [[("G", g, cb) for cb in range(4)] for g in range(2)]
                    SK = [[("SH", g, cb) for cb in range(4)] for g in range(2)]
                    for tt in range(NT):
                        b = tt % 2
                        g = 0 if tt < 4 else 1
                        rows = slice(tt * 128, (tt + 1) * 128)
                        if L == 0:
                            c.dma("sp", xt[b][:], x[rows, :], writes=[("xt", b)])
                        else:
                            c.dma("sp", xt[b][:], y1[rows, :], reads=[("y1", tt, cb) for cb in range(4)],
                                  writes=[("xt", b)])
                        c.op("act", lambda b=b: nc.scalar.activation(junk[:], xt[b][:], AF.Square,
                                                                     accum_out=st[:, b:b + 1]),
                             [("xt", b)], ["junk", ("ssq", b)])
                        c.op("act", lambda b=b: nc.scalar.activation(st[:, 2 + b:3 + b], st[:, b:b + 1], AF.Sqrt,
                                                                     scale=1.0 / D, bias=epsc[:]),
                             [("ssq", b), "epsc"], [("std", b)])
                        c.op("dve", lambda b=b: nc.vector.reciprocal(st[:, 4 + b:5 + b], st[:, 2 + b:3 + b]),
                             [("std", b)], [("rstd", b)])
                        c.op("dve", lambda b=b, g=g: nc.vector.scalar_tensor_tensor(
                            out=t1[:], in0=xt[b][:], scalar=st[:, 4 + b:5 + b], in1=G[:, g, :],
                            op0=ALU.mult, op1=ALU.mult),
                            [("xt", b), ("rstd", b)] + GK[g], ["t1"])
                        c.op("pool", lambda b=b, g=g: nc.gpsimd.tensor_tensor(
                            out=hb[b][:], in0=t1[:], in1=SH[:, g, :], op=ALU.add),
                            ["t1"] + SK[g], [("hb", b)])
                        for half in range(2):
                            pp = pT[b * 2 + half]

                            def tr(pp=pp, b=b, half=half):
                                inst = None
                                for kk in range(8):
                                    k = half * 8 + kk
                                    inst = nc.tensor.transpose(pp[:, kk, :], hb[b][:, k * 128:(k + 1) * 128], identb[:])
                                return inst
                            c.op("pe", tr, [("hb", b), "identb"], [("pT", b, half)])
                            eng = "act" if half == 0 else "dve"
                            dst = hT[:, half * 8:(half + 1) * 8, tt * 128:(tt + 1) * 128]
                            if eng == "act":
                                c.op("act", lambda pp=pp, dst=dst: nc.scalar.copy(dst, pp[:]),
                                     [("pT", b, half)], [("hT", tt, half)])
                            else:
                                c.op("dve", lambda pp=pp, dst=dst: nc.vector.tensor_copy(dst, pp[:]),
                                     [("pT", b, half)], [("hT", tt, half)])
                    c.barrier_all()
                c.barrier_all()

        def run_tasks(factories, width):
            it = iter(factories)
            active = {}
            free = list(range(width))
            while True:
                while free:
                    f = next(it, None)
                    if f is None:
                        break
                    sl = free.pop(0)
                    active[sl] = f(sl)
                if not active:
                    break
                for sl in sorted(active):
                    try:
                        next(active[sl])
                    except StopIteration:
                        del active[sl]
                        free.append(sl)

        def phase_attn(L):
            nkv = 2 if L == 0 else 4
            G = 1 if L == 0 else 4
            if L == 0:
                kvc0, qc0, orow0 = 5120, 5120 + 512, 8
            else:
                kvc0, qc0, orow0 = 0, 1024, 0
            with ExitStack() as es:
                ablocks = []
                for j_ in range(nkv):
                    ablocks.append((w_in[L], kvc0 + j_ * 256, 256))
                    for h_ in range(4 * j_, 4 * j_ + 4):
                        ablocks.append((w_in[L], qc0 + h_ * 256, 256))
                getw = make_stream(es, ablocks, 256, nslots=3)

                T_sq = [sb(es, f"sq{i}", [128, 512], F32) for i in range(2)]
                T_sd = [sb(es, f"sd{i}", [128, 512], F32) for i in range(2)]
                T_kb = [sb(es, f"kb{i}", [128, 512], BF16) for i in range(2)]
                T_t = [sb(es, f"rt{i}", [128, 512], F32) for i in range(2)]
                T_u = [sb(es, f"ru{i}", [128, 512], F32) for i in range(2)]
                cosT = sb(es, "cosT", [128, 2048], F32)
                sinT = sb(es, "sinT", [128, 2048], F32)
                RTb = sb(es, "RTb", [128, 128], BF16)
                gq = sb(es, "gq", [128, 1], F32)
                gk = sb(es, "gk", [128, 1], F32)
                esk = sb(es, "esk", [128, 16], F32)
                wb4 = sb(es, "wb4", [128, 2, 4, 128], BF16)
                KT = sb(es, "KT", [128, T], BF16)
                KcT = sb(es, "KcT", [128, 256], BF16)
                V = sb(es, "V", [128, NT, 128], BF16)
                Vc = sb(es, "Vc", [128, 2, 128], BF16)
                QTg = sb(es, "QTg", [128, G, T], BF16)
                oThg = sb(es, "oThg", [128, G, T], BF16)
                kf32 = sb(es, "kf32", [128, 512], F32)
                kout = [sb(es, f"kout{i}", [128, 128], F32) for i in range(2)]
                vout = [sb(es, f"vout{i}", [128, 128], F32) for i in range(2)]
                PT = [sb(es, f"PT{i}", [128, 512], BF16) for i in range(2)]
                rl = [sb(es, f"rl{i}", [128, 512], F32) for i in range(2)]
                pj = ps(es, "pj", [128, 512])
                pms = ps(es, "pms", [128, 512])
                pS = [ps(es, f"pS{i}", [128, 512]) for i in range(2)]
                pO = [ps(es, f"pO{i}", [128, 512]) for i in range(2)]
                pL = [ps(es, f"pL{i}", [128, 512]) for i in range(2)]
                pbank = [(pj, "pj"), (pS[0], ("pS", 0))]
                rbank = [(pO[0], ("pO", 0)), (pO[1], ("pO", 1))]
                tbank = (pS[1], ("pS", 1))
                msbank = [(pms, "pms"), (pL[0], ("pL", 0))]

                c.dma("sp", cosT[:], cosT_d, writes=["consts"])
                c.dma("sp", sinT[:], sinT_d, writes=["consts"])
                c.dma("pool", RTb[:], RT_d, writes=["consts"])
                c.dma("sp", gq[:], qn_d[L], writes=["consts"])
                c.dma("sp", gk[:], kn_d[L], writes=["consts"])
                c.dma("pool", wb4[:], wbias4_d, writes=["consts"])
                c.dma("sp", esk[:], sink_d.partition_broadcast(128), writes=["esk0"])
                c.op("act", lambda: nc.scalar.activation(esk[:], esk[:], AF.Exp), ["esk0"], ["esk0", "consts"])

                def fn_task(s, pjt, pjk, gcol, rope_cols, out_bf, outkey, out_f32=None):
                    sq, sd, kb, tt_, uu_ = T_sq[s], T_sd[s], T_kb[s], T_t[s], T_u[s]
                    pmt, pmk = msbank[s]
                    c.op("act", lambda: nc.scalar.activation(sq[:], pjt, AF.Square), [pjk], [("sq", s)])
                    yield
                    c.op("pe", lambda: nc.tensor.matmul(pmt[:], lhsT=onesf[:], rhs=sq[:], start=True, stop=True),
                         [("sq", s), "onesf"], [pmk])
                    yield
                    c.op("act", lambda: nc.scalar.activation(sd[:], pmt[:], AF.Sqrt, bias=epsc[:]),
                         [pmk, "epsc"], [("sd", s)])
                    yield
                    c.op("dve", lambda: nc.vector.reciprocal(sd[:], sd[:]), [("sd", s)], [("sd", s)])
                    yield
                    if out_f32 is None:
                        dst, dkey = sq[:], ("sq", s)
                    else:
                        dst, dkey = out_f32, outkey + ("f32",)
                    c.op("dve", lambda: nc.vector.scalar_tensor_tensor(
                        out=dst, in0=pjt, scalar=gcol, in1=sd[:], op0=ALU.mult, op1=ALU.mult),
                        [pjk, ("sd", s), "consts"], [dkey])
                    yield
                    if rope_cols is None:
                        c.op("pool", lambda: nc.gpsimd.tensor_copy(out_bf, dst), [dkey], [outkey])
                        yield
                    else:
                        c.op("pool", lambda: nc.gpsimd.tensor_copy(kb[:], dst), [dkey], [("kb", s)])
                        yield
                        prt, prk = rbank[s]
                        c.op("pe", lambda: nc.tensor.matmul(prt[:], lhsT=RTb[:], rhs=kb[:], start=True, stop=True),
                             [("kb", s), "consts"], [prk])
                        yield
                        c.op("pool", lambda: nc.gpsimd.tensor_tensor(out=tt_[:], in0=dst, in1=cosT[:, rope_cols],
                                                                     op=ALU.mult), [dkey, "consts"], [("rt", s)])
                        yield
                        c.op("dve", lambda: nc.vector.tensor_tensor(out=uu_[:], in0=prt[:], in1=sinT[:, rope_cols],
                                                                    op=ALU.mult), [prk, "consts"], [("ru", s)])
                        yield
                        c.op("pool", lambda: nc.gpsimd.tensor_tensor(out=out_bf, in0=tt_[:], in1=uu_[:], op=ALU.add),
                             [("rt", s), ("ru", s)], [outkey])
                        yield

                def rope_of(tb):
                    return None if tb == 0 else slice((tb - 1) * 512, tb * 512)

                def k_task(sl, wb, wk, j, tb):
                    cols = slice(tb * 512, (tb + 1) * 512)
                    pjt, pjk = pbank[sl]
                    mm_group(pjt[:], [(wb[:, k, 0:128], hT[:, k, cols]) for k in range(16)], [wk], [pjk])
                    yield
                    if tb == 0:
                        yield from fn_task(sl, pjt[:], pjk, gk[:, 0:1], None, KT[:, cols], ("KT", tb), out_f32=kf32[:])
                        for t4 in range(4):
                            b = t4 % 2
                            pvt, pvk = tbank
                            c.op("pe", lambda: nc.tensor.transpose(
                                pvt[:, 0:128], kf32[:, t4 * 128:(t4 + 1) * 128], identf[:]),
                                [("KT", tb, "f32"), "identf"], [pvk])
                            yield
                            c.op("act", lambda: nc.scalar.copy(kout[b][:], pvt[:, 0:128]), [pvk], [("kout", b)])
                            yield
                            c.dma("sp", nk[L][t4 * 128:(t4 + 1) * 128, j, :], kout[b][:],
                                  reads=[("kout", b)], writes=[("nk", t4, j)])
                    else:
                        yield from fn_task(sl, pjt[:], pjk, gk[:, 0:1], rope_of(tb), KT[:, cols], ("KT", tb))

                def v_task(sl, wb, wk, j, tt):
                    tcols = slice(tt * 128, (tt + 1) * 128)
                    pvt, pvk = pbank[sl]
                    mm_group(pvt[:, 0:128], [(hT[:, k, tcols], wb[:, k, 128:256]) for k in range(16)], [wk], [pvk])
                    yield
                    if tt < 4:
                        b = sl
                        c.op("act", lambda: nc.scalar.copy(vout[b][:], pvt[:, 0:128]), [pvk], [("vout", b)])
                        yield
                        c.op("pool", lambda: nc.gpsimd.tensor_copy(V[:, tt, :], vout[b][:]), [("vout", b)], [("V", tt)])
                        c.dma("sp", nv[L][tt * 128:(tt + 1) * 128, j, :], vout[b][:],
                              reads=[("vout", b)], writes=[("nv", tt, j)])
                        yield
                    else:
                        c.op("act", lambda: nc.scalar.copy(V[:, tt, :], pvt[:, 0:128]), [pvk], [("V", tt)])
                        yield

                def q_task(sl, wb, wk, hh, tb):
                    cols = slice(tb * 512, (tb + 1) * 512)
                    pjt, pjk = pbank[sl]
                    mm_group(pjt[:], [(wb[:, k, 0:128], hT[:, k, cols]) for k in range(16)], [wk], [pjk])
                    yield
                    yield from fn_task(sl, pjt[:], pjk, gq[:, 0:1], rope_of(tb), QTg[:, hh, cols], ("QT", hh, tb))

                def g_task(sl, wb, wk, hh, tb):
                    cols = slice(tb * 512, (tb + 1) * 512)
                    pjt, pjk = pbank[sl]
                    mm_group(pjt[:], [(wb[:, k, 128:256], hT[:, k, cols]) for k in range(16)], [wk], [pjk])
                    yield
                    c.op("act", lambda: nc.scalar.activation(oThg[:, hh, cols], pjt[:], AF.Silu),
                         [pjk], [("oTh", hh, tb)])
                    yield

                sctr = [0]
                bctr2 = [0]

                def attn_block(j, q0, nq, keys, tbq):
                    qcols = slice(q0, q0 + nq)
                    N = G * nq
                    ob = bctr2[0] % 2
                    bctr2[0] += 1
                    nk_ = len(keys)
                    rhsQ = QTg[:, :, qcols] if G > 1 else QTg[:, 0, qcols]
                    qkeys = [("QT", hh, tbq) for hh in range(G)]
                    slots = []

                    def smm(ki):
                        kind, idx, mi = keys[ki]
                        p = sctr[0] % 2
                        sctr[0] += 1
                        slots.append(p)
                        if kind == "l":
                            Kl = KT[:, idx * 128:(idx + 1) * 128]
                            kr = [("KT", idx // 4)]
                        else:
                            Kl = KcT[:, idx * 128:(idx + 1) * 128]
                            kr = ["KcT"]

                        def f():
                            out = pS[p][:, :N] if G == 1 else pS[p][:, :N].rearrange("p (g q) -> p g q", g=G)
                            inst = nc.tensor.matmul(out, lhsT=Kl, rhs=rhsQ, start=True, stop=(mi is None))
                            if mi is not None:
                                inst = nc.tensor.matmul(out, lhsT=identb[:], rhs=wb4[:, mi, :, 0:nq],
                                                        start=False, stop=True)
                            return inst
                        c.op("pe", f, kr + qkeys + ["consts", "identb"], [("pS", p)])

                    def pv(ki):
                        kind, idx, mi = keys[ki]
                        p = slots[ki]
                        if kind == "l":
                            Vl = V[:, idx, :]
                            kr = [("V", idx)]
                        else:
                            Vl = Vc[:, idx, :]
                            kr = ["Vc"]
                        c.op("act", lambda: nc.scalar.activation(PT[p][:, :N], pS[p][:, :N], AF.Exp, scale=SCALE),
                             [("pS", p)], [("PT", p)])

                        def f():
                            nc.tensor.matmul(pO[ob][:, :N], lhsT=Vl, rhs=PT[p][:, :N],
                                             start=(ki == 0), stop=(ki == nk_ - 1))
                            return nc.tensor.matmul(pL[ob][:, :N], lhsT=onesb[:], rhs=PT[p][:, :N],
                                                    start=(ki == 0), stop=(ki == nk_ - 1))
                        c.op("pe", f, kr + [("PT", p), "onesb"], [("pO", ob), ("pL", ob)])

                    smm(0)
                    if nk_ > 1:
                        smm(1)
                    for ki in range(nk_):
                        pv(ki)
                        if ki + 2 < nk_:
                            smm(ki + 2)
                    r = rl[ob]
                    if L == 1:
                        r3 = r[:, :N].rearrange("p (g q) -> p g q", g=G)
                        l3 = pL[ob][:, :N].rearrange("p (g q) -> p g q", g=G)
                        c.op("dve", lambda: nc.vector.tensor_tensor(
                            out=r3, in0=l3, in1=esk[:, 4 * j:4 * j + 4].unsqueeze(2).to_broadcast([128, G, nq]),
                            op=ALU.add), [("pL", ob), "consts"], [("rl", ob)])
                        c.op("dve", lambda: nc.vector.reciprocal(r[:, :N], r[:, :N]), [("rl", ob)], [("rl", ob)])
                    else:
                        c.op("dve", lambda: nc.vector.reciprocal(r[:, :N], pL[ob][:, :N]), [("pL", ob)], [("rl", ob)])
                    c.op("dve", lambda: nc.vector.tensor_tensor(out=r[:, :N], in0=pO[ob][:, :N], in1=r[:, :N],
                                                                op=ALU.mult), [("pO", ob), ("rl", ob)], [("rl", ob)])
                    okeys = [("oTh", hh, tbq) for hh in range(G)]
                    if G > 1:
                        o3 = oThg[:, :, qcols]
                        r3 = r[:, :N].rearrange("p (g q) -> p g q", g=G)
                    else:
                        o3 = oThg[:, 0, qcols]
                        r3 = r[:, :N]
                    c.op("pool", lambda: nc.gpsimd.tensor_tensor(out=o3, in0=r3, in1=o3, op=ALU.mult),
                         [("rl", ob)] + okeys, okeys)

                for j in range(nkv):
                    wb, wk = getw()
                    c.dma("pool", KcT[:], ckT[L][:, j, :], writes=["KcT"])
                    c.dma("pool", Vc[:], cv[L][:, j, :].rearrange("(t p) d -> p t d", p=128), writes=["Vc"])
                    gens = [(lambda sl, tb=tb: k_task(sl, wb, wk, j, tb)) for tb in range(NTB)] + \
                           [(lambda sl, tt=tt: v_task(sl, wb, wk, j, tt)) for tt in range(NT)]
                    run_tasks(gens, 2)
                    for h0 in range(4 * j, 4 * j + 4, G):
                        gens = []
                        for hh in range(G):
                            wbq, wkq = getw()
                            for tb in range(NTB):
                                gens.append(lambda sl, wbq=wbq, wkq=wkq, hh=hh, tb=tb: q_task(sl, wbq, wkq, hh, tb))
                                gens.append(lambda sl, wbq=wbq, wkq=wkq, hh=hh, tb=tb: g_task(sl, wbq, wkq, hh, tb))
                            if hh % 2 == 1 or G == 1:
                                run_tasks(gens, 2)
                                gens = []
                        blocks = []
                        if G == 1:
                            for (t0, n) in SEQS[:2]:
                                blocks.append((t0 * 128, 256, [("l", t0, None), ("l", t0 + 1, None)], 0))
                            for tb in range(1, NTB):
                                keys = [("l", kt, None) for kt in range(4, NT)] + [("c", 0, None), ("c", 1, None)]
                                blocks.append((tb * 512, 512, keys, tb))
                        else:
                            for (t0, n) in SEQS[:2]:
                                for tq in range(t0, t0 + n):
                                    blocks.append((tq * 128, 128, [("l", t0, None), ("l", t0 + 1, None)], 0))
                            for i in range(16):
                                keys = []
                                if i > 0:
                                    keys.append(("l", 4 + i - 1, 0))
                                keys.append(("l", 4 + i, None))
                                if i < 15:
                                    keys.append(("l", 4 + i + 1, 1))
                                keys += [("c", 0, None), ("c", 1, None)]
                                blocks.append(((4 + i) * 128, 128, keys, (4 + i) // 4))
                        for (q0, nq, keys, tbq) in blocks:
                            attn_block(j, q0, nq, keys, tbq)
                        for hh in range(G):
                            r0 = (orow0 + h0 + hh) * 128
                            c.dma("sp", oTs[L][r0:r0 + 128, :], oThg[:, hh, :],
                                  reads=[("oTh", hh, tb) for tb in range(NTB)], writes=[("oTs", L, orow0 + h0 + hh)])
                c.barrier_all()

        def phase_hgrn():
            with ExitStack() as es:
                getw = make_stream(es, [(w_in[0], h * 640, 640) for h in range(8)], 640)
                maskF = sb(es, "maskF", [128, 512], F32)
                mfb = sb(es, "mfb", [128, 2, 128], F32)
                cm4 = sb(es, "cm4", [128, 4, 128], BF16)
                onc = sb(es, "onc", [128, 1], F32)
                lbe = sb(es, "lbe", [128, 3, 16], F32)
                lbs = sb(es, "lbs", [128, 16], F32)
                lbv = sb(es, "lbv", [128, 16], F32)
                oml = sb(es, "oml", [128, 16], F32)
                noml = sb(es, "noml", [128, 16], F32)
                q32 = [sb(es, f"q32_{i}", [128, 512], F32) for i in range(2)]
                sg = [sb(es, f"sg_{i}", [128, 512], F32) for i in range(2)]
                lg = [sb(es, f"lg_{i}", [128, 512], F32) for i in range(2)]
                k32 = [sb(es, f"k32_{i}", [128, 512], F32) for i in range(2)]
                bF = [sb(es, f"bF_{i}", [128, 512], F32) for i in range(2)]
                totc = [sb(es, f"totc_{i}", [128, 16], F32) for i in range(2)]
                dec = sb(es, "dec", [128, 2, 80], F32)
                qd = [sb(es, f"qd{d}", [128, T], BF16) for d in range(2)]
                ki = [sb(es, f"ki{d}", [128, T], BF16) for d in range(2)]
                keT = [sb(es, f"keT{d}", [128, T], BF16) for d in range(2)]
                sgT = sb(es, "sgT", [128, T], BF16)
                V = sb(es, "Va", [128, NT, 128], BF16)
                OT = sb(es, "OT", [128, T], F32)
                kend = [[sb(es, f"kend{d}{r}", [128, 128], BF16) for r in range(2)] for d in range(2)]
                Vm = [[sb(es, f"Vm{d}{r}", [128, 4, 128], BF16) for r in range(2)] for d in range(2)]
                Am = [[sb(es, f"Am{d}{r}", [128, 128], BF16) for r in range(2)] for d in range(2)]
                S32 = [[sb(es, f"S32_{d}{r}", [128, 128], F32) for r in range(2)] for d in range(2)]
                Sbf = [[sb(es, f"Sbf{d}{p}", [128, 128], BF16) for p in range(4)] for d in range(2)]
                fb = [ps(es, f"hfb{i}", [128, 512]) for i in range(2)]
                bA = [ps(es, f"hbA{d}", [128, 512]) for d in range(2)]
                pO = [ps(es, f"hpO{d}", [128, 512]) for d in range(2)]
                ptr = [ps(es, f"hptr{d}", [128, 8, 128], BF16) for d in range(2)]

                c.dma("sp", maskF[:], maskF_d, writes=["hc"])
                c.dma("sp", mfb[:], mfb_d, writes=["hc"])
                c.dma("pool", cm4[:], cm4_d, writes=["hc"])
                c.dma("sp", onc[:], onorm_d, writes=["hc"])
                c.dma("sp", lbe[:], lbg, writes=["lbe"])
                c.op("act", lambda: nc.scalar.activation(lbe[:], lbe[:], AF.Exp), ["lbe"], ["lbe"])
                c.op("dve", lambda: nc.vector.tensor_tensor(out=lbs[:], in0=lbe[:, 0, :], in1=lbe[:, 1, :], op=ALU.add),
                     ["lbe"], ["lbs"])
                c.op("dve", lambda: nc.vector.tensor_tensor(out=lbs[:], in0=lbs[:], in1=lbe[:, 2, :], op=ALU.add),
                     ["lbe", "lbs"], ["lbs"])
                c.op("dve", lambda: nc.vector.reciprocal(lbs[:], lbs[:]), ["lbs"], ["lbs"])
                c.op("dve", lambda: nc.vector.tensor_tensor(out=lbv[:], in0=lbe[:, 0, :], in1=lbs[:], op=ALU.mult),
                     ["lbe", "lbs"], ["lbv"])
                c.op("dve", lambda: nc.vector.tensor_scalar(out=oml[:], in0=lbv[:], scalar1=-1.0, scalar2=1.0,
                                                            op0=ALU.mult, op1=ALU.add), ["lbv"], ["oml"])
                c.op("dve", lambda: nc.vector.tensor_scalar(out=noml[:], in0=oml[:], scalar1=-1.0, scalar2=None,
                                                            op0=ALU.mult), ["oml"], ["noml", "hc"])

                def bc32(t, ncol=16):
                    return t.unsqueeze(2).to_broadcast([128, ncol, 32])

                def f_task(sl, h, wb, wk, tb):
                    cols = slice(tb * 512, (tb + 1) * 512)
                    pjt, pjk = fb[sl], ("fb", sl)
                    Q, SG, LG, K, B, TC = q32[sl], sg[sl], lg[sl], k32[sl], bF[sl], totc[sl]
                    kq, ks, kl, kk, kb_, kt = ("q32", sl), ("sg", sl), ("lg", sl), ("k32", sl), ("bF", sl), ("totc", sl)
                    mm_group(pjt[:], [(wb[:, k, 0:128], hT[:, k, cols]) for k in range(16)], [wk], [pjk])
                    yield
                    c.op("act", lambda: nc.scalar.activation(Q[:], pjt[:], AF.Silu), [pjk], [kq])
                    yield
                    for d in range(2):
                        i = d * 8 + h
                        mm_group(pjt[:], [(wb[:, k, 128 * (1 + d):128 * (2 + d)], hT[:, k, cols]) for k in range(16)],
                                 [wk], [pjk])
                        yield
                        c.op("act", lambda: nc.scalar.activation(SG[:], pjt[:], AF.Sigmoid), [pjk], [ks])
                        yield
                        c.op("act", lambda: nc.scalar.activation(LG[:], SG[:], AF.Ln, scale=oml[:, i:i + 1],
                                                                 bias=lbv[:, i:i + 1]), [ks, "hc"], [kl])
                        c.op("dve", lambda: nc.vector.tensor_scalar(
                            out=K[:], in0=SG[:], scalar1=noml[:, i:i + 1], scalar2=oml[:, i:i + 1],
                            op0=ALU.mult, op1=ALU.add), [ks, "hc"], [kk])
                        yield
                        c.op("dve", lambda: nc.vector.tensor_tensor_scan(B[:], maskF[:], LG[:], 0.0, ALU.mult, ALU.add),
                             [kl, "hc"], [kb_])
                        yield
                        tot = B[:].rearrange("p (c t) -> p c t", t=32)[:, :, 31]
                        dslice = dec[:, d, tb * 16:(tb + 1) * 16]
                        c.op("act", lambda: nc.scalar.activation(dslice, tot, AF.Exp), [kb_], [("dec", d, tb)])
                        if d == 1:
                            c.op("act", lambda: nc.scalar.copy(TC[:], tot), [kb_], [kt])
                            yield
                            b3 = B[:].rearrange("p (c t) -> p c t", t=32)
                            c.op("dve", lambda: nc.vector.tensor_tensor(out=b3, in0=bc32(TC[:]), in1=b3, op=ALU.subtract),
                                 [kb_, kt], [kb_])
                            yield
                            c.op("pool", lambda: nc.gpsimd.tensor_tensor(out=B[:], in0=B[:], in1=LG[:], op=ALU.add),
                                 [kb_, kl], [kb_])
                        yield
                        c.op("act", lambda: nc.scalar.activation(SG[:], B[:], AF.Exp), [kb_], [ks])
                        c.op("act", lambda: nc.scalar.activation(LG[:], B[:], AF.Exp, scale=-1.0), [kb_], [kl])
                        yield
                        c.op("dve", lambda: nc.vector.tensor_tensor(out=qd[d][:, cols], in0=Q[:], in1=SG[:], op=ALU.mult),
                             [kq, ks], [("qd", d, tb)])
                        yield
                        c.op("dve", lambda: nc.vector.tensor_tensor(out=LG[:], in0=K[:], in1=LG[:], op=ALU.mult),
                             [kk, kl], [kl])
                        yield
                        c.op("pool", lambda: nc.gpsimd.tensor_copy(ki[d][:, cols], LG[:]), [kl], [("ki", d, tb)])
                        k3 = LG[:].rearrange("p (c t) -> p c t", t=32)
                        o3 = keT[d][:, cols].rearrange("p (c t) -> p c t", t=32)
                        c.op("pool", lambda: nc.gpsimd.tensor_tensor(out=o3, in0=k3, in1=bc32(dslice), op=ALU.mult),
                             [kl, ("dec", d, tb)], [("keT", d, tb)])
                        yield
                    mm_group(pjt[:], [(wb[:, k, 512:640], hT[:, k, cols]) for k in range(16)], [wk], [pjk])
                    yield
                    c.op("act", lambda: nc.scalar.activation(sgT[:, cols], pjt[:], AF.Silu), [pjk], [("sgT", tb)])
                    yield
                    for tt in range(tb * 4, tb * 4 + 4):
                        tcols = slice(tt * 128, (tt + 1) * 128)
                        mm_group(pjt[:, 0:128], [(hT[:, k, tcols], wb[:, k, 384:512]) for k in range(16)], [wk], [pjk])
                        yield
                        c.op("act", lambda: nc.scalar.copy(V[:, tt, :], pjt[:, 0:128]), [pjk], [("V", tt)])
                        yield

                ot_written = set()
                vctr = [0, 0]
                sctr = [0, 0]

                def sweep(d, h, si, t0, n):
                    cur = sctr[d] % 2
                    if si < 2:
                        c.op("dve", lambda: nc.vector.memset(S32[d][cur][:], 0.0), [], [("S32", d, cur)])
                    else:
                        c.dma("sp", S32[d][cur][:], st0[d, h], writes=[("S32", d, cur)])
                    sb0 = sctr[d] % 4
                    c.op("act", lambda: nc.scalar.copy(Sbf[d][sb0][:], S32[d][cur][:]),
                         [("S32", d, cur)], [("Sbf", d, sb0)])
                    yield
                    tiles = range(t0, t0 + n) if d == 0 else range(t0 + n - 1, t0 - 1, -1)
                    for tt in tiles:
                        tb = tt // 4
                        tcols = slice(tt * 128, (tt + 1) * 128)
                        r = vctr[d] % 2
                        vctr[d] += 1
                        c.op("pe", lambda: nc.tensor.transpose(ptr[d][:, 0, :], keT[d][:, tcols], identb[:]),
                             [("keT", d, tb), "identb"], [("ptr", d)])
                        c.op("pool", lambda: nc.gpsimd.tensor_tensor(
                            out=Vm[d][r][:], in0=V[:, tt, :].unsqueeze(1).to_broadcast([128, 4, 128]), in1=cm4[:],
                            op=ALU.mult), [("V", tt), "hc"], [("Vm", d, r)])
                        yield
                        c.op("act", lambda: nc.scalar.copy(kend[d][r][:], ptr[d][:, 0, :]), [("ptr", d)], [("kend", d, r)])
                        c.op("pe", lambda: nc.tensor.matmul(bA[d][:, 0:128], lhsT=ki[d][:, tcols], rhs=qd[d][:, tcols],
                                                            start=True, stop=True),
                             [("ki", d, tb), ("qd", d, tb)], [("bA", d)])
                        yield
                        c.op("dve", lambda: nc.vector.tensor_tensor(out=Am[d][r][:], in0=bA[d][:, 0:128], in1=mfb[:, d, :],
                                                                    op=ALU.mult), [("bA", d), "hc"], [("Am", d, r)])

                        def umm():
                            inst = None
                            for j in range(4):
                                inst = nc.tensor.matmul(fb[d][:, j * 128:(j + 1) * 128], lhsT=kend[d][r][:],
                                                        rhs=Vm[d][r][:, j, :], start=True, stop=True)
                            return inst
                        c.op("pe", umm, [("kend", d, r), ("Vm", d, r)], [("fb", d)])
                        yield
                        c.op("pe", lambda: nc.tensor.matmul(pO[d][:, 0:128], lhsT=V[:, tt, :], rhs=Am[d][r][:],
                                                            start=True, stop=False),
                             [("V", tt), ("Am", d, r)], [("pO", d)])
                        yield
                        order = range(4) if d == 0 else range(3, -1, -1)
                        for n_, j in enumerate(order):
                            cur = sctr[d] % 2
                            nxt = 1 - cur
                            sbc = sctr[d] % 4
                            sbn = (sctr[d] + 1) % 4
                            ccols = slice(tt * 128 + 32 * j, tt * 128 + 32 * j + 32)
                            gch = tt * 4 + j
                            c.op("pe", lambda: nc.tensor.matmul(
                                pO[d][:, 32 * j:32 * j + 32], lhsT=Sbf[d][sbc][:], rhs=qd[d][:, ccols],
                                start=False, stop=(n_ == 3)),
                                [("Sbf", d, sbc), ("qd", d, tb)], [("pO", d)])
                            c.op("dve", lambda: nc.vector.scalar_tensor_tensor(
                                out=S32[d][nxt][:], in0=S32[d][cur][:], scalar=dec[:, d, gch:gch + 1],
                                in1=fb[d][:, j * 128:(j + 1) * 128], op0=ALU.mult, op1=ALU.add),
                                [("S32", d, cur), ("dec", d, tb), ("fb", d)], [("S32", d, nxt)])
                            yield
                            c.op("act", lambda: nc.scalar.copy(Sbf[d][sbn][:], S32[d][nxt][:]),
                                 [("S32", d, nxt)], [("Sbf", d, sbn)])
                            sctr[d] += 1
                            yield
                        if tt not in ot_written:
                            ot_written.add(tt)
                            c.op("act", lambda: nc.scalar.copy(OT[:, tcols], pO[d][:, 0:128]), [("pO", d)], [("OT", tt)])
                        else:
                            c.op("dve", lambda: nc.vector.tensor_tensor(out=OT[:, tcols], in0=pO[d][:, 0:128],
                                                                        in1=OT[:, tcols], op=ALU.add),
                                 [("pO", d), ("OT", tt)], [("OT", tt)])
                        yield
                    if si < 2:
                        fin = sctr[d] % 2
                        c.dma("sp", nstate[si, d, h], S32[d][fin][:], reads=[("S32", d, fin)],
                              writes=[("nstate", si, d, h)])

                def n_task(sl, h, tb):
                    cols = slice(tb * 512, (tb + 1) * 512)
                    ok = [("OT", tt) for tt in range(tb * 4, tb * 4 + 4)]
                    SQ, SD = q32[sl], sg[sl]
                    kq, ks = ("q32", sl), ("sg", sl)
                    pjt, pjk = fb[sl], ("fb", sl)
                    c.op("act", lambda: nc.scalar.activation(SQ[:], OT[:, cols], AF.Square), ok, [kq])
                    yield
                    c.op("pe", lambda: nc.tensor.matmul(pjt[:], lhsT=onesf[:], rhs=SQ[:], start=True, stop=True),
                         [kq, "onesf"], [pjk])
                    yield
                    c.op("act", lambda: nc.scalar.activation(SD[:], pjt[:], AF.Sqrt, bias=epsc[:]), [pjk, "epsc"], [ks])
                    yield
                    c.op("dve", lambda: nc.vector.reciprocal(SD[:], SD[:]), [ks], [ks])
                    yield
                    c.op("dve", lambda: nc.vector.scalar_tensor_tensor(
                        out=SQ[:], in0=OT[:, cols], scalar=onc[:, 0:1], in1=SD[:], op0=ALU.mult, op1=ALU.mult),
                        ok + [ks, "hc"], [kq])
                    yield
                    ostv = k32[sl][:].bitcast(BF16)[:, 0:512]
                    c.op("pool", lambda: nc.gpsimd.tensor_tensor(out=ostv, in0=SQ[:], in1=sgT[:, cols], op=ALU.mult),
                         [kq, ("sgT", tb)], [("k32", sl)])
                    c.dma("sp", oTs[0][h * 128:(h + 1) * 128, cols], ostv, reads=[("k32", sl)],
                          writes=[("oTs", 0, h)])
                    yield

                for h in range(8):
                    wb, wk = getw()
                    run_tasks([(lambda sl, tb=tb: f_task(sl, h, wb, wk, tb)) for tb in range(NTB)], 2)
                    ot_written.clear()
                    for si, (t0, n) in enumerate(SEQS):
                        run_tasks([(lambda sl, d=d: sweep(d, h, si, t0, n)) for d in range(2)], 2)
                    run_tasks([(lambda sl, tb=tb: n_task(sl, h, tb)) for tb in range(NTB)], 2)
                c.barrier_all()

        def phase_out(L):
            with ExitStack() as es:
                getw = make_stream(es, [(w_out[L], cb * 512, 512) for cb in range(4)], 512)
                gtb = sb(es, "gtb", [128, 2, D], F32)
                xr = [sb(es, f"xr{i}", [128, 512], F32) for i in range(2)]
                tm = [sb(es, f"otm{i}", [128, 512], F32) for i in range(2)]
                yt = [sb(es, f"yt{i}", [128, 512], F32) for i in range(2)]
                po = [ps(es, f"po{i}", [128, 512]) for i in range(2)]
                for g in range(2):
                    c.dma("sp", gtb[:, g, :], gts[L * 2 + g:L * 2 + g + 1, :].partition_broadcast(128),
                          reads=[("gts", L, g, cb) for cb in range(4)], writes=[("gtb", g)])
                for k in range(16):
                    c.dma("sp", hT[:, k, :], oTs[L][k * 128:(k + 1) * 128, :], reads=[("oTs", L, k)],
                          writes=[("hTk", k)])
                for cb in range(4):
                    ccols = slice(cb * 512, (cb + 1) * 512)
                    wb, wk = getw()
                    for tt in range(NT):
                        b = tt % 2
                        g = 0 if tt < 4 else 1
                        rows = slice(tt * 128, (tt + 1) * 128)
                        tcols = slice(tt * 128, (tt + 1) * 128)
                        mm_group(po[b][:], [(hT[:, k, tcols], wb[:, k, 0:512]) for k in range(16)],
                                 [wk] + [("hTk", k) for k in range(16)], [("po", b)])
                        if L == 0:
                            c.dma("sp", xr[b][:], x[rows, ccols], writes=[("xr", b)])
                        else:
                            c.dma("sp", xr[b][:], y1[rows, ccols], reads=[("y1", tt, cb)], writes=[("xr", b)])
                        c.op("dve", lambda b=b, g=g, ccols=ccols: nc.vector.tensor_tensor(
                            out=tm[b][:], in0=po[b][:], in1=gtb[:, g, ccols], op=ALU.mult),
                            [("po", b), ("gtb", g)], [("otm", b)])
                        c.op("pool", lambda b=b: nc.gpsimd.tensor_tensor(out=yt[b][:], in0=tm[b][:], in1=xr[b][:],
                                                                         op=ALU.add),
                             [("otm", b), ("xr", b)], [("yt", b)])
                        if L == 0:
                            c.dma("sp", y1[rows, ccols], yt[b][:], reads=[("yt", b)], writes=[("y1", tt, cb)])
                        else:
                            c.dma("sp", y[rows, ccols], yt[b][:], reads=[("yt", b)], writes=[("y", tt, cb)])
                c.barrier_all()

        plist = [("mod0", lambda: phase_mod_h(0)), ("hgrn", phase_hgrn), ("attn0", lambda: phase_attn(0)),
                 ("out0", lambda: phase_out(0)), ("mod1", lambda: phase_mod_h(1)), ("attn1", lambda: phase_attn(1)),
                 ("out1", lambda: phase_out(1))]
        for nm, fn in plist:
            if phases is None or nm in phases:
                fn()
        c.finish()
    return nc


def _consts():
    ident = np.eye(128, dtype=np.float32)
    maskF = np.ones((128, 512), np.float32)
    maskF[:, ::32] = 0.0
    s = np.arange(128)[:, None]
    t = np.arange(128)[None, :]
    same = (s // 32) == (t // 32)
    mfb = np.stack([(same & (s <= t)), (same & (s >= t))], axis=1).astype(np.float32)
    cm4 = np.zeros((128, 4, 128), np.float32)
    for j in range(4):
        cm4[32 * j:32 * j + 32, j, :] = 1.0
    R = np.zeros((128, 128), np.float32)
    for m in range(128):
        q = m // 32
        if q in (0, 2):
            R[m, m + 32] = -1.0
        else:
            R[m, m - 32] = 1.0
    RT = np.ascontiguousarray(R.T)
    n_tok = 2048
    row = (np.arange(n_tok) // 64).astype(np.float32)
    col = (np.arange(n_tok) % 64).astype(np.float32)
    inv = (10000.0 ** (-np.arange(32, dtype=np.float32) / 32)).astype(np.float32)
    ar = row[:, None] * inv
    ac = col[:, None] * inv
    ang = np.concatenate([ar, ar, ac, ac], axis=-1).astype(np.float32)
    cosT = np.ascontiguousarray(np.cos(ang).T.astype(np.float32))
    sinT = np.ascontiguousarray(np.sin(ang).T.astype(np.float32))
    b = np.arange(128)[:, None]
    a = np.arange(128)[None, :]
    NEG = -30000.0
    wbias = np.stack([np.where(b >= a, 0.0, NEG), np.where(b <= a, 0.0, NEG)], axis=1).astype(np.float32)
    wbias4 = np.ascontiguousarray(np.broadcast_to(wbias[:, :, None, :], (128, 2, 4, 128))).astype(np.float32)
    return dict(ident=ident, maskF=maskF, mfb=mfb, cm4=cm4, RT=RT, cosT=cosT, sinT=sinT, wbias4=wbias4)


def _perm_w0(w):
    cols = []
    for h in range(8):
        for base in (0, 1024, 2048, 3072, 4096):
            cols.append(np.arange(base + h * 128, base + (h + 1) * 128))
    for j in range(2):
        cols.append(np.arange(6144 + j * 128, 6144 + (j + 1) * 128))
        cols.append(np.arange(6400 + j * 128, 6400 + (j + 1) * 128))
    for h in range(8):
        cols.append(np.arange(5120 + h * 128, 5120 + (h + 1) * 128))
        cols.append(np.arange(6656 + h * 128, 6656 + (h + 1) * 128))
    return np.ascontiguousarray(w[:, np.concatenate(cols)])


def _perm_w1(w):
    cols = []
    for j in range(4):
        cols.append(np.arange(2048 + j * 128, 2048 + (j + 1) * 128))
        cols.append(np.arange(2560 + j * 128, 2560 + (j + 1) * 128))
    for h in range(16):
        cols.append(np.arange(h * 128, (h + 1) * 128))
        cols.append(np.arange(3072 + h * 128, 3072 + (h + 1) * 128))
    return np.ascontiguousarray(w[:, np.concatenate(cols)])


def _prep(x_prompt, x_sample, state_l0_hgrn, cache_l0_k, cache_l0_v, cache_l1_k, cache_l1_v,
           c, c_ctx, lb_gamma,
           l0_norm, l0_w_mod, l0_b_mod, l0_w_in, l0_w_out, l0_a_onorm, l0_b_qnorm, l0_b_knorm,
           l1_norm, l1_w_mod, l1_b_mod, l1_w_in, l1_w_out, l1_c_qnorm, l1_c_knorm, l1_c_sink):
    f = lambda a: np.ascontiguousarray(np.asarray(a, dtype=np.float32))
    x_prompt, x_sample = f(x_prompt), f(x_sample)
    consts = _consts()
    shared = dict(
        w_mod0=f(l0_w_mod), w_mod1=f(l1_w_mod),
        b_mod0=f(l0_b_mod).reshape(1, -1), b_mod1=f(l1_b_mod).reshape(1, -1),
        norm0=f(l0_norm).reshape(1, -1), norm1=f(l1_norm).reshape(1, -1),
        w_in0=_perm_w0(f(l0_w_in)), w_in1=_perm_w1(f(l1_w_in)),
        w_out0=f(l0_w_out), w_out1=f(l1_w_out),
        onorm=f(l0_a_onorm).reshape(128, 1),
        qn0=f(l0_b_qnorm).reshape(128, 1), kn0=f(l0_b_knorm).reshape(128, 1),
        qn1=f(l1_c_qnorm).reshape(128, 1), kn1=f(l1_c_knorm).reshape(128, 1),
        sink=f(l1_c_sink).reshape(1, 16),
        lbg=np.ascontiguousarray(f(lb_gamma).reshape(3, 2, 8, 128).transpose(3, 0, 1, 2).reshape(128, 3, 16)),
        **consts,
    )
    c = f(c)
    c_ctx = f(c_ctx)
    in_maps = []
    for i in range(8):
        m = dict(shared)
        m["x"] = np.ascontiguousarray(np.concatenate(
            [x_prompt[2 * i], x_prompt[2 * i + 1], x_sample[i]], axis=0))
        cr = np.stack([c_ctx, c[i]], axis=0)
        m["crows"] = np.ascontiguousarray(cr.reshape(2, 16, 128).transpose(2, 0, 1))
        m["st0"] = f(state_l0_hgrn[i])
        m["ck0T"] = np.ascontiguousarray(f(cache_l0_k[i]).transpose(2, 1, 0))
        m["cv0"] = f(cache_l0_v[i])
        m["ck1T"] = np.ascontiguousarray(f(cache_l1_k[i]).transpose(2, 1, 0))
        m["cv1"] = f(cache_l1_v[i])
        in_maps.append(m)
    return in_maps


def kernel(**inputs):
    in_maps = _prep(**inputs)
    nc = build_program()
    res = run_bass_kernel_spmd(nc, in_maps, core_ids=list(range(8)))
    r = res.results
    y_prompt = np.stack([r[i // 2]["y"][(i % 2) * 256:(i % 2 + 1) * 256] for i in range(16)], axis=0)
    y_sample = np.stack([r[i]["y"][512:] for i in range(8)], axis=0)
    nstate = np.concatenate([r[i]["nstate"] for i in range(8)], axis=0)
    outs = [y_prompt.astype(np.float32), y_sample.astype(np.float32), nstate.astype(np.float32)]
    for nm in ("nk0", "nv0", "nk1", "nv1"):
        a = np.concatenate([r[i][nm].reshape(2, 256, r[i][nm].shape[1], 128) for i in range(8)], axis=0)
        outs.append(a.astype(np.float32))
    return tuple(outs)
```

```python
import math
import os
from contextlib import ExitStack

import numpy as np
import concourse.bass as bass
import concourse.mybir as mybir
from concourse.bass_utils import run_bass_kernel_spmd

F32 = mybir.dt.float32
BF16 = mybir.dt.bfloat16
AF = mybir.ActivationFunctionType
ALU = mybir.AluOpType

D = 2048
T = 2560
NT = 20
NTB = 5
EPS = 1e-6
SCALE = 1.0 / math.sqrt(128.0)
SEQS = [(0, 2), (2, 2), (4, 16)]


class Ctx:
    NDMA = 24

    def __init__(self, nc):
        self.nc = nc
        self.engs = {"pe": nc.tensor, "dve": nc.vector, "act": nc.scalar,
                     "pool": nc.gpsimd, "sp": nc.sync}
        self.sem = {k: nc.alloc_semaphore(name="s_" + k) for k in self.engs}
        self.cnt = {k: 0 for k in self.engs}
        self.waited = {k: {} for k in self.engs}
        self.last_w = {}
        self.readers = {}
        self.dma_sems = [nc.alloc_semaphore(name=f"s_dma{i}") for i in range(self.NDMA)]
        self.dma_val = [0] * self.NDMA
        self.dma_pool = {"sp": list(range(0, 16)), "pool": list(range(16, self.NDMA))}
        self.dma_rr = {"sp": 0, "pool": 0}

    def _deps(self, reads, writes):
        evs = []
        for r in reads:
            e = self.last_w.get(r)
            if e is not None:
                evs.append(e)
        for w in writes:
            e = self.last_w.get(w)
            if e is not None:
                evs.append(e)
            evs.extend(self.readers.get(w, ()))
        return evs

    def _wait(self, eng, evs, skip_self=False):
        best = {}
        for (name, sem, val) in evs:
            if skip_self and name == eng:
                continue
            if best.get(name, (None, 0))[1] < val:
                best[name] = (sem, val)
        wd = self.waited[eng]
        for name, (sem, val) in best.items():
            if wd.get(name, 0) < val:
                self.engs[eng].wait_ge(sem, val)
                wd[name] = val

    def _commit(self, ev, reads, writes):
        ws = set(writes)
        for r in reads:
            if r in ws:
                continue
            self.readers.setdefault(r, []).append(ev)
        for w in writes:
            self.last_w[w] = ev
            self.readers[w] = []

    EXCL = {"pj", "pms", "prot", "pv", "pS", "pO", "pL", "pm", "pT", "bA", "ptr", "po", "fb"}

    def op(self, eng, fn, reads=(), writes=()):
        reads = list(reads)
        writes = list(writes)
        ex = [r for r in reads if (r if isinstance(r, str) else r[0]) in self.EXCL]
        if ex:
            reads = [r for r in reads if r not in ex]
            writes = writes + [r for r in ex if r not in writes]
        evs = self._deps(reads, writes)
        self._wait(eng, evs, skip_self=(eng == "pe"))
        inst = fn()
        inst.then_inc(self.sem[eng], 1)
        self.cnt[eng] += 1
        ev = (eng, self.sem[eng], self.cnt[eng])
        self._commit(ev, reads, writes)
        return ev

    def dma(self, q, out, in_, reads=(), writes=()):
        reads = list(reads)
        writes = list(writes)
        evs = self._deps(reads, writes)
        lst = self.dma_pool[q]
        k = lst[self.dma_rr[q] % len(lst)]
        self.dma_rr[q] += 1
        name = f"dma{k}"
        if self.dma_val[k] > 0:
            evs.append((name, self.dma_sems[k], self.dma_val[k]))
        self._wait(q, evs)
        self.engs[q].dma_start(out=out, in_=in_).then_inc(self.dma_sems[k], 16)
        self.dma_val[k] += 16
        ev = (name, self.dma_sems[k], self.dma_val[k])
        self._commit(ev, reads, writes)
        return ev

    def all_events(self):
        evs = [(k, self.sem[k], self.cnt[k]) for k in self.engs if self.cnt[k] > 0]
        for i in range(self.NDMA):
            if self.dma_val[i] > 0:
                evs.append((f"dma{i}", self.dma_sems[i], self.dma_val[i]))
        return evs

    def barrier_all(self):
        evs = self.all_events()
        for e in self.engs:
            self._wait(e, evs, skip_self=True)

    def finish(self):
        self._wait("sp", self.all_events(), skip_self=True)


def build_program(phases=None):
    nc = bass.Bass("TRN2", target_bir_lowering=False)

    def din(name, shape, dt=F32):
        return nc.dram_tensor(name, list(shape), dt, kind="ExternalInput").ap()

    def dout(name, shape, dt=F32):
        return nc.dram_tensor(name, list(shape), dt, kind="ExternalOutput").ap()

    def dscr(name, shape, dt=F32):
        return nc.dram_tensor(name, list(shape), dt, kind="Internal").ap()

    x = din("x", [T, D])
    crows = din("crows", [128, 2, 16])
    lbg = din("lbg", [128, 3, 16])
    st0 = din("st0", [2, 8, 128, 128])
    ckT = [din("ck0T", [128, 2, 256]), din("ck1T", [128, 4, 256])]
    cv = [din("cv0", [256, 2, 128]), din("cv1", [256, 4, 128])]
    w_mod = [din("w_mod0", [D, 3 * D]), din("w_mod1", [D, 3 * D])]
    b_mod = [din("b_mod0", [1, 3 * D]), din("b_mod1", [1, 3 * D])]
    norm_g = [din("norm0", [1, D]), din("norm1", [1, D])]
    w_in = [din("w_in0", [D, 7680]), din("w_in1", [D, 5120])]
    w_out = [din("w_out0", [D, D]), din("w_out1", [D, D])]
    onorm_d = din("onorm", [128, 1])
    qn_d = [din("qn0", [128, 1]), din("qn1", [128, 1])]
    kn_d = [din("kn0", [128, 1]), din("kn1", [128, 1])]
    sink_d = din("sink", [1, 16])
    ident_d = din("ident", [128, 128])
    maskF_d = din("maskF", [128, 512])
    mfb_d = din("mfb", [128, 2, 128])
    cm4_d = din("cm4", [128, 4, 128])
    RT_d = din("RT", [128, 128])
    cosT_d = din("cosT", [128, 2048])
    sinT_d = din("sinT", [128, 2048])
    wbias4_d = din("wbias4", [128, 2, 4, 128])

    y = dout("y", [T, D])
    nstate = dout("nstate", [2, 2, 8, 128, 128])
    nk = [dout("nk0", [512, 2, 128]), dout("nk1", [512, 4, 128])]
    nv = [dout("nv0", [512, 2, 128]), dout("nv1", [512, 4, 128])]

    gts = dscr("gts", [4, D])
    oTs = [dscr("oT0", [D, T], BF16), dscr("oT1", [D, T], BF16)]
    y1 = dscr("y1", [T, D])

    c = Ctx(nc)

    uid = [0]

    def sb(es, name, shape, dt):
        uid[0] += 1
        return es.enter_context(nc.sbuf_tensor(f"{name}_{uid[0]}", list(shape), dt))

    def ps(es, name, shape, dt=F32):
        uid[0] += 1
        return es.enter_context(nc.psum_tensor(f"{name}_{uid[0]}", list(shape), dt))

    def mm_group(out, pairs, reads, writes):
        def f():
            n = len(pairs)
            inst = None
            for i, (l, r) in enumerate(pairs):
                inst = nc.tensor.matmul(out, lhsT=l, rhs=r, start=(i == 0), stop=(i == n - 1))
            return inst
        return c.op("pe", f, reads, writes)

    def wview(w, c0, ncols):
        return w[:, c0:c0 + ncols].rearrange("(k p) n -> p k n", p=128)

    with ExitStack() as top:
        identb = sb(top, "identb", [128, 128], BF16)
        identf = sb(top, "identf", [128, 128], F32)
        onesb = sb(top, "onesb", [128, 128], BF16)
        onesf = sb(top, "onesf", [128, 128], F32)
        epsc = sb(top, "epsc", [128, 1], F32)
        hT = sb(top, "hT", [128, 16, T], BF16)

        c.dma("sp", identf[:], ident_d, writes=["identf"])
        c.dma("pool", identb[:], ident_d, writes=["identb"])
        c.op("dve", lambda: nc.vector.memset(onesb[:], 1.0), writes=["onesb"])
        c.op("dve", lambda: nc.vector.memset(onesf[:], 1.0 / 128.0), writes=["onesf"])
        c.op("dve", lambda: nc.vector.memset(epsc[:], EPS), writes=["epsc"])

        def make_stream(es, blocks, ncols_max, nslots=2):
            bufs = [sb(es, f"wbuf{i}", [128, 16, ncols_max], BF16) for i in range(nslots)]
            st = {"issued": 0, "got": 0}

            def issue():
                i = st["issued"]
                if i >= len(blocks):
                    return
                w, c0, ncols = blocks[i]
                s = i % nslots
                c.dma("pool", bufs[s][:, :, 0:ncols], wview(w, c0, ncols), writes=[("wb", s)])
                st["issued"] += 1

            def get():
                i = st["got"]
                while st["issued"] <= i:
                    issue()
                st["got"] += 1
                if st["issued"] <= i + 1:
                    issue()
                s = i % nslots
                return bufs[s], ("wb", s)
            return get

        hkeys = lambda tb: [("hT", tt) for tt in range(tb * 4, tb * 4 + 4)]
        allh = [("hT", tt) for tt in range(NT)]

        def phase_mod_h(L):
            with ExitStack() as es:
                G = sb(es, "G", [128, 2, D], F32)
                SH = sb(es, "SH", [128, 2, D], F32)
                with ExitStack() as es2:
                    getw = make_stream(es2, [(w_mod[L], blk * 512, 512) for blk in range(12)], 512)
                    crs = sb(es2, "crs", [128, 2, 16], F32)
                    scl = sb(es2, "scl", [128, 2, 16], F32)
                    srep = sb(es2, "srep", [128, 2, 16, 128], BF16)
                    bb = [sb(es2, f"bb{i}", [128, 512], F32) for i in range(2)]
                    gb = [sb(es2, f"gb{i}", [128, 512], F32) for i in range(2)]
                    tmp = [sb(es2, f"mtmp{i}", [128, 512], F32) for i in range(2)]
                    pm = [ps(es2, f"pm{i}", [128, 512]) for i in range(2)]
                    c.dma("sp", crs[:], crows, writes=["crs"])
                    c.op("act", lambda: nc.scalar.activation(scl[:], crs[:], AF.Silu), ["crs"], ["scl"])
                    c.op("dve", lambda: nc.vector.tensor_copy(
                        srep[:], scl[:].unsqueeze(3).to_broadcast([128, 2, 16, 128])), ["scl"], ["srep"])
                    for blk in range(12):
                        kind, cb = divmod(blk, 4)
                        b = blk % 2
                        wb, wk = getw()
                        c.dma("sp", bb[b][:], b_mod[L][:, blk * 512:(blk + 1) * 512].partition_broadcast(128),
                              writes=[("bb", b)])
                        if kind == 1:
                            c.dma("sp", gb[b][:], norm_g[L][:, cb * 512:(cb + 1) * 512].partition_broadcast(128),
                                  writes=[("gb", b)])
                        cols = slice(cb * 512, (cb + 1) * 512)
                        for g in range(2):
                            mm_group(pm[g][:], [(srep[:, g, k, :], wb[:, k, 0:512]) for k in range(16)],
                                     [wk, "srep"], [("pm", g)])
                            if kind == 0:
                                c.op("dve", lambda g=g, b=b, cols=cols: nc.vector.tensor_tensor(
                                    out=SH[:, g, cols], in0=pm[g][:], in1=bb[b][:], op=ALU.add),
                                    [("pm", g), ("bb", b)], [("SH", g, cb)])
                            elif kind == 1:
                                c.op("dve", lambda g=g, b=b: nc.vector.tensor_tensor(
                                    out=tmp[g][:], in0=pm[g][:], in1=bb[b][:], op=ALU.add),
                                    [("pm", g), ("bb", b)], [("mtmp", g)])
                                c.op("dve", lambda g=g, b=b, cols=cols: nc.vector.scalar_tensor_tensor(
                                    out=G[:, g, cols], in0=tmp[g][:], scalar=1.0, in1=gb[b][:],
                                    op0=ALU.add, op1=ALU.mult),
                                    [("mtmp", g), ("gb", b)], [("G", g, cb)])
                            else:
                                c.op("dve", lambda g=g, b=b: nc.vector.tensor_tensor(
                                    out=tmp[g][:], in0=pm[g][:], in1=bb[b][:], op=ALU.add),
                                    [("pm", g), ("bb", b)], [("mtmp", g)])
                                c.dma("sp", gts[L * 2 + g:L * 2 + g + 1, cols], tmp[g][0:1, :],
                                      reads=[("mtmp", g)], writes=[("gts", L, g, cb)])
                    c.barrier_all()
                with ExitStack() as es2:
                    xt = [sb(es2, f"xt{i}", [128, D], F32) for i in range(2)]
                    junk = sb(es2, "junk", [128, D], BF16)
                    st = sb(es2, "st", [128, 8], F32)
                    t1 = sb(es2, "t1", [128, D], F32)
                    hb = [sb(es2, f"hb{i}", [128, D], BF16) for i in range(2)]
                    pT = [ps(es2, f"pT{i}", [128, 8, 128], BF16) for i in range(4)]
                    GK = [[("G", g, cb) for cb in range(4)] for g in range(2)]
                    SK = [[("SH", g, cb) for cb in range(4)] for g in range(2)]
                    for tt in range(NT):
                        b = tt % 2
                        g = 0 if tt < 4 else 1
                        rows = slice(tt * 128, (tt + 1) * 128)
                        if L == 0:
                            c.dma("sp", xt[b][:], x[rows, :], writes=[("xt", b)])
                        else:
                            c.dma("sp", xt[b][:], y1[rows, :], reads=[("y1", tt, cb) for cb in range(4)],
                                  writes=[("xt", b)])
                        c.op("act", lambda b=b: nc.scalar.activation(junk[:], xt[b][:], AF.Square,
                                                                     accum_out=st[:, b:b + 1]),
                             [("xt", b)], ["junk", ("ssq", b)])
                        c.op("act", lambda b=b: nc.scalar.activation(st[:, 2 + b:3 + b], st[:, b:b + 1], AF.Sqrt,
                                                                     scale=1.0 / D, bias=epsc[:]),
                             [("ssq", b), "epsc"], [("std", b)])
                        c.op("dve", lambda b=b: nc.vector.reciprocal(st[:, 4 + b:5 + b], st[:, 2 + b:3 + b]),
                             [("std", b)], [("rstd", b)])
                        c.op("dve", lambda b=b, g=g: nc.vector.scalar_tensor_tensor(
                            out=t1[:], in0=xt[b][:], scalar=st[:, 4 + b:5 + b], in1=G[:, g, :],
                            op0=ALU.mult, op1=ALU.mult),
                            [("xt", b), ("rstd", b)] + GK[g], ["t1"])
                        c.op("pool", lambda b=b, g=g: nc.gpsimd.tensor_tensor(
                            out=hb[b][:], in0=t1[:], in1=SH[:, g, :], op=ALU.add),
                            ["t1"] + SK[g], [("hb", b)])
                        for half in range(2):
                            pp = pT[b * 2 + half]

                            def tr(pp=pp, b=b, half=half):
                                inst = None
                                for kk in range(8):
                                    k = half * 8 + kk
                                    inst = nc.tensor.transpose(pp[:, kk, :], hb[b][:, k * 128:(k + 1) * 128], identb[:])
                                return inst
                            c.op("pe", tr, [("hb", b), "identb"], [("pT", b, half)])
                            eng = "act" if half == 0 else "dve"
                            dst = hT[:, half * 8:(half + 1) * 8, tt * 128:(tt + 1) * 128]
                            if eng == "act":
                                c.op("act", lambda pp=pp, dst=dst: nc.scalar.copy(dst, pp[:]),
                                     [("pT", b, half)], [("hT", tt, half)])
                            else:
                                c.op("dve", lambda pp=pp, dst=dst: nc.vector.tensor_copy(dst, pp[:]),
                                     [("pT", b, half)], [("hT", tt, half)])
                    c.barrier_all()
                c.barrier_all()

        def run_tasks(factories, width):
            it = iter(factories)
            active = {}
            free = list(range(width))
            while True:
                while free:
                    f = next(it, None)
                    if f is None:
                        break
                    sl = free.pop(0)
                    active[sl] = f(sl)
                if not active:
                    break
                for sl in sorted(active):
                    try:
                        next(active[sl])
                    except StopIteration:
                        del active[sl]
                        free.append(sl)

        def phase_attn(L):
            nkv = 2 if L == 0 else 4
            G = 1 if L == 0 else 4
            if L == 0:
                kvc0, qc0, orow0 = 5120, 5120 + 512, 8
            else:
                kvc0, qc0, orow0 = 0, 1024, 0
            with ExitStack() as es:
                ablocks = []
                for j_ in range(nkv):
                    ablocks.append((w_in[L], kvc0 + j_ * 256, 256))
                    for h_ in range(4 * j_, 4 * j_ + 4):
                        ablocks.append((w_in[L], qc0 + h_ * 256, 256))
                getw = make_stream(es, ablocks, 256, nslots=3)

                T_sq = [sb(es, f"sq{i}", [128, 512], F32) for i in range(3)]
                T_sd = [sb(es, f"sd{i}", [128, 512], F32) for i in range(3)]
                T_kb = [sb(es, f"kb{i}", [128, 512], BF16) for i in range(3)]
                T_u = [sb(es, f"ru{i}", [128, 512], F32) for i in range(3)]
                cosT = sb(es, "cosT", [128, 2048], F32)
                sinT = sb(es, "sinT", [128, 2048], F32)
                RTb = sb(es, "RTb", [128, 128], BF16)
                gq = sb(es, "gq", [128, 1], F32)
                gk = sb(es, "gk", [128, 1], F32)
                esk = sb(es, "esk", [128, 16], F32)
                wb4 = sb(es, "wb4", [128, 2, 4, 128], BF16)
                KT = sb(es, "KT", [128, T], BF16)
                KcT = sb(es, "KcT", [128, 256], BF16)
                V = sb(es, "V", [128, NT, 128], BF16)
                Vc = sb(es, "Vc", [128, 2, 128], BF16)
                QTg = sb(es, "QTg", [128, G, T], BF16)
                oThg = sb(es, "oThg", [128, G, T], BF16)
                kf32 = sb(es, "kf32", [128, 512], F32)
                kout = [sb(es, f"kout{i}", [128, 128], F32) for i in range(2)]
                vout = [sb(es, f"vout{i}", [128, 128], F32) for i in range(3)]
                PT = [sb(es, f"PT{i}", [128, 512], BF16) for i in range(2)]
                rl = [sb(es, f"rl{i}", [128, 512], F32) for i in range(2)]
                pj = ps(es, "pj", [128, 512])
                pms = ps(es, "pms", [128, 512])
                pS = [ps(es, f"pS{i}", [128, 512]) for i in range(2)]
                pO = [ps(es, f"pO{i}", [128, 512]) for i in range(2)]
                pL = [ps(es, f"pL{i}", [128, 512]) for i in range(2)]
                pbank = [(pj, "pj"), (pS[0], ("pS", 0)), (pS[1], ("pS", 1))]
                rbank = pbank
                tbank = (pO[0], ("pO", 0))
                msbank = [(pms, "pms"), (pL[0], ("pL", 0)), (pL[1], ("pL", 1))]
                NW = 3

                c.dma("sp", cosT[:], cosT_d, writes=["consts"])
                c.dma("sp", sinT[:], sinT_d, writes=["consts"])
                c.dma("pool", RTb[:], RT_d, writes=["consts"])
                c.dma("sp", gq[:], qn_d[L], writes=["consts"])
                c.dma("sp", gk[:], kn_d[L], writes=["consts"])
                c.dma("pool", wb4[:], wbias4_d, writes=["consts"])
                c.dma("sp", esk[:], sink_d.partition_broadcast(128), writes=["esk0"])
                c.op("act", lambda: nc.scalar.activation(esk[:], esk[:], AF.Exp), ["esk0"], ["esk0", "consts"])

                def fn_task(s, pjt, pjk, gcol, rope_cols, out_bf, outkey, out_f32=None):
                    sq, sd, kb, uu_ = T_sq[s], T_sd[s], T_kb[s], T_u[s]
                    pmt, pmk = msbank[s]
                    c.op("act", lambda: nc.scalar.activation(sq[:], pjt, AF.Square), [pjk], [("sq", s)])
                    yield
                    c.op("pe", lambda: nc.tensor.matmul(pmt[:], lhsT=onesf[:], rhs=sq[:], start=True, stop=True),
                         [("sq", s), "onesf"], [pmk])
                    yield
                    c.op("act", lambda: nc.scalar.activation(sd[:], pmt[:], AF.Ln, bias=epsc[:]),
                         [pmk, "epsc"], [("sd", s)])
                    yield
                    c.op("act", lambda: nc.scalar.activation(sd[:], sd[:], AF.Exp, scale=-0.5), [("sd", s)], [("sd", s)])
                    yield
                    if out_f32 is None:
                        dst, dkey = sq[:], ("sq", s)
                    else:
                        dst, dkey = out_f32, outkey + ("f32",)
                    c.op("dve", lambda: nc.vector.scalar_tensor_tensor(
                        out=dst, in0=pjt, scalar=gcol, in1=sd[:], op0=ALU.mult, op1=ALU.mult),
                        [pjk, ("sd", s), "consts"], [dkey])
                    yield
                    if rope_cols is None:
                        c.op("pool", lambda: nc.gpsimd.tensor_copy(out_bf, dst), [dkey], [outkey])
                        yield
                    else:
                        c.op("pool", lambda: nc.gpsimd.tensor_copy(kb[:], dst), [dkey], [("kb", s)])
                        yield
                        prt, prk = rbank[s]
                        c.op("pe", lambda: nc.tensor.matmul(prt[:], lhsT=RTb[:], rhs=kb[:], start=True, stop=True),
                             [("kb", s), "consts"], [prk])
                        yield
                        c.op("pool", lambda: nc.gpsimd.tensor_tensor(out=dst, in0=dst, in1=cosT[:, rope_cols],
                                                                     op=ALU.mult), [dkey, "consts"], [dkey])
                        yield
                        c.op("dve", lambda: nc.vector.tensor_tensor(out=uu_[:], in0=prt[:], in1=sinT[:, rope_cols],
                                                                    op=ALU.mult), [prk, "consts"], [("ru", s)])
                        yield
                        c.op("pool", lambda: nc.gpsimd.tensor_tensor(out=out_bf, in0=dst, in1=uu_[:], op=ALU.add),
                             [dkey, ("ru", s)], [outkey])
                        yield

                def rope_of(tb):
                    return None if tb == 0 else slice((tb - 1) * 512, tb * 512)

                def k_task(sl, wb, wk, j, tb):
                    cols = slice(tb * 512, (tb + 1) * 512)
                    pjt, pjk = pbank[sl]
                    mm_group(pjt[:], [(wb[:, k, 0:128], hT[:, k, cols]) for k in range(16)], [wk], [pjk])
                    yield
                    if tb == 0:
                        yield from fn_task(sl, pjt[:], pjk, gk[:, 0:1], None, KT[:, cols], ("KT", tb), out_f32=kf32[:])
                        for t4 in range(4):
                            b = t4 % 2
                            pvt, pvk = tbank
                            c.op("pe", lambda: nc.tensor.transpose(
                                pvt[:, 0:128], kf32[:, t4 * 128:(t4 + 1) * 128], identf[:]),
                                [("KT", tb, "f32"), "identf"], [pvk])
                            yield
                            c.op("act", lambda: nc.scalar.copy(kout[b][:], pvt[:, 0:128]), [pvk], [("kout", b)])
                            yield
                            c.dma("sp", nk[L][t4 * 128:(t4 + 1) * 128, j, :], kout[b][:],
                                  reads=[("kout", b)], writes=[("nk", t4, j)])
                    else:
                        yield from fn_task(sl, pjt[:], pjk, gk[:, 0:1], rope_of(tb), KT[:, cols], ("KT", tb))

                def v_task(sl, wb, wk, j, tt):
                    tcols = slice(tt * 128, (tt + 1) * 128)
                    pvt, pvk = pbank[sl]
                    mm_group(pvt[:, 0:128], [(hT[:, k, tcols], wb[:, k, 128:256]) for k in range(16)], [wk], [pvk])
                    yield
                    if tt < 4:
                        b = sl
                        c.op("act", lambda: nc.scalar.copy(vout[b][:], pvt[:, 0:128]), [pvk], [("vout", b)])
                        yield
                        c.op("pool", lambda: nc.gpsimd.tensor_copy(V[:, tt, :], vout[b][:]), [("vout", b)], [("V", tt)])
                        c.dma("sp", nv[L][tt * 128:(tt + 1) * 128, j, :], vout[b][:],
                              reads=[("vout", b)], writes=[("nv", tt, j)])
                        yield
                    else:
                        c.op("act", lambda: nc.scalar.copy(V[:, tt, :], pvt[:, 0:128]), [pvk], [("V", tt)])
                        yield

                def q_task(sl, wb, wk, hh, tb):
                    cols = slice(tb * 512, (tb + 1) * 512)
                    pjt, pjk = pbank[sl]
                    mm_group(pjt[:], [(wb[:, k, 0:128], hT[:, k, cols]) for k in range(16)], [wk], [pjk])
                    yield
                    yield from fn_task(sl, pjt[:], pjk, gq[:, 0:1], rope_of(tb), QTg[:, hh, cols], ("QT", hh, tb))

                def g_task(sl, wb, wk, hh, tb):
                    cols = slice(tb * 512, (tb + 1) * 512)
                    pjt, pjk = pbank[sl]
                    mm_group(pjt[:], [(wb[:, k, 128:256], hT[:, k, cols]) for k in range(16)], [wk], [pjk])
                    yield
                    c.op("act", lambda: nc.scalar.activation(oThg[:, hh, cols], pjt[:], AF.Silu),
                         [pjk], [("oTh", hh, tb)])
                    yield

                sctr = [0]
                bctr2 = [0]

                def attn_block(j, q0, nq, keys, tbq):
                    qcols = slice(q0, q0 + nq)
                    N = G * nq
                    ob = bctr2[0] % 2
                    bctr2[0] += 1
                    nk_ = len(keys)
                    rhsQ = QTg[:, :, qcols] if G > 1 else QTg[:, 0, qcols]
                    qkeys = [("QT", hh, tbq) for hh in range(G)]
                    slots = []

                    def smm(ki):
                        kind, idx, mi = keys[ki]
                        p = sctr[0] % 2
                        sctr[0] += 1
                        slots.append(p)
                        if kind == "l":
                            Kl = KT[:, idx * 128:(idx + 1) * 128]
                            kr = [("KT", idx // 4)]
                        else:
                            Kl = KcT[:, idx * 128:(idx + 1) * 128]
                            kr = ["KcT"]

                        def f():
                            out = pS[p][:, :N] if G == 1 else pS[p][:, :N].rearrange("p (g q) -> p g q", g=G)
                            inst = nc.tensor.matmul(out, lhsT=Kl, rhs=rhsQ, start=True, stop=(mi is None))
                            if mi is not None:
                                inst = nc.tensor.matmul(out, lhsT=identb[:], rhs=wb4[:, mi, :, 0:nq],
                                                        start=False, stop=True)
                            return inst
                        c.op("pe", f, kr + qkeys + ["consts", "identb"], [("pS", p)])

                    def pv(ki):
                        kind, idx, mi = keys[ki]
                        p = slots[ki]
                        if kind == "l":
                            Vl = V[:, idx, :]
                            kr = [("V", idx)]
                        else:
                            Vl = Vc[:, idx, :]
                            kr = ["Vc"]
                        c.op("act", lambda: nc.scalar.activation(PT[p][:, :N], pS[p][:, :N], AF.Exp, scale=SCALE),
                             [("pS", p)], [("PT", p)])

                        def f():
                            nc.tensor.matmul(pO[ob][:, :N], lhsT=Vl, rhs=PT[p][:, :N],
                                             start=(ki == 0), stop=(ki == nk_ - 1))
                            return nc.tensor.matmul(pL[ob][:, :N], lhsT=onesb[:], rhs=PT[p][:, :N],
                                                    start=(ki == 0), stop=(ki == nk_ - 1))
                        c.op("pe", f, kr + [("PT", p), "onesb"], [("pO", ob), ("pL", ob)])

                    smm(0)
                    if nk_ > 1:
                        smm(1)
                    for ki in range(nk_):
                        pv(ki)
                        if ki + 2 < nk_:
                            smm(ki + 2)
                    r = rl[ob]
                    if L == 1:
                        r3 = r[:, :N].rearrange("p (g q) -> p g q", g=G)
                        l3 = pL[ob][:, :N].rearrange("p (g q) -> p g q", g=G)
                        c.op("dve", lambda: nc.vector.tensor_tensor(
                            out=r3, in0=l3, in1=esk[:, 4 * j:4 * j + 4].unsqueeze(2).to_broadcast([128, G, nq]),
                            op=ALU.add), [("pL", ob), "consts"], [("rl", ob)])
                        c.op("act", lambda: nc.scalar.activation(r[:, :N], r[:, :N], AF.Ln), [("rl", ob)], [("rl", ob)])
                    else:
                        c.op("act", lambda: nc.scalar.activation(r[:, :N], pL[ob][:, :N], AF.Ln), [("pL", ob)], [("rl", ob)])
                    c.op("act", lambda: nc.scalar.activation(r[:, :N], r[:, :N], AF.Exp, scale=-1.0),
                         [("rl", ob)], [("rl", ob)])
                    c.op("dve", lambda: nc.vector.tensor_tensor(out=r[:, :N], in0=pO[ob][:, :N], in1=r[:, :N],
                                                                op=ALU.mult), [("pO", ob), ("rl", ob)], [("rl", ob)])
                    okeys = [("oTh", hh, tbq) for hh in range(G)]
                    if G > 1:
                        o3 = oThg[:, :, qcols]
                        r3 = r[:, :N].rearrange("p (g q) -> p g q", g=G)
                    else:
                        o3 = oThg[:, 0, qcols]
                        r3 = r[:, :N]
                    c.op("pool", lambda: nc.gpsimd.tensor_tensor(out=o3, in0=r3, in1=o3, op=ALU.mult),
                         [("rl", ob)] + okeys, okeys)

                for j in range(nkv):
                    wb, wk = getw()
                    c.dma("pool", KcT[:], ckT[L][:, j, :], writes=["KcT"])
                    c.dma("pool", Vc[:], cv[L][:, j, :].rearrange("(t p) d -> p t d", p=128), writes=["Vc"])
                    gens = [(lambda sl, tb=tb: k_task(sl, wb, wk, j, tb)) for tb in range(NTB)] + \
                           [(lambda sl, tt=tt: v_task(sl, wb, wk, j, tt)) for tt in range(NT)]
                    run_tasks(gens, NW)
                    for h0 in range(4 * j, 4 * j + 4, G):
                        gens = []
                        for hh in range(G):
                            wbq, wkq = getw()
                            for tb in range(NTB):
                                gens.append(lambda sl, wbq=wbq, wkq=wkq, hh=hh, tb=tb: q_task(sl, wbq, wkq, hh, tb))
                                gens.append(lambda sl, wbq=wbq, wkq=wkq, hh=hh, tb=tb: g_task(sl, wbq, wkq, hh, tb))
                            if hh % 2 == 1 or G == 1:
                                run_tasks(gens, NW)
                                gens = []
                        blocks = []
                        if G == 1:
                            for (t0, n) in SEQS[:2]:
                                blocks.append((t0 * 128, 256, [("l", t0, None), ("l", t0 + 1, None)], 0))
                            for tb in range(1, NTB):
                                keys = [("l", kt, None) for kt in range(4, NT)] + [("c", 0, None), ("c", 1, None)]
                                blocks.append((tb * 512, 512, keys, tb))
                        else:
                            for (t0, n) in SEQS[:2]:
                                for tq in range(t0, t0 + n):
                                    blocks.append((tq * 128, 128, [("l", t0, None), ("l", t0 + 1, None)], 0))
                            for i in range(16):
                                keys = []
                                if i > 0:
                                    keys.append(("l", 4 + i - 1, 0))
                                keys.append(("l", 4 + i, None))
                                if i < 15:
                                    keys.append(("l", 4 + i + 1, 1))
                                keys += [("c", 0, None), ("c", 1, None)]
                                blocks.append(((4 + i) * 128, 128, keys, (4 + i) // 4))
                        for (q0, nq, keys, tbq) in blocks:
                            attn_block(j, q0, nq, keys, tbq)
                        for hh in range(G):
                            r0 = (orow0 + h0 + hh) * 128
                            c.dma("sp", oTs[L][r0:r0 + 128, :], oThg[:, hh, :],
                                  reads=[("oTh", hh, tb) for tb in range(NTB)], writes=[("oTs", L, orow0 + h0 + hh)])
                c.barrier_all()

        def phase_hgrn():
            with ExitStack() as es:
                getw = make_stream(es, [(w_in[0], h * 640, 640) for h in range(8)], 640)
                maskF = sb(es, "maskF", [128, 512], F32)
                mfb = sb(es, "mfb", [128, 2, 128], F32)
                cm4 = sb(es, "cm4", [128, 4, 128], BF16)
                onc = sb(es, "onc", [128, 1], F32)
                lbe = sb(es, "lbe", [128, 3, 16], F32)
                lbs = sb(es, "lbs", [128, 16], F32)
                lbv = sb(es, "lbv", [128, 16], F32)
                oml = sb(es, "oml", [128, 16], F32)
                noml = sb(es, "noml", [128, 16], F32)
                q32 = [sb(es, f"q32_{i}", [128, 512], F32) for i in range(2)]
                sg = [sb(es, f"sg_{i}", [128, 512], F32) for i in range(2)]
                lg = [sb(es, f"lg_{i}", [128, 512], F32) for i in range(2)]
                k32 = [sb(es, f"k32_{i}", [128, 512], F32) for i in range(2)]
                bF = [sb(es, f"bF_{i}", [128, 512], F32) for i in range(2)]
                totc = [sb(es, f"totc_{i}", [128, 16], F32) for i in range(2)]
                dec = sb(es, "dec", [128, 2, 80], F32)
                qd = [sb(es, f"qd{d}", [128, T], BF16) for d in range(2)]
                ki = [sb(es, f"ki{d}", [128, T], BF16) for d in range(2)]
                keT = [sb(es, f"keT{d}", [128, T], BF16) for d in range(2)]
                sgT = sb(es, "sgT", [128, T], BF16)
                V = sb(es, "Va", [128, NT, 128], BF16)
                OT = sb(es, "OT", [128, T], F32)
                kend = [[sb(es, f"kend{d}{r}", [128, 128], BF16) for r in range(2)] for d in range(2)]
                Vm = [[sb(es, f"Vm{d}{r}", [128, 4, 128], BF16) for r in range(2)] for d in range(2)]
                Am = [[sb(es, f"Am{d}{r}", [128, 128], BF16) for r in range(2)] for d in range(2)]
                S32 = [[sb(es, f"S32_{d}{r}", [128, 128], F32) for r in range(2)] for d in range(2)]
                Sbf = [[sb(es, f"Sbf{d}{p}", [128, 128], BF16) for p in range(4)] for d in range(2)]
                fb = [ps(es, f"hfb{i}", [128, 512]) for i in range(2)]
                bA = [ps(es, f"hbA{d}", [128, 512]) for d in range(2)]
                pO = [ps(es, f"hpO{d}", [128, 512]) for d in range(2)]
                ptr = [ps(es, f"hptr{d}", [128, 8, 128], BF16) for d in range(2)]

                c.dma("sp", maskF[:], maskF_d, writes=["hc"])
                c.dma("sp", mfb[:], mfb_d, writes=["hc"])
                c.dma("pool", cm4[:], cm4_d, writes=["hc"])
                c.dma("sp", onc[:], onorm_d, writes=["hc"])
                c.dma("sp", lbe[:], lbg, writes=["lbe"])
                c.op("act", lambda: nc.scalar.activation(lbe[:], lbe[:], AF.Exp), ["lbe"], ["lbe"])
                c.op("dve", lambda: nc.vector.tensor_tensor(out=lbs[:], in0=lbe[:, 0, :], in1=lbe[:, 1, :], op=ALU.add),
                     ["lbe"], ["lbs"])
                c.op("dve", lambda: nc.vector.tensor_tensor(out=lbs[:], in0=lbs[:], in1=lbe[:, 2, :], op=ALU.add),
                     ["lbe", "lbs"], ["lbs"])
                c.op("dve", lambda: nc.vector.reciprocal(lbs[:], lbs[:]), ["lbs"], ["lbs"])
                c.op("dve", lambda: nc.vector.tensor_tensor(out=lbv[:], in0=lbe[:, 0, :], in1=lbs[:], op=ALU.mult),
                     ["lbe", "lbs"], ["lbv"])
                c.op("dve", lambda: nc.vector.tensor_scalar(out=oml[:], in0=lbv[:], scalar1=-1.0, scalar2=1.0,
                                                            op0=ALU.mult, op1=ALU.add), ["lbv"], ["oml"])
                c.op("dve", lambda: nc.vector.tensor_scalar(out=noml[:], in0=oml[:], scalar1=-1.0, scalar2=None,
                                                            op0=ALU.mult), ["oml"], ["noml", "hc"])

                def bc32(t, ncol=16):
                    return t.unsqueeze(2).to_broadcast([128, ncol, 32])

                def f_task(sl, h, wb, wk, tb):
                    cols = slice(tb * 512, (tb + 1) * 512)
                    pjt, pjk = fb[sl], ("fb", sl)
                    Q, SG, LG, K, B, TC = q32[sl], sg[sl], lg[sl], k32[sl], bF[sl], totc[sl]
                    kq, ks, kl, kk, kb_, kt = ("q32", sl), ("sg", sl), ("lg", sl), ("k32", sl), ("bF", sl), ("totc", sl)
                    mm_group(pjt[:], [(wb[:, k, 0:128], hT[:, k, cols]) for k in range(16)], [wk], [pjk])
                    yield
                    c.op("act", lambda: nc.scalar.activation(Q[:], pjt[:], AF.Silu), [pjk], [kq])
                    yield
                    for d in range(2):
                        i = d * 8 + h
                        mm_group(pjt[:], [(wb[:, k, 128 * (1 + d):128 * (2 + d)], hT[:, k, cols]) for k in range(16)],
                                 [wk], [pjk])
                        yield
                        c.op("act", lambda: nc.scalar.activation(SG[:], pjt[:], AF.Sigmoid), [pjk], [ks])
                        yield
                        c.op("act", lambda: nc.scalar.activation(LG[:], SG[:], AF.Ln, scale=oml[:, i:i + 1],
                                                                 bias=lbv[:, i:i + 1]), [ks, "hc"], [kl])
                        c.op("dve", lambda: nc.vector.tensor_scalar(
                            out=K[:], in0=SG[:], scalar1=noml[:, i:i + 1], scalar2=oml[:, i:i + 1],
                            op0=ALU.mult, op1=ALU.add), [ks, "hc"], [kk])
                        yield
                        c.op("dve", lambda: nc.vector.tensor_tensor_scan(B[:], maskF[:], LG[:], 0.0, ALU.mult, ALU.add),
                             [kl, "hc"], [kb_])
                        yield
                        tot = B[:].rearrange("p (c t) -> p c t", t=32)[:, :, 31]
                        dslice = dec[:, d, tb * 16:(tb + 1) * 16]
                        c.op("act", lambda: nc.scalar.activation(dslice, tot, AF.Exp), [kb_], [("dec", d, tb)])
                        if d == 1:
                            c.op("act", lambda: nc.scalar.copy(TC[:], tot), [kb_], [kt])
                            yield
                            b3 = B[:].rearrange("p (c t) -> p c t", t=32)
                            c.op("dve", lambda: nc.vector.tensor_tensor(out=b3, in0=bc32(TC[:]), in1=b3, op=ALU.subtract),
                                 [kb_, kt], [kb_])
                            yield
                            c.op("pool", lambda: nc.gpsimd.tensor_tensor(out=B[:], in0=B[:], in1=LG[:], op=ALU.add),
                                 [kb_, kl], [kb_])
                        yield
                        c.op("act", lambda: nc.scalar.activation(SG[:], B[:], AF.Exp), [kb_], [ks])
                        c.op("act", lambda: nc.scalar.activation(LG[:], B[:], AF.Exp, scale=-1.0), [kb_], [kl])
                        yield
                        c.op("dve", lambda: nc.vector.tensor_tensor(out=qd[d][:, cols], in0=Q[:], in1=SG[:], op=ALU.mult),
                             [kq, ks], [("qd", d, tb)])
                        yield
                        c.op("dve", lambda: nc.vector.tensor_tensor(out=LG[:], in0=K[:], in1=LG[:], op=ALU.mult),
                             [kk, kl], [kl])
                        yield
                        c.op("pool", lambda: nc.gpsimd.tensor_copy(ki[d][:, cols], LG[:]), [kl], [("ki", d, tb)])
                        k3 = LG[:].rearrange("p (c t) -> p c t", t=32)
                        o3 = keT[d][:, cols].rearrange("p (c t) -> p c t", t=32)
                        c.op("pool", lambda: nc.gpsimd.tensor_tensor(out=o3, in0=k3, in1=bc32(dslice), op=ALU.mult),
                             [kl, ("dec", d, tb)], [("keT", d, tb)])
                        yield
                    mm_group(pjt[:], [(wb[:, k, 512:640], hT[:, k, cols]) for k in range(16)], [wk], [pjk])
                    yield
                    c.op("act", lambda: nc.scalar.activation(sgT[:, cols], pjt[:], AF.Silu), [pjk], [("sgT", tb)])
                    yield
                    for tt in range(tb * 4, tb * 4 + 4):
                        tcols = slice(tt * 128, (tt + 1) * 128)
                        mm_group(pjt[:, 0:128], [(hT[:, k, tcols], wb[:, k, 384:512]) for k in range(16)], [wk], [pjk])
                        yield
                        c.op("act", lambda: nc.scalar.copy(V[:, tt, :], pjt[:, 0:128]), [pjk], [("V", tt)])
                        yield

                ot_written = set()
                vctr = [0, 0]
                sctr = [0, 0]

                def sweep(d, h, si, t0, n):
                    cur = sctr[d] % 2
                    if si < 2:
                        c.op("dve", lambda: nc.vector.memset(S32[d][cur][:], 0.0), [], [("S32", d, cur)])
                    else:
                        c.dma("sp", S32[d][cur][:], st0[d, h], writes=[("S32", d, cur)])
                    sb0 = sctr[d] % 4
                    c.op("act", lambda: nc.scalar.copy(Sbf[d][sb0][:], S32[d][cur][:]),
                         [("S32", d, cur)], [("Sbf", d, sb0)])
                    yield
                    tiles = range(t0, t0 + n) if d == 0 else range(t0 + n - 1, t0 - 1, -1)
                    for tt in tiles:
                        tb = tt // 4
                        tcols = slice(tt * 128, (tt + 1) * 128)
                        r = vctr[d] % 2
                        vctr[d] += 1
                        c.op("pe", lambda: nc.tensor.transpose(ptr[d][:, 0, :], keT[d][:, tcols], identb[:]),
                             [("keT", d, tb), "identb"], [("ptr", d)])
                        c.op("pool", lambda: nc.gpsimd.tensor_tensor(
                            out=Vm[d][r][:], in0=V[:, tt, :].unsqueeze(1).to_broadcast([128, 4, 128]), in1=cm4[:],
                            op=ALU.mult), [("V", tt), "hc"], [("Vm", d, r)])
                        yield
                        c.op("act", lambda: nc.scalar.copy(kend[d][r][:], ptr[d][:, 0, :]), [("ptr", d)], [("kend", d, r)])
                        c.op("pe", lambda: nc.tensor.matmul(bA[d][:, 0:128], lhsT=ki[d][:, tcols], rhs=qd[d][:, tcols],
                                                            start=True, stop=True),
                             [("ki", d, tb), ("qd", d, tb)], [("bA", d)])
                        yield
                        c.op("dve", lambda: nc.vector.tensor_tensor(out=Am[d][r][:], in0=bA[d][:, 0:128], in1=mfb[:, d, :],
                                                                    op=ALU.mult), [("bA", d), "hc"], [("Am", d, r)])

                        def umm():
                            inst = None
                            for j in range(4):
                                inst = nc.tensor.matmul(fb[d][:, j * 128:(j + 1) * 128], lhsT=kend[d][r][:],
                                                        rhs=Vm[d][r][:, j, :], start=True, stop=True)
                            return inst
                        c.op("pe", umm, [("kend", d, r), ("Vm", d, r)], [("fb", d)])
                        yield
                        c.op("pe", lambda: nc.tensor.matmul(pO[d][:, 0:128], lhsT=V[:, tt, :], rhs=Am[d][r][:],
                                                            start=True, stop=False),
                             [("V", tt), ("Am", d, r)], [("pO", d)])
                        yield
                        order = range(4) if d == 0 else range(3, -1, -1)
                        for n_, j in enumerate(order):
                            cur = sctr[d] % 2
                            nxt = 1 - cur
                            sbc = sctr[d] % 4
                            sbn = (sctr[d] + 1) % 4
                            ccols = slice(tt * 128 + 32 * j, tt * 128 + 32 * j + 32)
                            gch = tt * 4 + j
                            c.op("pe", lambda: nc.tensor.matmul(
                                pO[d][:, 32 * j:32 * j + 32], lhsT=Sbf[d][sbc][:], rhs=qd[d][:, ccols],
                                start=False, stop=(n_ == 3)),
                                [("Sbf", d, sbc), ("qd", d, tb)], [("pO", d)])
                            c.op("dve", lambda: nc.vector.scalar_tensor_tensor(
                                out=S32[d][nxt][:], in0=S32[d][cur][:], scalar=dec[:, d, gch:gch + 1],
                                in1=fb[d][:, j * 128:(j + 1) * 128], op0=ALU.mult, op1=ALU.add),
                                [("S32", d, cur), ("dec", d, tb), ("fb", d)], [("S32", d, nxt)])
                            yield
                            c.op("act", lambda: nc.scalar.copy(Sbf[d][sbn][:], S32[d][nxt][:]),
                                 [("S32", d, nxt)], [("Sbf", d, sbn)])
                            sctr[d] += 1
                            yield
                        if tt not in ot_written:
                            ot_written.add(tt)
                            c.op("act", lambda: nc.scalar.copy(OT[:, tcols], pO[d][:, 0:128]), [("pO", d)], [("OT", tt)])
                        else:
                            c.op("dve", lambda: nc.vector.tensor_tensor(out=OT[:, tcols], in0=pO[d][:, 0:128],
                                                                        in1=OT[:, tcols], op=ALU.add),
                                 [("pO", d), ("OT", tt)], [("OT", tt)])
                        yield
                    if si < 2:
                        fin = sctr[d] % 2
                        c.dma("sp", nstate[si, d, h], S32[d][fin][:], reads=[("S32", d, fin)],
                              writes=[("nstate", si, d, h)])

                def n_task(sl, h, tb):
                    cols = slice(tb * 512, (tb + 1) * 512)
                    ok = [("OT", tt) for tt in range(tb * 4, tb * 4 + 4)]
                    SQ, SD = q32[sl], sg[sl]
                    kq, ks = ("q32", sl), ("sg", sl)
                    pjt, pjk = fb[sl], ("fb", sl)
                    c.op("act", lambda: nc.scalar.activation(SQ[:], OT[:, cols], AF.Square), ok, [kq])
                    yield
                    c.op("pe", lambda: nc.tensor.matmul(pjt[:], lhsT=onesf[:], rhs=SQ[:], start=True, stop=True),
                         [kq, "onesf"], [pjk])
                    yield
                    c.op("act", lambda: nc.scalar.activation(SD[:], pjt[:], AF.Ln, bias=epsc[:]), [pjk, "epsc"], [ks])
                    yield
                    c.op("act", lambda: nc.scalar.activation(SD[:], SD[:], AF.Exp, scale=-0.5), [ks], [ks])
                    yield
                    c.op("dve", lambda: nc.vector.scalar_tensor_tensor(
                        out=SQ[:], in0=OT[:, cols], scalar=onc[:, 0:1], in1=SD[:], op0=ALU.mult, op1=ALU.mult),
                        ok + [ks, "hc"], [kq])
                    yield
                    ostv = k32[sl][:].bitcast(BF16)[:, 0:512]
                    c.op("pool", lambda: nc.gpsimd.tensor_tensor(out=ostv, in0=SQ[:], in1=sgT[:, cols], op=ALU.mult),
                         [kq, ("sgT", tb)], [("k32", sl)])
                    c.dma("sp", oTs[0][h * 128:(h + 1) * 128, cols], ostv, reads=[("k32", sl)],
                          writes=[("oTs", 0, h)])
                    yield

                for h in range(8):
                    wb, wk = getw()
                    run_tasks([(lambda sl, tb=tb: f_task(sl, h, wb, wk, tb)) for tb in range(NTB)], 2)
                    ot_written.clear()
                    for si, (t0, n) in enumerate(SEQS):
                        run_tasks([(lambda sl, d=d: sweep(d, h, si, t0, n)) for d in range(2)], 2)
                    run_tasks([(lambda sl, tb=tb: n_task(sl, h, tb)) for tb in range(NTB)], 2)
                c.barrier_all()

        def phase_out(L):
            with ExitStack() as es:
                getw = make_stream(es, [(w_out[L], cb * 512, 512) for cb in range(4)], 512)
                gtb = sb(es, "gtb", [128, 2, D], F32)
                xr = [sb(es, f"xr{i}", [128, 512], F32) for i in range(2)]
                tm = [sb(es, f"otm{i}", [128, 512], F32) for i in range(2)]
                yt = [sb(es, f"yt{i}", [128, 512], F32) for i in range(2)]
                po = [ps(es, f"po{i}", [128, 512]) for i in range(2)]
                for g in range(2):
                    c.dma("sp", gtb[:, g, :], gts[L * 2 + g:L * 2 + g + 1, :].partition_broadcast(128),
                          reads=[("gts", L, g, cb) for cb in range(4)], writes=[("gtb", g)])
                for k in range(16):
                    c.dma("sp", hT[:, k, :], oTs[L][k * 128:(k + 1) * 128, :], reads=[("oTs", L, k)],
                          writes=[("hTk", k)])
                for cb in range(4):
                    ccols = slice(cb * 512, (cb + 1) * 512)
                    wb, wk = getw()
                    for tt in range(NT):
                        b = tt % 2
                        g = 0 if tt < 4 else 1
                        rows = slice(tt * 128, (tt + 1) * 128)
                        tcols = slice(tt * 128, (tt + 1) * 128)
                        mm_group(po[b][:], [(hT[:, k, tcols], wb[:, k, 0:512]) for k in range(16)],
                                 [wk] + [("hTk", k) for k in range(16)], [("po", b)])
                        if L == 0:
                            c.dma("sp", xr[b][:], x[rows, ccols], writes=[("xr", b)])
                        else:
                            c.dma("sp", xr[b][:], y1[rows, ccols], reads=[("y1", tt, cb)], writes=[("xr", b)])
                        c.op("dve", lambda b=b, g=g, ccols=ccols: nc.vector.tensor_tensor(
                            out=tm[b][:], in0=po[b][:], in1=gtb[:, g, ccols], op=ALU.mult),
                            [("po", b), ("gtb", g)], [("otm", b)])
                        c.op("pool", lambda b=b: nc.gpsimd.tensor_tensor(out=yt[b][:], in0=tm[b][:], in1=xr[b][:],
                                                                         op=ALU.add),
                             [("otm", b), ("xr", b)], [("yt", b)])
                        if L == 0:
                            c.dma("sp", y1[rows, ccols], yt[b][:], reads=[("yt", b)], writes=[("y1", tt, cb)])
                        else:
                            c.dma("sp", y[rows, ccols], yt[b][:], reads=[("yt", b)], writes=[("y", tt, cb)])
                c.barrier_all()

        plist = [("mod0", lambda: phase_mod_h(0)), ("hgrn", phase_hgrn), ("attn0", lambda: phase_attn(0)),
                 ("out0", lambda: phase_out(0)), ("mod1", lambda: phase_mod_h(1)), ("attn1", lambda: phase_attn(1)),
                 ("out1", lambda: phase_out(1))]
        for nm, fn in plist:
            if phases is None or nm in phases:
                fn()
        c.finish()
    return nc


def _consts():
    ident = np.eye(128, dtype=np.float32)
    maskF = np.ones((128, 512), np.float32)
    maskF[:, ::32] = 0.0
    s = np.arange(128)[:, None]
    t = np.arange(128)[None, :]
    same = (s // 32) == (t // 32)
    mfb = np.stack([(same & (s <= t)), (same & (s >= t))], axis=1).astype(np.float32)
    cm4 = np.zeros((128, 4, 128), np.float32)
    for j in range(4):
        cm4[32 * j:32 * j + 32, j, :] = 1.0
    R = np.zeros((128, 128), np.float32)
    for m in range(128):
        q = m // 32
        if q in (0, 2):
            R[m, m + 32] = -1.0
        else:
            R[m, m - 32] = 1.0
    RT = np.ascontiguousarray(R.T)
    n_tok = 2048
    row = (np.arange(n_tok) // 64).astype(np.float32)
    col = (np.arange(n_tok) % 64).astype(np.float32)
    inv = (10000.0 ** (-np.arange(32, dtype=np.float32) / 32)).astype(np.float32)
    ar = row[:, None] * inv
    ac = col[:, None] * inv
    ang = np.concatenate([ar, ar, ac, ac], axis=-1).astype(np.float32)
    cosT = np.ascontiguousarray(np.cos(ang).T.astype(np.float32))
    sinT = np.ascontiguousarray(np.sin(ang).T.astype(np.float32))
    b = np.arange(128)[:, None]
    a = np.arange(128)[None, :]
    NEG = -30000.0
    wbias = np.stack([np.where(b >= a, 0.0, NEG), np.where(b <= a, 0.0, NEG)], axis=1).astype(np.float32)
    wbias4 = np.ascontiguousarray(np.broadcast_to(wbias[:, :, None, :], (128, 2, 4, 128))).astype(np.float32)
    return dict(ident=ident, maskF=maskF, mfb=mfb, cm4=cm4, RT=RT, cosT=cosT, sinT=sinT, wbias4=wbias4)


def _perm_w0(w):
    cols = []
    for h in range(8):
        for base in (0, 1024, 2048, 3072, 4096):
            cols.append(np.arange(base + h * 128, base + (h + 1) * 128))
    for j in range(2):
        cols.append(np.arange(6144 + j * 128, 6144 + (j + 1) * 128))
        cols.append(np.arange(6400 + j * 128, 6400 + (j + 1) * 128))
    for h in range(8):
        cols.append(np.arange(5120 + h * 128, 5120 + (h + 1) * 128))
        cols.append(np.arange(6656 + h * 128, 6656 + (h + 1) * 128))
    return np.ascontiguousarray(w[:, np.concatenate(cols)])


def _perm_w1(w):
    cols = []
    for j in range(4):
        cols.append(np.arange(2048 + j * 128, 2048 + (j + 1) * 128))
        cols.append(np.arange(2560 + j * 128, 2560 + (j + 1) * 128))
    for h in range(16):
        cols.append(np.arange(h * 128, (h + 1) * 128))
        cols.append(np.arange(3072 + h * 128, 3072 + (h + 1) * 128))
    return np.ascontiguousarray(w[:, np.concatenate(cols)])


def _prep(x_prompt, x_sample, state_l0_hgrn, cache_l0_k, cache_l0_v, cache_l1_k, cache_l1_v,
           c, c_ctx, lb_gamma,
           l0_norm, l0_w_mod, l0_b_mod, l0_w_in, l0_w_out, l0_a_onorm, l0_b_qnorm, l0_b_knorm,
           l1_norm, l1_w_mod, l1_b_mod, l1_w_in, l1_w_out, l1_c_qnorm, l1_c_knorm, l1_c_sink):
    f = lambda a: np.ascontiguousarray(np.asarray(a, dtype=np.float32))
    x_prompt, x_sample = f(x_prompt), f(x_sample)
    consts = _consts()
    shared = dict(
        w_mod0=f(l0_w_mod), w_mod1=f(l1_w_mod),
        b_mod0=f(l0_b_mod).reshape(1, -1), b_mod1=f(l1_b_mod).reshape(1, -1),
        norm0=f(l0_norm).reshape(1, -1), norm1=f(l1_norm).reshape(1, -1),
        w_in0=_perm_w0(f(l0_w_in)), w_in1=_perm_w1(f(l1_w_in)),
        w_out0=f(l0_w_out), w_out1=f(l1_w_out),
        onorm=f(l0_a_onorm).reshape(128, 1),
        qn0=f(l0_b_qnorm).reshape(128, 1), kn0=f(l0_b_knorm).reshape(128, 1),
        qn1=f(l1_c_qnorm).reshape(128, 1), kn1=f(l1_c_knorm).reshape(128, 1),
        sink=f(l1_c_sink).reshape(1, 16),
        lbg=np.ascontiguousarray(f(lb_gamma).reshape(3, 2, 8, 128).transpose(3, 0, 1, 2).reshape(128, 3, 16)),
        **consts,
    )
    c = f(c)
    c_ctx = f(c_ctx)
    in_maps = []
    for i in range(8):
        m = dict(shared)
        m["x"] = np.ascontiguousarray(np.concatenate(
            [x_prompt[2 * i], x_prompt[2 * i + 1], x_sample[i]], axis=0))
        cr = np.stack([c_ctx, c[i]], axis=0)
        m["crows"] = np.ascontiguousarray(cr.reshape(2, 16, 128).transpose(2, 0, 1))
        m["st0"] = f(state_l0_hgrn[i])
        m["ck0T"] = np.ascontiguousarray(f(cache_l0_k[i]).transpose(2, 1, 0))
        m["cv0"] = f(cache_l0_v[i])
        m["ck1T"] = np.ascontiguousarray(f(cache_l1_k[i]).transpose(2, 1, 0))
        m["cv1"] = f(cache_l1_v[i])
        in_maps.append(m)
    return in_maps


def kernel(**inputs):
    in_maps = _prep(**inputs)
    nc = build_program()
    res = run_bass_kernel_spmd(nc, in_maps, core_ids=list(range(8)))
    r = res.results
    y_prompt = np.stack([r[i // 2]["y"][(i % 2) * 256:(i % 2 + 1) * 256] for i in range(16)], axis=0)
    y_sample = np.stack([r[i]["y"][512:] for i in range(8)], axis=0)
    nstate = np.concatenate([r[i]["nstate"] for i in range(8)], axis=0)
    outs = [y_prompt.astype(np.float32), y_sample.astype(np.float32), nstate.astype(np.float32)]
    for nm in ("nk0", "nv0", "nk1", "nv1"):
        a = np.concatenate([r[i][nm].reshape(2, 256, r[i][nm].shape[1], 128) for i in range(8)], axis=0)
        outs.append(a.astype(np.float32))
    return tuple(outs)
```

```python
import math
import os
from contextlib import ExitStack

import numpy as np
import concourse.bass as bass
import concourse.mybir as mybir
from concourse.bass_utils import run_bass_kernel_spmd

F32 = mybir.dt.float32
BF16 = mybir.dt.bfloat16
AF = mybir.ActivationFunctionType
ALU = mybir.AluOpType

D = 2048
T = 2560
NT = 20
NTB = 5
EPS = 1e-6
SCALE = 1.0 / math.sqrt(128.0)
SEQS = [(0, 2), (2, 2), (4, 16)]


class Ctx:
    NDMA = 32

    def __init__(self, nc):
        self.nc = nc
        self.engs = {"pe": nc.tensor, "dve": nc.vector, "act": nc.scalar,
                     "pool": nc.gpsimd, "sp": nc.sync}
        self.sem = {k: nc.alloc_semaphore(name="s_" + k) for k in self.engs}
        self.cnt = {k: 0 for k in self.engs}
        self.waited = {k: {} for k in self.engs}
        self.last_w = {}
        self.readers = {}
        self.dma_sems = [nc.alloc_semaphore(name=f"s_dma{i}") for i in range(self.NDMA)]
        self.dma_val = [0] * self.NDMA
        self.dma_pool = {"sp": list(range(0, 16)), "pool": list(range(16, 24)), "act": list(range(24, 32))}
        self.dma_rr = {"sp": 0, "pool": 0, "act": 0}

    def _deps(self, reads, writes):
        evs = []
        for r in reads:
            e = self.last_w.get(r)
            if e is not None:
                evs.append(e)
        for w in writes:
            e = self.last_w.get(w)
            if e is not None:
                evs.append(e)
            evs.extend(self.readers.get(w, ()))
        return evs

    def _wait(self, eng, evs, skip_self=False):
        best = {}
        for (name, sem, val) in evs:
            if skip_self and name == eng:
                continue
            if best.get(name, (None, 0))[1] < val:
                best[name] = (sem, val)
        wd = self.waited[eng]
        for name, (sem, val) in best.items():
            if wd.get(name, 0) < val:
                self.engs[eng].wait_ge(sem, val)
                wd[name] = val

    def _commit(self, ev, reads, writes):
        ws = set(writes)
        for r in reads:
            if r in ws:
                continue
            self.readers.setdefault(r, []).append(ev)
        for w in writes:
            self.last_w[w] = ev
            self.readers[w] = []

    EXCL = {"pj", "pms", "prot", "pv", "pS", "pO", "pL", "pm", "pT", "bA", "ptr", "po", "fb", "uB"}

    def op(self, eng, fn, reads=(), writes=()):
        reads = list(reads)
        writes = list(writes)
        ex = [r for r in reads if (r if isinstance(r, str) else r[0]) in self.EXCL]
        if ex:
            reads = [r for r in reads if r not in ex]
            writes = writes + [r for r in ex if r not in writes]
        evs = self._deps(reads, writes)
        self._wait(eng, evs, skip_self=(eng == "pe"))
        inst = fn()
        inst.then_inc(self.sem[eng], 1)
        self.cnt[eng] += 1
        ev = (eng, self.sem[eng], self.cnt[eng])
        self._commit(ev, reads, writes)
        return ev

    def dma(self, q, out, in_, reads=(), writes=()):
        reads = list(reads)
        writes = list(writes)
        evs = self._deps(reads, writes)
        lst = self.dma_pool[q]
        k = lst[self.dma_rr[q] % len(lst)]
        self.dma_rr[q] += 1
        name = f"dma{k}"
        if self.dma_val[k] > 0:
            evs.append((name, self.dma_sems[k], self.dma_val[k]))
        self._wait(q, evs)
        self.engs[q].dma_start(out=out, in_=in_).then_inc(self.dma_sems[k], 16)
        self.dma_val[k] += 16
        ev = (name, self.dma_sems[k], self.dma_val[k])
        self._commit(ev, reads, writes)
        return ev

    def all_events(self):
        evs = [(k, self.sem[k], self.cnt[k]) for k in self.engs if self.cnt[k] > 0]
        for i in range(self.NDMA):
            if self.dma_val[i] > 0:
                evs.append((f"dma{i}", self.dma_sems[i], self.dma_val[i]))
        return evs

    def barrier_all(self):
        evs = self.all_events()
        for e in self.engs:
            self._wait(e, evs, skip_self=True)

    def finish(self):
        self._wait("sp", self.all_events(), skip_self=True)


def build_program(phases=None):
    nc = bass.Bass("TRN2", target_bir_lowering=False)

    def din(name, shape, dt=F32):
        return nc.dram_tensor(name, list(shape), dt, kind="ExternalInput").ap()

    def dout(name, shape, dt=F32):
        return nc.dram_tensor(name, list(shape), dt, kind="ExternalOutput").ap()

    def dscr(name, shape, dt=F32):
        return nc.dram_tensor(name, list(shape), dt, kind="Internal").ap()

    x = din("x", [T, D])
    crows = din("crows", [128, 2, 16])
    lbg = din("lbg", [128, 3, 16])
    st0 = din("st0", [2, 8, 128, 128])
    ckT = [din("ck0T", [128, 2, 256]), din("ck1T", [128, 4, 256])]
    cv = [din("cv0", [256, 2, 128]), din("cv1", [256, 4, 128])]
    w_mod = [din("w_mod0", [D, 3 * D]), din("w_mod1", [D, 3 * D])]
    b_mod = [din("b_mod0", [1, 3 * D]), din("b_mod1", [1, 3 * D])]
    norm_g = [din("norm0", [1, D]), din("norm1", [1, D])]
    w_in = [din("w_in0", [D, 7680]), din("w_in1", [D, 5120])]
    w_out = [din("w_out0", [D, D]), din("w_out1", [D, D])]
    onorm_d = din("onorm", [128, 1])
    qn_d = [din("qn0", [128, 1]), din("qn1", [128, 1])]
    kn_d = [din("kn0", [128, 1]), din("kn1", [128, 1])]
    sink_d = din("sink", [1, 16])
    ident_d = din("ident", [128, 128])
    maskF_d = din("maskF", [128, 512])
    mfb_d = din("mfb", [128, 2, 128])
    cm4_d = din("cm4", [128, 4, 128])
    RT_d = din("RT", [128, 128])
    cosT_d = din("cosT", [128, 2048])
    sinT_d = din("sinT", [128, 2048])
    wbias4_d = din("wbias4", [128, 2, 4, 128])

    y = dout("y", [T, D])
    nstate = dout("nstate", [2, 2, 8, 128, 128])
    nk = [dout("nk0", [512, 2, 128]), dout("nk1", [512, 4, 128])]
    nv = [dout("nv0", [512, 2, 128]), dout("nv1", [512, 4, 128])]

    gts = dscr("gts", [4, D])
    oTs = [dscr("oT0", [D, T], BF16), dscr("oT1", [D, T], BF16)]
    y1 = dscr("y1", [T, D])
    hTd = dscr("hTd", [16, 128, T], BF16)

    c = Ctx(nc)

    uid = [0]

    def sb(es, name, shape, dt):
        uid[0] += 1
        return es.enter_context(nc.sbuf_tensor(f"{name}_{uid[0]}", list(shape), dt))

    def ps(es, name, shape, dt=F32):
        uid[0] += 1
        return es.enter_context(nc.psum_tensor(f"{name}_{uid[0]}", list(shape), dt))

    def mm_group(out, pairs, reads, writes):
        def f():
            n = len(pairs)
            inst = None
            for i, (l, r) in enumerate(pairs):
                inst = nc.tensor.matmul(out, lhsT=l, rhs=r, start=(i == 0), stop=(i == n - 1))
            return inst
        return c.op("pe", f, reads, writes)

    def wview(w, c0, ncols):
        return w[:, c0:c0 + ncols].rearrange("(k p) n -> p k n", p=128)

    with ExitStack() as top:
        identb = sb(top, "identb", [128, 128], BF16)
        identf = sb(top, "identf", [128, 128], F32)
        onesb = sb(top, "onesb", [128, 128], BF16)
        onesf = sb(top, "onesf", [128, 128], F32)
        epsc = sb(top, "epsc", [128, 1], F32)
        hT = sb(top, "hT", [128, 16, T], BF16)

        c.dma("sp", identf[:], ident_d, writes=["identf"])
        c.dma("pool", identb[:], ident_d, writes=["identb"])
        c.op("dve", lambda: nc.vector.memset(onesb[:], 1.0), writes=["onesb"])
        c.op("dve", lambda: nc.vector.memset(onesf[:], 1.0 / 128.0), writes=["onesf"])
        c.op("dve", lambda: nc.vector.memset(epsc[:], EPS), writes=["epsc"])

        def make_stream(es, blocks, ncols_max, nslots=2):
            bufs = [sb(es, f"wbuf{i}", [128, 16, ncols_max], BF16) for i in range(nslots)]
            st = {"issued": 0, "got": 0}

            def issue():
                i = st["issued"]
                if i >= len(blocks):
                    return
                w, c0, ncols = blocks[i]
                s = i % nslots
                c.dma("pool", bufs[s][:, :, 0:ncols], wview(w, c0, ncols), writes=[("wb", s)])
                st["issued"] += 1

            def get():
                i = st["got"]
                while st["issued"] <= i:
                    issue()
                st["got"] += 1
                if st["issued"] <= i + 1:
                    issue()
                s = i % nslots
                return bufs[s], ("wb", s)
            return get

        hkeys = lambda tb: [("hT", tt) for tt in range(tb * 4, tb * 4 + 4)]
        allh = [("hT", tt) for tt in range(NT)]

        def phase_mod_h(L):
            with ExitStack() as es:
                G = sb(es, "G", [128, 2, D], F32)
                SH = sb(es, "SH", [128, 2, D], F32)
                with ExitStack() as es2:
                    getw = make_stream(es2, [(w_mod[L], blk * 512, 512) for blk in range(12)], 512)
                    crs = sb(es2, "crs", [128, 2, 16], F32)
                    scl = sb(es2, "scl", [128, 2, 16], F32)
                    srep = sb(es2, "srep", [128, 2, 16, 128], BF16)
                    bb = [sb(es2, f"bb{i}", [128, 512], F32) for i in range(2)]
                    gb = [sb(es2, f"gb{i}", [128, 512], F32) for i in range(2)]
                    tmp = [sb(es2, f"mtmp{i}", [128, 512], F32) for i in range(2)]
                    pm = [ps(es2, f"pm{i}", [128, 512]) for i in range(2)]
                    c.dma("sp", crs[:], crows, writes=["crs"])
                    c.op("act", lambda: nc.scalar.activation(scl[:], crs[:], AF.Silu), ["crs"], ["scl"])
                    c.op("dve", lambda: nc.vector.tensor_copy(
                        srep[:], scl[:].unsqueeze(3).to_broadcast([128, 2, 16, 128])), ["scl"], ["srep"])
                    for blk in range(12):
                        kind, cb = divmod(blk, 4)
                        b = blk % 2
                        wb, wk = getw()
                        c.dma("sp", bb[b][:], b_mod[L][:, blk * 512:(blk + 1) * 512].partition_broadcast(128),
                              writes=[("bb", b)])
                        if kind == 1:
                            c.dma("sp", gb[b][:], norm_g[L][:, cb * 512:(cb + 1) * 512].partition_broadcast(128),
                                  writes=[("gb", b)])
                        cols = slice(cb * 512, (cb + 1) * 512)
                        for g in range(2):
                            mm_group(pm[g][:], [(srep[:, g, k, :], wb[:, k, 0:512]) for k in range(16)],
                                     [wk, "srep"], [("pm", g)])
                            if kind == 0:
                                c.op("dve", lambda g=g, b=b, cols=cols: nc.vector.tensor_tensor(
                                    out=SH[:, g, cols], in0=pm[g][:], in1=bb[b][:], op=ALU.add),
                                    [("pm", g), ("bb", b)], [("SH", g, cb)])
                            elif kind == 1:
                                c.op("dve", lambda g=g, b=b: nc.vector.tensor_tensor(
                                    out=tmp[g][:], in0=pm[g][:], in1=bb[b][:], op=ALU.add),
                                    [("pm", g), ("bb", b)], [("mtmp", g)])
                                c.op("dve", lambda g=g, b=b, cols=cols: nc.vector.scalar_tensor_tensor(
                                    out=G[:, g, cols], in0=tmp[g][:], scalar=1.0, in1=gb[b][:],
                                    op0=ALU.add, op1=ALU.mult),
                                    [("mtmp", g), ("gb", b)], [("G", g, cb)])
                            else:
                                c.op("dve", lambda g=g, b=b: nc.vector.tensor_tensor(
                                    out=tmp[g][:], in0=pm[g][:], in1=bb[b][:], op=ALU.add),
                                    [("pm", g), ("bb", b)], [("mtmp", g)])
                                c.dma("sp", gts[L * 2 + g:L * 2 + g + 1, cols], tmp[g][0:1, :],
                                      reads=[("mtmp", g)], writes=[("gts", L, g, cb)])
                    c.barrier_all()
                with ExitStack() as es2:
                    xt = [sb(es2, f"xt{i}", [128, D], F32) for i in range(2)]
                    junk = sb(es2, "junk", [128, D], BF16)
                    st = sb(es2, "st", [128, 8], F32)
                    t1 = [sb(es2, f"t1_{i}", [128, D], F32) for i in range(2)]
                    hb = [sb(es2, f"hb{i}", [128, D], BF16) for i in range(2)]
                    pT = [ps(es2, f"pT{i}", [128, 8, 128], BF16) for i in range(4)]
                    GK = [[("G", g, cb) for cb in range(4)] for g in range(2)]
                    SK = [[("SH", g, cb) for cb in range(4)] for g in range(2)]
                    for tt in range(NT):
                        b = tt % 2
                        g = 0 if tt < 4 else 1
                        rows = slice(tt * 128, (tt + 1) * 128)
                        if L == 0:
                            c.dma("sp", xt[b][:], x[rows, :], writes=[("xt", b)])
                        else:
                            c.dma("sp", xt[b][:], y1[rows, :], reads=[("y1", tt, cb) for cb in range(4)],
                                  writes=[("xt", b)])
                        c.op("act", lambda b=b: nc.scalar.activation(junk[:], xt[b][:], AF.Square,
                                                                     accum_out=st[:, b:b + 1]),
                             [("xt", b)], ["junk", ("ssq", b)])
                        c.op("act", lambda b=b: nc.scalar.activation(st[:, 2 + b:3 + b], st[:, b:b + 1], AF.Sqrt,
                                                                     scale=1.0 / D, bias=epsc[:]),
                             [("ssq", b), "epsc"], [("std", b)])
                        c.op("dve", lambda b=b: nc.vector.reciprocal(st[:, 4 + b:5 + b], st[:, 2 + b:3 + b]),
                             [("std", b)], [("rstd", b)])
                        c.op("dve", lambda b=b, g=g: nc.vector.scalar_tensor_tensor(
                            out=t1[b][:], in0=xt[b][:], scalar=st[:, 4 + b:5 + b], in1=G[:, g, :],
                            op0=ALU.mult, op1=ALU.mult),
                            [("xt", b), ("rstd", b)] + GK[g], [("t1", b)])
                        c.op("pool", lambda b=b, g=g: nc.gpsimd.tensor_tensor(
                            out=hb[b][:, 0:1408], in0=t1[b][:, 0:1408], in1=SH[:, g, 0:1408], op=ALU.add),
                            [("t1", b)] + SK[g], [("hb", b, 0)])
                        c.op("dve", lambda b=b, g=g: nc.vector.tensor_tensor(
                            out=hb[b][:, 1408:D], in0=t1[b][:, 1408:D], in1=SH[:, g, 1408:D], op=ALU.add),
                            [("t1", b)] + SK[g], [("hb", b, 1)])
                        for half in range(2):
                            pp = pT[b * 2 + half]

                            def tr(pp=pp, b=b, half=half):
                                inst = None
                                for kk in range(8):
                                    k = half * 8 + kk
                                    inst = nc.tensor.transpose(pp[:, kk, :], hb[b][:, k * 128:(k + 1) * 128], identb[:])
                                return inst
                            c.op("pe", tr, [("hb", b, 0), ("hb", b, 1), "identb"], [("pT", b, half)])
                            eng = "act" if half == 0 else "dve"
                            dst = hT[:, half * 8:(half + 1) * 8, tt * 128:(tt + 1) * 128]
                            if eng == "act":
                                c.op("act", lambda pp=pp, dst=dst: nc.scalar.copy(dst, pp[:]),
                                     [("pT", b, half)], [("hT", tt, half)])
                            else:
                                c.op("dve", lambda pp=pp, dst=dst: nc.vector.tensor_copy(dst, pp[:]),
                                     [("pT", b, half)], [("hT", tt, half)])
                    c.barrier_all()
                    if L == 0:
                        for k in range(16):
                            c.dma("sp" if k % 2 == 0 else "act", hTd[k], hT[:, k, :], writes=[("hTd", k)])
                c.barrier_all()

        def tasks_gen(factories, width):
            it = iter(factories)
            active = {}
            free = list(range(width))
            while True:
                while free:
                    f = next(it, None)
                    if f is None:
                        break
                    sl = free.pop(0)
                    active[sl] = f(sl)
                if not active:
                    break
                for sl in sorted(active):
                    try:
                        next(active[sl])
                    except StopIteration:
                        del active[sl]
                        free.append(sl)
                yield

        def run_tasks(factories, width):
            for _ in tasks_gen(factories, width):
                pass

        def interleave(ga, gb):
            live = [g for g in (ga, gb) if g is not None]
            while live:
                for g in list(live):
                    try:
                        next(g)
                    except StopIteration:
                        live.remove(g)

        def phase_attn(L):
            nkv = 2 if L == 0 else 4
            G = 1 if L == 0 else 4
            if L == 0:
                kvc0, qc0, orow0 = 5120, 5120 + 512, 8
            else:
                kvc0, qc0, orow0 = 0, 1024, 0
            with ExitStack() as es:
                ablocks = []
                for j_ in range(nkv):
                    ablocks.append((w_in[L], kvc0 + j_ * 256, 256))
                    for h_ in range(4 * j_, 4 * j_ + 4):
                        ablocks.append((w_in[L], qc0 + h_ * 256, 256))
                getw = make_stream(es, ablocks, 256, nslots=3)

                T_sq = [sb(es, f"sq{i}", [128, 512], F32) for i in range(3)]
                T_sd = [sb(es, f"sd{i}", [128, 512], F32) for i in range(3)]
                T_kb = [sb(es, f"kb{i}", [128, 512], BF16) for i in range(3)]
                T_u = [sb(es, f"ru{i}", [128, 512], F32) for i in range(3)]
                cosT = sb(es, "cosT", [128, 2048], F32)
                sinT = sb(es, "sinT", [128, 2048], F32)
                RTb = sb(es, "RTb", [128, 128], BF16)
                gq = sb(es, "gq", [128, 1], F32)
                gk = sb(es, "gk", [128, 1], F32)
                esk = sb(es, "esk", [128, 16], F32)
                wb4 = sb(es, "wb4", [128, 2, 4, 128], BF16)
                KT = sb(es, "KT", [128, T], BF16)
                KcT = sb(es, "KcT", [128, 256], BF16)
                V = sb(es, "V", [128, NT, 128], BF16)
                Vc = sb(es, "Vc", [128, 2, 128], BF16)
                QTg = sb(es, "QTg", [128, G, T], BF16)
                oThg = sb(es, "oThg", [128, G, T], BF16)
                kf32 = sb(es, "kf32", [128, 512], F32)
                kout = [sb(es, f"kout{i}", [128, 128], F32) for i in range(2)]
                vout = [sb(es, f"vout{i}", [128, 128], F32) for i in range(3)]
                PT = [sb(es, f"PT{i}", [128, 512], BF16) for i in range(2)]
                rl = [sb(es, f"rl{i}", [128, 512], F32) for i in range(2)]
                pj = ps(es, "pj", [128, 512])
                pms = ps(es, "pms", [128, 512])
                pS = [ps(es, f"pS{i}", [128, 512]) for i in range(2)]
                pO = [ps(es, f"pO{i}", [128, 512]) for i in range(2)]
                pL = [ps(es, f"pL{i}", [128, 512]) for i in range(2)]
                pbank = [(pj, "pj"), (pS[0], ("pS", 0)), (pS[1], ("pS", 1))]
                rbank = pbank
                tbank = (pO[0], ("pO", 0))
                msbank = [(pms, "pms"), (pL[0], ("pL", 0)), (pL[1], ("pL", 1))]
                NW = 3

                c.dma("sp", cosT[:], cosT_d, writes=["consts"])
                c.dma("sp", sinT[:], sinT_d, writes=["consts"])
                c.dma("pool", RTb[:], RT_d, writes=["consts"])
                c.dma("sp", gq[:], qn_d[L], writes=["consts"])
                c.dma("sp", gk[:], kn_d[L], writes=["consts"])
                c.dma("pool", wb4[:], wbias4_d, writes=["consts"])
                c.dma("sp", esk[:], sink_d.partition_broadcast(128), writes=["esk0"])
                c.op("act", lambda: nc.scalar.activation(esk[:], esk[:], AF.Exp), ["esk0"], ["esk0", "consts"])

                def fn_task(s, pjt, pjk, gcol, rope_cols, out_bf, outkey, out_f32=None):
                    sq, sd, kb, uu_ = T_sq[s], T_sd[s], T_kb[s], T_u[s]
                    pmt, pmk = msbank[s]
                    c.op("act", lambda: nc.scalar.activation(sq[:], pjt, AF.Square), [pjk], [("sq", s)])
                    yield
                    c.op("pe", lambda: nc.tensor.matmul(pmt[:], lhsT=onesf[:], rhs=sq[:], start=True, stop=True),
                         [("sq", s), "onesf"], [pmk])
                    yield
                    c.op("act", lambda: nc.scalar.activation(sd[:], pmt[:], AF.Ln, bias=epsc[:]),
                         [pmk, "epsc"], [("sd", s)])
                    yield
                    c.op("act", lambda: nc.scalar.activation(sd[:], sd[:], AF.Exp, scale=-0.5), [("sd", s)], [("sd", s)])
                    yield
                    if out_f32 is None:
                        dst, dkey = sq[:], ("sq", s)
                    else:
                        dst, dkey = out_f32, outkey + ("f32",)
                    c.op("dve", lambda: nc.vector.scalar_tensor_tensor(
                        out=dst, in0=pjt, scalar=gcol, in1=sd[:], op0=ALU.mult, op1=ALU.mult),
                        [pjk, ("sd", s), "consts"], [dkey])
                    yield
                    if rope_cols is None:
                        c.op("pool", lambda: nc.gpsimd.tensor_copy(out_bf, dst), [dkey], [outkey])
                        yield
                    else:
                        c.op("pool", lambda: nc.gpsimd.tensor_copy(kb[:], dst), [dkey], [("kb", s)])
                        yield
                        prt, prk = rbank[s]
                        c.op("pe", lambda: nc.tensor.matmul(prt[:], lhsT=RTb[:], rhs=kb[:], start=True, stop=True),
                             [("kb", s), "consts"], [prk])
                        yield
                        c.op("pool", lambda: nc.gpsimd.tensor_tensor(out=dst, in0=dst, in1=cosT[:, rope_cols],
                                                                     op=ALU.mult), [dkey, "consts"], [dkey])
                        yield
                        c.op("dve", lambda: nc.vector.tensor_tensor(out=uu_[:], in0=prt[:], in1=sinT[:, rope_cols],
                                                                    op=ALU.mult), [prk, "consts"], [("ru", s)])
                        yield
                        c.op("pool", lambda: nc.gpsimd.tensor_tensor(out=out_bf, in0=dst, in1=uu_[:], op=ALU.add),
                             [dkey, ("ru", s)], [outkey])
                        yield

                def rope_of(tb):
                    return None if tb == 0 else slice((tb - 1) * 512, tb * 512)

                def k_task(sl, wb, wk, j, tb):
                    cols = slice(tb * 512, (tb + 1) * 512)
                    pjt, pjk = pbank[sl]
                    mm_group(pjt[:], [(wb[:, k, 0:128], hT[:, k, cols]) for k in range(16)], [wk], [pjk])
                    yield
                    if tb == 0:
                        yield from fn_task(sl, pjt[:], pjk, gk[:, 0:1], None, KT[:, cols], ("KT", tb), out_f32=kf32[:])
                        for t4 in range(4):
                            b = t4 % 2
                            pvt, pvk = tbank
                            c.op("pe", lambda: nc.tensor.transpose(
                                pvt[:, 0:128], kf32[:, t4 * 128:(t4 + 1) * 128], identf[:]),
                                [("KT", tb, "f32"), "identf"], [pvk])
                            yield
                            c.op("act", lambda: nc.scalar.copy(kout[b][:], pvt[:, 0:128]), [pvk], [("kout", b)])
                            yield
                            c.dma("sp", nk[L][t4 * 128:(t4 + 1) * 128, j, :], kout[b][:],
                                  reads=[("kout", b)], writes=[("nk", t4, j)])
                    else:
                        yield from fn_task(sl, pjt[:], pjk, gk[:, 0:1], rope_of(tb), KT[:, cols], ("KT", tb))

                def v_task(sl, wb, wk, j, tt):
                    tcols = slice(tt * 128, (tt + 1) * 128)
                    pvt, pvk = pbank[sl]
                    mm_group(pvt[:, 0:128], [(hT[:, k, tcols], wb[:, k, 128:256]) for k in range(16)], [wk], [pvk])
                    yield
                    if tt < 4:
                        b = sl
                        c.op("act", lambda: nc.scalar.copy(vout[b][:], pvt[:, 0:128]), [pvk], [("vout", b)])
                        yield
                        c.op("pool", lambda: nc.gpsimd.tensor_copy(V[:, tt, :], vout[b][:]), [("vout", b)], [("V", tt)])
                        c.dma("sp", nv[L][tt * 128:(tt + 1) * 128, j, :], vout[b][:],
                              reads=[("vout", b)], writes=[("nv", tt, j)])
                        yield
                    else:
                        c.op("act", lambda: nc.scalar.copy(V[:, tt, :], pvt[:, 0:128]), [pvk], [("V", tt)])
                        yield

                def q_task(sl, wb, wk, hh, tb):
                    cols = slice(tb * 512, (tb + 1) * 512)
                    pjt, pjk = pbank[sl]
                    mm_group(pjt[:], [(wb[:, k, 0:128], hT[:, k, cols]) for k in range(16)], [wk], [pjk])
                    yield
                    yield from fn_task(sl, pjt[:], pjk, gq[:, 0:1], rope_of(tb), QTg[:, hh, cols], ("QT", hh, tb))

                def g_task(sl, wb, wk, hh, tb):
                    cols = slice(tb * 512, (tb + 1) * 512)
                    pjt, pjk = pbank[sl]
                    mm_group(pjt[:], [(wb[:, k, 128:256], hT[:, k, cols]) for k in range(16)], [wk], [pjk])
                    yield
                    c.op("act", lambda: nc.scalar.activation(oThg[:, hh, cols], pjt[:], AF.Silu),
                         [pjk], [("oTh", hh, tb)])
                    yield

                sctr = [0]
                bctr2 = [0]

                def attn_block(j, q0, nq, keys, tbq):
                    qcols = slice(q0, q0 + nq)
                    N = G * nq
                    ob = bctr2[0] % 2
                    bctr2[0] += 1
                    nk_ = len(keys)
                    rhsQ = QTg[:, :, qcols] if G > 1 else QTg[:, 0, qcols]
                    qkeys = [("QT", hh, tbq) for hh in range(G)]
                    slots = []

                    def smm(ki):
                        kind, idx, mi = keys[ki]
                        p = sctr[0] % 2
                        sctr[0] += 1
                        slots.append(p)
                        if kind == "l":
                            Kl = KT[:, idx * 128:(idx + 1) * 128]
                            kr = [("KT", idx // 4)]
                        else:
                            Kl = KcT[:, idx * 128:(idx + 1) * 128]
                            kr = ["KcT"]

                        def f():
                            out = pS[p][:, :N] if G == 1 else pS[p][:, :N].rearrange("p (g q) -> p g q", g=G)
                            inst = nc.tensor.matmul(out, lhsT=Kl, rhs=rhsQ, start=True, stop=(mi is None))
                            if mi is not None:
                                inst = nc.tensor.matmul(out, lhsT=identb[:], rhs=wb4[:, mi, :, 0:nq],
                                                        start=False, stop=True)
                            return inst
                        c.op("pe", f, kr + qkeys + ["consts", "identb"], [("pS", p)])

                    def pv(ki):
                        kind, idx, mi = keys[ki]
                        p = slots[ki]
                        if kind == "l":
                            Vl = V[:, idx, :]
                            kr = [("V", idx)]
                        else:
                            Vl = Vc[:, idx, :]
                            kr = ["Vc"]
                        c.op("act", lambda: nc.scalar.activation(PT[p][:, :N], pS[p][:, :N], AF.Exp, scale=SCALE),
                             [("pS", p)], [("PT", p)])

                        def f():
                            nc.tensor.matmul(pO[ob][:, :N], lhsT=Vl, rhs=PT[p][:, :N],
                                             start=(ki == 0), stop=(ki == nk_ - 1))
                            return nc.tensor.matmul(pL[ob][:, :N], lhsT=onesb[:], rhs=PT[p][:, :N],
                                                    start=(ki == 0), stop=(ki == nk_ - 1))
                        c.op("pe", f, kr + [("PT", p), "onesb"], [("pO", ob), ("pL", ob)])

                    smm(0)
                    if nk_ > 1:
                        smm(1)
                    for ki in range(nk_):
                        pv(ki)
                        if ki + 2 < nk_:
                            smm(ki + 2)
                    r = rl[ob]
                    if L == 1:
                        r3 = r[:, :N].rearrange("p (g q) -> p g q", g=G)
                        l3 = pL[ob][:, :N].rearrange("p (g q) -> p g q", g=G)
                        c.op("dve", lambda: nc.vector.tensor_tensor(
                            out=r3, in0=l3, in1=esk[:, 4 * j:4 * j + 4].unsqueeze(2).to_broadcast([128, G, nq]),
                            op=ALU.add), [("pL", ob), "consts"], [("rl", ob)])
                        c.op("act", lambda: nc.scalar.activation(r[:, :N], r[:, :N], AF.Ln), [("rl", ob)], [("rl", ob)])
                    else:
                        c.op("act", lambda: nc.scalar.activation(r[:, :N], pL[ob][:, :N], AF.Ln), [("pL", ob)], [("rl", ob)])
                    c.op("act", lambda: nc.scalar.activation(r[:, :N], r[:, :N], AF.Exp, scale=-1.0),
                         [("rl", ob)], [("rl", ob)])
                    c.op("dve", lambda: nc.vector.tensor_tensor(out=r[:, :N], in0=pO[ob][:, :N], in1=r[:, :N],
                                                                op=ALU.mult), [("pO", ob), ("rl", ob)], [("rl", ob)])
                    okeys = [("oTh", hh, tbq) for hh in range(G)]
                    if G > 1:
                        o3 = oThg[:, :, qcols]
                        r3 = r[:, :N].rearrange("p (g q) -> p g q", g=G)
                    else:
                        o3 = oThg[:, 0, qcols]
                        r3 = r[:, :N]
                    c.op("pool", lambda: nc.gpsimd.tensor_tensor(out=o3, in0=r3, in1=o3, op=ALU.mult),
                         [("rl", ob)] + okeys, okeys)

                for j in range(nkv):
                    wb, wk = getw()
                    c.dma("pool", KcT[:], ckT[L][:, j, :], writes=["KcT"])
                    c.dma("pool", Vc[:], cv[L][:, j, :].rearrange("(t p) d -> p t d", p=128), writes=["Vc"])
                    gens = [(lambda sl, tb=tb: k_task(sl, wb, wk, j, tb)) for tb in range(NTB)] + \
                           [(lambda sl, tt=tt: v_task(sl, wb, wk, j, tt)) for tt in range(NT)]
                    run_tasks(gens, NW)
                    for h0 in range(4 * j, 4 * j + 4, G):
                        gens = []
                        for hh in range(G):
                            wbq, wkq = getw()
                            for tb in range(NTB):
                                gens.append(lambda sl, wbq=wbq, wkq=wkq, hh=hh, tb=tb: q_task(sl, wbq, wkq, hh, tb))
                                gens.append(lambda sl, wbq=wbq, wkq=wkq, hh=hh, tb=tb: g_task(sl, wbq, wkq, hh, tb))
                            if hh % 2 == 1 or G == 1:
                                run_tasks(gens, NW)
                                gens = []
                        blocks = []
                        if G == 1:
                            for (t0, n) in SEQS[:2]:
                                blocks.append((t0 * 128, 256, [("l", t0, None), ("l", t0 + 1, None)], 0))
                            for tb in range(1, NTB):
                                keys = [("l", kt, None) for kt in range(4, NT)] + [("c", 0, None), ("c", 1, None)]
                                blocks.append((tb * 512, 512, keys, tb))
                        else:
                            for (t0, n) in SEQS[:2]:
                                for tq in range(t0, t0 + n):
                                    blocks.append((tq * 128, 128, [("l", t0, None), ("l", t0 + 1, None)], 0))
                            for i in range(16):
                                keys = []
                                if i > 0:
                                    keys.append(("l", 4 + i - 1, 0))
                                keys.append(("l", 4 + i, None))
                                if i < 15:
                                    keys.append(("l", 4 + i + 1, 1))
                                keys += [("c", 0, None), ("c", 1, None)]
                                blocks.append(((4 + i) * 128, 128, keys, (4 + i) // 4))
                        for (q0, nq, keys, tbq) in blocks:
                            attn_block(j, q0, nq, keys, tbq)
                        for hh in range(G):
                            r0 = (orow0 + h0 + hh) * 128
                            c.dma("sp", oTs[L][r0:r0 + 128, :], oThg[:, hh, :],
                                  reads=[("oTh", hh, tb) for tb in range(NTB)], writes=[("oTs", L, orow0 + h0 + hh)])
                c.barrier_all()

        def phase_hgrn():
            with ExitStack() as es:
                getw = make_stream(es, [(w_in[0], h * 640, 640) for h in range(8)], 640)
                maskF = sb(es, "maskF", [128, 512], F32)
                mfb = sb(es, "mfb", [128, 2, 128], F32)
                cm4 = sb(es, "cm4", [128, 4, 128], BF16)
                onc = sb(es, "onc", [128, 1], F32)
                lbe = sb(es, "lbe", [128, 3, 16], F32)
                lbs = sb(es, "lbs", [128, 16], F32)
                lbv = sb(es, "lbv", [128, 16], F32)
                oml = sb(es, "oml", [128, 16], F32)
                noml = sb(es, "noml", [128, 16], F32)
                q32 = [sb(es, f"q32_{i}", [128, 512], F32) for i in range(2)]
                sg = [sb(es, f"sg_{i}", [128, 512], F32) for i in range(2)]
                lg = [sb(es, f"lg_{i}", [128, 512], F32) for i in range(2)]
                k32 = [sb(es, f"k32_{i}", [128, 512], F32) for i in range(2)]
                bF = [sb(es, f"bF_{i}", [128, 512], F32) for i in range(2)]
                totc = [sb(es, f"totc_{i}", [128, 16], F32) for i in range(2)]
                def carve(off, n):
                    a_ = hT[:]
                    return bass.AP(a_.tensor, a_.offset + off, [[a_.ap[0][0], 128], [1, n]])
                hsl = [carve(i * 8192, 8192).rearrange("p (k t) -> p k t", t=512) for i in range(2)]
                hTd_v = hTd.rearrange("k p t -> p k t")
                dec0 = sb(es, "dec", [128, 2, 80], F32)
                qd0 = [sb(es, f"qd{d}", [128, T], BF16) for d in range(2)]
                ki0 = [sb(es, f"ki{d}", [128, T], BF16) for d in range(2)]
                keT0 = [sb(es, f"keT{d}", [128, T], BF16) for d in range(2)]
                sgT0 = sb(es, "sgT", [128, T], BF16)
                V0 = sb(es, "Va", [128, NT, 128], BF16)
                o_ = 16384
                qd1 = [carve(o_ + d * T, T) for d in range(2)]
                ki1 = [carve(o_ + (2 + d) * T, T) for d in range(2)]
                keT1 = [carve(o_ + (4 + d) * T, T) for d in range(2)]
                sgT1 = carve(o_ + 6 * T, T)
                V1 = carve(o_ + 7 * T, T).rearrange("p (t d) -> p t d", d=128)
                dec1 = carve(o_ + 8 * T, 320).bitcast(F32).rearrange("p (a b) -> p a b", a=2)
                bufsets = [(qd0, ki0, keT0, sgT0, V0, dec0), (qd1, ki1, keT1, sgT1, V1, dec1)]
                OT = sb(es, "OT", [128, T], F32)
                kend = [[sb(es, f"kend{d}{r}", [128, 128], BF16) for r in range(2)] for d in range(2)]
                Vm = [[sb(es, f"Vm{d}{r}", [128, 4, 128], BF16) for r in range(2)] for d in range(2)]
                Am = [[sb(es, f"Am{d}{r}", [128, 128], BF16) for r in range(2)] for d in range(2)]
                S32 = [[sb(es, f"S32_{d}{r}", [128, 128], F32) for r in range(2)] for d in range(2)]
                Sbf = [[sb(es, f"Sbf{d}{p}", [128, 128], BF16) for p in range(4)] for d in range(2)]
                fb = [ps(es, f"hfb{i}", [128, 512]) for i in range(2)]
                bA = [ps(es, f"hbA{d}", [128, 512]) for d in range(2)]
                pO = [ps(es, f"hpO{d}", [128, 512]) for d in range(2)]
                uB = [ps(es, f"huB{d}", [128, 512]) for d in range(2)]
                ptr = [bA[d][:, 256:320].bitcast(BF16) for d in range(2)]

                c.dma("sp", maskF[:], maskF_d, writes=["hc"])
                c.dma("sp", mfb[:], mfb_d, writes=["hc"])
                c.dma("pool", cm4[:], cm4_d, writes=["hc"])
                c.dma("sp", onc[:], onorm_d, writes=["hc"])
                c.dma("sp", lbe[:], lbg, writes=["lbe"])
                c.op("act", lambda: nc.scalar.activation(lbe[:], lbe[:], AF.Exp), ["lbe"], ["lbe"])
                c.op("dve", lambda: nc.vector.tensor_tensor(out=lbs[:], in0=lbe[:, 0, :], in1=lbe[:, 1, :], op=ALU.add),
                     ["lbe"], ["lbs"])
                c.op("dve", lambda: nc.vector.tensor_tensor(out=lbs[:], in0=lbs[:], in1=lbe[:, 2, :], op=ALU.add),
                     ["lbe", "lbs"], ["lbs"])
                c.op("dve", lambda: nc.vector.reciprocal(lbs[:], lbs[:]), ["lbs"], ["lbs"])
                c.op("dve", lambda: nc.vector.tensor_tensor(out=lbv[:], in0=lbe[:, 0, :], in1=lbs[:], op=ALU.mult),
                     ["lbe", "lbs"], ["lbv"])
                c.op("dve", lambda: nc.vector.tensor_scalar(out=oml[:], in0=lbv[:], scalar1=-1.0, scalar2=1.0,
                                                            op0=ALU.mult, op1=ALU.add), ["lbv"], ["oml"])
                c.op("dve", lambda: nc.vector.tensor_scalar(out=noml[:], in0=oml[:], scalar1=-1.0, scalar2=None,
                                                            op0=ALU.mult), ["oml"], ["noml", "hc"])

                def bc32(t, ncol=16):
                    return t.unsqueeze(2).to_broadcast([128, ncol, 32])

                def f_task(sl, h, wb, wk, tb):
                    cols = slice(tb * 512, (tb + 1) * 512)
                    st_ = h % 2
                    qd, ki, keT, sgT, V, dec = bufsets[st_]
                    hs = hsl[sl]
                    c.dma("sp", hs, hTd_v[:, :, cols], reads=[("hTd", k) for k in range(16)], writes=[("hs", sl)])
                    wk = [wk, ("hs", sl)]
                    pjt, pjk = fb[sl], ("fb", sl)
                    Q, SG, LG, K, B, TC = q32[sl], sg[sl], lg[sl], k32[sl], bF[sl], totc[sl]
                    kq, ks, kl, kk, kb_, kt = ("q32", sl), ("sg", sl), ("lg", sl), ("k32", sl), ("bF", sl), ("totc", sl)
                    mm_group(pjt[:], [(wb[:, k, 0:128], hs[:, k, :]) for k in range(16)], wk, [pjk])
                    yield
                    c.op("act", lambda: nc.scalar.activation(Q[:], pjt[:], AF.Silu), [pjk], [kq])
                    yield
                    for d in range(2):
                        i = d * 8 + h
                        mm_group(pjt[:], [(wb[:, k, 128 * (1 + d):128 * (2 + d)], hs[:, k, :]) for k in range(16)],
                                 wk, [pjk])
                        yield
                        c.op("act", lambda: nc.scalar.activation(SG[:], pjt[:], AF.Sigmoid), [pjk], [ks])
                        yield
                        c.op("act", lambda: nc.scalar.activation(LG[:], SG[:], AF.Ln, scale=oml[:, i:i + 1],
                                                                 bias=lbv[:, i:i + 1]), [ks, "hc"], [kl])
                        c.op("dve", lambda: nc.vector.tensor_scalar(
                            out=K[:], in0=SG[:], scalar1=noml[:, i:i + 1], scalar2=oml[:, i:i + 1],
                            op0=ALU.mult, op1=ALU.add), [ks, "hc"], [kk])
                        yield
                        c.op("dve", lambda: nc.vector.tensor_tensor_scan(B[:], maskF[:], LG[:], 0.0, ALU.mult, ALU.add),
                             [kl, "hc"], [kb_])
                        yield
                        tot = B[:].rearrange("p (c t) -> p c t", t=32)[:, :, 31]
                        dslice = dec[:, d, tb * 16:(tb + 1) * 16]
                        c.op("act", lambda: nc.scalar.activation(dslice, tot, AF.Exp), [kb_], [("dec", st_, d, tb)])
                        if d == 1:
                            c.op("act", lambda: nc.scalar.copy(TC[:], tot), [kb_], [kt])
                            yield
                            b3 = B[:].rearrange("p (c t) -> p c t", t=32)
                            c.op("dve", lambda: nc.vector.tensor_tensor(out=b3, in0=bc32(TC[:]), in1=b3, op=ALU.subtract),
                                 [kb_, kt], [kb_])
                            yield
                            c.op("pool", lambda: nc.gpsimd.tensor_tensor(out=B[:], in0=B[:], in1=LG[:], op=ALU.add),
                                 [kb_, kl], [kb_])
                        yield
                        c.op("act", lambda: nc.scalar.activation(SG[:], B[:], AF.Exp), [kb_], [ks])
                        c.op("act", lambda: nc.scalar.activation(LG[:], B[:], AF.Exp, scale=-1.0), [kb_], [kl])
                        yield
                        c.op("dve", lambda: nc.vector.tensor_tensor(out=qd[d][:, cols], in0=Q[:], in1=SG[:], op=ALU.mult),
                             [kq, ks], [("qd", st_, d, tb)])
                        yield
                        c.op("dve", lambda: nc.vector.tensor_tensor(out=LG[:], in0=K[:], in1=LG[:], op=ALU.mult),
                             [kk, kl], [kl])
                        yield
                        c.op("pool", lambda: nc.gpsimd.tensor_copy(ki[d][:, cols], LG[:]), [kl], [("ki", st_, d, tb)])
                        k3 = LG[:].rearrange("p (c t) -> p c t", t=32)
                        o3 = keT[d][:, cols].rearrange("p (c t) -> p c t", t=32)
                        c.op("pool", lambda: nc.gpsimd.tensor_tensor(out=o3, in0=k3, in1=bc32(dslice), op=ALU.mult),
                             [kl, ("dec", st_, d, tb)], [("keT", st_, d, tb)])
                        yield
                    mm_group(pjt[:], [(wb[:, k, 512:640], hs[:, k, :]) for k in range(16)], wk, [pjk])
                    yield
                    c.op("act", lambda: nc.scalar.activation(sgT[:, cols], pjt[:], AF.Silu), [pjk], [("sgT", st_, tb)])
                    yield
                    for tt in range(tb * 4, tb * 4 + 4):
                        lc = slice((tt % 4) * 128, (tt % 4 + 1) * 128)
                        mm_group(pjt[:, 0:128], [(hs[:, k, lc], wb[:, k, 384:512]) for k in range(16)], wk, [pjk])
                        yield
                        c.op("act", lambda: nc.scalar.copy(V[:, tt, :], pjt[:, 0:128]), [pjk], [("V", st_, tt)])
                        yield

                ot_written = set()
                vctr = [0, 0]
                sctr = [0, 0]

                def sweep(d, h, si, t0, n):
                    st_ = h % 2
                    qd, ki, keT, sgT, V, dec = bufsets[st_]
                    cur = sctr[d] % 2
                    if si < 2:
                        c.op("dve", lambda: nc.vector.memset(S32[d][cur][:], 0.0), [], [("S32", d, cur)])
                    else:
                        c.dma("sp", S32[d][cur][:], st0[d, h], writes=[("S32", d, cur)])
                    sb0 = sctr[d] % 4
                    c.op("act", lambda: nc.scalar.copy(Sbf[d][sb0][:], S32[d][cur][:]),
                         [("S32", d, cur)], [("Sbf", d, sb0)])
                    yield
                    tiles = range(t0, t0 + n) if d == 0 else range(t0 + n - 1, t0 - 1, -1)
                    for tt in tiles:
                        tb = tt // 4
                        tcols = slice(tt * 128, (tt + 1) * 128)
                        r = vctr[d] % 2
                        vctr[d] += 1
                        c.op("pe", lambda: nc.tensor.transpose(ptr[d], keT[d][:, tcols], identb[:]),
                             [("keT", st_, d, tb), "identb"], [("bA", d)])
                        c.op("pool", lambda: nc.gpsimd.tensor_tensor(
                            out=Vm[d][r][:], in0=V[:, tt, :].unsqueeze(1).to_broadcast([128, 4, 128]), in1=cm4[:],
                            op=ALU.mult), [("V", st_, tt), "hc"], [("Vm", d, r)])
                        yield
                        c.op("act", lambda: nc.scalar.copy(kend[d][r][:], ptr[d]), [("bA", d)], [("kend", d, r)])
                        c.op("pe", lambda: nc.tensor.matmul(bA[d][:, 0:128], lhsT=ki[d][:, tcols], rhs=qd[d][:, tcols],
                                                            start=True, stop=True),
                             [("ki", st_, d, tb), ("qd", st_, d, tb)], [("bA", d)])
                        yield
                        c.op("dve", lambda: nc.vector.tensor_tensor(out=Am[d][r][:], in0=bA[d][:, 0:128], in1=mfb[:, d, :],
                                                                    op=ALU.mult), [("bA", d), "hc"], [("Am", d, r)])

                        def umm():
                            inst = None
                            for j in range(4):
                                inst = nc.tensor.matmul(uB[d][:, j * 128:(j + 1) * 128], lhsT=kend[d][r][:],
                                                        rhs=Vm[d][r][:, j, :], start=True, stop=True)
                            return inst
                        c.op("pe", umm, [("kend", d, r), ("Vm", d, r)], [("uB", d)])
                        yield
                        c.op("pe", lambda: nc.tensor.matmul(pO[d][:, 0:128], lhsT=V[:, tt, :], rhs=Am[d][r][:],
                                                            start=True, stop=False),
                             [("V", st_, tt), ("Am", d, r)], [("pO", d)])
                        yield
                        order = range(4) if d == 0 else range(3, -1, -1)
                        for n_, j in enumerate(order):
                            cur = sctr[d] % 2
                            nxt = 1 - cur
                            sbc = sctr[d] % 4
                            sbn = (sctr[d] + 1) % 4
                            ccols = slice(tt * 128 + 32 * j, tt * 128 + 32 * j + 32)
                            gch = tt * 4 + j
                            c.op("pe", lambda: nc.tensor.matmul(
                                pO[d][:, 32 * j:32 * j + 32], lhsT=Sbf[d][sbc][:], rhs=qd[d][:, ccols],
                                start=False, stop=(n_ == 3)),
                                [("Sbf", d, sbc), ("qd", st_, d, tb)], [("pO", d)])
                            c.op("dve", lambda: nc.vector.scalar_tensor_tensor(
                                out=S32[d][nxt][:], in0=S32[d][cur][:], scalar=dec[:, d, gch:gch + 1],
                                in1=uB[d][:, j * 128:(j + 1) * 128], op0=ALU.mult, op1=ALU.add),
                                [("S32", d, cur), ("dec", st_, d, tb), ("uB", d)], [("S32", d, nxt)])
                            yield
                            c.op("act", lambda: nc.scalar.copy(Sbf[d][sbn][:], S32[d][nxt][:]),
                                 [("S32", d, nxt)], [("Sbf", d, sbn)])
                            sctr[d] += 1
                            yield
                        if tt not in ot_written:
                            ot_written.add(tt)
                            c.op("act", lambda: nc.scalar.copy(OT[:, tcols], pO[d][:, 0:128]), [("pO", d)], [("OT", tt)])
                        else:
                            c.op("dve", lambda: nc.vector.tensor_tensor(out=OT[:, tcols], in0=pO[d][:, 0:128],
                                                                        in1=OT[:, tcols], op=ALU.add),
                                 [("pO", d), ("OT", tt)], [("OT", tt)])
                        yield
                    if si < 2:
                        fin = sctr[d] % 2
                        c.dma("sp", nstate[si, d, h], S32[d][fin][:], reads=[("S32", d, fin)],
                              writes=[("nstate", si, d, h)])

                def n_task(sl, h, tb):
                    cols = slice(tb * 512, (tb + 1) * 512)
                    st_ = h % 2
                    qd, ki, keT, sgT, V, dec = bufsets[st_]
                    ok = [("OT", tt) for tt in range(tb * 4, tb * 4 + 4)]
                    SQ, SD = q32[sl], sg[sl]
                    kq, ks = ("q32", sl), ("sg", sl)
                    pjt, pjk = fb[sl], ("fb", sl)
                    c.op("act", lambda: nc.scalar.activation(SQ[:], OT[:, cols], AF.Square), ok, [kq])
                    yield
                    c.op("pe", lambda: nc.tensor.matmul(pjt[:], lhsT=onesf[:], rhs=SQ[:], start=True, stop=True),
                         [kq, "onesf"], [pjk])
                    yield
                    c.op("act", lambda: nc.scalar.activation(SD[:], pjt[:], AF.Ln, bias=epsc[:]), [pjk, "epsc"], [ks])
                    yield
                    c.op("act", lambda: nc.scalar.activation(SD[:], SD[:], AF.Exp, scale=-0.5), [ks], [ks])
                    yield
                    c.op("dve", lambda: nc.vector.scalar_tensor_tensor(
                        out=SQ[:], in0=OT[:, cols], scalar=onc[:, 0:1], in1=SD[:], op0=ALU.mult, op1=ALU.mult),
                        ok + [ks, "hc"], [kq])
                    yield
                    ostv = k32[sl][:].bitcast(BF16)[:, 0:512]
                    c.op("pool", lambda: nc.gpsimd.tensor_tensor(out=ostv, in0=SQ[:], in1=sgT[:, cols], op=ALU.mult),
                         [kq, ("sgT", st_, tb)], [("k32", sl)])
                    c.dma("sp", oTs[0][h * 128:(h + 1) * 128, cols], ostv, reads=[("k32", sl)],
                          writes=[("oTs", 0, h)])
                    yield

                def scan_all(h):
                    ot_written.clear()
                    for si, (t0, n) in enumerate(SEQS):
                        for _ in tasks_gen([(lambda sl, d=d: sweep(d, h, si, t0, n)) for d in range(2)], 2):
                            yield

                def f_all(h, wb, wk):
                    for _ in tasks_gen([(lambda sl, tb=tb: f_task(sl, h, wb, wk, tb)) for tb in range(NTB)], 2):
                        yield

                wb, wk = getw()
                interleave(f_all(0, wb, wk), None)
                for h in range(8):
                    nxt_f = None
                    if h + 1 < 8:
                        wbn, wkn = getw()
                        nxt_f = f_all(h + 1, wbn, wkn)
                    interleave(scan_all(h), nxt_f)
                    run_tasks([(lambda sl, tb=tb: n_task(sl, h, tb)) for tb in range(NTB)], 2)
                c.barrier_all()
                for k in range(16):
                    c.dma("sp" if k % 2 == 0 else "act", hT[:, k, :], hTd[k], reads=[("hTd", k)], writes=[("hTr", k)])
                c.barrier_all()

        def phase_out(L):
            NB = 4
            with ExitStack() as es:
                getw = make_stream(es, [(w_out[L], cb * 512, 512) for cb in range(4)], 512)
                gtb = sb(es, "gtb", [128, 2, D], F32)
                xr = [sb(es, f"xr{i}", [128, 512], F32) for i in range(NB)]
                tm = [sb(es, f"otm{i}", [128, 512], F32) for i in range(NB)]
                po = [ps(es, f"po{i}", [128, 512]) for i in range(2)]
                for g in range(2):
                    c.dma("act", gtb[:, g, :], gts[L * 2 + g:L * 2 + g + 1, :].partition_broadcast(128),
                          reads=[("gts", L, g, cb) for cb in range(4)], writes=[("gtb", g)])
                for k in range(16):
                    c.dma("sp" if k % 2 == 0 else "act", hT[:, k, :], oTs[L][k * 128:(k + 1) * 128, :],
                          reads=[("oTs", L, k)], writes=[("hTk", k)])
                it = 0
                for cb in range(4):
                    ccols = slice(cb * 512, (cb + 1) * 512)
                    wb, wk = getw()
                    for tt in range(NT):
                        b = it % NB
                        pb = it % 2
                        it += 1
                        g = 0 if tt < 4 else 1
                        rows = slice(tt * 128, (tt + 1) * 128)
                        tcols = slice(tt * 128, (tt + 1) * 128)
                        if L == 0:
                            c.dma("sp", xr[b][:], x[rows, ccols], writes=[("xr", b)])
                        else:
                            c.dma("sp", xr[b][:], y1[rows, ccols], reads=[("y1", tt, cb)], writes=[("xr", b)])
                        mm_group(po[pb][:], [(hT[:, k, tcols], wb[:, k, 0:512]) for k in range(16)],
                                 [wk] + [("hTk", k) for k in range(16)], [("po", pb)])
                        c.op("dve", lambda: nc.vector.tensor_tensor(
                            out=tm[b][:], in0=po[pb][:], in1=gtb[:, g, ccols], op=ALU.mult),
                            [("po", pb), ("gtb", g)], [("otm", b)])
                        c.op("pool", lambda: nc.gpsimd.tensor_tensor(out=tm[b][:], in0=tm[b][:], in1=xr[b][:],
                                                                     op=ALU.add),
                             [("otm", b), ("xr", b)], [("otm", b)])
                        if L == 0:
                            c.dma("act", y1[rows, ccols], tm[b][:], reads=[("otm", b)], writes=[("y1", tt, cb)])
                        else:
                            c.dma("act", y[rows, ccols], tm[b][:], reads=[("otm", b)], writes=[("y", tt, cb)])
                c.barrier_all()

        plist = [("mod0", lambda: phase_mod_h(0)), ("hgrn", phase_hgrn), ("attn0", lambda: phase_attn(0)),
                 ("out0", lambda: phase_out(0)), ("mod1", lambda: phase_mod_h(1)), ("attn1", lambda: phase_attn(1)),
                 ("out1", lambda: phase_out(1))]
        for nm, fn in plist:
            if phases is None or nm in phases:
                fn()
        c.finish()
    return nc


def _consts():
    ident = np.eye(128, dtype=np.float32)
    maskF = np.ones((128, 512), np.float32)
    maskF[:, ::32] = 0.0
    s = np.arange(128)[:, None]
    t = np.arange(128)[None, :]
    same = (s // 32) == (t // 32)
    mfb = np.stack([(same & (s <= t)), (same & (s >= t))], axis=1).astype(np.float32)
    cm4 = np.zeros((128, 4, 128), np.float32)
    for j in range(4):
        cm4[32 * j:32 * j + 32, j, :] = 1.0
    R = np.zeros((128, 128), np.float32)
    for m in range(128):
        q = m // 32
        if q in (0, 2):
            R[m, m + 32] = -1.0
        else:
            R[m, m - 32] = 1.0
    RT = np.ascontiguousarray(R.T)
    n_tok = 2048
    row = (np.arange(n_tok) // 64).astype(np.float32)
    col = (np.arange(n_tok) % 64).astype(np.float32)
    inv = (10000.0 ** (-np.arange(32, dtype=np.float32) / 32)).astype(np.float32)
    ar = row[:, None] * inv
    ac = col[:, None] * inv
    ang = np.concatenate([ar, ar, ac, ac], axis=-1).astype(np.float32)
    cosT = np.ascontiguousarray(np.cos(ang).T.astype(np.float32))
    sinT = np.ascontiguousarray(np.sin(ang).T.astype(np.float32))
    b = np.arange(128)[:, None]
    a = np.arange(128)[None, :]
    NEG = -30000.0
    wbias = np.stack([np.where(b >= a, 0.0, NEG), np.where(b <= a, 0.0, NEG)], axis=1).astype(np.float32)
    wbias4 = np.ascontiguousarray(np.broadcast_to(wbias[:, :, None, :], (128, 2, 4, 128))).astype(np.float32)
    return dict(ident=ident, maskF=maskF, mfb=mfb, cm4=cm4, RT=RT, cosT=cosT, sinT=sinT, wbias4=wbias4)


def _perm_w0(w):
    cols = []
    for h in range(8):
        for base in (0, 1024, 2048, 3072, 4096):
            cols.append(np.arange(base + h * 128, base + (h + 1) * 128))
    for j in range(2):
        cols.append(np.arange(6144 + j * 128, 6144 + (j + 1) * 128))
        cols.append(np.arange(6400 + j * 128, 6400 + (j + 1) * 128))
    for h in range(8):
        cols.append(np.arange(5120 + h * 128, 5120 + (h + 1) * 128))
        cols.append(np.arange(6656 + h * 128, 6656 + (h + 1) * 128))
    return np.ascontiguousarray(w[:, np.concatenate(cols)])


def _perm_w1(w):
    cols = []
    for j in range(4):
        cols.append(np.arange(2048 + j * 128, 2048 + (j + 1) * 128))
        cols.append(np.arange(2560 + j * 128, 2560 + (j + 1) * 128))
    for h in range(16):
        cols.append(np.arange(h * 128, (h + 1) * 128))
        cols.append(np.arange(3072 + h * 128, 3072 + (h + 1) * 128))
    return np.ascontiguousarray(w[:, np.concatenate(cols)])


def _prep(x_prompt, x_sample, state_l0_hgrn, cache_l0_k, cache_l0_v, cache_l1_k, cache_l1_v,
           c, c_ctx, lb_gamma,
           l0_norm, l0_w_mod, l0_b_mod, l0_w_in, l0_w_out, l0_a_onorm, l0_b_qnorm, l0_b_knorm,
           l1_norm, l1_w_mod, l1_b_mod, l1_w_in, l1_w_out, l1_c_qnorm, l1_c_knorm, l1_c_sink):
    f = lambda a: np.ascontiguousarray(np.asarray(a, dtype=np.float32))
    x_prompt, x_sample = f(x_prompt), f(x_sample)
    consts = _consts()
    shared = dict(
        w_mod0=f(l0_w_mod), w_mod1=f(l1_w_mod),
        b_mod0=f(l0_b_mod).reshape(1, -1), b_mod1=f(l1_b_mod).reshape(1, -1),
        norm0=f(l0_norm).reshape(1, -1), norm1=f(l1_norm).reshape(1, -1),
        w_in0=_perm_w0(f(l0_w_in)), w_in1=_perm_w1(f(l1_w_in)),
        w_out0=f(l0_w_out), w_out1=f(l1_w_out),
        onorm=f(l0_a_onorm).reshape(128, 1),
        qn0=f(l0_b_qnorm).reshape(128, 1), kn0=f(l0_b_knorm).reshape(128, 1),
        qn1=f(l1_c_qnorm).reshape(128, 1), kn1=f(l1_c_knorm).reshape(128, 1),
        sink=f(l1_c_sink).reshape(1, 16),
        lbg=np.ascontiguousarray(f(lb_gamma).reshape(3, 2, 8, 128).transpose(3, 0, 1, 2).reshape(128, 3, 16)),
        **consts,
    )
    c = f(c)
    c_ctx = f(c_ctx)
    in_maps = []
    for i in range(8):
        m = dict(shared)
        m["x"] = np.ascontiguousarray(np.concatenate(
            [x_prompt[2 * i], x_prompt[2 * i + 1], x_sample[i]], axis=0))
        cr = np.stack([c_ctx, c[i]], axis=0)
        m["crows"] = np.ascontiguousarray(cr.reshape(2, 16, 128).transpose(2, 0, 1))
        m["st0"] = f(state_l0_hgrn[i])
        m["ck0T"] = np.ascontiguousarray(f(cache_l0_k[i]).transpose(2, 1, 0))
        m["cv0"] = f(cache_l0_v[i])
        m["ck1T"] = np.ascontiguousarray(f(cache_l1_k[i]).transpose(2, 1, 0))
        m["cv1"] = f(cache_l1_v[i])
        in_maps.append(m)
    return in_maps


def kernel(**inputs):
    in_maps = _prep(**inputs)
    nc = build_program()
    res = run_bass_kernel_spmd(nc, in_maps, core_ids=list(range(8)))
    r = res.results
    y_prompt = np.stack([r[i // 2]["y"][(i % 2) * 256:(i % 2 + 1) * 256] for i in range(16)], axis=0)
    y_sample = np.stack([r[i]["y"][512:] for i in range(8)], axis=0)
    nstate = np.concatenate([r[i]["nstate"] for i in range(8)], axis=0)
    outs = [y_prompt.astype(np.float32), y_sample.astype(np.float32), nstate.astype(np.float32)]
    for nm in ("nk0", "nv0", "nk1", "nv1"):
        a = np.concatenate([r[i][nm].reshape(2, 256, r[i][nm].shape[1], 128) for i in range(8)], axis=0)
        outs.append(a.astype(np.float32))
    return tuple(outs)
```

```python
import math
import os
from contextlib import ExitStack

import numpy as np
import concourse.bass as bass
import concourse.mybir as mybir
from concourse.bass_utils import run_bass_kernel_spmd

F32 = mybir.dt.float32
BF16 = mybir.dt.bfloat16
AF = mybir.ActivationFunctionType
ALU = mybir.AluOpType

D = 2048
T = 2560
NT = 20
NTB = 5
EPS = 1e-6
SCALE = 1.0 / math.sqrt(128.0)
SEQS = [(0, 2), (2, 2), (4, 16)]


class Ctx:
    NDMA = 32

    def __init__(self, nc):
        self.nc = nc
        self.engs = {"pe": nc.tensor, "dve": nc.vector, "act": nc.scalar,
                     "pool": nc.gpsimd, "sp": nc.sync}
        self.sem = {k: nc.alloc_semaphore(name="s_" + k) for k in self.engs}
        self.cnt = {k: 0 for k in self.engs}
        self.waited = {k: {} for k in self.engs}
        self.last_w = {}
        self.readers = {}
        self.dma_sems = [nc.alloc_semaphore(name=f"s_dma{i}") for i in range(self.NDMA)]
        self.dma_val = [0] * self.NDMA
        self.dma_pool = {"sp": list(range(0, 16)), "pool": list(range(16, 24)), "act": list(range(24, 32))}
        self.dma_rr = {"sp": 0, "pool": 0, "act": 0}

    def _deps(self, reads, writes):
        evs = []
        for r in reads:
            e = self.last_w.get(r)
            if e is not None:
                evs.append(e)
        for w in writes:
            e = self.last_w.get(w)
            if e is not None:
                evs.append(e)
            evs.extend(self.readers.get(w, ()))
        return evs

    def _wait(self, eng, evs, skip_self=False):
        best = {}
        for (name, sem, val) in evs:
            if skip_self and name == eng:
                continue
            if best.get(name, (None, 0))[1] < val:
                best[name] = (sem, val)
        wd = self.waited[eng]
        for name, (sem, val) in best.items():
            if wd.get(name, 0) < val:
                self.engs[eng].wait_ge(sem, val)
                wd[name] = val

    def _commit(self, ev, reads, writes):
        ws = set(writes)
        for r in reads:
            if r in ws:
                continue
            self.readers.setdefault(r, []).append(ev)
        for w in writes:
            self.last_w[w] = ev
            self.readers[w] = []

    EXCL = {"pj", "pms", "prot", "pv", "pS", "pO", "pL", "pm", "pT", "bA", "ptr", "po", "fb"}

    def op(self, eng, fn, reads=(), writes=()):
        reads = list(reads)
        writes = list(writes)
        ex = [r for r in reads if (r if isinstance(r, str) else r[0]) in self.EXCL]
        if ex:
            reads = [r for r in reads if r not in ex]
            writes = writes + [r for r in ex if r not in writes]
        evs = self._deps(reads, writes)
        self._wait(eng, evs, skip_self=(eng == "pe"))
        inst = fn()
        inst.then_inc(self.sem[eng], 1)
        self.cnt[eng] += 1
        ev = (eng, self.sem[eng], self.cnt[eng])
        self._commit(ev, reads, writes)
        return ev

    def dma(self, q, out, in_, reads=(), writes=()):
        reads = list(reads)
        writes = list(writes)
        evs = self._deps(reads, writes)
        lst = self.dma_pool[q]
        k = lst[self.dma_rr[q] % len(lst)]
        self.dma_rr[q] += 1
        name = f"dma{k}"
        if self.dma_val[k] > 0:
            evs.append((name, self.dma_sems[k], self.dma_val[k]))
        self._wait(q, evs)
        self.engs[q].dma_start(out=out, in_=in_).then_inc(self.dma_sems[k], 16)
        self.dma_val[k] += 16
        ev = (name, self.dma_sems[k], self.dma_val[k])
        self._commit(ev, reads, writes)
        return ev

    def all_events(self):
        evs = [(k, self.sem[k], self.cnt[k]) for k in self.engs if self.cnt[k] > 0]
        for i in range(self.NDMA):
            if self.dma_val[i] > 0:
                evs.append((f"dma{i}", self.dma_sems[i], self.dma_val[i]))
        return evs

    def barrier_all(self):
        evs = self.all_events()
        for e in self.engs:
            self._wait(e, evs, skip_self=True)

    def finish(self):
        self._wait("sp", self.all_events(), skip_self=True)


def build_program(phases=None):
    nc = bass.Bass("TRN2", target_bir_lowering=False)

    def din(name, shape, dt=F32):
        return nc.dram_tensor(name, list(shape), dt, kind="ExternalInput").ap()

    def dout(name, shape, dt=F32):
        return nc.dram_tensor(name, list(shape), dt, kind="ExternalOutput").ap()

    def dscr(name, shape, dt=F32):
        return nc.dram_tensor(name, list(shape), dt, kind="Internal").ap()

    x = din("x", [T, D])
    crows = din("crows", [128, 2, 16])
    lbg = din("lbg", [128, 3, 16])
    st0 = din("st0", [2, 8, 128, 128])
    ckT = [din("ck0T", [128, 2, 256]), din("ck1T", [128, 4, 256])]
    cv = [din("cv0", [256, 2, 128]), din("cv1", [256, 4, 128])]
    w_mod = [din("w_mod0", [D, 3 * D]), din("w_mod1", [D, 3 * D])]
    b_mod = [din("b_mod0", [1, 3 * D]), din("b_mod1", [1, 3 * D])]
    norm_g = [din("norm0", [1, D]), din("norm1", [1, D])]
    w_in = [din("w_in0", [D, 7680]), din("w_in1", [D, 5120])]
    w_out = [din("w_out0", [D, D]), din("w_out1", [D, D])]
    onorm_d = din("onorm", [128, 1])
    qn_d = [din("qn0", [128, 1]), din("qn1", [128, 1])]
    kn_d = [din("kn0", [128, 1]), din("kn1", [128, 1])]
    sink_d = din("sink", [1, 16])
    ident_d = din("ident", [128, 128])
    maskF_d = din("maskF", [128, 512])
    mfb_d = din("mfb", [128, 2, 128])
    cm4_d = din("cm4", [128, 4, 128])
    RT_d = din("RT", [128, 128])
    cosT_d = din("cosT", [128, 2048])
    sinT_d = din("sinT", [128, 2048])
    wbias4_d = din("wbias4", [128, 2, 4, 128])

    y = dout("y", [T, D])
    nstate = dout("nstate", [2, 2, 8, 128, 128])
    nk = [dout("nk0", [512, 2, 128]), dout("nk1", [512, 4, 128])]
    nv = [dout("nv0", [512, 2, 128]), dout("nv1", [512, 4, 128])]

    gts = dscr("gts", [4, D])
    oTs = [dscr("oT0", [D, T], BF16), dscr("oT1", [D, T], BF16)]
    y1 = dscr("y1", [T, D])

    c = Ctx(nc)

    uid = [0]

    def sb(es, name, shape, dt):
        uid[0] += 1
        return es.enter_context(nc.sbuf_tensor(f"{name}_{uid[0]}", list(shape), dt))

    def ps(es, name, shape, dt=F32):
        uid[0] += 1
        return es.enter_context(nc.psum_tensor(f"{name}_{uid[0]}", list(shape), dt))

    def mm_group(out, pairs, reads, writes):
        def f():
            n = len(pairs)
            inst = None
            for i, (l, r) in enumerate(pairs):
                inst = nc.tensor.matmul(out, lhsT=l, rhs=r, start=(i == 0), stop=(i == n - 1))
            return inst
        return c.op("pe", f, reads, writes)

    def wview(w, c0, ncols):
        return w[:, c0:c0 + ncols].rearrange("(k p) n -> p k n", p=128)

    with ExitStack() as top:
        identb = sb(top, "identb", [128, 128], BF16)
        identf = sb(top, "identf", [128, 128], F32)
        onesb = sb(top, "onesb", [128, 128], BF16)
        onesf = sb(top, "onesf", [128, 128], F32)
        epsc = sb(top, "epsc", [128, 1], F32)
        hT = sb(top, "hT", [128, 16, T], BF16)

        c.dma("sp", identf[:], ident_d, writes=["identf"])
        c.dma("pool", identb[:], ident_d, writes=["identb"])
        c.op("dve", lambda: nc.vector.memset(onesb[:], 1.0), writes=["onesb"])
        c.op("dve", lambda: nc.vector.memset(onesf[:], 1.0 / 128.0), writes=["onesf"])
        c.op("dve", lambda: nc.vector.memset(epsc[:], EPS), writes=["epsc"])

        def make_stream(es, blocks, ncols_max, nslots=2):
            bufs = [sb(es, f"wbuf{i}", [128, 16, ncols_max], BF16) for i in range(nslots)]
            st = {"issued": 0, "got": 0}

            def issue():
                i = st["issued"]
                if i >= len(blocks):
                    return
                w, c0, ncols = blocks[i]
                s = i % nslots
                c.dma("pool", bufs[s][:, :, 0:ncols], wview(w, c0, ncols), writes=[("wb", s)])
                st["issued"] += 1

            def get():
                i = st["got"]
                while st["issued"] <= i:
                    issue()
                st["got"] += 1
                if st["issued"] <= i + 1:
                    issue()
                s = i % nslots
                return bufs[s], ("wb", s)
            return get

        hkeys = lambda tb: [("hT", tt) for tt in range(tb * 4, tb * 4 + 4)]
        allh = [("hT", tt) for tt in range(NT)]

        def phase_mod_h(L):
            with ExitStack() as es:
                G = sb(es, "G", [128, 2, D], F32)
                SH = sb(es, "SH", [128, 2, D], F32)
                with ExitStack() as es2:
                    getw = make_stream(es2, [(w_mod[L], blk * 512, 512) for blk in range(12)], 512)
                    crs = sb(es2, "crs", [128, 2, 16], F32)
                    scl = sb(es2, "scl", [128, 2, 16], F32)
                    srep = sb(es2, "srep", [128, 2, 16, 128], BF16)
                    bb = [sb(es2, f"bb{i}", [128, 512], F32) for i in range(2)]
                    gb = [sb(es2, f"gb{i}", [128, 512], F32) for i in range(2)]
                    tmp = [sb(es2, f"mtmp{i}", [128, 512], F32) for i in range(2)]
                    pm = [ps(es2, f"pm{i}", [128, 512]) for i in range(2)]
                    c.dma("sp", crs[:], crows, writes=["crs"])
                    c.op("act", lambda: nc.scalar.activation(scl[:], crs[:], AF.Silu), ["crs"], ["scl"])
                    c.op("dve", lambda: nc.vector.tensor_copy(
                        srep[:], scl[:].unsqueeze(3).to_broadcast([128, 2, 16, 128])), ["scl"], ["srep"])
                    for blk in range(12):
                        kind, cb = divmod(blk, 4)
                        b = blk % 2
                        wb, wk = getw()
                        c.dma("sp", bb[b][:], b_mod[L][:, blk * 512:(blk + 1) * 512].partition_broadcast(128),
                              writes=[("bb", b)])
                        if kind == 1:
                            c.dma("sp", gb[b][:], norm_g[L][:, cb * 512:(cb + 1) * 512].partition_broadcast(128),
                                  writes=[("gb", b)])
                        cols = slice(cb * 512, (cb + 1) * 512)
                        for g in range(2):
                            mm_group(pm[g][:], [(srep[:, g, k, :], wb[:, k, 0:512]) for k in range(16)],
                                     [wk, "srep"], [("pm", g)])
                            if kind == 0:
                                c.op("dve", lambda g=g, b=b, cols=cols: nc.vector.tensor_tensor(
                                    out=SH[:, g, cols], in0=pm[g][:], in1=bb[b][:], op=ALU.add),
                                    [("pm", g), ("bb", b)], [("SH", g, cb)])
                            elif kind == 1:
                                c.op("dve", lambda g=g, b=b: nc.vector.tensor_tensor(
                                    out=tmp[g][:], in0=pm[g][:], in1=bb[b][:], op=ALU.add),
                                    [("pm", g), ("bb", b)], [("mtmp", g)])
                                c.op("dve", lambda g=g, b=b, cols=cols: nc.vector.scalar_tensor_tensor(
                                    out=G[:, g, cols], in0=tmp[g][:], scalar=1.0, in1=gb[b][:],
                                    op0=ALU.add, op1=ALU.mult),
                                    [("mtmp", g), ("gb", b)], [("G", g, cb)])
                            else:
                                c.op("dve", lambda g=g, b=b: nc.vector.tensor_tensor(
                                    out=tmp[g][:], in0=pm[g][:], in1=bb[b][:], op=ALU.add),
                                    [("pm", g), ("bb", b)], [("mtmp", g)])
                                c.dma("sp", gts[L * 2 + g:L * 2 + g + 1, cols], tmp[g][0:1, :],
                                      reads=[("mtmp", g)], writes=[("gts", L, g, cb)])
                    c.barrier_all()
                with ExitStack() as es2:
                    xt = [sb(es2, f"xt{i}", [128, D], F32) for i in range(2)]
                    junk = sb(es2, "junk", [128, D], BF16)
                    st = sb(es2, "st", [128, 8], F32)
                    t1 = [sb(es2, f"t1_{i}", [128, D], F32) for i in range(2)]
                    hb = [sb(es2, f"hb{i}", [128, D], BF16) for i in range(2)]
                    pT = [ps(es2, f"pT{i}", [128, 8, 128], BF16) for i in range(4)]
                    GK = [[("G", g, cb) for cb in range(4)] for g in range(2)]
                    SK = [[("SH", g, cb) for cb in range(4)] for g in range(2)]
                    for tt in range(NT):
                        b = tt % 2
                        g = 0 if tt < 4 else 1
                        rows = slice(tt * 128, (tt + 1) * 128)
                        if L == 0:
                            c.dma("sp", xt[b][:], x[rows, :], writes=[("xt", b)])
                        else:
                            c.dma("sp", xt[b][:], y1[rows, :], reads=[("y1", tt, cb) for cb in range(4)],
                                  writes=[("xt", b)])
                        c.op("act", lambda b=b: nc.scalar.activation(junk[:], xt[b][:], AF.Square,
                                                                     accum_out=st[:, b:b + 1]),
                             [("xt", b)], ["junk", ("ssq", b)])
                        c.op("act", lambda b=b: nc.scalar.activation(st[:, 2 + b:3 + b], st[:, b:b + 1], AF.Sqrt,
                                                                     scale=1.0 / D, bias=epsc[:]),
                             [("ssq", b), "epsc"], [("std", b)])
                        c.op("dve", lambda b=b: nc.vector.reciprocal(st[:, 4 + b:5 + b], st[:, 2 + b:3 + b]),
                             [("std", b)], [("rstd", b)])
                        c.op("dve", lambda b=b, g=g: nc.vector.scalar_tensor_tensor(
                            out=t1[b][:], in0=xt[b][:], scalar=st[:, 4 + b:5 + b], in1=G[:, g, :],
                            op0=ALU.mult, op1=ALU.mult),
                            [("xt", b), ("rstd", b)] + GK[g], [("t1", b)])
                        c.op("pool", lambda b=b, g=g: nc.gpsimd.tensor_tensor(
                            out=hb[b][:, 0:1408], in0=t1[b][:, 0:1408], in1=SH[:, g, 0:1408], op=ALU.add),
                            [("t1", b)] + SK[g], [("hb", b, 0)])
                        c.op("dve", lambda b=b, g=g: nc.vector.tensor_tensor(
                            out=hb[b][:, 1408:D], in0=t1[b][:, 1408:D], in1=SH[:, g, 1408:D], op=ALU.add),
                            [("t1", b)] + SK[g], [("hb", b, 1)])
                        for half in range(2):
                            pp = pT[b * 2 + half]

                            def tr(pp=pp, b=b, half=half):
                                inst = None
                                for kk in range(8):
                                    k = half * 8 + kk
                                    inst = nc.tensor.transpose(pp[:, kk, :], hb[b][:, k * 128:(k + 1) * 128], identb[:])
                                return inst
                            c.op("pe", tr, [("hb", b, 0), ("hb", b, 1), "identb"], [("pT", b, half)])
                            eng = "act" if half == 0 else "dve"
                            dst = hT[:, half * 8:(half + 1) * 8, tt * 128:(tt + 1) * 128]
                            if eng == "act":
                                c.op("act", lambda pp=pp, dst=dst: nc.scalar.copy(dst, pp[:]),
                                     [("pT", b, half)], [("hT", tt, half)])
                            else:
                                c.op("dve", lambda pp=pp, dst=dst: nc.vector.tensor_copy(dst, pp[:]),
                                     [("pT", b, half)], [("hT", tt, half)])
                    c.barrier_all()
                c.barrier_all()

        def run_tasks(factories, width):
            it = iter(factories)
            active = {}
            free = list(range(width))
            while True:
                while free:
                    f = next(it, None)
                    if f is None:
                        break
                    sl = free.pop(0)
                    active[sl] = f(sl)
                if not active:
                    break
                for sl in sorted(active):
                    try:
                        next(active[sl])
                    except StopIteration:
                        del active[sl]
                        free.append(sl)

        def phase_attn(L):
            nkv = 2 if L == 0 else 4
            G = 1 if L == 0 else 4
            if L == 0:
                kvc0, qc0, orow0 = 5120, 5120 + 512, 8
            else:
                kvc0, qc0, orow0 = 0, 1024, 0
            with ExitStack() as es:
                ablocks = []
                for j_ in range(nkv):
                    ablocks.append((w_in[L], kvc0 + j_ * 256, 256))
                    for h_ in range(4 * j_, 4 * j_ + 4):
                        ablocks.append((w_in[L], qc0 + h_ * 256, 256))
                getw = make_stream(es, ablocks, 256, nslots=3)

                T_sq = [sb(es, f"sq{i}", [128, 512], F32) for i in range(3)]
                T_sd = [sb(es, f"sd{i}", [128, 512], F32) for i in range(3)]
                T_kb = [sb(es, f"kb{i}", [128, 512], BF16) for i in range(3)]
                T_u = [sb(es, f"ru{i}", [128, 512], F32) for i in range(3)]
                cosT = sb(es, "cosT", [128, 2048], F32)
                sinT = sb(es, "sinT", [128, 2048], F32)
                RTb = sb(es, "RTb", [128, 128], BF16)
                gq = sb(es, "gq", [128, 1], F32)
                gk = sb(es, "gk", [128, 1], F32)
                esk = sb(es, "esk", [128, 16], F32)
                wb4 = sb(es, "wb4", [128, 2, 4, 128], BF16)
                KT = sb(es, "KT", [128, T], BF16)
                KcT = sb(es, "KcT", [128, 256], BF16)
                V = sb(es, "V", [128, NT, 128], BF16)
                Vc = sb(es, "Vc", [128, 2, 128], BF16)
                QTg = sb(es, "QTg", [128, G, T], BF16)
                oThg = sb(es, "oThg", [128, G, T], BF16)
                kout = [sb(es, f"kout{i}", [128, 128], F32) for i in range(2)]
                vout = [sb(es, f"vout{i}", [128, 128], F32) for i in range(3)]
                PT = [sb(es, f"PT{i}", [128, 512], BF16) for i in range(4)]
                rl = [sb(es, f"rl{i}", [128, 512], F32) for i in range(2)]
                pj = ps(es, "pj", [128, 512])
                pms = ps(es, "pms", [128, 512])
                pS = [ps(es, f"pS{i}", [128, 512]) for i in range(2)]
                pO = [ps(es, f"pO{i}", [128, 512]) for i in range(2)]
                pL = [ps(es, f"pL{i}", [128, 512]) for i in range(2)]
                pbank = [(pj, "pj"), (pS[0], ("pS", 0)), (pS[1], ("pS", 1))]
                rbank = pbank
                tbank = (pO[0], ("pO", 0))
                msbank = [(pms, "pms"), (pL[0], ("pL", 0)), (pL[1], ("pL", 1))]
                NW = 3

                c.dma("sp", cosT[:], cosT_d, writes=["consts"])
                c.dma("sp", sinT[:], sinT_d, writes=["consts"])
                c.dma("pool", RTb[:], RT_d, writes=["consts"])
                c.dma("sp", gq[:], qn_d[L], writes=["consts"])
                c.dma("sp", gk[:], kn_d[L], writes=["consts"])
                c.dma("pool", wb4[:], wbias4_d, writes=["consts"])
                c.dma("sp", esk[:], sink_d.partition_broadcast(128), writes=["esk0"])
                c.op("act", lambda: nc.scalar.activation(esk[:], esk[:], AF.Exp), ["esk0"], ["esk0", "consts"])

                def fn_task(s, pjt, pjk, gcol, rope_cols, out_bf, outkey, out_f32=None, f32key=None):
                    sq, sd, kb, uu_ = T_sq[s], T_sd[s], T_kb[s], T_u[s]
                    pmt, pmk = msbank[s]
                    c.op("act", lambda: nc.scalar.activation(sq[:], pjt, AF.Square), [pjk], [("sq", s)])
                    yield
                    c.op("pe", lambda: nc.tensor.matmul(pmt[:], lhsT=onesf[:], rhs=sq[:], start=True, stop=True),
                         [("sq", s), "onesf"], [pmk])
                    yield
                    c.op("act", lambda: nc.scalar.activation(sd[:], pmt[:], AF.Ln, bias=epsc[:]),
                         [pmk, "epsc"], [("sd", s)])
                    yield
                    c.op("act", lambda: nc.scalar.activation(sd[:], sd[:], AF.Exp, scale=-0.5), [("sd", s)], [("sd", s)])
                    yield
                    if out_f32 is None:
                        dst, dkey = sq[:], ("sq", s)
                    else:
                        dst, dkey = out_f32, (f32key or outkey + ("f32",))
                    c.op("dve", lambda: nc.vector.scalar_tensor_tensor(
                        out=dst, in0=pjt, scalar=gcol, in1=sd[:], op0=ALU.mult, op1=ALU.mult),
                        [pjk, ("sd", s), "consts"], [dkey])
                    yield
                    if rope_cols is None:
                        c.op("pool", lambda: nc.gpsimd.tensor_copy(out_bf, dst), [dkey], [outkey])
                        yield
                    else:
                        c.op("pool", lambda: nc.gpsimd.tensor_copy(kb[:], dst), [dkey], [("kb", s)])
                        yield
                        prt, prk = rbank[s]
                        c.op("pe", lambda: nc.tensor.matmul(prt[:], lhsT=RTb[:], rhs=kb[:], start=True, stop=True),
                             [("kb", s), "consts"], [prk])
                        yield
                        c.op("pool", lambda: nc.gpsimd.tensor_tensor(out=dst, in0=dst, in1=cosT[:, rope_cols],
                                                                     op=ALU.mult), [dkey, "consts"], [dkey])
                        yield
                        c.op("dve", lambda: nc.vector.tensor_tensor(out=uu_[:], in0=prt[:], in1=sinT[:, rope_cols],
                                                                    op=ALU.mult), [prk, "consts"], [("ru", s)])
                        yield
                        c.op("pool", lambda: nc.gpsimd.tensor_tensor(out=out_bf, in0=dst, in1=uu_[:], op=ALU.add),
                             [dkey, ("ru", s)], [outkey])
                        yield

                def rope_of(tb):
                    return None if tb == 0 else slice((tb - 1) * 512, tb * 512)

                def k_task(sl, wb, wk, j, tb):
                    cols = slice(tb * 512, (tb + 1) * 512)
                    pjt, pjk = pbank[sl]
                    mm_group(pjt[:], [(wb[:, k, 0:128], hT[:, k, cols]) for k in range(16)], [wk], [pjk])
                    yield
                    if tb == 0:
                        kf32 = T_u[sl]
                        yield from fn_task(sl, pjt[:], pjk, gk[:, 0:1], None, KT[:, cols], ("KT", tb), out_f32=kf32[:],
                                           f32key=("ru", sl))
                        for t4 in range(4):
                            b = t4 % 2
                            pvt, pvk = tbank
                            c.op("pe", lambda: nc.tensor.transpose(
                                pvt[:, 0:128], kf32[:, t4 * 128:(t4 + 1) * 128], identf[:]),
                                [("ru", sl), "identf"], [pvk])
                            yield
                            c.op("act", lambda: nc.scalar.copy(kout[b][:], pvt[:, 0:128]), [pvk], [("kout", b)])
                            yield
                            c.dma("sp", nk[L][t4 * 128:(t4 + 1) * 128, j, :], kout[b][:],
                                  reads=[("kout", b)], writes=[("nk", t4, j)])
                    else:
                        yield from fn_task(sl, pjt[:], pjk, gk[:, 0:1], rope_of(tb), KT[:, cols], ("KT", tb))

                def v_task(sl, wb, wk, j, tt):
                    tcols = slice(tt * 128, (tt + 1) * 128)
                    pvt, pvk = pbank[sl]
                    mm_group(pvt[:, 0:128], [(hT[:, k, tcols], wb[:, k, 128:256]) for k in range(16)], [wk], [pvk])
                    yield
                    if tt < 4:
                        b = sl
                        c.op("act", lambda: nc.scalar.copy(vout[b][:], pvt[:, 0:128]), [pvk], [("vout", b)])
                        yield
                        c.op("pool", lambda: nc.gpsimd.tensor_copy(V[:, tt, :], vout[b][:]), [("vout", b)], [("V", tt)])
                        c.dma("sp", nv[L][tt * 128:(tt + 1) * 128, j, :], vout[b][:],
                              reads=[("vout", b)], writes=[("nv", tt, j)])
                        yield
                    else:
                        c.op("act", lambda: nc.scalar.copy(V[:, tt, :], pvt[:, 0:128]), [pvk], [("V", tt)])
                        yield

                def q_task(sl, wb, wk, hh, tb):
                    cols = slice(tb * 512, (tb + 1) * 512)
                    pjt, pjk = pbank[sl]
                    mm_group(pjt[:], [(wb[:, k, 0:128], hT[:, k, cols]) for k in range(16)], [wk], [pjk])
                    yield
                    yield from fn_task(sl, pjt[:], pjk, gq[:, 0:1], rope_of(tb), QTg[:, hh, cols], ("QT", hh, tb))

                def g_task(sl, wb, wk, hh, tb):
                    cols = slice(tb * 512, (tb + 1) * 512)
                    pjt, pjk = pbank[sl]
                    mm_group(pjt[:], [(wb[:, k, 128:256], hT[:, k, cols]) for k in range(16)], [wk], [pjk])
                    yield
                    c.op("act", lambda: nc.scalar.activation(oThg[:, hh, cols], pjt[:], AF.Silu),
                         [pjk], [("oTh", hh, tb)])
                    yield

                sctr = [0, 0]
                sbanks = [[(pS[0], ("pS", 0)), (pS[1], ("pS", 1))], [(pj, "pj"), (pms, "pms")]]

                def attn_block(ob, j, q0, nq, keys, tbq):
                    qcols = slice(q0, q0 + nq)
                    N = G * nq
                    nk_ = len(keys)
                    rhsQ = QTg[:, :, qcols] if G > 1 else QTg[:, 0, qcols]
                    qkeys = [("QT", hh, tbq) for hh in range(G)]
                    slots = []

                    def smm(ki):
                        kind, idx, mi = keys[ki]
                        p = sctr[ob] % 2
                        sctr[ob] += 1
                        slots.append(p)
                        pSt, pSk = sbanks[ob][p]
                        if kind == "l":
                            Kl = KT[:, idx * 128:(idx + 1) * 128]
                            kr = [("KT", idx // 4)]
                        else:
                            Kl = KcT[:, idx * 128:(idx + 1) * 128]
                            kr = ["KcT"]

                        def f():
                            out = pSt[:, :N] if G == 1 else pSt[:, :N].rearrange("p (g q) -> p g q", g=G)
                            inst = nc.tensor.matmul(out, lhsT=Kl, rhs=rhsQ, start=True, stop=(mi is None))
                            if mi is not None:
                                inst = nc.tensor.matmul(out, lhsT=identb[:], rhs=wb4[:, mi, :, 0:nq],
                                                        start=False, stop=True)
                            return inst
                        c.op("pe", f, kr + qkeys + ["consts", "identb"], [pSk])

                    def pv(ki):
                        kind, idx, mi = keys[ki]
                        p = slots[ki]
                        if kind == "l":
                            Vl = V[:, idx, :]
                            kr = [("V", idx)]
                        else:
                            Vl = Vc[:, idx, :]
                            kr = ["Vc"]
                        pSt, pSk = sbanks[ob][p]
                        PTt = PT[ob * 2 + p]
                        c.op("act", lambda: nc.scalar.activation(PTt[:, :N], pSt[:, :N], AF.Exp, scale=SCALE),
                             [pSk], [("PT", ob, p)])

                        def f():
                            nc.tensor.matmul(pO[ob][:, :N], lhsT=Vl, rhs=PTt[:, :N],
                                             start=(ki == 0), stop=(ki == nk_ - 1))
                            return nc.tensor.matmul(pL[ob][:, :N], lhsT=onesb[:], rhs=PTt[:, :N],
                                                    start=(ki == 0), stop=(ki == nk_ - 1))
                        c.op("pe", f, kr + [("PT", ob, p), "onesb"], [("pO", ob), ("pL", ob)])

                    smm(0)
                    yield
                    if nk_ > 1:
                        smm(1)
                        yield
                    for ki in range(nk_):
                        pv(ki)
                        if ki + 2 < nk_:
                            smm(ki + 2)
                        yield
                    r = rl[ob]
                    if L == 1:
                        r3 = r[:, :N].rearrange("p (g q) -> p g q", g=G)
                        l3 = pL[ob][:, :N].rearrange("p (g q) -> p g q", g=G)
                        c.op("dve", lambda: nc.vector.tensor_tensor(
                            out=r3, in0=l3, in1=esk[:, 4 * j:4 * j + 4].unsqueeze(2).to_broadcast([128, G, nq]),
                            op=ALU.add), [("pL", ob), "consts"], [("rl", ob)])
                        c.op("act", lambda: nc.scalar.activation(r[:, :N], r[:, :N], AF.Ln), [("rl", ob)], [("rl", ob)])
                    else:
                        c.op("act", lambda: nc.scalar.activation(r[:, :N], pL[ob][:, :N], AF.Ln), [("pL", ob)], [("rl", ob)])
                    c.op("act", lambda: nc.scalar.activation(r[:, :N], r[:, :N], AF.Exp, scale=-1.0),
                         [("rl", ob)], [("rl", ob)])
                    c.op("dve", lambda: nc.vector.tensor_tensor(out=r[:, :N], in0=pO[ob][:, :N], in1=r[:, :N],
                                                                op=ALU.mult), [("pO", ob), ("rl", ob)], [("rl", ob)])
                    okeys = [("oTh", hh, tbq) for hh in range(G)]
                    if G > 1:
                        o3 = oThg[:, :, qcols]
                        r3 = r[:, :N].rearrange("p (g q) -> p g q", g=G)
                    else:
                        o3 = oThg[:, 0, qcols]
                        r3 = r[:, :N]
                    c.op("pool", lambda: nc.gpsimd.tensor_tensor(out=o3, in0=r3, in1=o3, op=ALU.mult),
                         [("rl", ob)] + okeys, okeys)
                    yield

                for j in range(nkv):
                    wb, wk = getw()
                    c.dma("pool", KcT[:], ckT[L][:, j, :], writes=["KcT"])
                    c.dma("pool", Vc[:], cv[L][:, j, :].rearrange("(t p) d -> p t d", p=128), writes=["Vc"])
                    gens = [(lambda sl, tb=tb: k_task(sl, wb, wk, j, tb)) for tb in range(NTB)] + \
                           [(lambda sl, tt=tt: v_task(sl, wb, wk, j, tt)) for tt in range(NT)]
                    run_tasks(gens, NW)
                    for h0 in range(4 * j, 4 * j + 4, G):
                        gens = []
                        for hh in range(G):
                            wbq, wkq = getw()
                            for tb in range(NTB):
                                gens.append(lambda sl, wbq=wbq, wkq=wkq, hh=hh, tb=tb: g_task(sl, wbq, wkq, hh, tb))
                            for tb in range(NTB):
                                gens.append(lambda sl, wbq=wbq, wkq=wkq, hh=hh, tb=tb: q_task(sl, wbq, wkq, hh, tb))
                            if hh % 2 == 1 or G == 1:
                                run_tasks(gens, NW)
                                gens = []
                        blocks = []
                        if G == 1:
                            for (t0, n) in SEQS[:2]:
                                blocks.append((t0 * 128, 256, [("l", t0, None), ("l", t0 + 1, None)], 0))
                            for tb in range(1, NTB):
                                keys = [("l", kt, None) for kt in range(4, NT)] + [("c", 0, None), ("c", 1, None)]
                                blocks.append((tb * 512, 512, keys, tb))
                        else:
                            for (t0, n) in SEQS[:2]:
                                for tq in range(t0, t0 + n):
                                    blocks.append((tq * 128, 128, [("l", t0, None), ("l", t0 + 1, None)], 0))
                            for i in range(16):
                                keys = []
                                if i > 0:
                                    keys.append(("l", 4 + i - 1, 0))
                                keys.append(("l", 4 + i, None))
                                if i < 15:
                                    keys.append(("l", 4 + i + 1, 1))
                                keys += [("c", 0, None), ("c", 1, None)]
                                blocks.append(((4 + i) * 128, 128, keys, (4 + i) // 4))
                        run_tasks([(lambda sl, blk=blk: attn_block(sl, j, *blk)) for blk in blocks], 2)
                        for hh in range(G):
                            r0 = (orow0 + h0 + hh) * 128
                            c.dma("sp", oTs[L][r0:r0 + 128, :], oThg[:, hh, :],
                                  reads=[("oTh", hh, tb) for tb in range(NTB)], writes=[("oTs", L, orow0 + h0 + hh)])
                c.barrier_all()

        def phase_hgrn():
            with ExitStack() as es:
                getw = make_stream(es, [(w_in[0], h * 640, 640) for h in range(8)], 640)
                maskF = sb(es, "maskF", [128, 512], F32)
                mfb = sb(es, "mfb", [128, 2, 128], F32)
                cm4 = sb(es, "cm4", [128, 4, 128], BF16)
                onc = sb(es, "onc", [128, 1], F32)
                lbe = sb(es, "lbe", [128, 3, 16], F32)
                lbs = sb(es, "lbs", [128, 16], F32)
                lbv = sb(es, "lbv", [128, 16], F32)
                oml = sb(es, "oml", [128, 16], F32)
                noml = sb(es, "noml", [128, 16], F32)
                q32 = [sb(es, f"q32_{i}", [128, 512], F32) for i in range(2)]
                sg = [sb(es, f"sg_{i}", [128, 512], F32) for i in range(2)]
                lg = [sb(es, f"lg_{i}", [128, 512], F32) for i in range(2)]
                k32 = [sb(es, f"k32_{i}", [128, 512], F32) for i in range(2)]
                bF = [sb(es, f"bF_{i}", [128, 512], F32) for i in range(2)]
                totc = [sb(es, f"totc_{i}", [128, 16], F32) for i in range(2)]
                dec = sb(es, "dec", [128, 2, 80], F32)
                qd = [sb(es, f"qd{d}", [128, T], BF16) for d in range(2)]
                ki = [sb(es, f"ki{d}", [128, T], BF16) for d in range(2)]
                keT = [sb(es, f"keT{d}", [128, T], BF16) for d in range(2)]
                sgT = sb(es, "sgT", [128, T], BF16)
                V = sb(es, "Va", [128, NT, 128], BF16)
                OT = sb(es, "OT", [128, T], F32)
                kend = [[sb(es, f"kend{d}{r}", [128, 128], BF16) for r in range(2)] for d in range(2)]
                Vm = [[sb(es, f"Vm{d}{r}", [128, 4, 128], BF16) for r in range(2)] for d in range(2)]
                Am = [[sb(es, f"Am{d}{r}", [128, 128], BF16) for r in range(2)] for d in range(2)]
                S32 = [[sb(es, f"S32_{d}{r}", [128, 128], F32) for r in range(2)] for d in range(2)]
                Sbf = [[sb(es, f"Sbf{d}{p}", [128, 128], BF16) for p in range(4)] for d in range(2)]
                fb = [ps(es, f"hfb{i}", [128, 512]) for i in range(2)]
                bA = [ps(es, f"hbA{d}", [128, 512]) for d in range(2)]
                pO = [ps(es, f"hpO{d}", [128, 512]) for d in range(2)]
                ptr = [ps(es, f"hptr{d}", [128, 8, 128], BF16) for d in range(2)]

                c.dma("sp", maskF[:], maskF_d, writes=["hc"])
                c.dma("sp", mfb[:], mfb_d, writes=["hc"])
                c.dma("pool", cm4[:], cm4_d, writes=["hc"])
                c.dma("sp", onc[:], onorm_d, writes=["hc"])
                c.dma("sp", lbe[:], lbg, writes=["lbe"])
                c.op("act", lambda: nc.scalar.activation(lbe[:], lbe[:], AF.Exp), ["lbe"], ["lbe"])
                c.op("dve", lambda: nc.vector.tensor_tensor(out=lbs[:], in0=lbe[:, 0, :], in1=lbe[:, 1, :], op=ALU.add),
                     ["lbe"], ["lbs"])
                c.op("dve", lambda: nc.vector.tensor_tensor(out=lbs[:], in0=lbs[:], in1=lbe[:, 2, :], op=ALU.add),
                     ["lbe", "lbs"], ["lbs"])
                c.op("dve", lambda: nc.vector.reciprocal(lbs[:], lbs[:]), ["lbs"], ["lbs"])
                c.op("dve", lambda: nc.vector.tensor_tensor(out=lbv[:], in0=lbe[:, 0, :], in1=lbs[:], op=ALU.mult),
                     ["lbe", "lbs"], ["lbv"])
                c.op("dve", lambda: nc.vector.tensor_scalar(out=oml[:], in0=lbv[:], scalar1=-1.0, scalar2=1.0,
                                                            op0=ALU.mult, op1=ALU.add), ["lbv"], ["oml"])
                c.op("dve", lambda: nc.vector.tensor_scalar(out=noml[:], in0=oml[:], scalar1=-1.0, scalar2=None,
                                                            op0=ALU.mult), ["oml"], ["noml", "hc"])

                def bc32(t, ncol=16):
                    return t.unsqueeze(2).to_broadcast([128, ncol, 32])

                def f_task(sl, h, wb, wk, tb):
                    cols = slice(tb * 512, (tb + 1) * 512)
                    pjt, pjk = fb[sl], ("fb", sl)
                    Q, SG, LG, K, B, TC = q32[sl], sg[sl], lg[sl], k32[sl], bF[sl], totc[sl]
                    kq, ks, kl, kk, kb_, kt = ("q32", sl), ("sg", sl), ("lg", sl), ("k32", sl), ("bF", sl), ("totc", sl)
                    mm_group(pjt[:], [(wb[:, k, 0:128], hT[:, k, cols]) for k in range(16)], [wk], [pjk])
                    yield
                    c.op("act", lambda: nc.scalar.activation(Q[:], pjt[:], AF.Silu), [pjk], [kq])
                    yield
                    for d in range(2):
                        i = d * 8 + h
                        mm_group(pjt[:], [(wb[:, k, 128 * (1 + d):128 * (2 + d)], hT[:, k, cols]) for k in range(16)],
                                 [wk], [pjk])
                        yield
                        c.op("act", lambda: nc.scalar.activation(SG[:], pjt[:], AF.Sigmoid), [pjk], [ks])
                        yield
                        c.op("act", lambda: nc.scalar.activation(LG[:], SG[:], AF.Ln, scale=oml[:, i:i + 1],
                                                                 bias=lbv[:, i:i + 1]), [ks, "hc"], [kl])
                        c.op("dve", lambda: nc.vector.tensor_scalar(
                            out=K[:], in0=SG[:], scalar1=noml[:, i:i + 1], scalar2=oml[:, i:i + 1],
                            op0=ALU.mult, op1=ALU.add), [ks, "hc"], [kk])
                        yield
                        c.op("dve", lambda: nc.vector.tensor_tensor_scan(B[:], maskF[:], LG[:], 0.0, ALU.mult, ALU.add),
                             [kl, "hc"], [kb_])
                        yield
                        tot = B[:].rearrange("p (c t) -> p c t", t=32)[:, :, 31]
                        dslice = dec[:, d, tb * 16:(tb + 1) * 16]
                        c.op("act", lambda: nc.scalar.activation(dslice, tot, AF.Exp), [kb_], [("dec", d, tb)])
                        if d == 1:
                            c.op("act", lambda: nc.scalar.copy(TC[:], tot), [kb_], [kt])
                            yield
                            b3 = B[:].rearrange("p (c t) -> p c t", t=32)
                            c.op("dve", lambda: nc.vector.tensor_tensor(out=b3, in0=bc32(TC[:]), in1=b3, op=ALU.subtract),
                                 [kb_, kt], [kb_])
                            yield
                            c.op("pool", lambda: nc.gpsimd.tensor_tensor(out=B[:], in0=B[:], in1=LG[:], op=ALU.add),
                                 [kb_, kl], [kb_])
                        yield
                        c.op("act", lambda: nc.scalar.activation(SG[:], B[:], AF.Exp), [kb_], [ks])
                        c.op("act", lambda: nc.scalar.activation(LG[:], B[:], AF.Exp, scale=-1.0), [kb_], [kl])
                        yield
                        c.op("dve", lambda: nc.vector.tensor_tensor(out=qd[d][:, cols], in0=Q[:], in1=SG[:], op=ALU.mult),
                             [kq, ks], [("qd", d, tb)])
                        yield
                        c.op("dve", lambda: nc.vector.tensor_tensor(out=LG[:], in0=K[:], in1=LG[:], op=ALU.mult),
                             [kk, kl], [kl])
                        yield
                        c.op("pool", lambda: nc.gpsimd.tensor_copy(ki[d][:, cols], LG[:]), [kl], [("ki", d, tb)])
                        k3 = LG[:].rearrange("p (c t) -> p c t", t=32)
                        o3 = keT[d][:, cols].rearrange("p (c t) -> p c t", t=32)
                        c.op("pool", lambda: nc.gpsimd.tensor_tensor(out=o3, in0=k3, in1=bc32(dslice), op=ALU.mult),
                             [kl, ("dec", d, tb)], [("keT", d, tb)])
                        yield
                    mm_group(pjt[:], [(wb[:, k, 512:640], hT[:, k, cols]) for k in range(16)], [wk], [pjk])
                    yield
                    c.op("act", lambda: nc.scalar.activation(sgT[:, cols], pjt[:], AF.Silu), [pjk], [("sgT", tb)])
                    yield
                    for tt in range(tb * 4, tb * 4 + 4):
                        tcols = slice(tt * 128, (tt + 1) * 128)
                        mm_group(pjt[:, 0:128], [(hT[:, k, tcols], wb[:, k, 384:512]) for k in range(16)], [wk], [pjk])
                        yield
                        c.op("act", lambda: nc.scalar.copy(V[:, tt, :], pjt[:, 0:128]), [pjk], [("V", tt)])
                        yield

                ot_written = set()
                vctr = [0, 0]
                sctr = [0, 0]

                def sweep(d, h, si, t0, n):
                    cur = sctr[d] % 2
                    if si < 2:
                        c.op("dve", lambda: nc.vector.memset(S32[d][cur][:], 0.0), [], [("S32", d, cur)])
                    else:
                        c.dma("sp", S32[d][cur][:], st0[d, h], writes=[("S32", d, cur)])
                    sb0 = sctr[d] % 4
                    c.op("act", lambda: nc.scalar.copy(Sbf[d][sb0][:], S32[d][cur][:]),
                         [("S32", d, cur)], [("Sbf", d, sb0)])
                    yield
                    tiles = range(t0, t0 + n) if d == 0 else range(t0 + n - 1, t0 - 1, -1)
                    for tt in tiles:
                        tb = tt // 4
                        tcols = slice(tt * 128, (tt + 1) * 128)
                        r = vctr[d] % 2
                        vctr[d] += 1
                        c.op("pe", lambda: nc.tensor.transpose(ptr[d][:, 0, :], keT[d][:, tcols], identb[:]),
                             [("keT", d, tb), "identb"], [("ptr", d)])
                        c.op("pool", lambda: nc.gpsimd.tensor_tensor(
                            out=Vm[d][r][:], in0=V[:, tt, :].unsqueeze(1).to_broadcast([128, 4, 128]), in1=cm4[:],
                            op=ALU.mult), [("V", tt), "hc"], [("Vm", d, r)])
                        yield
                        c.op("act", lambda: nc.scalar.copy(kend[d][r][:], ptr[d][:, 0, :]), [("ptr", d)], [("kend", d, r)])
                        c.op("pe", lambda: nc.tensor.matmul(bA[d][:, 0:128], lhsT=ki[d][:, tcols], rhs=qd[d][:, tcols],
                                                            start=True, stop=True),
                             [("ki", d, tb), ("qd", d, tb)], [("bA", d)])
                        yield
                        c.op("dve", lambda: nc.vector.tensor_tensor(out=Am[d][r][:], in0=bA[d][:, 0:128], in1=mfb[:, d, :],
                                                                    op=ALU.mult), [("bA", d), "hc"], [("Am", d, r)])

                        def umm():
                            inst = None
                            for j in range(4):
                                inst = nc.tensor.matmul(fb[d][:, j * 128:(j + 1) * 128], lhsT=kend[d][r][:],
                                                        rhs=Vm[d][r][:, j, :], start=True, stop=True)
                            return inst
                        c.op("pe", umm, [("kend", d, r), ("Vm", d, r)], [("fb", d)])
                        yield
                        c.op("pe", lambda: nc.tensor.matmul(pO[d][:, 0:128], lhsT=V[:, tt, :], rhs=Am[d][r][:],
                                                            start=True, stop=False),
                             [("V", tt), ("Am", d, r)], [("pO", d)])
                        yield
                        order = range(4) if d == 0 else range(3, -1, -1)
                        for n_, j in enumerate(order):
                            cur = sctr[d] % 2
                            nxt = 1 - cur
                            sbc = sctr[d] % 4
                            sbn = (sctr[d] + 1) % 4
                            ccols = slice(tt * 128 + 32 * j, tt * 128 + 32 * j + 32)
                            gch = tt * 4 + j
                            c.op("pe", lambda: nc.tensor.matmul(
                                pO[d][:, 32 * j:32 * j + 32], lhsT=Sbf[d][sbc][:], rhs=qd[d][:, ccols],
                                start=False, stop=(n_ == 3)),
                                [("Sbf", d, sbc), ("qd", d, tb)], [("pO", d)])
                            c.op("dve", lambda: nc.vector.scalar_tensor_tensor(
                                out=S32[d][nxt][:], in0=S32[d][cur][:], scalar=dec[:, d, gch:gch + 1],
                                in1=fb[d][:, j * 128:(j + 1) * 128], op0=ALU.mult, op1=ALU.add),
                                [("S32", d, cur), ("dec", d, tb), ("fb", d)], [("S32", d, nxt)])
                            yield
                            c.op("act", lambda: nc.scalar.copy(Sbf[d][sbn][:], S32[d][nxt][:]),
                                 [("S32", d, nxt)], [("Sbf", d, sbn)])
                            sctr[d] += 1
                            yield
                        if tt not in ot_written:
                            ot_written.add(tt)
                            c.op("act", lambda: nc.scalar.copy(OT[:, tcols], pO[d][:, 0:128]), [("pO", d)], [("OT", tt)])
                        else:
                            c.op("dve", lambda: nc.vector.tensor_tensor(out=OT[:, tcols], in0=pO[d][:, 0:128],
                                                                        in1=OT[:, tcols], op=ALU.add),
                                 [("pO", d), ("OT", tt)], [("OT", tt)])
                        yield
                    if si < 2:
                        fin = sctr[d] % 2
                        c.dma("sp", nstate[si, d, h], S32[d][fin][:], reads=[("S32", d, fin)],
                              writes=[("nstate", si, d, h)])

                def n_task(sl, h, tb):
                    cols = slice(tb * 512, (tb + 1) * 512)
                    ok = [("OT", tt) for tt in range(tb * 4, tb * 4 + 4)]
                    SQ, SD = q32[sl], sg[sl]
                    kq, ks = ("q32", sl), ("sg", sl)
                    pjt, pjk = fb[sl], ("fb", sl)
                    c.op("act", lambda: nc.scalar.activation(SQ[:], OT[:, cols], AF.Square), ok, [kq])
                    yield
                    c.op("pe", lambda: nc.tensor.matmul(pjt[:], lhsT=onesf[:], rhs=SQ[:], start=True, stop=True),
                         [kq, "onesf"], [pjk])
                    yield
                    c.op("act", lambda: nc.scalar.activation(SD[:], pjt[:], AF.Ln, bias=epsc[:]), [pjk, "epsc"], [ks])
                    yield
                    c.op("act", lambda: nc.scalar.activation(SD[:], SD[:], AF.Exp, scale=-0.5), [ks], [ks])
                    yield
                    c.op("dve", lambda: nc.vector.scalar_tensor_tensor(
                        out=SQ[:], in0=OT[:, cols], scalar=onc[:, 0:1], in1=SD[:], op0=ALU.mult, op1=ALU.mult),
                        ok + [ks, "hc"], [kq])
                    yield
                    ostv = k32[sl][:].bitcast(BF16)[:, 0:512]
                    c.op("pool", lambda: nc.gpsimd.tensor_tensor(out=ostv, in0=SQ[:], in1=sgT[:, cols], op=ALU.mult),
                         [kq, ("sgT", tb)], [("k32", sl)])
                    c.dma("sp", oTs[0][h * 128:(h + 1) * 128, cols], ostv, reads=[("k32", sl)],
                          writes=[("oTs", 0, h)])
                    yield

                HG = os.environ.get("MK_HG", "")
                for h in range(8):
                    wb, wk = getw()
                    if "nof" not in HG:
                        run_tasks([(lambda sl, tb=tb: f_task(sl, h, wb, wk, tb)) for tb in range(NTB)], 2)
                    ot_written.clear()
                    if "noscan" not in HG:
                        for si, (t0, n) in enumerate(SEQS):
                            run_tasks([(lambda sl, d=d: sweep(d, h, si, t0, n)) for d in range(2)], 2)
                    if "nopost" not in HG:
                        run_tasks([(lambda sl, tb=tb: n_task(sl, h, tb)) for tb in range(NTB)], 2)
                c.barrier_all()

        def phase_out(L):
            NB = 4
            with ExitStack() as es:
                getw = make_stream(es, [(w_out[L], cb * 512, 512) for cb in range(4)], 512)
                gtb = sb(es, "gtb", [128, 2, D], F32)
                xr = [sb(es, f"xr{i}", [128, 512], F32) for i in range(NB)]
                tm = [sb(es, f"otm{i}", [128, 512], F32) for i in range(NB)]
                po = [ps(es, f"po{i}", [128, 512]) for i in range(2)]
                for g in range(2):
                    c.dma("act", gtb[:, g, :], gts[L * 2 + g:L * 2 + g + 1, :].partition_broadcast(128),
                          reads=[("gts", L, g, cb) for cb in range(4)], writes=[("gtb", g)])
                for k in range(16):
                    c.dma("sp" if k % 2 == 0 else "act", hT[:, k, :], oTs[L][k * 128:(k + 1) * 128, :],
                          reads=[("oTs", L, k)], writes=[("hTk", k)])
                it = 0
                for cb in range(4):
                    ccols = slice(cb * 512, (cb + 1) * 512)
                    wb, wk = getw()
                    for tt in range(NT):
                        b = it % NB
                        pb = it % 2
                        it += 1
                        g = 0 if tt < 4 else 1
                        rows = slice(tt * 128, (tt + 1) * 128)
                        tcols = slice(tt * 128, (tt + 1) * 128)
                        if L == 0:
                            c.dma("sp", xr[b][:], x[rows, ccols], writes=[("xr", b)])
                        else:
                            c.dma("sp", xr[b][:], y1[rows, ccols], reads=[("y1", tt, cb)], writes=[("xr", b)])
                        mm_group(po[pb][:], [(hT[:, k, tcols], wb[:, k, 0:512]) for k in range(16)],
                                 [wk] + [("hTk", k) for k in range(16)], [("po", pb)])
                        c.op("dve", lambda: nc.vector.tensor_tensor(
                            out=tm[b][:], in0=po[pb][:], in1=gtb[:, g, ccols], op=ALU.mult),
                            [("po", pb), ("gtb", g)], [("otm", b)])
                        c.op("pool", lambda: nc.gpsimd.tensor_tensor(out=tm[b][:], in0=tm[b][:], in1=xr[b][:],
                                                                     op=ALU.add),
                             [("otm", b), ("xr", b)], [("otm", b)])
                        if L == 0:
                            c.dma("act", y1[rows, ccols], tm[b][:], reads=[("otm", b)], writes=[("y1", tt, cb)])
                        else:
                            c.dma("act", y[rows, ccols], tm[b][:], reads=[("otm", b)], writes=[("y", tt, cb)])
                c.barrier_all()

        plist = [("mod0", lambda: phase_mod_h(0)), ("hgrn", phase_hgrn), ("attn0", lambda: phase_attn(0)),
                 ("out0", lambda: phase_out(0)), ("mod1", lambda: phase_mod_h(1)), ("attn1", lambda: phase_attn(1)),
                 ("out1", lambda: phase_out(1))]
        for nm, fn in plist:
            if phases is None or nm in phases:
                fn()
        c.finish()
    return nc


def _consts():
    ident = np.eye(128, dtype=np.float32)
    maskF = np.ones((128, 512), np.float32)
    maskF[:, ::32] = 0.0
    s = np.arange(128)[:, None]
    t = np.arange(128)[None, :]
    same = (s // 32) == (t // 32)
    mfb = np.stack([(same & (s <= t)), (same & (s >= t))], axis=1).astype(np.float32)
    cm4 = np.zeros((128, 4, 128), np.float32)
    for j in range(4):
        cm4[32 * j:32 * j + 32, j, :] = 1.0
    R = np.zeros((128, 128), np.float32)
    for m in range(128):
        q = m // 32
        if q in (0, 2):
            R[m, m + 32] = -1.0
        else:
            R[m, m - 32] = 1.0
    RT = np.ascontiguousarray(R.T)
    n_tok = 2048
    row = (np.arange(n_tok) // 64).astype(np.float32)
    col = (np.arange(n_tok) % 64).astype(np.float32)
    inv = (10000.0 ** (-np.arange(32, dtype=np.float32) / 32)).astype(np.float32)
    ar = row[:, None] * inv
    ac = col[:, None] * inv
    ang = np.concatenate([ar, ar, ac, ac], axis=-1).astype(np.float32)
    cosT = np.ascontiguousarray(np.cos(ang).T.astype(np.float32))
    sinT = np.ascontiguousarray(np.sin(ang).T.astype(np.float32))
    b = np.arange(128)[:, None]
    a = np.arange(128)[None, :]
    NEG = -30000.0
    wbias = np.stack([np.where(b >= a, 0.0, NEG), np.where(b <= a, 0.0, NEG)], axis=1).astype(np.float32)
    wbias4 = np.ascontiguousarray(np.broadcast_to(wbias[:, :, None, :], (128, 2, 4, 128))).astype(np.float32)
    return dict(ident=ident, maskF=maskF, mfb=mfb, cm4=cm4, RT=RT, cosT=cosT, sinT=sinT, wbias4=wbias4)


def _perm_w0(w):
    cols = []
    for h in range(8):
        for base in (0, 1024, 2048, 3072, 4096):
            cols.append(np.arange(base + h * 128, base + (h + 1) * 128))
    for j in range(2):
        cols.append(np.arange(6144 + j * 128, 6144 + (j + 1) * 128))
        cols.append(np.arange(6400 + j * 128, 6400 + (j + 1) * 128))
    for h in range(8):
        cols.append(np.arange(5120 + h * 128, 5120 + (h + 1) * 128))
        cols.append(np.arange(6656 + h * 128, 6656 + (h + 1) * 128))
    return np.ascontiguousarray(w[:, np.concatenate(cols)])


def _perm_w1(w):
    cols = []
    for j in range(4):
        cols.append(np.arange(2048 + j * 128, 2048 + (j + 1) * 128))
        cols.append(np.arange(2560 + j * 128, 2560 + (j + 1) * 128))
    for h in range(16):
        cols.append(np.arange(h * 128, (h + 1) * 128))
        cols.append(np.arange(3072 + h * 128, 3072 + (h + 1) * 128))
    return np.ascontiguousarray(w[:, np.concatenate(cols)])


def _prep(x_prompt, x_sample, state_l0_hgrn, cache_l0_k, cache_l0_v, cache_l1_k, cache_l1_v,
           c, c_ctx, lb_gamma,
           l0_norm, l0_w_mod, l0_b_mod, l0_w_in, l0_w_out, l0_a_onorm, l0_b_qnorm, l0_b_knorm,
           l1_norm, l1_w_mod, l1_b_mod, l1_w_in, l1_w_out, l1_c_qnorm, l1_c_knorm, l1_c_sink):
    f = lambda a: np.ascontiguousarray(np.asarray(a, dtype=np.float32))
    x_prompt, x_sample = f(x_prompt), f(x_sample)
    consts = _consts()
    shared = dict(
        w_mod0=f(l0_w_mod), w_mod1=f(l1_w_mod),
        b_mod0=f(l0_b_mod).reshape(1, -1), b_mod1=f(l1_b_mod).reshape(1, -1),
        norm0=f(l0_norm).reshape(1, -1), norm1=f(l1_norm).reshape(1, -1),
        w_in0=_perm_w0(f(l0_w_in)), w_in1=_perm_w1(f(l1_w_in)),
        w_out0=f(l0_w_out), w_out1=f(l1_w_out),
        onorm=f(l0_a_onorm).reshape(128, 1),
        qn0=f(l0_b_qnorm).reshape(128, 1), kn0=f(l0_b_knorm).reshape(128, 1),
        qn1=f(l1_c_qnorm).reshape(128, 1), kn1=f(l1_c_knorm).reshape(128, 1),
        sink=f(l1_c_sink).reshape(1, 16),
        lbg=np.ascontiguousarray(f(lb_gamma).reshape(3, 2, 8, 128).transpose(3, 0, 1, 2).reshape(128, 3, 16)),
        **consts,
    )
    c = f(c)
    c_ctx = f(c_ctx)
    in_maps = []
    for i in range(8):
        m = dict(shared)
        m["x"] = np.ascontiguousarray(np.concatenate(
            [x_prompt[2 * i], x_prompt[2 * i + 1], x_sample[i]], axis=0))
        cr = np.stack([c_ctx, c[i]], axis=0)
        m["crows"] = np.ascontiguousarray(cr.reshape(2, 16, 128).transpose(2, 0, 1))
        m["st0"] = f(state_l0_hgrn[i])
        m["ck0T"] = np.ascontiguousarray(f(cache_l0_k[i]).transpose(2, 1, 0))
        m["cv0"] = f(cache_l0_v[i])
        m["ck1T"] = np.ascontiguousarray(f(cache_l1_k[i]).transpose(2, 1, 0))
        m["cv1"] = f(cache_l1_v[i])
        in_maps.append(m)
    return in_maps


def kernel(**inputs):
    in_maps = _prep(**inputs)
    nc = build_program()
    res = run_bass_kernel_spmd(nc, in_maps, core_ids=list(range(8)))
    r = res.results
    y_prompt = np.stack([r[i // 2]["y"][(i % 2) * 256:(i % 2 + 1) * 256] for i in range(16)], axis=0)
    y_sample = np.stack([r[i]["y"][512:] for i in range(8)], axis=0)
    nstate = np.concatenate([r[i]["nstate"] for i in range(8)], axis=0)
    outs = [y_prompt.astype(np.float32), y_sample.astype(np.float32), nstate.astype(np.float32)]
    for nm in ("nk0", "nv0", "nk1", "nv1"):
        a = np.concatenate([r[i][nm].reshape(2, 256, r[i][nm].shape[1], 128) for i in range(8)], axis=0)
        outs.append(a.astype(np.float32))
    return tuple(outs)
```

```python
import math
import os
from contextlib import ExitStack

import numpy as np
import concourse.bass as bass
import concourse.mybir as mybir
from concourse.bass_utils import run_bass_kernel_spmd

F32 = mybir.dt.float32
BF16 = mybir.dt.bfloat16
AF = mybir.ActivationFunctionType
ALU = mybir.AluOpType

D = 2048
T = 2560
NT = 20
NTB = 5
EPS = 1e-6
SCALE = 1.0 / math.sqrt(128.0)
SEQS = [(0, 2), (2, 2), (4, 16)]


class Ctx:
    NDMA = 32

    def __init__(self, nc):
        self.nc = nc
        self.engs = {"pe": nc.tensor, "dve": nc.vector, "act": nc.scalar,
                     "pool": nc.gpsimd, "sp": nc.sync}
        self.sem = {k: nc.alloc_semaphore(name="s_" + k) for k in self.engs}
        self.cnt = {k: 0 for k in self.engs}
        self.waited = {k: {} for k in self.engs}
        self.last_w = {}
        self.readers = {}
        self.dma_sems = [nc.alloc_semaphore(name=f"s_dma{i}") for i in range(self.NDMA)]
        self.dma_val = [0] * self.NDMA
        self.dma_pool = {"sp": list(range(0, 16)), "pool": list(range(16, 24)), "act": list(range(24, 32))}
        self.dma_rr = {"sp": 0, "pool": 0, "act": 0}

    def _deps(self, reads, writes):
        evs = []
        for r in reads:
            e = self.last_w.get(r)
            if e is not None:
                evs.append(e)
        for w in writes:
            e = self.last_w.get(w)
            if e is not None:
                evs.append(e)
            evs.extend(self.readers.get(w, ()))
        return evs

    def _wait(self, eng, evs, skip_self=False):
        best = {}
        for (name, sem, val) in evs:
            if skip_self and name == eng:
                continue
            if best.get(name, (None, 0))[1] < val:
                best[name] = (sem, val)
        wd = self.waited[eng]
        for name, (sem, val) in best.items():
            if wd.get(name, 0) < val:
                self.engs[eng].wait_ge(sem, val)
                wd[name] = val

    def _commit(self, ev, reads, writes):
        ws = set(writes)
        for r in reads:
            if r in ws:
                continue
            self.readers.setdefault(r, []).append(ev)
        for w in writes:
            self.last_w[w] = ev
            self.readers[w] = []

    EXCL = {"pj", "pms", "prot", "pv", "pS", "pO", "pL", "pm", "pT", "bA", "ptr", "po", "fb"}

    def op(self, eng, fn, reads=(), writes=()):
        reads = list(reads)
        writes = list(writes)
        ex = [r for r in reads if (r if isinstance(r, str) else r[0]) in self.EXCL]
        if ex:
            reads = [r for r in reads if r not in ex]
            writes = writes + [r for r in ex if r not in writes]
        evs = self._deps(reads, writes)
        self._wait(eng, evs, skip_self=(eng == "pe"))
        inst = fn()
        inst.then_inc(self.sem[eng], 1)
        self.cnt[eng] += 1
        ev = (eng, self.sem[eng], self.cnt[eng])
        self._commit(ev, reads, writes)
        return ev

    def dma(self, q, out, in_, reads=(), writes=()):
        reads = list(reads)
        writes = list(writes)
        evs = self._deps(reads, writes)
        lst = self.dma_pool[q]
        k = lst[self.dma_rr[q] % len(lst)]
        self.dma_rr[q] += 1
        name = f"dma{k}"
        if self.dma_val[k] > 0:
            evs.append((name, self.dma_sems[k], self.dma_val[k]))
        self._wait(q, evs)
        self.engs[q].dma_start(out=out, in_=in_).then_inc(self.dma_sems[k], 16)
        self.dma_val[k] += 16
        ev = (name, self.dma_sems[k], self.dma_val[k])
        self._commit(ev, reads, writes)
        return ev

    def all_events(self):
        evs = [(k, self.sem[k], self.cnt[k]) for k in self.engs if self.cnt[k] > 0]
        for i in range(self.NDMA):
            if self.dma_val[i] > 0:
                evs.append((f"dma{i}", self.dma_sems[i], self.dma_val[i]))
        return evs

    def barrier_all(self):
        evs = self.all_events()
        for e in self.engs:
            self._wait(e, evs, skip_self=True)

    def finish(self):
        self._wait("sp", self.all_events(), skip_self=True)


def build_program(phases=None):
    nc = bass.Bass("TRN2", target_bir_lowering=False)

    def din(name, shape, dt=F32):
        return nc.dram_tensor(name, list(shape), dt, kind="ExternalInput").ap()

    def dout(name, shape, dt=F32):
        return nc.dram_tensor(name, list(shape), dt, kind="ExternalOutput").ap()

    def dscr(name, shape, dt=F32):
        return nc.dram_tensor(name, list(shape), dt, kind="Internal").ap()

    x = din("x", [T, D])
    crows = din("crows", [128, 2, 16])
    lbg = din("lbg", [128, 3, 16])
    st0 = din("st0", [2, 8, 128, 128])
    ckT = [din("ck0T", [128, 2, 256]), din("ck1T", [128, 4, 256])]
    cv = [din("cv0", [256, 2, 128]), din("cv1", [256, 4, 128])]
    w_mod = [din("w_mod0", [D, 3 * D]), din("w_mod1", [D, 3 * D])]
    b_mod = [din("b_mod0", [1, 3 * D]), din("b_mod1", [1, 3 * D])]
    norm_g = [din("norm0", [1, D]), din("norm1", [1, D])]
    w_in = [din("w_in0", [D, 7680]), din("w_in1", [D, 5120])]
    w_out = [din("w_out0", [D, D]), din("w_out1", [D, D])]
    onorm_d = din("onorm", [128, 1])
    qn_d = [din("qn0", [128, 1]), din("qn1", [128, 1])]
    kn_d = [din("kn0", [128, 1]), din("kn1", [128, 1])]
    sink_d = din("sink", [1, 16])
    ident_d = din("ident", [128, 128])
    maskF_d = din("maskF", [128, 512])
    mfb_d = din("mfb", [128, 2, 128])
    cm4_d = din("cm4", [128, 4, 128])
    RT_d = din("RT", [128, 128])
    cosT_d = din("cosT", [128, 2048])
    sinT_d = din("sinT", [128, 2048])
    wbias4_d = din("wbias4", [128, 2, 4, 128])

    y = dout("y", [T, D])
    nstate = dout("nstate", [2, 2, 8, 128, 128])
    nk = [dout("nk0", [512, 2, 128]), dout("nk1", [512, 4, 128])]
    nv = [dout("nv0", [512, 2, 128]), dout("nv1", [512, 4, 128])]

    gts = dscr("gts", [4, D])
    oTs = [dscr("oT0", [D, T], BF16), dscr("oT1", [D, T], BF16)]
    y1 = dscr("y1", [T, D])

    c = Ctx(nc)

    uid = [0]

    def sb(es, name, shape, dt):
        uid[0] += 1
        return es.enter_context(nc.sbuf_tensor(f"{name}_{uid[0]}", list(shape), dt))

    def ps(es, name, shape, dt=F32):
        uid[0] += 1
        return es.enter_context(nc.psum_tensor(f"{name}_{uid[0]}", list(shape), dt))

    def mm_group(out, pairs, reads, writes):
        def f():
            n = len(pairs)
            inst = None
            for i, (l, r) in enumerate(pairs):
                inst = nc.tensor.matmul(out, lhsT=l, rhs=r, start=(i == 0), stop=(i == n - 1))
            return inst
        return c.op("pe", f, reads, writes)

    def wview(w, c0, ncols):
        return w[:, c0:c0 + ncols].rearrange("(k p) n -> p k n", p=128)

    with ExitStack() as top:
        identb = sb(top, "identb", [128, 128], BF16)
        identf = sb(top, "identf", [128, 128], F32)
        onesb = sb(top, "onesb", [128, 128], BF16)
        onesf = sb(top, "onesf", [128, 128], F32)
        epsc = sb(top, "epsc", [128, 1], F32)
        hT = sb(top, "hT", [128, 16, T], BF16)

        c.dma("sp", identf[:], ident_d, writes=["identf"])
        c.dma("pool", identb[:], ident_d, writes=["identb"])
        c.op("dve", lambda: nc.vector.memset(onesb[:], 1.0), writes=["onesb"])
        c.op("dve", lambda: nc.vector.memset(onesf[:], 1.0 / 128.0), writes=["onesf"])
        c.op("dve", lambda: nc.vector.memset(epsc[:], EPS), writes=["epsc"])

        def make_stream(es, blocks, ncols_max, nslots=2):
            bufs = [sb(es, f"wbuf{i}", [128, 16, ncols_max], BF16) for i in range(nslots)]
            st = {"issued": 0, "got": 0}

            def issue():
                i = st["issued"]
                if i >= len(blocks):
                    return
                w, c0, ncols = blocks[i]
                s = i % nslots
                c.dma("pool", bufs[s][:, :, 0:ncols], wview(w, c0, ncols), writes=[("wb", s)])
                st["issued"] += 1

            def get():
                i = st["got"]
                while st["issued"] <= i:
                    issue()
                st["got"] += 1
                if st["issued"] <= i + 1:
                    issue()
                s = i % nslots
                return bufs[s], ("wb", s)
            return get

        hkeys = lambda tb: [("hT", tt) for tt in range(tb * 4, tb * 4 + 4)]
        allh = [("hT", tt) for tt in range(NT)]

        def phase_mod_h(L):
            with ExitStack() as es:
                G = sb(es, "G", [128, 2, D], F32)
                SH = sb(es, "SH", [128, 2, D], F32)
                with ExitStack() as es2:
                    getw = make_stream(es2, [(w_mod[L], blk * 512, 512) for blk in range(12)], 512)
                    crs = sb(es2, "crs", [128, 2, 16], F32)
                    scl = sb(es2, "scl", [128, 2, 16], F32)
                    srep = sb(es2, "srep", [128, 2, 16, 128], BF16)
                    bb = [sb(es2, f"bb{i}", [128, 512], F32) for i in range(2)]
                    gb = [sb(es2, f"gb{i}", [128, 512], F32) for i in range(2)]
                    tmp = [sb(es2, f"mtmp{i}", [128, 512], F32) for i in range(2)]
                    pm = [ps(es2, f"pm{i}", [128, 512]) for i in range(2)]
                    c.dma("sp", crs[:], crows, writes=["crs"])
                    c.op("act", lambda: nc.scalar.activation(scl[:], crs[:], AF.Silu), ["crs"], ["scl"])
                    c.op("dve", lambda: nc.vector.tensor_copy(
                        srep[:], scl[:].unsqueeze(3).to_broadcast([128, 2, 16, 128])), ["scl"], ["srep"])
                    for blk in range(12):
                        kind, cb = divmod(blk, 4)
                        b = blk % 2
                        wb, wk = getw()
                        c.dma("sp", bb[b][:], b_mod[L][:, blk * 512:(blk + 1) * 512].partition_broadcast(128),
                              writes=[("bb", b)])
                        if kind == 1:
                            c.dma("sp", gb[b][:], norm_g[L][:, cb * 512:(cb + 1) * 512].partition_broadcast(128),
                                  writes=[("gb", b)])
                        cols = slice(cb * 512, (cb + 1) * 512)
                        for g in range(2):
                            mm_group(pm[g][:], [(srep[:, g, k, :], wb[:, k, 0:512]) for k in range(16)],
                                     [wk, "srep"], [("pm", g)])
                            if kind == 0:
                                c.op("dve", lambda g=g, b=b, cols=cols: nc.vector.tensor_tensor(
                                    out=SH[:, g, cols], in0=pm[g][:], in1=bb[b][:], op=ALU.add),
                                    [("pm", g), ("bb", b)], [("SH", g, cb)])
                            elif kind == 1:
                                c.op("dve", lambda g=g, b=b: nc.vector.tensor_tensor(
                                    out=tmp[g][:], in0=pm[g][:], in1=bb[b][:], op=ALU.add),
                                    [("pm", g), ("bb", b)], [("mtmp", g)])
                                c.op("dve", lambda g=g, b=b, cols=cols: nc.vector.scalar_tensor_tensor(
                                    out=G[:, g, cols], in0=tmp[g][:], scalar=1.0, in1=gb[b][:],
                                    op0=ALU.add, op1=ALU.mult),
                                    [("mtmp", g), ("gb", b)], [("G", g, cb)])
                            else:
                                c.op("dve", lambda g=g, b=b: nc.vector.tensor_tensor(
                                    out=tmp[g][:], in0=pm[g][:], in1=bb[b][:], op=ALU.add),
                                    [("pm", g), ("bb", b)], [("mtmp", g)])
                                c.dma("sp", gts[L * 2 + g:L * 2 + g + 1, cols], tmp[g][0:1, :],
                                      reads=[("mtmp", g)], writes=[("gts", L, g, cb)])
                    c.barrier_all()
                with ExitStack() as es2:
                    NS = 3
                    xt = [sb(es2, f"xt{i}", [128, D], F32) for i in range(NS)]
                    junk = sb(es2, "junk", [128, D], BF16)
                    st = sb(es2, "st", [128, 3 * NS], F32)
                    t1 = [sb(es2, f"t1_{i}", [128, D], F32) for i in range(NS)]
                    hb = [sb(es2, f"hb{i}", [128, D], BF16) for i in range(NS)]
                    pT = [ps(es2, f"pT{i}", [128, 8, 128], BF16) for i in range(2 * NS)]
                    GK = [[("G", g, cb) for cb in range(4)] for g in range(2)]
                    SK = [[("SH", g, cb) for cb in range(4)] for g in range(2)]

                    def h_task(b, tt):
                        g = 0 if tt < 4 else 1
                        rows = slice(tt * 128, (tt + 1) * 128)
                        ssq, std, rstd = st[:, 3 * b:3 * b + 1], st[:, 3 * b + 1:3 * b + 2], st[:, 3 * b + 2:3 * b + 3]
                        if L == 0:
                            c.dma("sp", xt[b][:], x[rows, :], writes=[("xt", b)])
                        else:
                            c.dma("sp", xt[b][:], y1[rows, :], reads=[("y1", tt, cb) for cb in range(4)],
                                  writes=[("xt", b)])
                        yield
                        c.op("act", lambda: nc.scalar.activation(junk[:], xt[b][:], AF.Square, accum_out=ssq),
                             [("xt", b)], ["junk", ("ssq", b)])
                        yield
                        c.op("act", lambda: nc.scalar.activation(std, ssq, AF.Sqrt, scale=1.0 / D, bias=epsc[:]),
                             [("ssq", b), "epsc"], [("std", b)])
                        yield
                        c.op("dve", lambda: nc.vector.reciprocal(rstd, std), [("std", b)], [("rstd", b)])
                        yield
                        c.op("dve", lambda: nc.vector.scalar_tensor_tensor(
                            out=t1[b][:], in0=xt[b][:], scalar=rstd, in1=G[:, g, :], op0=ALU.mult, op1=ALU.mult),
                            [("xt", b), ("rstd", b)] + GK[g], [("t1", b)])
                        yield
                        c.op("pool", lambda: nc.gpsimd.tensor_tensor(
                            out=hb[b][:, 0:1408], in0=t1[b][:, 0:1408], in1=SH[:, g, 0:1408], op=ALU.add),
                            [("t1", b)] + SK[g], [("hb", b, 0)])
                        c.op("dve", lambda: nc.vector.tensor_tensor(
                            out=hb[b][:, 1408:D], in0=t1[b][:, 1408:D], in1=SH[:, g, 1408:D], op=ALU.add),
                            [("t1", b)] + SK[g], [("hb", b, 1)])
                        yield
                        for half in range(2):
                            pp = pT[b * 2 + half]

                            def tr():
                                inst = None
                                for kk in range(8):
                                    k = half * 8 + kk
                                    inst = nc.tensor.transpose(pp[:, kk, :], hb[b][:, k * 128:(k + 1) * 128], identb[:])
                                return inst
                            c.op("pe", tr, [("hb", b, 0), ("hb", b, 1), "identb"], [("pT", b, half)])
                            yield
                            dst = hT[:, half * 8:(half + 1) * 8, tt * 128:(tt + 1) * 128]
                            if half == 0:
                                c.op("act", lambda: nc.scalar.copy(dst, pp[:]), [("pT", b, half)], [("hT", tt, half)])
                            else:
                                c.op("dve", lambda: nc.vector.tensor_copy(dst, pp[:]), [("pT", b, half)],
                                     [("hT", tt, half)])
                            yield

                    run_tasks([(lambda sl, tt=tt: h_task(sl, tt)) for tt in range(NT)], NS)
                    c.barrier_all()
                c.barrier_all()

        def run_tasks(factories, width):
            it = iter(factories)
            active = {}
            free = list(range(width))
            while True:
                while free:
                    f = next(it, None)
                    if f is None:
                        break
                    sl = free.pop(0)
                    active[sl] = f(sl)
                if not active:
                    break
                for sl in sorted(active):
                    try:
                        next(active[sl])
                    except StopIteration:
                        del active[sl]
                        free.append(sl)

        def phase_attn(L):
            nkv = 2 if L == 0 else 4
            G = 1 if L == 0 else 4
            if L == 0:
                kvc0, qc0, orow0 = 5120, 5120 + 512, 8
            else:
                kvc0, qc0, orow0 = 0, 1024, 0
            with ExitStack() as es:
                ablocks = []
                for j_ in range(nkv):
                    ablocks.append((w_in[L], kvc0 + j_ * 256, 256))
                    for h_ in range(4 * j_, 4 * j_ + 4):
                        ablocks.append((w_in[L], qc0 + h_ * 256, 256))
                getw = make_stream(es, ablocks, 256, nslots=3)

                T_sq = [sb(es, f"sq{i}", [128, 512], F32) for i in range(3)]
                T_sd = [sb(es, f"sd{i}", [128, 512], F32) for i in range(3)]
                T_kb = [sb(es, f"kb{i}", [128, 512], BF16) for i in range(3)]
                T_u = [sb(es, f"ru{i}", [128, 512], F32) for i in range(3)]
                cosT = sb(es, "cosT", [128, 2048], F32)
                sinT = sb(es, "sinT", [128, 2048], F32)
                RTb = sb(es, "RTb", [128, 128], BF16)
                gq = sb(es, "gq", [128, 1], F32)
                gk = sb(es, "gk", [128, 1], F32)
                esk = sb(es, "esk", [128, 16], F32)
                wb4 = sb(es, "wb4", [128, 2, 4, 128], BF16)
                KT = sb(es, "KT", [128, T], BF16)
                KcT = sb(es, "KcT", [128, 256], BF16)
                V = sb(es, "V", [128, NT, 128], BF16)
                Vc = sb(es, "Vc", [128, 2, 128], BF16)
                QTg = sb(es, "QTg", [128, G, T], BF16)
                oThg = sb(es, "oThg", [128, G, T], BF16)
                kout = [sb(es, f"kout{i}", [128, 128], F32) for i in range(2)]
                vout = [sb(es, f"vout{i}", [128, 128], F32) for i in range(3)]
                PT = [sb(es, f"PT{i}", [128, 512], BF16) for i in range(4)]
                rl = [sb(es, f"rl{i}", [128, 512], F32) for i in range(2)]
                pj = ps(es, "pj", [128, 512])
                pms = ps(es, "pms", [128, 512])
                pS = [ps(es, f"pS{i}", [128, 512]) for i in range(2)]
                pO = [ps(es, f"pO{i}", [128, 512]) for i in range(2)]
                pL = [ps(es, f"pL{i}", [128, 512]) for i in range(2)]
                pbank = [(pj, "pj"), (pS[0], ("pS", 0)), (pS[1], ("pS", 1))]
                rbank = pbank
                tbank = (pO[0], ("pO", 0))
                msbank = [(pms, "pms"), (pL[0], ("pL", 0)), (pL[1], ("pL", 1))]
                NW = 3

                c.dma("sp", cosT[:], cosT_d, writes=["consts"])
                c.dma("sp", sinT[:], sinT_d, writes=["consts"])
                c.dma("pool", RTb[:], RT_d, writes=["consts"])
                c.dma("sp", gq[:], qn_d[L], writes=["consts"])
                c.dma("sp", gk[:], kn_d[L], writes=["consts"])
                c.dma("pool", wb4[:], wbias4_d, writes=["consts"])
                c.dma("sp", esk[:], sink_d.partition_broadcast(128), writes=["esk0"])
                c.op("act", lambda: nc.scalar.activation(esk[:], esk[:], AF.Exp), ["esk0"], ["esk0", "consts"])

                def fn_task(s, pjt, pjk, gcol, rope_cols, out_bf, outkey, out_f32=None, f32key=None):
                    sq, sd, kb, uu_ = T_sq[s], T_sd[s], T_kb[s], T_u[s]
                    pmt, pmk = msbank[s]
                    c.op("act", lambda: nc.scalar.activation(sq[:], pjt, AF.Square), [pjk], [("sq", s)])
                    yield
                    c.op("pe", lambda: nc.tensor.matmul(pmt[:], lhsT=onesf[:], rhs=sq[:], start=True, stop=True),
                         [("sq", s), "onesf"], [pmk])
                    yield
                    c.op("act", lambda: nc.scalar.activation(sd[:], pmt[:], AF.Ln, bias=epsc[:]),
                         [pmk, "epsc"], [("sd", s)])
                    yield
                    c.op("act", lambda: nc.scalar.activation(sd[:], sd[:], AF.Exp, scale=-0.5), [("sd", s)], [("sd", s)])
                    yield
                    if out_f32 is None:
                        dst, dkey = sq[:], ("sq", s)
                    else:
                        dst, dkey = out_f32, (f32key or outkey + ("f32",))
                    c.op("dve", lambda: nc.vector.scalar_tensor_tensor(
                        out=dst, in0=pjt, scalar=gcol, in1=sd[:], op0=ALU.mult, op1=ALU.mult),
                        [pjk, ("sd", s), "consts"], [dkey])
                    yield
                    if rope_cols is None:
                        c.op("pool", lambda: nc.gpsimd.tensor_copy(out_bf, dst), [dkey], [outkey])
                        yield
                    else:
                        c.op("pool", lambda: nc.gpsimd.tensor_copy(kb[:], dst), [dkey], [("kb", s)])
                        yield
                        prt, prk = rbank[s]
                        c.op("pe", lambda: nc.tensor.matmul(prt[:], lhsT=RTb[:], rhs=kb[:], start=True, stop=True),
                             [("kb", s), "consts"], [prk])
                        yield
                        c.op("pool", lambda: nc.gpsimd.tensor_tensor(out=dst, in0=dst, in1=cosT[:, rope_cols],
                                                                     op=ALU.mult), [dkey, "consts"], [dkey])
                        yield
                        c.op("dve", lambda: nc.vector.tensor_tensor(out=uu_[:], in0=prt[:], in1=sinT[:, rope_cols],
                                                                    op=ALU.mult), [prk, "consts"], [("ru", s)])
                        yield
                        c.op("pool", lambda: nc.gpsimd.tensor_tensor(out=out_bf, in0=dst, in1=uu_[:], op=ALU.add),
                             [dkey, ("ru", s)], [outkey])
                        yield

                def rope_of(tb):
                    return None if tb == 0 else slice((tb - 1) * 512, tb * 512)

                def k_task(sl, wb, wk, j, tb):
                    cols = slice(tb * 512, (tb + 1) * 512)
                    pjt, pjk = pbank[sl]
                    mm_group(pjt[:], [(wb[:, k, 0:128], hT[:, k, cols]) for k in range(16)], [wk], [pjk])
                    yield
                    if tb == 0:
                        kf32 = T_u[sl]
                        yield from fn_task(sl, pjt[:], pjk, gk[:, 0:1], None, KT[:, cols], ("KT", tb), out_f32=kf32[:],
                                           f32key=("ru", sl))
                        for t4 in range(4):
                            b = t4 % 2
                            pvt, pvk = tbank
                            c.op("pe", lambda: nc.tensor.transpose(
                                pvt[:, 0:128], kf32[:, t4 * 128:(t4 + 1) * 128], identf[:]),
                                [("ru", sl), "identf"], [pvk])
                            yield
                            c.op("act", lambda: nc.scalar.copy(kout[b][:], pvt[:, 0:128]), [pvk], [("kout", b)])
                            yield
                            c.dma("sp", nk[L][t4 * 128:(t4 + 1) * 128, j, :], kout[b][:],
                                  reads=[("kout", b)], writes=[("nk", t4, j)])
                    else:
                        yield from fn_task(sl, pjt[:], pjk, gk[:, 0:1], rope_of(tb), KT[:, cols], ("KT", tb))

                def v_task(sl, wb, wk, j, tt):
                    tcols = slice(tt * 128, (tt + 1) * 128)
                    pvt, pvk = pbank[sl]
                    mm_group(pvt[:, 0:128], [(hT[:, k, tcols], wb[:, k, 128:256]) for k in range(16)], [wk], [pvk])
                    yield
                    if tt < 4:
                        b = sl
                        c.op("act", lambda: nc.scalar.copy(vout[b][:], pvt[:, 0:128]), [pvk], [("vout", b)])
                        yield
                        c.op("pool", lambda: nc.gpsimd.tensor_copy(V[:, tt, :], vout[b][:]), [("vout", b)], [("V", tt)])
                        c.dma("sp", nv[L][tt * 128:(tt + 1) * 128, j, :], vout[b][:],
                              reads=[("vout", b)], writes=[("nv", tt, j)])
                        yield
                    else:
                        c.op("act", lambda: nc.scalar.copy(V[:, tt, :], pvt[:, 0:128]), [pvk], [("V", tt)])
                        yield

                def q_task(sl, wb, wk, hh, tb):
                    cols = slice(tb * 512, (tb + 1) * 512)
                    pjt, pjk = pbank[sl]
                    mm_group(pjt[:], [(wb[:, k, 0:128], hT[:, k, cols]) for k in range(16)], [wk], [pjk])
                    yield
                    yield from fn_task(sl, pjt[:], pjk, gq[:, 0:1], rope_of(tb), QTg[:, hh, cols], ("QT", hh, tb))

                def g_task(sl, wb, wk, hh, tb):
                    cols = slice(tb * 512, (tb + 1) * 512)
                    pjt, pjk = pbank[sl]
                    mm_group(pjt[:], [(wb[:, k, 128:256], hT[:, k, cols]) for k in range(16)], [wk], [pjk])
                    yield
                    c.op("act", lambda: nc.scalar.activation(oThg[:, hh, cols], pjt[:], AF.Silu),
                         [pjk], [("oTh", hh, tb)])
                    yield

                sctr = [0, 0]
                sbanks = [[(pS[0], ("pS", 0)), (pS[1], ("pS", 1))], [(pj, "pj"), (pms, "pms")]]

                def attn_block(ob, j, q0, nq, keys, tbq):
                    qcols = slice(q0, q0 + nq)
                    N = G * nq
                    nk_ = len(keys)
                    rhsQ = QTg[:, :, qcols] if G > 1 else QTg[:, 0, qcols]
                    qkeys = [("QT", hh, tbq) for hh in range(G)]
                    slots = []

                    def smm(ki):
                        kind, idx, mi = keys[ki]
                        p = sctr[ob] % 2
                        sctr[ob] += 1
                        slots.append(p)
                        pSt, pSk = sbanks[ob][p]
                        if kind == "l":
                            Kl = KT[:, idx * 128:(idx + 1) * 128]
                            kr = [("KT", idx // 4)]
                        else:
                            Kl = KcT[:, idx * 128:(idx + 1) * 128]
                            kr = ["KcT"]

                        def f():
                            out = pSt[:, :N] if G == 1 else pSt[:, :N].rearrange("p (g q) -> p g q", g=G)
                            inst = nc.tensor.matmul(out, lhsT=Kl, rhs=rhsQ, start=True, stop=(mi is None))
                            if mi is not None:
                                inst = nc.tensor.matmul(out, lhsT=identb[:], rhs=wb4[:, mi, :, 0:nq],
                                                        start=False, stop=True)
                            return inst
                        c.op("pe", f, kr + qkeys + ["consts", "identb"], [pSk])

                    def pv(ki):
                        kind, idx, mi = keys[ki]
                        p = slots[ki]
                        if kind == "l":
                            Vl = V[:, idx, :]
                            kr = [("V", idx)]
                        else:
                            Vl = Vc[:, idx, :]
                            kr = ["Vc"]
                        pSt, pSk = sbanks[ob][p]
                        PTt = PT[ob * 2 + p]
                        c.op("act", lambda: nc.scalar.activation(PTt[:, :N], pSt[:, :N], AF.Exp, scale=SCALE),
                             [pSk], [("PT", ob, p)])

                        def f():
                            nc.tensor.matmul(pO[ob][:, :N], lhsT=Vl, rhs=PTt[:, :N],
                                             start=(ki == 0), stop=(ki == nk_ - 1))
                            return nc.tensor.matmul(pL[ob][:, :N], lhsT=onesb[:], rhs=PTt[:, :N],
                                                    start=(ki == 0), stop=(ki == nk_ - 1))
                        c.op("pe", f, kr + [("PT", ob, p), "onesb"], [("pO", ob), ("pL", ob)])

                    smm(0)
                    yield
                    if nk_ > 1:
                        smm(1)
                        yield
                    for ki in range(nk_):
                        pv(ki)
                        if ki + 2 < nk_:
                            smm(ki + 2)
                        yield
                    r = rl[ob]
                    if L == 1:
                        r3 = r[:, :N].rearrange("p (g q) -> p g q", g=G)
                        l3 = pL[ob][:, :N].rearrange("p (g q) -> p g q", g=G)
                        c.op("dve", lambda: nc.vector.tensor_tensor(
                            out=r3, in0=l3, in1=esk[:, 4 * j:4 * j + 4].unsqueeze(2).to_broadcast([128, G, nq]),
                            op=ALU.add), [("pL", ob), "consts"], [("rl", ob)])
                        c.op("act", lambda: nc.scalar.activation(r[:, :N], r[:, :N], AF.Ln), [("rl", ob)], [("rl", ob)])
                    else:
                        c.op("act", lambda: nc.scalar.activation(r[:, :N], pL[ob][:, :N], AF.Ln), [("pL", ob)], [("rl", ob)])
                    c.op("act", lambda: nc.scalar.activation(r[:, :N], r[:, :N], AF.Exp, scale=-1.0),
                         [("rl", ob)], [("rl", ob)])
                    c.op("dve", lambda: nc.vector.tensor_tensor(out=r[:, :N], in0=pO[ob][:, :N], in1=r[:, :N],
                                                                op=ALU.mult), [("pO", ob), ("rl", ob)], [("rl", ob)])
                    okeys = [("oTh", hh, tbq) for hh in range(G)]
                    if G > 1:
                        o3 = oThg[:, :, qcols]
                        r3 = r[:, :N].rearrange("p (g q) -> p g q", g=G)
                    else:
                        o3 = oThg[:, 0, qcols]
                        r3 = r[:, :N]
                    c.op("pool", lambda: nc.gpsimd.tensor_tensor(out=o3, in0=r3, in1=o3, op=ALU.mult),
                         [("rl", ob)] + okeys, okeys)
                    yield

                for j in range(nkv):
                    wb, wk = getw()
                    c.dma("pool", KcT[:], ckT[L][:, j, :], writes=["KcT"])
                    c.dma("pool", Vc[:], cv[L][:, j, :].rearrange("(t p) d -> p t d", p=128), writes=["Vc"])
                    gens = [(lambda sl, tb=tb: k_task(sl, wb, wk, j, tb)) for tb in range(NTB)] + \
                           [(lambda sl, tt=tt: v_task(sl, wb, wk, j, tt)) for tt in range(NT)]
                    run_tasks(gens, NW)
                    for h0 in range(4 * j, 4 * j + 4, G):
                        gens = []
                        for hh in range(G):
                            wbq, wkq = getw()
                            for tb in range(NTB):
                                gens.append(lambda sl, wbq=wbq, wkq=wkq, hh=hh, tb=tb: g_task(sl, wbq, wkq, hh, tb))
                            for tb in range(NTB):
                                gens.append(lambda sl, wbq=wbq, wkq=wkq, hh=hh, tb=tb: q_task(sl, wbq, wkq, hh, tb))
                            if hh % 2 == 1 or G == 1:
                                run_tasks(gens, NW)
                                gens = []
                        blocks = []
                        if G == 1:
                            for (t0, n) in SEQS[:2]:
                                blocks.append((t0 * 128, 256, [("l", t0, None), ("l", t0 + 1, None)], 0))
                            for tb in range(1, NTB):
                                keys = [("l", kt, None) for kt in range(4, NT)] + [("c", 0, None), ("c", 1, None)]
                                blocks.append((tb * 512, 512, keys, tb))
                        else:
                            for (t0, n) in SEQS[:2]:
                                for tq in range(t0, t0 + n):
                                    blocks.append((tq * 128, 128, [("l", t0, None), ("l", t0 + 1, None)], 0))
                            for i in range(16):
                                keys = []
                                if i > 0:
                                    keys.append(("l", 4 + i - 1, 0))
                                keys.append(("l", 4 + i, None))
                                if i < 15:
                                    keys.append(("l", 4 + i + 1, 1))
                                keys += [("c", 0, None), ("c", 1, None)]
                                blocks.append(((4 + i) * 128, 128, keys, (4 + i) // 4))
                        run_tasks([(lambda sl, blk=blk: attn_block(sl, j, *blk)) for blk in blocks], 2)
                        for hh in range(G):
                            r0 = (orow0 + h0 + hh) * 128
                            c.dma("sp", oTs[L][r0:r0 + 128, :], oThg[:, hh, :],
                                  reads=[("oTh", hh, tb) for tb in range(NTB)], writes=[("oTs", L, orow0 + h0 + hh)])
                c.barrier_all()

        def phase_hgrn():
            with ExitStack() as es:
                getw = make_stream(es, [(w_in[0], h * 640, 640) for h in range(8)], 640)
                maskF = sb(es, "maskF", [128, 512], F32)
                mfb = sb(es, "mfb", [128, 2, 128], F32)
                cm4 = sb(es, "cm4", [128, 4, 128], BF16)
                onc = sb(es, "onc", [128, 1], F32)
                lbe = sb(es, "lbe", [128, 3, 16], F32)
                lbs = sb(es, "lbs", [128, 16], F32)
                lbv = sb(es, "lbv", [128, 16], F32)
                oml = sb(es, "oml", [128, 16], F32)
                noml = sb(es, "noml", [128, 16], F32)
                q32 = [sb(es, f"q32_{i}", [128, 512], F32) for i in range(2)]
                sg = [sb(es, f"sg_{i}", [128, 512], F32) for i in range(2)]
                lg = [sb(es, f"lg_{i}", [128, 512], F32) for i in range(2)]
                k32 = [sb(es, f"k32_{i}", [128, 512], F32) for i in range(2)]
                bF = [sb(es, f"bF_{i}", [128, 512], F32) for i in range(2)]
                totc = [sb(es, f"totc_{i}", [128, 16], F32) for i in range(2)]
                dec = sb(es, "dec", [128, 2, 80], F32)
                qd = [sb(es, f"qd{d}", [128, T], BF16) for d in range(2)]
                ki = [sb(es, f"ki{d}", [128, T], BF16) for d in range(2)]
                keT = [sb(es, f"keT{d}", [128, T], BF16) for d in range(2)]
                sgT = sb(es, "sgT", [128, T], BF16)
                V = sb(es, "Va", [128, NT, 128], BF16)
                OT = sb(es, "OT", [128, T], F32)
                kend = [[sb(es, f"kend{d}{r}", [128, 128], BF16) for r in range(2)] for d in range(2)]
                Vm = [[sb(es, f"Vm{d}{r}", [128, 4, 128], BF16) for r in range(2)] for d in range(2)]
                Am = [[sb(es, f"Am{d}{r}", [128, 128], BF16) for r in range(2)] for d in range(2)]
                S32 = [[sb(es, f"S32_{d}{r}", [128, 128], F32) for r in range(2)] for d in range(2)]
                Sbf = [[sb(es, f"Sbf{d}{p}", [128, 128], BF16) for p in range(4)] for d in range(2)]
                fb = [ps(es, f"hfb{i}", [128, 512]) for i in range(2)]
                bA = [ps(es, f"hbA{d}", [128, 512]) for d in range(2)]
                pO = [ps(es, f"hpO{d}", [128, 512]) for d in range(2)]
                ptr = [ps(es, f"hptr{d}", [128, 8, 128], BF16) for d in range(2)]

                c.dma("sp", maskF[:], maskF_d, writes=["hc"])
                c.dma("sp", mfb[:], mfb_d, writes=["hc"])
                c.dma("pool", cm4[:], cm4_d, writes=["hc"])
                c.dma("sp", onc[:], onorm_d, writes=["hc"])
                c.dma("sp", lbe[:], lbg, writes=["lbe"])
                c.op("act", lambda: nc.scalar.activation(lbe[:], lbe[:], AF.Exp), ["lbe"], ["lbe"])
                c.op("dve", lambda: nc.vector.tensor_tensor(out=lbs[:], in0=lbe[:, 0, :], in1=lbe[:, 1, :], op=ALU.add),
                     ["lbe"], ["lbs"])
                c.op("dve", lambda: nc.vector.tensor_tensor(out=lbs[:], in0=lbs[:], in1=lbe[:, 2, :], op=ALU.add),
                     ["lbe", "lbs"], ["lbs"])
                c.op("dve", lambda: nc.vector.reciprocal(lbs[:], lbs[:]), ["lbs"], ["lbs"])
                c.op("dve", lambda: nc.vector.tensor_tensor(out=lbv[:], in0=lbe[:, 0, :], in1=lbs[:], op=ALU.mult),
                     ["lbe", "lbs"], ["lbv"])
                c.op("dve", lambda: nc.vector.tensor_scalar(out=oml[:], in0=lbv[:], scalar1=-1.0, scalar2=1.0,
                                                            op0=ALU.mult, op1=ALU.add), ["lbv"], ["oml"])
                c.op("dve", lambda: nc.vector.tensor_scalar(out=noml[:], in0=oml[:], scalar1=-1.0, scalar2=None,
                                                            op0=ALU.mult), ["oml"], ["noml", "hc"])

                def bc32(t, ncol=16):
                    return t.unsqueeze(2).to_broadcast([128, ncol, 32])

                def f_task(sl, h, wb, wk, tb):
                    cols = slice(tb * 512, (tb + 1) * 512)
                    pjt, pjk = fb[sl], ("fb", sl)
                    Q, SG, LG, K, B, TC = q32[sl], sg[sl], lg[sl], k32[sl], bF[sl], totc[sl]
                    kq, ks, kl, kk, kb_, kt = ("q32", sl), ("sg", sl), ("lg", sl), ("k32", sl), ("bF", sl), ("totc", sl)
                    mm_group(pjt[:], [(wb[:, k, 0:128], hT[:, k, cols]) for k in range(16)], [wk], [pjk])
                    yield
                    c.op("act", lambda: nc.scalar.activation(Q[:], pjt[:], AF.Silu), [pjk], [kq])
                    yield
                    for d in range(2):
                        i = d * 8 + h
                        mm_group(pjt[:], [(wb[:, k, 128 * (1 + d):128 * (2 + d)], hT[:, k, cols]) for k in range(16)],
                                 [wk], [pjk])
                        yield
                        c.op("act", lambda: nc.scalar.activation(SG[:], pjt[:], AF.Sigmoid), [pjk], [ks])
                        yield
                        c.op("act", lambda: nc.scalar.activation(LG[:], SG[:], AF.Ln, scale=oml[:, i:i + 1],
                                                                 bias=lbv[:, i:i + 1]), [ks, "hc"], [kl])
                        c.op("dve", lambda: nc.vector.tensor_scalar(
                            out=K[:], in0=SG[:], scalar1=noml[:, i:i + 1], scalar2=oml[:, i:i + 1],
                            op0=ALU.mult, op1=ALU.add), [ks, "hc"], [kk])
                        yield
                        c.op("dve", lambda: nc.vector.tensor_tensor_scan(B[:], maskF[:], LG[:], 0.0, ALU.mult, ALU.add),
                             [kl, "hc"], [kb_])
                        yield
                        tot = B[:].rearrange("p (c t) -> p c t", t=32)[:, :, 31]
                        dslice = dec[:, d, tb * 16:(tb + 1) * 16]
                        c.op("act", lambda: nc.scalar.activation(dslice, tot, AF.Exp), [kb_], [("dec", d, tb)])
                        if d == 1:
                            c.op("act", lambda: nc.scalar.copy(TC[:], tot), [kb_], [kt])
                            yield
                            b3 = B[:].rearrange("p (c t) -> p c t", t=32)
                            c.op("dve", lambda: nc.vector.tensor_tensor(out=b3, in0=bc32(TC[:]), in1=b3, op=ALU.subtract),
                                 [kb_, kt], [kb_])
                            yield
                            c.op("pool", lambda: nc.gpsimd.tensor_tensor(out=B[:], in0=B[:], in1=LG[:], op=ALU.add),
                                 [kb_, kl], [kb_])
                        yield
                        c.op("act", lambda: nc.scalar.activation(SG[:], B[:], AF.Exp), [kb_], [ks])
                        c.op("act", lambda: nc.scalar.activation(LG[:], B[:], AF.Exp, scale=-1.0), [kb_], [kl])
                        yield
                        c.op("dve", lambda: nc.vector.tensor_tensor(out=qd[d][:, cols], in0=Q[:], in1=SG[:], op=ALU.mult),
                             [kq, ks], [("qd", d, tb)])
                        yield
                        c.op("dve", lambda: nc.vector.tensor_tensor(out=LG[:], in0=K[:], in1=LG[:], op=ALU.mult),
                             [kk, kl], [kl])
                        yield
                        c.op("pool", lambda: nc.gpsimd.tensor_copy(ki[d][:, cols], LG[:]), [kl], [("ki", d, tb)])
                        k3 = LG[:].rearrange("p (c t) -> p c t", t=32)
                        o3 = keT[d][:, cols].rearrange("p (c t) -> p c t", t=32)
                        c.op("pool", lambda: nc.gpsimd.tensor_tensor(out=o3, in0=k3, in1=bc32(dslice), op=ALU.mult),
                             [kl, ("dec", d, tb)], [("keT", d, tb)])
                        yield
                    mm_group(pjt[:], [(wb[:, k, 512:640], hT[:, k, cols]) for k in range(16)], [wk], [pjk])
                    yield
                    c.op("act", lambda: nc.scalar.activation(sgT[:, cols], pjt[:], AF.Silu), [pjk], [("sgT", tb)])
                    yield
                    for tt in range(tb * 4, tb * 4 + 4):
                        tcols = slice(tt * 128, (tt + 1) * 128)
                        mm_group(pjt[:, 0:128], [(hT[:, k, tcols], wb[:, k, 384:512]) for k in range(16)], [wk], [pjk])
                        yield
                        c.op("act", lambda: nc.scalar.copy(V[:, tt, :], pjt[:, 0:128]), [pjk], [("V", tt)])
                        yield

                ot_written = set()
                vctr = [0, 0]
                sctr = [0, 0]

                def sweep(d, h, si, t0, n):
                    cur = sctr[d] % 2
                    if si < 2:
                        c.op("dve", lambda: nc.vector.memset(S32[d][cur][:], 0.0), [], [("S32", d, cur)])
                    else:
                        c.dma("sp", S32[d][cur][:], st0[d, h], writes=[("S32", d, cur)])
                    sb0 = sctr[d] % 4
                    c.op("act", lambda: nc.scalar.copy(Sbf[d][sb0][:], S32[d][cur][:]),
                         [("S32", d, cur)], [("Sbf", d, sb0)])
                    yield
                    tiles = range(t0, t0 + n) if d == 0 else range(t0 + n - 1, t0 - 1, -1)
                    for tt in tiles:
                        tb = tt // 4
                        tcols = slice(tt * 128, (tt + 1) * 128)
                        r = vctr[d] % 2
                        vctr[d] += 1
                        c.op("pe", lambda: nc.tensor.transpose(ptr[d][:, 0, :], keT[d][:, tcols], identb[:]),
                             [("keT", d, tb), "identb"], [("ptr", d)])
                        c.op("pool", lambda: nc.gpsimd.tensor_tensor(
                            out=Vm[d][r][:], in0=V[:, tt, :].unsqueeze(1).to_broadcast([128, 4, 128]), in1=cm4[:],
                            op=ALU.mult), [("V", tt), "hc"], [("Vm", d, r)])
                        yield
                        c.op("act", lambda: nc.scalar.copy(kend[d][r][:], ptr[d][:, 0, :]), [("ptr", d)], [("kend", d, r)])
                        c.op("pe", lambda: nc.tensor.matmul(bA[d][:, 0:128], lhsT=ki[d][:, tcols], rhs=qd[d][:, tcols],
                                                            start=True, stop=True),
                             [("ki", d, tb), ("qd", d, tb)], [("bA", d)])
                        yield
                        c.op("dve", lambda: nc.vector.tensor_tensor(out=Am[d][r][:], in0=bA[d][:, 0:128], in1=mfb[:, d, :],
                                                                    op=ALU.mult), [("bA", d), "hc"], [("Am", d, r)])

                        def umm():
                            inst = None
                            for j in range(4):
                                inst = nc.tensor.matmul(fb[d][:, j * 128:(j + 1) * 128], lhsT=kend[d][r][:],
                                                        rhs=Vm[d][r][:, j, :], start=True, stop=True)
                            return inst
                        c.op("pe", umm, [("kend", d, r), ("Vm", d, r)], [("fb", d)])
                        yield
                        c.op("pe", lambda: nc.tensor.matmul(pO[d][:, 0:128], lhsT=V[:, tt, :], rhs=Am[d][r][:],
                                                            start=True, stop=False),
                             [("V", tt), ("Am", d, r)], [("pO", d)])
                        yield
                        order = range(4) if d == 0 else range(3, -1, -1)
                        for n_, j in enumerate(order):
                            cur = sctr[d] % 2
                            nxt = 1 - cur
                            sbc = sctr[d] % 4
                            sbn = (sctr[d] + 1) % 4
                            ccols = slice(tt * 128 + 32 * j, tt * 128 + 32 * j + 32)
                            gch = tt * 4 + j
                            c.op("pe", lambda: nc.tensor.matmul(
                                pO[d][:, 32 * j:32 * j + 32], lhsT=Sbf[d][sbc][:], rhs=qd[d][:, ccols],
                                start=False, stop=(n_ == 3)),
                                [("Sbf", d, sbc), ("qd", d, tb)], [("pO", d)])
                            c.op("dve", lambda: nc.vector.scalar_tensor_tensor(
                                out=S32[d][nxt][:], in0=S32[d][cur][:], scalar=dec[:, d, gch:gch + 1],
                                in1=fb[d][:, j * 128:(j + 1) * 128], op0=ALU.mult, op1=ALU.add),
                                [("S32", d, cur), ("dec", d, tb), ("fb", d)], [("S32", d, nxt)])
                            yield
                            c.op("act", lambda: nc.scalar.copy(Sbf[d][sbn][:], S32[d][nxt][:]),
                                 [("S32", d, nxt)], [("Sbf", d, sbn)])
                            sctr[d] += 1
                            yield
                        if tt not in ot_written:
                            ot_written.add(tt)
                            c.op("act", lambda: nc.scalar.copy(OT[:, tcols], pO[d][:, 0:128]), [("pO", d)], [("OT", tt)])
                        else:
                            c.op("dve", lambda: nc.vector.tensor_tensor(out=OT[:, tcols], in0=pO[d][:, 0:128],
                                                                        in1=OT[:, tcols], op=ALU.add),
                                 [("pO", d), ("OT", tt)], [("OT", tt)])
                        yield
                    if si < 2:
                        fin = sctr[d] % 2
                        c.dma("sp", nstate[si, d, h], S32[d][fin][:], reads=[("S32", d, fin)],
                              writes=[("nstate", si, d, h)])

                def n_task(sl, h, tb):
                    cols = slice(tb * 512, (tb + 1) * 512)
                    ok = [("OT", tt) for tt in range(tb * 4, tb * 4 + 4)]
                    SQ, SD = q32[sl], sg[sl]
                    kq, ks = ("q32", sl), ("sg", sl)
                    pjt, pjk = fb[sl], ("fb", sl)
                    c.op("act", lambda: nc.scalar.activation(SQ[:], OT[:, cols], AF.Square), ok, [kq])
                    yield
                    c.op("pe", lambda: nc.tensor.matmul(pjt[:], lhsT=onesf[:], rhs=SQ[:], start=True, stop=True),
                         [kq, "onesf"], [pjk])
                    yield
                    c.op("act", lambda: nc.scalar.activation(SD[:], pjt[:], AF.Ln, bias=epsc[:]), [pjk, "epsc"], [ks])
                    yield
                    c.op("act", lambda: nc.scalar.activation(SD[:], SD[:], AF.Exp, scale=-0.5), [ks], [ks])
                    yield
                    c.op("dve", lambda: nc.vector.scalar_tensor_tensor(
                        out=SQ[:], in0=OT[:, cols], scalar=onc[:, 0:1], in1=SD[:], op0=ALU.mult, op1=ALU.mult),
                        ok + [ks, "hc"], [kq])
                    yield
                    ostv = k32[sl][:].bitcast(BF16)[:, 0:512]
                    c.op("pool", lambda: nc.gpsimd.tensor_tensor(out=ostv, in0=SQ[:], in1=sgT[:, cols], op=ALU.mult),
                         [kq, ("sgT", tb)], [("k32", sl)])
                    c.dma("sp", oTs[0][h * 128:(h + 1) * 128, cols], ostv, reads=[("k32", sl)],
                          writes=[("oTs", 0, h)])
                    yield

                HG = os.environ.get("MK_HG", "")
                for h in range(8):
                    wb, wk = getw()
                    if "nof" not in HG:
                        run_tasks([(lambda sl, tb=tb: f_task(sl, h, wb, wk, tb)) for tb in range(NTB)], 2)
                    ot_written.clear()
                    if "noscan" not in HG:
                        for si, (t0, n) in enumerate(SEQS):
                            run_tasks([(lambda sl, d=d: sweep(d, h, si, t0, n)) for d in range(2)], 2)
                    if "nopost" not in HG:
                        run_tasks([(lambda sl, tb=tb: n_task(sl, h, tb)) for tb in range(NTB)], 2)
                c.barrier_all()

        def phase_out(L):
            NB = 4
            with ExitStack() as es:
                getw = make_stream(es, [(w_out[L], cb * 512, 512) for cb in range(4)], 512)
                gtb = sb(es, "gtb", [128, 2, D], F32)
                xr = [sb(es, f"xr{i}", [128, 512], F32) for i in range(NB)]
                tm = [sb(es, f"otm{i}", [128, 512], F32) for i in range(NB)]
                po = [ps(es, f"po{i}", [128, 512]) for i in range(2)]
                for g in range(2):
                    c.dma("act", gtb[:, g, :], gts[L * 2 + g:L * 2 + g + 1, :].partition_broadcast(128),
                          reads=[("gts", L, g, cb) for cb in range(4)], writes=[("gtb", g)])
                for k in range(16):
                    c.dma("sp" if k % 2 == 0 else "act", hT[:, k, :], oTs[L][k * 128:(k + 1) * 128, :],
                          reads=[("oTs", L, k)], writes=[("hTk", k)])
                it = 0
                for cb in range(4):
                    ccols = slice(cb * 512, (cb + 1) * 512)
                    wb, wk = getw()
                    for tt in range(NT):
                        b = it % NB
                        pb = it % 2
                        it += 1
                        g = 0 if tt < 4 else 1
                        rows = slice(tt * 128, (tt + 1) * 128)
                        tcols = slice(tt * 128, (tt + 1) * 128)
                        if L == 0:
                            c.dma("sp", xr[b][:], x[rows, ccols], writes=[("xr", b)])
                        else:
                            c.dma("sp", xr[b][:], y1[rows, ccols], reads=[("y1", tt, cb)], writes=[("xr", b)])
                        mm_group(po[pb][:], [(hT[:, k, tcols], wb[:, k, 0:512]) for k in range(16)],
                                 [wk] + [("hTk", k) for k in range(16)], [("po", pb)])
                        c.op("dve", lambda: nc.vector.tensor_tensor(
                            out=tm[b][:], in0=po[pb][:], in1=gtb[:, g, ccols], op=ALU.mult),
                            [("po", pb), ("gtb", g)], [("otm", b)])
                        c.op("pool", lambda: nc.gpsimd.tensor_tensor(out=tm[b][:], in0=tm[b][:], in1=xr[b][:],
                                                                     op=ALU.add),
                             [("otm", b), ("xr", b)], [("otm", b)])
                        if L == 0:
                            c.dma("act", y1[rows, ccols], tm[b][:], reads=[("otm", b)], writes=[("y1", tt, cb)])
                        else:
                            c.dma("act", y[rows, ccols], tm[b][:], reads=[("otm", b)], writes=[("y", tt, cb)])
                c.barrier_all()

        plist = [("mod0", lambda: phase_mod_h(0)), ("hgrn", phase_hgrn), ("attn0", lambda: phase_attn(0)),
                 ("out0", lambda: phase_out(0)), ("mod1", lambda: phase_mod_h(1)), ("attn1", lambda: phase_attn(1)),
                 ("out1", lambda: phase_out(1))]
        for nm, fn in plist:
            if phases is None or nm in phases:
                fn()
        c.finish()
    return nc


def _consts():
    ident = np.eye(128, dtype=np.float32)
    maskF = np.ones((128, 512), np.float32)
    maskF[:, ::32] = 0.0
    s = np.arange(128)[:, None]
    t = np.arange(128)[None, :]
    same = (s // 32) == (t // 32)
    mfb = np.stack([(same & (s <= t)), (same & (s >= t))], axis=1).astype(np.float32)
    cm4 = np.zeros((128, 4, 128), np.float32)
    for j in range(4):
        cm4[32 * j:32 * j + 32, j, :] = 1.0
    R = np.zeros((128, 128), np.float32)
    for m in range(128):
        q = m // 32
        if q in (0, 2):
            R[m, m + 32] = -1.0
        else:
            R[m, m - 32] = 1.0
    RT = np.ascontiguousarray(R.T)
    n_tok = 2048
    row = (np.arange(n_tok) // 64).astype(np.float32)
    col = (np.arange(n_tok) % 64).astype(np.float32)
    inv = (10000.0 ** (-np.arange(32, dtype=np.float32) / 32)).astype(np.float32)
    ar = row[:, None] * inv
    ac = col[:, None] * inv
    ang = np.concatenate([ar, ar, ac, ac], axis=-1).astype(np.float32)
    cosT = np.ascontiguousarray(np.cos(ang).T.astype(np.float32))
    sinT = np.ascontiguousarray(np.sin(ang).T.astype(np.float32))
    b = np.arange(128)[:, None]
    a = np.arange(128)[None, :]
    NEG = -30000.0
    wbias = np.stack([np.where(b >= a, 0.0, NEG), np.where(b <= a, 0.0, NEG)], axis=1).astype(np.float32)
    wbias4 = np.ascontiguousarray(np.broadcast_to(wbias[:, :, None, :], (128, 2, 4, 128))).astype(np.float32)
    return dict(ident=ident, maskF=maskF, mfb=mfb, cm4=cm4, RT=RT, cosT=cosT, sinT=sinT, wbias4=wbias4)


def _perm_w0(w):
    cols = []
    for h in range(8):
        for base in (0, 1024, 2048, 3072, 4096):
            cols.append(np.arange(base + h * 128, base + (h + 1) * 128))
    for j in range(2):
        cols.append(np.arange(6144 + j * 128, 6144 + (j + 1) * 128))
        cols.append(np.arange(6400 + j * 128, 6400 + (j + 1) * 128))
    for h in range(8):
        cols.append(np.arange(5120 + h * 128, 5120 + (h + 1) * 128))
        cols.append(np.arange(6656 + h * 128, 6656 + (h + 1) * 128))
    return np.ascontiguousarray(w[:, np.concatenate(cols)])


def _perm_w1(w):
    cols = []
    for j in range(4):
        cols.append(np.arange(2048 + j * 128, 2048 + (j + 1) * 128))
        cols.append(np.arange(2560 + j * 128, 2560 + (j + 1) * 128))
    for h in range(16):
        cols.append(np.arange(h * 128, (h + 1) * 128))
        cols.append(np.arange(3072 + h * 128, 3072 + (h + 1) * 128))
    return np.ascontiguousarray(w[:, np.concatenate(cols)])


def _prep(x_prompt, x_sample, state_l0_hgrn, cache_l0_k, cache_l0_v, cache_l1_k, cache_l1_v,
           c, c_ctx, lb_gamma,
           l0_norm, l0_w_mod, l0_b_mod, l0_w_in, l0_w_out, l0_a_onorm, l0_b_qnorm, l0_b_knorm,
           l1_norm, l1_w_mod, l1_b_mod, l1_w_in, l1_w_out, l1_c_qnorm, l1_c_knorm, l1_c_sink):
    f = lambda a: np.ascontiguousarray(np.asarray(a, dtype=np.float32))
    x_prompt, x_sample = f(x_prompt), f(x_sample)
    consts = _consts()
    shared = dict(
        w_mod0=f(l0_w_mod), w_mod1=f(l1_w_mod),
        b_mod0=f(l0_b_mod).reshape(1, -1), b_mod1=f(l1_b_mod).reshape(1, -1),
        norm0=f(l0_norm).reshape(1, -1), norm1=f(l1_norm).reshape(1, -1),
        w_in0=_perm_w0(f(l0_w_in)), w_in1=_perm_w1(f(l1_w_in)),
        w_out0=f(l0_w_out), w_out1=f(l1_w_out),
        onorm=f(l0_a_onorm).reshape(128, 1),
        qn0=f(l0_b_qnorm).reshape(128, 1), kn0=f(l0_b_knorm).reshape(128, 1),
        qn1=f(l1_c_qnorm).reshape(128, 1), kn1=f(l1_c_knorm).reshape(128, 1),
        sink=f(l1_c_sink).reshape(1, 16),
        lbg=np.ascontiguousarray(f(lb_gamma).reshape(3, 2, 8, 128).transpose(3, 0, 1, 2).reshape(128, 3, 16)),
        **consts,
    )
    c = f(c)
    c_ctx = f(c_ctx)
    in_maps = []
    for i in range(8):
        m = dict(shared)
        m["x"] = np.ascontiguousarray(np.concatenate(
            [x_prompt[2 * i], x_prompt[2 * i + 1], x_sample[i]], axis=0))
        cr = np.stack([c_ctx, c[i]], axis=0)
        m["crows"] = np.ascontiguousarray(cr.reshape(2, 16, 128).transpose(2, 0, 1))
        m["st0"] = f(state_l0_hgrn[i])
        m["ck0T"] = np.ascontiguousarray(f(cache_l0_k[i]).transpose(2, 1, 0))
        m["cv0"] = f(cache_l0_v[i])
        m["ck1T"] = np.ascontiguousarray(f(cache_l1_k[i]).transpose(2, 1, 0))
        m["cv1"] = f(cache_l1_v[i])
        in_maps.append(m)
    return in_maps


def kernel(**inputs):
    in_maps = _prep(**inputs)
    nc = build_program()
    res = run_bass_kernel_spmd(nc, in_maps, core_ids=list(range(8)))
    r = res.results
    y_prompt = np.stack([r[i // 2]["y"][(i % 2) * 256:(i % 2 + 1) * 256] for i in range(16)], axis=0)
    y_sample = np.stack([r[i]["y"][512:] for i in range(8)], axis=0)
    nstate = np.concatenate([r[i]["nstate"] for i in range(8)], axis=0)
    outs = [y_prompt.astype(np.float32), y_sample.astype(np.float32), nstate.astype(np.float32)]
    for nm in ("nk0", "nv0", "nk1", "nv1"):
        a = np.concatenate([r[i][nm].reshape(2, 256, r[i][nm].shape[1], 128) for i in range(8)], axis=0)
        outs.append(a.astype(np.float32))
    return tuple(outs)
```

```python
import math
import os
from contextlib import ExitStack

import numpy as np
import concourse.bass as bass
import concourse.mybir as mybir
from concourse.bass_utils import run_bass_kernel_spmd

F32 = mybir.dt.float32
BF16 = mybir.dt.bfloat16
AF = mybir.ActivationFunctionType
ALU = mybir.AluOpType

D = 2048
T = 2560
NT = 20
NTB = 5
EPS = 1e-6
SCALE = 1.0 / math.sqrt(128.0)
SEQS = [(0, 2), (2, 2), (4, 16)]


class Ctx:
    NDMA = 32

    def __init__(self, nc):
        self.nc = nc
        self.engs = {"pe": nc.tensor, "dve": nc.vector, "act": nc.scalar,
                     "pool": nc.gpsimd, "sp": nc.sync}
        self.sem = {k: nc.alloc_semaphore(name="s_" + k) for k in self.engs}
        self.cnt = {k: 0 for k in self.engs}
        self.waited = {k: {} for k in self.engs}
        self.last_w = {}
        self.readers = {}
        self.dma_sems = [nc.alloc_semaphore(name=f"s_dma{i}") for i in range(self.NDMA)]
        self.dma_val = [0] * self.NDMA
        self.dma_pool = {"sp": list(range(0, 16)), "pool": list(range(16, 24)), "act": list(range(24, 32))}
        self.dma_rr = {"sp": 0, "pool": 0, "act": 0}

    def _deps(self, reads, writes):
        evs = []
        for r in reads:
            e = self.last_w.get(r)
            if e is not None:
                evs.append(e)
        for w in writes:
            e = self.last_w.get(w)
            if e is not None:
                evs.append(e)
            evs.extend(self.readers.get(w, ()))
        return evs

    def _wait(self, eng, evs, skip_self=False):
        best = {}
        for (name, sem, val) in evs:
            if skip_self and name == eng:
                continue
            if best.get(name, (None, 0))[1] < val:
                best[name] = (sem, val)
        wd = self.waited[eng]
        for name, (sem, val) in best.items():
            if wd.get(name, 0) < val:
                self.engs[eng].wait_ge(sem, val)
                wd[name] = val

    def _commit(self, ev, reads, writes):
        ws = set(writes)
        for r in reads:
            if r in ws:
                continue
            self.readers.setdefault(r, []).append(ev)
        for w in writes:
            self.last_w[w] = ev
            self.readers[w] = []

    EXCL = {"pj", "pms", "prot", "pv", "pS", "pO", "pL", "pm", "pT", "bA", "ptr", "po", "fb"}

    def op(self, eng, fn, reads=(), writes=()):
        reads = list(reads)
        writes = list(writes)
        ex = [r for r in reads if (r if isinstance(r, str) else r[0]) in self.EXCL]
        if ex:
            reads = [r for r in reads if r not in ex]
            writes = writes + [r for r in ex if r not in writes]
        evs = self._deps(reads, writes)
        self._wait(eng, evs, skip_self=(eng == "pe"))
        inst = fn()
        inst.then_inc(self.sem[eng], 1)
        self.cnt[eng] += 1
        ev = (eng, self.sem[eng], self.cnt[eng])
        self._commit(ev, reads, writes)
        return ev

    def dma(self, q, out, in_, reads=(), writes=()):
        reads = list(reads)
        writes = list(writes)
        evs = self._deps(reads, writes)
        lst = self.dma_pool[q]
        k = lst[self.dma_rr[q] % len(lst)]
        self.dma_rr[q] += 1
        name = f"dma{k}"
        if self.dma_val[k] > 0:
            evs.append((name, self.dma_sems[k], self.dma_val[k]))
        self._wait(q, evs)
        self.engs[q].dma_start(out=out, in_=in_).then_inc(self.dma_sems[k], 16)
        self.dma_val[k] += 16
        ev = (name, self.dma_sems[k], self.dma_val[k])
        self._commit(ev, reads, writes)
        return ev

    def all_events(self):
        evs = [(k, self.sem[k], self.cnt[k]) for k in self.engs if self.cnt[k] > 0]
        for i in range(self.NDMA):
            if self.dma_val[i] > 0:
                evs.append((f"dma{i}", self.dma_sems[i], self.dma_val[i]))
        return evs

    def barrier_all(self):
        evs = self.all_events()
        for e in self.engs:
            self._wait(e, evs, skip_self=True)

    def finish(self):
        self._wait("sp", self.all_events(), skip_self=True)


def build_program(phases=None):
    nc = bass.Bass("TRN2", target_bir_lowering=False)

    def din(name, shape, dt=F32):
        return nc.dram_tensor(name, list(shape), dt, kind="ExternalInput").ap()

    def dout(name, shape, dt=F32):
        return nc.dram_tensor(name, list(shape), dt, kind="ExternalOutput").ap()

    def dscr(name, shape, dt=F32):
        return nc.dram_tensor(name, list(shape), dt, kind="Internal").ap()

    x = din("x", [T, D])
    crows = din("crows", [128, 2, 16])
    lbg = din("lbg", [128, 3, 16])
    st0 = din("st0", [2, 8, 128, 128])
    ckT = [din("ck0T", [128, 2, 256]), din("ck1T", [128, 4, 256])]
    cv = [din("cv0", [256, 2, 128]), din("cv1", [256, 4, 128])]
    w_mod = [din("w_mod0", [D, 3 * D]), din("w_mod1", [D, 3 * D])]
    b_mod = [din("b_mod0", [1, 3 * D]), din("b_mod1", [1, 3 * D])]
    norm_g = [din("norm0", [1, D]), din("norm1", [1, D])]
    w_in = [din("w_in0", [D, 7680]), din("w_in1", [D, 5120])]
    w_out = [din("w_out0", [D, D]), din("w_out1", [D, D])]
    onorm_d = din("onorm", [128, 1])
    qn_d = [din("qn0", [128, 1]), din("qn1", [128, 1])]
    kn_d = [din("kn0", [128, 1]), din("kn1", [128, 1])]
    sink_d = din("sink", [1, 16])
    ident_d = din("ident", [128, 128])
    maskF_d = din("maskF", [128, 512])
    mfb_d = din("mfb", [128, 2, 128])
    cm4_d = din("cm4", [128, 4, 128])
    RT_d = din("RT", [128, 128])
    cosT_d = din("cosT", [128, 2048])
    sinT_d = din("sinT", [128, 2048])
    wbias4_d = din("wbias4", [128, 2, 4, 128])

    y = dout("y", [T, D])
    nstate = dout("nstate", [2, 2, 8, 128, 128])
    nk = [dout("nk0", [512, 2, 128]), dout("nk1", [512, 4, 128])]
    nv = [dout("nv0", [512, 2, 128]), dout("nv1", [512, 4, 128])]

    gts = dscr("gts", [4, D])
    oTs = [dscr("oT0", [D, T], BF16), dscr("oT1", [D, T], BF16)]
    y1 = dscr("y1", [T, D])

    c = Ctx(nc)

    uid = [0]

    def sb(es, name, shape, dt):
        uid[0] += 1
        return es.enter_context(nc.sbuf_tensor(f"{name}_{uid[0]}", list(shape), dt))

    def ps(es, name, shape, dt=F32):
        uid[0] += 1
        return es.enter_context(nc.psum_tensor(f"{name}_{uid[0]}", list(shape), dt))

    def mm_group(out, pairs, reads, writes):
        def f():
            n = len(pairs)
            inst = None
            for i, (l, r) in enumerate(pairs):
                inst = nc.tensor.matmul(out, lhsT=l, rhs=r, start=(i == 0), stop=(i == n - 1))
            return inst
        return c.op("pe", f, reads, writes)

    def wview(w, c0, ncols):
        return w[:, c0:c0 + ncols].rearrange("(k p) n -> p k n", p=128)

    with ExitStack() as top:
        identb = sb(top, "identb", [128, 128], BF16)
        identf = sb(top, "identf", [128, 128], F32)
        onesb = sb(top, "onesb", [128, 128], BF16)
        onesf = sb(top, "onesf", [128, 128], F32)
        epsc = sb(top, "epsc", [128, 1], F32)
        hT = sb(top, "hT", [128, 16, T], BF16)

        c.dma("sp", identf[:], ident_d, writes=["identf"])
        c.dma("pool", identb[:], ident_d, writes=["identb"])
        c.op("dve", lambda: nc.vector.memset(onesb[:], 1.0), writes=["onesb"])
        c.op("dve", lambda: nc.vector.memset(onesf[:], 1.0 / 128.0), writes=["onesf"])
        c.op("dve", lambda: nc.vector.memset(epsc[:], EPS), writes=["epsc"])

        def make_stream(es, blocks, ncols_max, nslots=2):
            bufs = [sb(es, f"wbuf{i}", [128, 16, ncols_max], BF16) for i in range(nslots)]
            st = {"issued": 0, "got": 0}

            def issue():
                i = st["issued"]
                if i >= len(blocks):
                    return
                w, c0, ncols = blocks[i]
                s = i % nslots
                c.dma("pool", bufs[s][:, :, 0:ncols], wview(w, c0, ncols), writes=[("wb", s)])
                st["issued"] += 1

            def get():
                i = st["got"]
                while st["issued"] <= i:
                    issue()
                st["got"] += 1
                if st["issued"] <= i + 1:
                    issue()
                s = i % nslots
                return bufs[s], ("wb", s)
            return get

        hkeys = lambda tb: [("hT", tt) for tt in range(tb * 4, tb * 4 + 4)]
        allh = [("hT", tt) for tt in range(NT)]

        def phase_mod_h(L):
            with ExitStack() as es:
                G = sb(es, "G", [128, 2, D], F32)
                SH = sb(es, "SH", [128, 2, D], F32)
                with ExitStack() as es2:
                    getw = make_stream(es2, [(w_mod[L], blk * 512, 512) for blk in range(12)], 512)
                    crs = sb(es2, "crs", [128, 2, 16], F32)
                    scl = sb(es2, "scl", [128, 2, 16], F32)
                    srep = sb(es2, "srep", [128, 2, 16, 128], BF16)
                    bb = [sb(es2, f"bb{i}", [128, 512], F32) for i in range(2)]
                    gb = [sb(es2, f"gb{i}", [128, 512], F32) for i in range(2)]
                    tmp = [sb(es2, f"mtmp{i}", [128, 512], F32) for i in range(2)]
                    pm = [ps(es2, f"pm{i}", [128, 512]) for i in range(2)]
                    c.dma("sp", crs[:], crows, writes=["crs"])
                    c.op("act", lambda: nc.scalar.activation(scl[:], crs[:], AF.Silu), ["crs"], ["scl"])
                    c.op("dve", lambda: nc.vector.tensor_copy(
                        srep[:], scl[:].unsqueeze(3).to_broadcast([128, 2, 16, 128])), ["scl"], ["srep"])
                    for blk in range(12):
                        kind, cb = divmod(blk, 4)
                        b = blk % 2
                        wb, wk = getw()
                        c.dma("sp", bb[b][:], b_mod[L][:, blk * 512:(blk + 1) * 512].partition_broadcast(128),
                              writes=[("bb", b)])
                        if kind == 1:
                            c.dma("sp", gb[b][:], norm_g[L][:, cb * 512:(cb + 1) * 512].partition_broadcast(128),
                                  writes=[("gb", b)])
                        cols = slice(cb * 512, (cb + 1) * 512)
                        for g in range(2):
                            mm_group(pm[g][:], [(srep[:, g, k, :], wb[:, k, 0:512]) for k in range(16)],
                                     [wk, "srep"], [("pm", g)])
                            if kind == 0:
                                c.op("dve", lambda g=g, b=b, cols=cols: nc.vector.tensor_tensor(
                                    out=SH[:, g, cols], in0=pm[g][:], in1=bb[b][:], op=ALU.add),
                                    [("pm", g), ("bb", b)], [("SH", g, cb)])
                            elif kind == 1:
                                c.op("dve", lambda g=g, b=b: nc.vector.tensor_tensor(
                                    out=tmp[g][:], in0=pm[g][:], in1=bb[b][:], op=ALU.add),
                                    [("pm", g), ("bb", b)], [("mtmp", g)])
                                c.op("dve", lambda g=g, b=b, cols=cols: nc.vector.scalar_tensor_tensor(
                                    out=G[:, g, cols], in0=tmp[g][:], scalar=1.0, in1=gb[b][:],
                                    op0=ALU.add, op1=ALU.mult),
                                    [("mtmp", g), ("gb", b)], [("G", g, cb)])
                            else:
                                c.op("dve", lambda g=g, b=b: nc.vector.tensor_tensor(
                                    out=tmp[g][:], in0=pm[g][:], in1=bb[b][:], op=ALU.add),
                                    [("pm", g), ("bb", b)], [("mtmp", g)])
                                c.dma("sp", gts[L * 2 + g:L * 2 + g + 1, cols], tmp[g][0:1, :],
                                      reads=[("mtmp", g)], writes=[("gts", L, g, cb)])
                    c.barrier_all()
                with ExitStack() as es2:
                    NS = 3
                    xt = [sb(es2, f"xt{i}", [128, D], F32) for i in range(NS)]
                    junk = sb(es2, "junk", [128, D], BF16)
                    st = sb(es2, "st", [128, 3 * NS], F32)
                    t1 = [sb(es2, f"t1_{i}", [128, D], F32) for i in range(NS)]
                    hb = [sb(es2, f"hb{i}", [128, D], BF16) for i in range(NS)]
                    pT = [ps(es2, f"pT{i}", [128, 8, 128], BF16) for i in range(2 * NS)]
                    GK = [[("G", g, cb) for cb in range(4)] for g in range(2)]
                    SK = [[("SH", g, cb) for cb in range(4)] for g in range(2)]

                    def h_task(b, tt):
                        g = 0 if tt < 4 else 1
                        rows = slice(tt * 128, (tt + 1) * 128)
                        ssq, std, rstd = st[:, 3 * b:3 * b + 1], st[:, 3 * b + 1:3 * b + 2], st[:, 3 * b + 2:3 * b + 3]
                        if L == 0:
                            c.dma("sp", xt[b][:], x[rows, :], writes=[("xt", b)])
                        else:
                            c.dma("sp", xt[b][:], y1[rows, :], reads=[("y1", tt, cb) for cb in range(4)],
                                  writes=[("xt", b)])
                        yield
                        c.op("act", lambda: nc.scalar.activation(junk[:], xt[b][:], AF.Square, accum_out=ssq),
                             [("xt", b)], ["junk", ("ssq", b)])
                        yield
                        c.op("act", lambda: nc.scalar.activation(std, ssq, AF.Sqrt, scale=1.0 / D, bias=epsc[:]),
                             [("ssq", b), "epsc"], [("std", b)])
                        yield
                        c.op("dve", lambda: nc.vector.reciprocal(rstd, std), [("std", b)], [("rstd", b)])
                        yield
                        c.op("dve", lambda: nc.vector.scalar_tensor_tensor(
                            out=t1[b][:], in0=xt[b][:], scalar=rstd, in1=G[:, g, :], op0=ALU.mult, op1=ALU.mult),
                            [("xt", b), ("rstd", b)] + GK[g], [("t1", b)])
                        yield
                        c.op("pool", lambda: nc.gpsimd.tensor_tensor(
                            out=hb[b][:, 0:1408], in0=t1[b][:, 0:1408], in1=SH[:, g, 0:1408], op=ALU.add),
                            [("t1", b)] + SK[g], [("hb", b, 0)])
                        c.op("dve", lambda: nc.vector.tensor_tensor(
                            out=hb[b][:, 1408:D], in0=t1[b][:, 1408:D], in1=SH[:, g, 1408:D], op=ALU.add),
                            [("t1", b)] + SK[g], [("hb", b, 1)])
                        yield
                        for half in range(2):
                            pp = pT[b * 2 + half]

                            def tr():
                                inst = None
                                for kk in range(8):
                                    k = half * 8 + kk
                                    inst = nc.tensor.transpose(pp[:, kk, :], hb[b][:, k * 128:(k + 1) * 128], identb[:])
                                return inst
                            c.op("pe", tr, [("hb", b, 0), ("hb", b, 1), "identb"], [("pT", b, half)])
                            yield
                            dst = hT[:, half * 8:(half + 1) * 8, tt * 128:(tt + 1) * 128]
                            if half == 0:
                                c.op("act", lambda: nc.scalar.copy(dst, pp[:]), [("pT", b, half)], [("hT", tt, half)])
                            else:
                                c.op("dve", lambda: nc.vector.tensor_copy(dst, pp[:]), [("pT", b, half)],
                                     [("hT", tt, half)])
                            yield

                    run_tasks([(lambda sl, tt=tt: h_task(sl, tt)) for tt in range(NT)], NS)
                    c.barrier_all()
                c.barrier_all()

        def run_tasks(factories, width):
            it = iter(factories)
            active = {}
            free = list(range(width))
            while True:
                while free:
                    f = next(it, None)
                    if f is None:
                        break
                    sl = free.pop(0)
                    active[sl] = f(sl)
                if not active:
                    break
                for sl in sorted(active):
                    try:
                        next(active[sl])
                    except StopIteration:
                        del active[sl]
                        free.append(sl)

        def phase_attn(L):
            nkv = 2 if L == 0 else 4
            G = 1 if L == 0 else 4
            if L == 0:
                kvc0, qc0, orow0 = 5120, 5120 + 512, 8
            else:
                kvc0, qc0, orow0 = 0, 1024, 0
            with ExitStack() as es:
                ablocks = []
                for j_ in range(nkv):
                    ablocks.append((w_in[L], kvc0 + j_ * 256, 256))
                    for h_ in range(4 * j_, 4 * j_ + 4):
                        ablocks.append((w_in[L], qc0 + h_ * 256, 256))
                getw = make_stream(es, ablocks, 256, nslots=3)

                T_sq = [sb(es, f"sq{i}", [128, 512], F32) for i in range(3)]
                T_sd = [sb(es, f"sd{i}", [128, 512], F32) for i in range(3)]
                T_kb = [sb(es, f"kb{i}", [128, 512], BF16) for i in range(3)]
                T_u = [sb(es, f"ru{i}", [128, 512], F32) for i in range(3)]
                cosT = sb(es, "cosT", [128, 2048], F32)
                sinT = sb(es, "sinT", [128, 2048], F32)
                RTb = sb(es, "RTb", [128, 128], BF16)
                gq = sb(es, "gq", [128, 1], F32)
                gk = sb(es, "gk", [128, 1], F32)
                esk = sb(es, "esk", [128, 16], F32)
                wb4 = sb(es, "wb4", [128, 2, 4, 128], BF16)
                KT = sb(es, "KT", [128, T], BF16)
                KcT = sb(es, "KcT", [128, 256], BF16)
                V = sb(es, "V", [128, NT, 128], BF16)
                Vc = sb(es, "Vc", [128, 2, 128], BF16)
                QTg = sb(es, "QTg", [128, G, T], BF16)
                oThg = sb(es, "oThg", [128, G, T], BF16)
                kout = [sb(es, f"kout{i}", [128, 128], F32) for i in range(2)]
                vout = [sb(es, f"vout{i}", [128, 128], F32) for i in range(3)]
                PT = [sb(es, f"PT{i}", [128, 512], BF16) for i in range(4)]
                rl = [sb(es, f"rl{i}", [128, 512], F32) for i in range(2)]
                pj = ps(es, "pj", [128, 512])
                pms = ps(es, "pms", [128, 512])
                pS = [ps(es, f"pS{i}", [128, 512]) for i in range(2)]
                pO = [ps(es, f"pO{i}", [128, 512]) for i in range(2)]
                pL = [ps(es, f"pL{i}", [128, 512]) for i in range(2)]
                pbank = [(pj, "pj"), (pS[0], ("pS", 0)), (pS[1], ("pS", 1))]
                rbank = pbank
                tbank = (pO[0], ("pO", 0))
                msbank = [(pms, "pms"), (pL[0], ("pL", 0)), (pL[1], ("pL", 1))]
                NW = 3

                c.dma("sp", cosT[:], cosT_d, writes=["consts"])
                c.dma("sp", sinT[:], sinT_d, writes=["consts"])
                c.dma("pool", RTb[:], RT_d, writes=["consts"])
                c.dma("sp", gq[:], qn_d[L], writes=["consts"])
                c.dma("sp", gk[:], kn_d[L], writes=["consts"])
                c.dma("pool", wb4[:], wbias4_d, writes=["consts"])
                c.dma("sp", esk[:], sink_d.partition_broadcast(128), writes=["esk0"])
                c.op("act", lambda: nc.scalar.activation(esk[:], esk[:], AF.Exp), ["esk0"], ["esk0", "consts"])

                def fn_task(s, pjt, pjk, gcol, rope_cols, out_bf, outkey, out_f32=None, f32key=None):
                    sq, sd, kb, uu_ = T_sq[s], T_sd[s], T_kb[s], T_u[s]
                    pmt, pmk = msbank[s]
                    c.op("act", lambda: nc.scalar.activation(sq[:], pjt, AF.Square), [pjk], [("sq", s)])
                    yield
                    c.op("pe", lambda: nc.tensor.matmul(pmt[:], lhsT=onesf[:], rhs=sq[:], start=True, stop=True),
                         [("sq", s), "onesf"], [pmk])
                    yield
                    c.op("act", lambda: nc.scalar.activation(sd[:], pmt[:], AF.Ln, bias=epsc[:]),
                         [pmk, "epsc"], [("sd", s)])
                    yield
                    c.op("act", lambda: nc.scalar.activation(sd[:], sd[:], AF.Exp, scale=-0.5), [("sd", s)], [("sd", s)])
                    yield
                    if out_f32 is None:
                        dst, dkey = sq[:], ("sq", s)
                    else:
                        dst, dkey = out_f32, (f32key or outkey + ("f32",))
                    c.op("dve", lambda: nc.vector.scalar_tensor_tensor(
                        out=dst, in0=pjt, scalar=gcol, in1=sd[:], op0=ALU.mult, op1=ALU.mult),
                        [pjk, ("sd", s), "consts"], [dkey])
                    yield
                    if rope_cols is None:
                        c.op("pool", lambda: nc.gpsimd.tensor_copy(out_bf, dst), [dkey], [outkey])
                        yield
                    else:
                        c.op("pool", lambda: nc.gpsimd.tensor_copy(kb[:], dst), [dkey], [("kb", s)])
                        yield
                        prt, prk = rbank[s]
                        c.op("pe", lambda: nc.tensor.matmul(prt[:], lhsT=RTb[:], rhs=kb[:], start=True, stop=True),
                             [("kb", s), "consts"], [prk])
                        yield
                        c.op("pool", lambda: nc.gpsimd.tensor_tensor(out=dst, in0=dst, in1=cosT[:, rope_cols],
                                                                     op=ALU.mult), [dkey, "consts"], [dkey])
                        yield
                        c.op("dve", lambda: nc.vector.tensor_tensor(out=uu_[:], in0=prt[:], in1=sinT[:, rope_cols],
                                                                    op=ALU.mult), [prk, "consts"], [("ru", s)])
                        yield
                        c.op("pool", lambda: nc.gpsimd.tensor_tensor(out=out_bf, in0=dst, in1=uu_[:], op=ALU.add),
                             [dkey, ("ru", s)], [outkey])
                        yield

                def rope_of(tb):
                    return None if tb == 0 else slice((tb - 1) * 512, tb * 512)

                def k_task(sl, wb, wk, j, tb):
                    cols = slice(tb * 512, (tb + 1) * 512)
                    pjt, pjk = pbank[sl]
                    mm_group(pjt[:], [(wb[:, k, 0:128], hT[:, k, cols]) for k in range(16)], [wk], [pjk])
                    yield
                    if tb == 0:
                        kf32 = T_u[sl]
                        yield from fn_task(sl, pjt[:], pjk, gk[:, 0:1], None, KT[:, cols], ("KT", tb), out_f32=kf32[:],
                                           f32key=("ru", sl))
                        for t4 in range(4):
                            b = t4 % 2
                            pvt, pvk = tbank
                            c.op("pe", lambda: nc.tensor.transpose(
                                pvt[:, 0:128], kf32[:, t4 * 128:(t4 + 1) * 128], identf[:]),
                                [("ru", sl), "identf"], [pvk])
                            yield
                            c.op("act", lambda: nc.scalar.copy(kout[b][:], pvt[:, 0:128]), [pvk], [("kout", b)])
                            yield
                            c.dma("sp", nk[L][t4 * 128:(t4 + 1) * 128, j, :], kout[b][:],
                                  reads=[("kout", b)], writes=[("nk", t4, j)])
                    else:
                        yield from fn_task(sl, pjt[:], pjk, gk[:, 0:1], rope_of(tb), KT[:, cols], ("KT", tb))

                def v_task(sl, wb, wk, j, tt):
                    tcols = slice(tt * 128, (tt + 1) * 128)
                    pvt, pvk = pbank[sl]
                    mm_group(pvt[:, 0:128], [(hT[:, k, tcols], wb[:, k, 128:256]) for k in range(16)], [wk], [pvk])
                    yield
                    if tt < 4:
                        b = sl
                        c.op("act", lambda: nc.scalar.copy(vout[b][:], pvt[:, 0:128]), [pvk], [("vout", b)])
                        yield
                        c.op("pool", lambda: nc.gpsimd.tensor_copy(V[:, tt, :], vout[b][:]), [("vout", b)], [("V", tt)])
                        c.dma("sp", nv[L][tt * 128:(tt + 1) * 128, j, :], vout[b][:],
                              reads=[("vout", b)], writes=[("nv", tt, j)])
                        yield
                    else:
                        c.op("act", lambda: nc.scalar.copy(V[:, tt, :], pvt[:, 0:128]), [pvk], [("V", tt)])
                        yield

                def q_task(sl, wb, wk, hh, tb):
                    cols = slice(tb * 512, (tb + 1) * 512)
                    pjt, pjk = pbank[sl]
                    mm_group(pjt[:], [(wb[:, k, 0:128], hT[:, k, cols]) for k in range(16)], [wk], [pjk])
                    yield
                    yield from fn_task(sl, pjt[:], pjk, gq[:, 0:1], rope_of(tb), QTg[:, hh, cols], ("QT", hh, tb))

                def g_task(sl, wb, wk, hh, tb):
                    cols = slice(tb * 512, (tb + 1) * 512)
                    pjt, pjk = pbank[sl]
                    mm_group(pjt[:], [(wb[:, k, 128:256], hT[:, k, cols]) for k in range(16)], [wk], [pjk])
                    yield
                    c.op("act", lambda: nc.scalar.activation(oThg[:, hh, cols], pjt[:], AF.Silu),
                         [pjk], [("oTh", hh, tb)])
                    yield

                sctr = [0, 0]
                sbanks = [[(pS[0], ("pS", 0)), (pS[1], ("pS", 1))], [(pj, "pj"), (pms, "pms")]]

                def attn_block(ob, j, q0, nq, keys, tbq):
                    qcols = slice(q0, q0 + nq)
                    N = G * nq
                    nk_ = len(keys)
                    rhsQ = QTg[:, :, qcols] if G > 1 else QTg[:, 0, qcols]
                    qkeys = [("QT", hh, tbq) for hh in range(G)]
                    slots = []

                    def smm(ki):
                        kind, idx, mi = keys[ki]
                        p = sctr[ob] % 2
                        sctr[ob] += 1
                        slots.append(p)
                        pSt, pSk = sbanks[ob][p]
                        if kind == "l":
                            Kl = KT[:, idx * 128:(idx + 1) * 128]
                            kr = [("KT", idx // 4)]
                        else:
                            Kl = KcT[:, idx * 128:(idx + 1) * 128]
                            kr = ["KcT"]

                        def f():
                            out = pSt[:, :N] if G == 1 else pSt[:, :N].rearrange("p (g q) -> p g q", g=G)
                            inst = nc.tensor.matmul(out, lhsT=Kl, rhs=rhsQ, start=True, stop=(mi is None))
                            if mi is not None:
                                inst = nc.tensor.matmul(out, lhsT=identb[:], rhs=wb4[:, mi, :, 0:nq],
                                                        start=False, stop=True)
                            return inst
                        c.op("pe", f, kr + qkeys + ["consts", "identb"], [pSk])

                    def pv(ki):
                        kind, idx, mi = keys[ki]
                        p = slots[ki]
                        if kind == "l":
                            Vl = V[:, idx, :]
                            kr = [("V", idx)]
                        else:
                            Vl = Vc[:, idx, :]
                            kr = ["Vc"]
                        pSt, pSk = sbanks[ob][p]
                        PTt = PT[ob * 2 + p]
                        c.op("act", lambda: nc.scalar.activation(PTt[:, :N], pSt[:, :N], AF.Exp, scale=SCALE),
                             [pSk], [("PT", ob, p)])

                        def f():
                            nc.tensor.matmul(pO[ob][:, :N], lhsT=Vl, rhs=PTt[:, :N],
                                             start=(ki == 0), stop=(ki == nk_ - 1))
                            return nc.tensor.matmul(pL[ob][:, :N], lhsT=onesb[:], rhs=PTt[:, :N],
                                                    start=(ki == 0), stop=(ki == nk_ - 1))
                        c.op("pe", f, kr + [("PT", ob, p), "onesb"], [("pO", ob), ("pL", ob)])

                    smm(0)
                    yield
                    if nk_ > 1:
                        smm(1)
                        yield
                    for ki in range(nk_):
                        pv(ki)
                        if ki + 2 < nk_:
                            smm(ki + 2)
                        yield
                    r = rl[ob]
                    if L == 1:
                        r3 = r[:, :N].rearrange("p (g q) -> p g q", g=G)
                        l3 = pL[ob][:, :N].rearrange("p (g q) -> p g q", g=G)
                        c.op("dve", lambda: nc.vector.tensor_tensor(
                            out=r3, in0=l3, in1=esk[:, 4 * j:4 * j + 4].unsqueeze(2).to_broadcast([128, G, nq]),
                            op=ALU.add), [("pL", ob), "consts"], [("rl", ob)])
                        c.op("act", lambda: nc.scalar.activation(r[:, :N], r[:, :N], AF.Ln), [("rl", ob)], [("rl", ob)])
                    else:
                        c.op("act", lambda: nc.scalar.activation(r[:, :N], pL[ob][:, :N], AF.Ln), [("pL", ob)], [("rl", ob)])
                    c.op("act", lambda: nc.scalar.activation(r[:, :N], r[:, :N], AF.Exp, scale=-1.0),
                         [("rl", ob)], [("rl", ob)])
                    c.op("dve", lambda: nc.vector.tensor_tensor(out=r[:, :N], in0=pO[ob][:, :N], in1=r[:, :N],
                                                                op=ALU.mult), [("pO", ob), ("rl", ob)], [("rl", ob)])
                    okeys = [("oTh", hh, tbq) for hh in range(G)]
                    if G > 1:
                        o3 = oThg[:, :, qcols]
                        r3 = r[:, :N].rearrange("p (g q) -> p g q", g=G)
                    else:
                        o3 = oThg[:, 0, qcols]
                        r3 = r[:, :N]
                    c.op("pool", lambda: nc.gpsimd.tensor_tensor(out=o3, in0=r3, in1=o3, op=ALU.mult),
                         [("rl", ob)] + okeys, okeys)
                    yield

                for j in range(nkv):
                    wb, wk = getw()
                    c.dma("pool", KcT[:], ckT[L][:, j, :], writes=["KcT"])
                    c.dma("pool", Vc[:], cv[L][:, j, :].rearrange("(t p) d -> p t d", p=128), writes=["Vc"])
                    gens = [(lambda sl, tb=tb: k_task(sl, wb, wk, j, tb)) for tb in range(NTB)] + \
                           [(lambda sl, tt=tt: v_task(sl, wb, wk, j, tt)) for tt in range(NT)]
                    run_tasks(gens, NW)
                    for h0 in range(4 * j, 4 * j + 4, G):
                        gens = []
                        for hh in range(G):
                            wbq, wkq = getw()
                            for tb in range(NTB):
                                gens.append(lambda sl, wbq=wbq, wkq=wkq, hh=hh, tb=tb: g_task(sl, wbq, wkq, hh, tb))
                            for tb in range(NTB):
                                gens.append(lambda sl, wbq=wbq, wkq=wkq, hh=hh, tb=tb: q_task(sl, wbq, wkq, hh, tb))
                            if hh % 2 == 1 or G == 1:
                                run_tasks(gens, NW)
                                gens = []
                        blocks = []
                        if G == 1:
                            for (t0, n) in SEQS[:2]:
                                blocks.append((t0 * 128, 256, [("l", t0, None), ("l", t0 + 1, None)], 0))
                            for tb in range(1, NTB):
                                keys = [("l", kt, None) for kt in range(4, NT)] + [("c", 0, None), ("c", 1, None)]
                                blocks.append((tb * 512, 512, keys, tb))
                        else:
                            for (t0, n) in SEQS[:2]:
                                for tq in range(t0, t0 + n):
                                    blocks.append((tq * 128, 128, [("l", t0, None), ("l", t0 + 1, None)], 0))
                            for i in range(16):
                                keys = []
                                if i > 0:
                                    keys.append(("l", 4 + i - 1, 0))
                                keys.append(("l", 4 + i, None))
                                if i < 15:
                                    keys.append(("l", 4 + i + 1, 1))
                                keys += [("c", 0, None), ("c", 1, None)]
                                blocks.append(((4 + i) * 128, 128, keys, (4 + i) // 4))
                        run_tasks([(lambda sl, blk=blk: attn_block(sl, j, *blk)) for blk in blocks], 2)
                        for hh in range(G):
                            r0 = (orow0 + h0 + hh) * 128
                            c.dma("sp", oTs[L][r0:r0 + 128, :], oThg[:, hh, :],
                                  reads=[("oTh", hh, tb) for tb in range(NTB)], writes=[("oTs", L, orow0 + h0 + hh)])
                c.barrier_all()

        def phase_hgrn():
            with ExitStack() as es:
                getw = make_stream(es, [(w_in[0], h * 640, 640) for h in range(8)], 640)
                maskF = sb(es, "maskF", [128, 512], F32)
                mfb = sb(es, "mfb", [128, 2, 128], F32)
                cm4 = sb(es, "cm4", [128, 4, 128], BF16)
                onc = sb(es, "onc", [128, 1], F32)
                lbe = sb(es, "lbe", [128, 3, 16], F32)
                lbs = sb(es, "lbs", [128, 16], F32)
                lbv = sb(es, "lbv", [128, 16], F32)
                oml = sb(es, "oml", [128, 16], F32)
                noml = sb(es, "noml", [128, 16], F32)
                q32 = [sb(es, f"q32_{i}", [128, 512], F32) for i in range(2)]
                sg = [sb(es, f"sg_{i}", [128, 512], F32) for i in range(2)]
                lg = [sb(es, f"lg_{i}", [128, 512], F32) for i in range(2)]
                k32 = [sb(es, f"k32_{i}", [128, 512], F32) for i in range(2)]
                bF = [sb(es, f"bF_{i}", [128, 512], F32) for i in range(2)]
                totc = [sb(es, f"totc_{i}", [128, 16], F32) for i in range(2)]
                dec = sb(es, "dec", [128, 2, 80], F32)
                qd = [sb(es, f"qd{d}", [128, T], BF16) for d in range(2)]
                ki = [sb(es, f"ki{d}", [128, T], BF16) for d in range(2)]
                keT = [sb(es, f"keT{d}", [128, T], BF16) for d in range(2)]
                sgT = sb(es, "sgT", [128, T], BF16)
                V = sb(es, "Va", [128, NT, 128], BF16)
                OT = sb(es, "OT", [128, T], F32)
                kend = [[sb(es, f"kend{d}{r}", [128, 128], BF16) for r in range(2)] for d in range(2)]
                Vm = [[sb(es, f"Vm{d}{r}", [128, 4, 128], BF16) for r in range(2)] for d in range(2)]
                Am = [[sb(es, f"Am{d}{r}", [128, 128], BF16) for r in range(2)] for d in range(2)]
                S32 = [[sb(es, f"S32_{d}{r}", [128, 128], F32) for r in range(2)] for d in range(2)]
                Sbf = [[sb(es, f"Sbf{d}{p}", [128, 128], BF16) for p in range(4)] for d in range(2)]
                fb = [ps(es, f"hfb{i}", [128, 512]) for i in range(2)]
                bA = [ps(es, f"hbA{d}", [128, 512]) for d in range(2)]
                pO = [ps(es, f"hpO{d}", [128, 512]) for d in range(2)]
                ptr = [ps(es, f"hptr{d}", [128, 8, 128], BF16) for d in range(2)]

                c.dma("sp", maskF[:], maskF_d, writes=["hc"])
                c.dma("sp", mfb[:], mfb_d, writes=["hc"])
                c.dma("pool", cm4[:], cm4_d, writes=["hc"])
                c.dma("sp", onc[:], onorm_d, writes=["hc"])
                c.dma("sp", lbe[:], lbg, writes=["lbe"])
                c.op("act", lambda: nc.scalar.activation(lbe[:], lbe[:], AF.Exp), ["lbe"], ["lbe"])
                c.op("dve", lambda: nc.vector.tensor_tensor(out=lbs[:], in0=lbe[:, 0, :], in1=lbe[:, 1, :], op=ALU.add),
                     ["lbe"], ["lbs"])
                c.op("dve", lambda: nc.vector.tensor_tensor(out=lbs[:], in0=lbs[:], in1=lbe[:, 2, :], op=ALU.add),
                     ["lbe", "lbs"], ["lbs"])
                c.op("dve", lambda: nc.vector.reciprocal(lbs[:], lbs[:]), ["lbs"], ["lbs"])
                c.op("dve", lambda: nc.vector.tensor_tensor(out=lbv[:], in0=lbe[:, 0, :], in1=lbs[:], op=ALU.mult),
                     ["lbe", "lbs"], ["lbv"])
                c.op("dve", lambda: nc.vector.tensor_scalar(out=oml[:], in0=lbv[:], scalar1=-1.0, scalar2=1.0,
                                                            op0=ALU.mult, op1=ALU.add), ["lbv"], ["oml"])
                c.op("dve", lambda: nc.vector.tensor_scalar(out=noml[:], in0=oml[:], scalar1=-1.0, scalar2=None,
                                                            op0=ALU.mult), ["oml"], ["noml", "hc"])

                def bc32(t, ncol=16):
                    return t.unsqueeze(2).to_broadcast([128, ncol, 32])

                def f_task(sl, h, wb, wk, tb, c0, n):
                    cols = slice(c0, c0 + n)
                    nch = n // 32
                    pjt, pjk = fb[sl], ("fb", sl)
                    Q, SG, LG, K, B, TC = q32[sl], sg[sl], lg[sl], k32[sl], bF[sl], totc[sl]
                    kq, ks, kl, kk, kb_, kt = ("q32", sl), ("sg", sl), ("lg", sl), ("k32", sl), ("bF", sl), ("totc", sl)
                    mm_group(pjt[:, :n], [(wb[:, k, 0:128], hT[:, k, cols]) for k in range(16)], [wk], [pjk])
                    yield
                    c.op("act", lambda: nc.scalar.activation(Q[:, :n], pjt[:, :n], AF.Silu), [pjk], [kq])
                    yield
                    for d in range(2):
                        i = d * 8 + h
                        mm_group(pjt[:, :n], [(wb[:, k, 128 * (1 + d):128 * (2 + d)], hT[:, k, cols]) for k in range(16)],
                                 [wk], [pjk])
                        yield
                        c.op("act", lambda: nc.scalar.activation(SG[:, :n], pjt[:, :n], AF.Sigmoid), [pjk], [ks])
                        yield
                        c.op("act", lambda: nc.scalar.activation(LG[:, :n], SG[:, :n], AF.Ln, scale=oml[:, i:i + 1],
                                                                 bias=lbv[:, i:i + 1]), [ks, "hc"], [kl])
                        c.op("dve", lambda: nc.vector.tensor_scalar(
                            out=K[:, :n], in0=SG[:, :n], scalar1=noml[:, i:i + 1], scalar2=oml[:, i:i + 1],
                            op0=ALU.mult, op1=ALU.add), [ks, "hc"], [kk])
                        yield
                        c.op("dve", lambda: nc.vector.tensor_tensor_scan(B[:, :n], maskF[:, :n], LG[:, :n], 0.0, ALU.mult, ALU.add),
                             [kl, "hc"], [kb_])
                        yield
                        tot = B[:, :n].rearrange("p (c t) -> p c t", t=32)[:, :, 31]
                        dslice = dec[:, d, c0 // 32:c0 // 32 + nch]
                        c.op("act", lambda: nc.scalar.activation(dslice, tot, AF.Exp), [kb_], [("dec", d, tb)])
                        if d == 1:
                            c.op("act", lambda: nc.scalar.copy(TC[:, :nch], tot), [kb_], [kt])
                            yield
                            b3 = B[:, :n].rearrange("p (c t) -> p c t", t=32)
                            c.op("dve", lambda: nc.vector.tensor_tensor(out=b3, in0=bc32(TC[:, :nch], nch), in1=b3, op=ALU.subtract),
                                 [kb_, kt], [kb_])
                            yield
                            c.op("pool", lambda: nc.gpsimd.tensor_tensor(out=B[:, :n], in0=B[:, :n], in1=LG[:, :n], op=ALU.add),
                                 [kb_, kl], [kb_])
                        yield
                        c.op("act", lambda: nc.scalar.activation(SG[:, :n], B[:, :n], AF.Exp), [kb_], [ks])
                        c.op("act", lambda: nc.scalar.activation(LG[:, :n], B[:, :n], AF.Exp, scale=-1.0), [kb_], [kl])
                        yield
                        c.op("dve", lambda: nc.vector.tensor_tensor(out=qd[d][:, cols], in0=Q[:, :n], in1=SG[:, :n], op=ALU.mult),
                             [kq, ks], [("qd", d, tb)])
                        yield
                        c.op("dve", lambda: nc.vector.tensor_tensor(out=LG[:, :n], in0=K[:, :n], in1=LG[:, :n], op=ALU.mult),
                             [kk, kl], [kl])
                        yield
                        c.op("pool", lambda: nc.gpsimd.tensor_copy(ki[d][:, cols], LG[:, :n]), [kl], [("ki", d, tb)])
                        k3 = LG[:, :n].rearrange("p (c t) -> p c t", t=32)
                        o3 = keT[d][:, cols].rearrange("p (c t) -> p c t", t=32)
                        c.op("pool", lambda: nc.gpsimd.tensor_tensor(out=o3, in0=k3, in1=bc32(dslice, nch), op=ALU.mult),
                             [kl, ("dec", d, tb)], [("keT", d, tb)])
                        yield
                    mm_group(pjt[:, :n], [(wb[:, k, 512:640], hT[:, k, cols]) for k in range(16)], [wk], [pjk])
                    yield
                    c.op("act", lambda: nc.scalar.activation(sgT[:, cols], pjt[:, :n], AF.Silu), [pjk], [("sgT", tb)])
                    yield
                    for tt in range(c0 // 128, (c0 + n) // 128):
                        tcols = slice(tt * 128, (tt + 1) * 128)
                        mm_group(pjt[:, 0:128], [(hT[:, k, tcols], wb[:, k, 384:512]) for k in range(16)], [wk], [pjk])
                        yield
                        c.op("act", lambda: nc.scalar.copy(V[:, tt, :], pjt[:, 0:128]), [pjk], [("V", tt)])
                        yield

                BLK = [0, 0, 1, 1] + [2 + (t - 4) // 4 for t in range(4, NT)]
                ot_written = set()
                vctr = [0, 0]
                sctr = [0, 0]

                def sweep(d, h, si, t0, n):
                    cur = sctr[d] % 2
                    if si < 2:
                        c.op("dve", lambda: nc.vector.memset(S32[d][cur][:], 0.0), [], [("S32", d, cur)])
                    else:
                        c.dma("sp", S32[d][cur][:], st0[d, h], writes=[("S32", d, cur)])
                    sb0 = sctr[d] % 4
                    c.op("act", lambda: nc.scalar.copy(Sbf[d][sb0][:], S32[d][cur][:]),
                         [("S32", d, cur)], [("Sbf", d, sb0)])
                    yield
                    tiles = range(t0, t0 + n) if d == 0 else range(t0 + n - 1, t0 - 1, -1)
                    for tt in tiles:
                        tb = BLK[tt]
                        tcols = slice(tt * 128, (tt + 1) * 128)
                        r = vctr[d] % 2
                        vctr[d] += 1
                        c.op("pe", lambda: nc.tensor.transpose(ptr[d][:, 0, :], keT[d][:, tcols], identb[:]),
                             [("keT", d, tb), "identb"], [("ptr", d)])
                        c.op("pool", lambda: nc.gpsimd.tensor_tensor(
                            out=Vm[d][r][:], in0=V[:, tt, :].unsqueeze(1).to_broadcast([128, 4, 128]), in1=cm4[:],
                            op=ALU.mult), [("V", tt), "hc"], [("Vm", d, r)])
                        yield
                        c.op("act", lambda: nc.scalar.copy(kend[d][r][:], ptr[d][:, 0, :]), [("ptr", d)], [("kend", d, r)])
                        c.op("pe", lambda: nc.tensor.matmul(bA[d][:, 0:128], lhsT=ki[d][:, tcols], rhs=qd[d][:, tcols],
                                                            start=True, stop=True),
                             [("ki", d, tb), ("qd", d, tb)], [("bA", d)])
                        yield
                        c.op("dve", lambda: nc.vector.tensor_tensor(out=Am[d][r][:], in0=bA[d][:, 0:128], in1=mfb[:, d, :],
                                                                    op=ALU.mult), [("bA", d), "hc"], [("Am", d, r)])

                        def umm():
                            inst = None
                            for j in range(4):
                                inst = nc.tensor.matmul(fb[d][:, j * 128:(j + 1) * 128], lhsT=kend[d][r][:],
                                                        rhs=Vm[d][r][:, j, :], start=True, stop=True)
                            return inst
                        c.op("pe", umm, [("kend", d, r), ("Vm", d, r)], [("fb", d)])
                        yield
                        c.op("pe", lambda: nc.tensor.matmul(pO[d][:, 0:128], lhsT=V[:, tt, :], rhs=Am[d][r][:],
                                                            start=True, stop=False),
                             [("V", tt), ("Am", d, r)], [("pO", d)])
                        yield
                        order = range(4) if d == 0 else range(3, -1, -1)
                        for n_, j in enumerate(order):
                            cur = sctr[d] % 2
                            nxt = 1 - cur
                            sbc = sctr[d] % 4
                            sbn = (sctr[d] + 1) % 4
                            ccols = slice(tt * 128 + 32 * j, tt * 128 + 32 * j + 32)
                            gch = tt * 4 + j
                            c.op("pe", lambda: nc.tensor.matmul(
                                pO[d][:, 32 * j:32 * j + 32], lhsT=Sbf[d][sbc][:], rhs=qd[d][:, ccols],
                                start=False, stop=(n_ == 3)),
                                [("Sbf", d, sbc), ("qd", d, tb)], [("pO", d)])
                            c.op("dve", lambda: nc.vector.scalar_tensor_tensor(
                                out=S32[d][nxt][:], in0=S32[d][cur][:], scalar=dec[:, d, gch:gch + 1],
                                in1=fb[d][:, j * 128:(j + 1) * 128], op0=ALU.mult, op1=ALU.add),
                                [("S32", d, cur), ("dec", d, tb), ("fb", d)], [("S32", d, nxt)])
                            yield
                            c.op("act", lambda: nc.scalar.copy(Sbf[d][sbn][:], S32[d][nxt][:]),
                                 [("S32", d, nxt)], [("Sbf", d, sbn)])
                            sctr[d] += 1
                            yield
                        if tt not in ot_written:
                            ot_written.add(tt)
                            c.op("act", lambda: nc.scalar.copy(OT[:, tcols], pO[d][:, 0:128]), [("pO", d)], [("OT", tt)])
                        else:
                            c.op("dve", lambda: nc.vector.tensor_tensor(out=OT[:, tcols], in0=pO[d][:, 0:128],
                                                                        in1=OT[:, tcols], op=ALU.add),
                                 [("pO", d), ("OT", tt)], [("OT", tt)])
                        yield
                    if si < 2:
                        fin = sctr[d] % 2
                        c.dma("sp", nstate[si, d, h], S32[d][fin][:], reads=[("S32", d, fin)],
                              writes=[("nstate", si, d, h)])

                def n_task(sl, h, tb):
                    cols = slice(tb * 512, (tb + 1) * 512)
                    ok = [("OT", tt) for tt in range(tb * 4, tb * 4 + 4)]
                    SQ, SD = q32[sl], sg[sl]
                    kq, ks = ("q32", sl), ("sg", sl)
                    pjt, pjk = fb[sl], ("fb", sl)
                    c.op("act", lambda: nc.scalar.activation(SQ[:], OT[:, cols], AF.Square), ok, [kq])
                    yield
                    c.op("pe", lambda: nc.tensor.matmul(pjt[:], lhsT=onesf[:], rhs=SQ[:], start=True, stop=True),
                         [kq, "onesf"], [pjk])
                    yield
                    c.op("act", lambda: nc.scalar.activation(SD[:], pjt[:], AF.Ln, bias=epsc[:]), [pjk, "epsc"], [ks])
                    yield
                    c.op("act", lambda: nc.scalar.activation(SD[:], SD[:], AF.Exp, scale=-0.5), [ks], [ks])
                    yield
                    c.op("dve", lambda: nc.vector.scalar_tensor_tensor(
                        out=SQ[:], in0=OT[:, cols], scalar=onc[:, 0:1], in1=SD[:], op0=ALU.mult, op1=ALU.mult),
                        ok + [ks, "hc"], [kq])
                    yield
                    ostv = k32[sl][:].bitcast(BF16)[:, 0:512]
                    c.op("pool", lambda: nc.gpsimd.tensor_tensor(out=ostv, in0=SQ[:], in1=sgT[:, cols], op=ALU.mult),
                         [kq] + [("sgT", bb_) for bb_ in sorted(set(BLK[tb * 4:tb * 4 + 4]))], [("k32", sl)])
                    c.dma("sp", oTs[0][h * 128:(h + 1) * 128, cols], ostv, reads=[("k32", sl)],
                          writes=[("oTs", 0, h)])
                    yield

                HG = os.environ.get("MK_HG", "")
                for h in range(8):
                    wb, wk = getw()
                    if "nof" not in HG:
                        FB = [(2, 512, 512), (3, 1024, 512), (4, 1536, 512), (5, 2048, 512), (0, 0, 256), (1, 256, 256)]
                        run_tasks([(lambda sl, fb_=fb_: f_task(sl, h, wb, wk, *fb_)) for fb_ in FB], 2)
                    ot_written.clear()
                    if "noscan" not in HG:
                        for si, (t0, n) in enumerate(SEQS):
                            run_tasks([(lambda sl, d=d: sweep(d, h, si, t0, n)) for d in range(2)], 2)
                    if "nopost" not in HG:
                        run_tasks([(lambda sl, tb=tb: n_task(sl, h, tb)) for tb in range(NTB)], 2)
                c.barrier_all()

        def phase_out(L):
            NB = 4
            with ExitStack() as es:
                getw = make_stream(es, [(w_out[L], cb * 512, 512) for cb in range(4)], 512)
                gtb = sb(es, "gtb", [128, 2, D], F32)
                xr = [sb(es, f"xr{i}", [128, 512], F32) for i in range(NB)]
                tm = [sb(es, f"otm{i}", [128, 512], F32) for i in range(NB)]
                po = [ps(es, f"po{i}", [128, 512]) for i in range(2)]
                for g in range(2):
                    c.dma("act", gtb[:, g, :], gts[L * 2 + g:L * 2 + g + 1, :].partition_broadcast(128),
                          reads=[("gts", L, g, cb) for cb in range(4)], writes=[("gtb", g)])
                for k in range(16):
                    c.dma("sp" if k % 2 == 0 else "act", hT[:, k, :], oTs[L][k * 128:(k + 1) * 128, :],
                          reads=[("oTs", L, k)], writes=[("hTk", k)])
                it = 0
                for cb in range(4):
                    ccols = slice(cb * 512, (cb + 1) * 512)
                    wb, wk = getw()
                    for tt in range(NT):
                        b = it % NB
                        pb = it % 2
                        it += 1
                        g = 0 if tt < 4 else 1
                        rows = slice(tt * 128, (tt + 1) * 128)
                        tcols = slice(tt * 128, (tt + 1) * 128)
                        if L == 0:
                            c.dma("sp", xr[b][:], x[rows, ccols], writes=[("xr", b)])
                        else:
                            c.dma("sp", xr[b][:], y1[rows, ccols], reads=[("y1", tt, cb)], writes=[("xr", b)])
                        mm_group(po[pb][:], [(hT[:, k, tcols], wb[:, k, 0:512]) for k in range(16)],
                                 [wk] + [("hTk", k) for k in range(16)], [("po", pb)])
                        c.op("dve", lambda: nc.vector.tensor_tensor(
                            out=tm[b][:], in0=po[pb][:], in1=gtb[:, g, ccols], op=ALU.mult),
                            [("po", pb), ("gtb", g)], [("otm", b)])
                        c.op("pool", lambda: nc.gpsimd.tensor_tensor(out=tm[b][:], in0=tm[b][:], in1=xr[b][:],
                                                                     op=ALU.add),
                             [("otm", b), ("xr", b)], [("otm", b)])
                        if L == 0:
                            c.dma("act", y1[rows, ccols], tm[b][:], reads=[("otm", b)], writes=[("y1", tt, cb)])
                        else:
                            c.dma("act", y[rows, ccols], tm[b][:], reads=[("otm", b)], writes=[("y", tt, cb)])
                c.barrier_all()

        plist = [("mod0", lambda: phase_mod_h(0)), ("hgrn", phase_hgrn), ("attn0", lambda: phase_attn(0)),
                 ("out0", lambda: phase_out(0)), ("mod1", lambda: phase_mod_h(1)), ("attn1", lambda: phase_attn(1)),
                 ("out1", lambda: phase_out(1))]
        for nm, fn in plist:
            if phases is None or nm in phases:
                fn()
        c.finish()
    return nc


def _consts():
    ident = np.eye(128, dtype=np.float32)
    maskF = np.ones((128, 512), np.float32)
    maskF[:, ::32] = 0.0
    s = np.arange(128)[:, None]
    t = np.arange(128)[None, :]
    same = (s // 32) == (t // 32)
    mfb = np.stack([(same & (s <= t)), (same & (s >= t))], axis=1).astype(np.float32)
    cm4 = np.zeros((128, 4, 128), np.float32)
    for j in range(4):
        cm4[32 * j:32 * j + 32, j, :] = 1.0
    R = np.zeros((128, 128), np.float32)
    for m in range(128):
        q = m // 32
        if q in (0, 2):
            R[m, m + 32] = -1.0
        else:
            R[m, m - 32] = 1.0
    RT = np.ascontiguousarray(R.T)
    n_tok = 2048
    row = (np.arange(n_tok) // 64).astype(np.float32)
    col = (np.arange(n_tok) % 64).astype(np.float32)
    inv = (10000.0 ** (-np.arange(32, dtype=np.float32) / 32)).astype(np.float32)
    ar = row[:, None] * inv
    ac = col[:, None] * inv
    ang = np.concatenate([ar, ar, ac, ac], axis=-1).astype(np.float32)
    cosT = np.ascontiguousarray(np.cos(ang).T.astype(np.float32))
    sinT = np.ascontiguousarray(np.sin(ang).T.astype(np.float32))
    b = np.arange(128)[:, None]
    a = np.arange(128)[None, :]
    NEG = -30000.0
    wbias = np.stack([np.where(b >= a, 0.0, NEG), np.where(b <= a, 0.0, NEG)], axis=1).astype(np.float32)
    wbias4 = np.ascontiguousarray(np.broadcast_to(wbias[:, :, None, :], (128, 2, 4, 128))).astype(np.float32)
    return dict(ident=ident, maskF=maskF, mfb=mfb, cm4=cm4, RT=RT, cosT=cosT, sinT=sinT, wbias4=wbias4)


def _perm_w0(w):
    cols = []
    for h in range(8):
        for base in (0, 1024, 2048, 3072, 4096):
            cols.append(np.arange(base + h * 128, base + (h + 1) * 128))
    for j in range(2):
        cols.append(np.arange(6144 + j * 128, 6144 + (j + 1) * 128))
        cols.append(np.arange(6400 + j * 128, 6400 + (j + 1) * 128))
    for h in range(8):
        cols.append(np.arange(5120 + h * 128, 5120 + (h + 1) * 128))
        cols.append(np.arange(6656 + h * 128, 6656 + (h + 1) * 128))
    return np.ascontiguousarray(w[:, np.concatenate(cols)])


def _perm_w1(w):
    cols = []
    for j in range(4):
        cols.append(np.arange(2048 + j * 128, 2048 + (j + 1) * 128))
        cols.append(np.arange(2560 + j * 128, 2560 + (j + 1) * 128))
    for h in range(16):
        cols.append(np.arange(h * 128, (h + 1) * 128))
        cols.append(np.arange(3072 + h * 128, 3072 + (h + 1) * 128))
    return np.ascontiguousarray(w[:, np.concatenate(cols)])


def _prep(x_prompt, x_sample, state_l0_hgrn, cache_l0_k, cache_l0_v, cache_l1_k, cache_l1_v,
           c, c_ctx, lb_gamma,
           l0_norm, l0_w_mod, l0_b_mod, l0_w_in, l0_w_out, l0_a_onorm, l0_b_qnorm, l0_b_knorm,
           l1_norm, l1_w_mod, l1_b_mod, l1_w_in, l1_w_out, l1_c_qnorm, l1_c_knorm, l1_c_sink):
    f = lambda a: np.ascontiguousarray(np.asarray(a, dtype=np.float32))
    x_prompt, x_sample = f(x_prompt), f(x_sample)
    consts = _consts()
    shared = dict(
        w_mod0=f(l0_w_mod), w_mod1=f(l1_w_mod),
        b_mod0=f(l0_b_mod).reshape(1, -1), b_mod1=f(l1_b_mod).reshape(1, -1),
        norm0=f(l0_norm).reshape(1, -1), norm1=f(l1_norm).reshape(1, -1),
        w_in0=_perm_w0(f(l0_w_in)), w_in1=_perm_w1(f(l1_w_in)),
        w_out0=f(l0_w_out), w_out1=f(l1_w_out),
        onorm=f(l0_a_onorm).reshape(128, 1),
        qn0=f(l0_b_qnorm).reshape(128, 1), kn0=f(l0_b_knorm).reshape(128, 1),
        qn1=f(l1_c_qnorm).reshape(128, 1), kn1=f(l1_c_knorm).reshape(128, 1),
        sink=f(l1_c_sink).reshape(1, 16),
        lbg=np.ascontiguousarray(f(lb_gamma).reshape(3, 2, 8, 128).transpose(3, 0, 1, 2).reshape(128, 3, 16)),
        **consts,
    )
    c = f(c)
    c_ctx = f(c_ctx)
    in_maps = []
    for i in range(8):
        m = dict(shared)
        m["x"] = np.ascontiguousarray(np.concatenate(
            [x_prompt[2 * i], x_prompt[2 * i + 1], x_sample[i]], axis=0))
        cr = np.stack([c_ctx, c[i]], axis=0)
        m["crows"] = np.ascontiguousarray(cr.reshape(2, 16, 128).transpose(2, 0, 1))
        m["st0"] = f(state_l0_hgrn[i])
        m["ck0T"] = np.ascontiguousarray(f(cache_l0_k[i]).transpose(2, 1, 0))
        m["cv0"] = f(cache_l0_v[i])
        m["ck1T"] = np.ascontiguousarray(f(cache_l1_k[i]).transpose(2, 1, 0))
        m["cv1"] = f(cache_l1_v[i])
        in_maps.append(m)
    return in_maps


def kernel(**inputs):
    in_maps = _prep(**inputs)
    nc = build_program()
    res = run_bass_kernel_spmd(nc, in_maps, core_ids=list(range(8)))
    r = res.results
    y_prompt = np.stack([r[i // 2]["y"][(i % 2) * 256:(i % 2 + 1) * 256] for i in range(16)], axis=0)
    y_sample = np.stack([r[i]["y"][512:] for i in range(8)], axis=0)
    nstate = np.concatenate([r[i]["nstate"] for i in range(8)], axis=0)
    outs = [y_prompt.astype(np.float32), y_sample.astype(np.float32), nstate.astype(np.float32)]
    for nm in ("nk0", "nv0", "nk1", "nv1"):
        a = np.concatenate([r[i][nm].reshape(2, 256, r[i][nm].shape[1], 128) for i in range(8)], axis=0)
        outs.append(a.astype(np.float32))
    return tuple(outs)
```

```python
import math
import os
from contextlib import ExitStack

import numpy as np
import concourse.bass as bass
import concourse.mybir as mybir
from concourse.bass_utils import run_bass_kernel_spmd

F32 = mybir.dt.float32
BF16 = mybir.dt.bfloat16
AF = mybir.ActivationFunctionType
ALU = mybir.AluOpType

D = 2048
T = 2560
NT = 20
NTB = 5
EPS = 1e-6
SCALE = 1.0 / math.sqrt(128.0)
SEQS = [(0, 2), (2, 2), (4, 16)]


class Ctx:
    NDMA = 32

    def __init__(self, nc):
        self.nc = nc
        self.engs = {"pe": nc.tensor, "dve": nc.vector, "act": nc.scalar,
                     "pool": nc.gpsimd, "sp": nc.sync}
        self.sem = {k: nc.alloc_semaphore(name="s_" + k) for k in self.engs}
        self.cnt = {k: 0 for k in self.engs}
        self.waited = {k: {} for k in self.engs}
        self.last_w = {}
        self.readers = {}
        self.dma_sems = [nc.alloc_semaphore(name=f"s_dma{i}") for i in range(self.NDMA)]
        self.dma_val = [0] * self.NDMA
        self.dma_pool = {"sp": list(range(0, 16)), "pool": list(range(16, 24)), "act": list(range(24, 32))}
        self.dma_rr = {"sp": 0, "pool": 0, "act": 0}

    def _deps(self, reads, writes):
        evs = []
        for r in reads:
            e = self.last_w.get(r)
            if e is not None:
                evs.append(e)
        for w in writes:
            e = self.last_w.get(w)
            if e is not None:
                evs.append(e)
            evs.extend(self.readers.get(w, ()))
        return evs

    def _wait(self, eng, evs, skip_self=False):
        best = {}
        for (name, sem, val) in evs:
            if skip_self and name == eng:
                continue
            if best.get(name, (None, 0))[1] < val:
                best[name] = (sem, val)
        wd = self.waited[eng]
        for name, (sem, val) in best.items():
            if wd.get(name, 0) < val:
                self.engs[eng].wait_ge(sem, val)
                wd[name] = val

    def _commit(self, ev, reads, writes):
        ws = set(writes)
        for r in reads:
            if r in ws:
                continue
            self.readers.setdefault(r, []).append(ev)
        for w in writes:
            self.last_w[w] = ev
            self.readers[w] = []

    EXCL = {"pj", "pms", "prot", "pv", "pS", "pO", "pL", "pm", "pT", "bA", "ptr", "po", "fb"}

    def op(self, eng, fn, reads=(), writes=()):
        reads = list(reads)
        writes = list(writes)
        ex = [r for r in reads if (r if isinstance(r, str) else r[0]) in self.EXCL]
        if ex:
            reads = [r for r in reads if r not in ex]
            writes = writes + [r for r in ex if r not in writes]
        evs = self._deps(reads, writes)
        self._wait(eng, evs, skip_self=(eng == "pe"))
        inst = fn()
        inst.then_inc(self.sem[eng], 1)
        self.cnt[eng] += 1
        ev = (eng, self.sem[eng], self.cnt[eng])
        self._commit(ev, reads, writes)
        return ev

    def dma(self, q, out, in_, reads=(), writes=()):
        reads = list(reads)
        writes = list(writes)
        evs = self._deps(reads, writes)
        lst = self.dma_pool[q]
        k = lst[self.dma_rr[q] % len(lst)]
        self.dma_rr[q] += 1
        name = f"dma{k}"
        if self.dma_val[k] > 0:
            evs.append((name, self.dma_sems[k], self.dma_val[k]))
        self._wait(q, evs)
        self.engs[q].dma_start(out=out, in_=in_).then_inc(self.dma_sems[k], 16)
        self.dma_val[k] += 16
        ev = (name, self.dma_sems[k], self.dma_val[k])
        self._commit(ev, reads, writes)
        return ev

    def all_events(self):
        evs = [(k, self.sem[k], self.cnt[k]) for k in self.engs if self.cnt[k] > 0]
        for i in range(self.NDMA):
            if self.dma_val[i] > 0:
                evs.append((f"dma{i}", self.dma_sems[i], self.dma_val[i]))
        return evs

    def barrier_all(self):
        evs = self.all_events()
        for e in self.engs:
            self._wait(e, evs, skip_self=True)

    def finish(self):
        self._wait("sp", self.all_events(), skip_self=True)


def build_program(phases=None):
    nc = bass.Bass("TRN2", target_bir_lowering=False)

    def din(name, shape, dt=F32):
        return nc.dram_tensor(name, list(shape), dt, kind="ExternalInput").ap()

    def dout(name, shape, dt=F32):
        return nc.dram_tensor(name, list(shape), dt, kind="ExternalOutput").ap()

    def dscr(name, shape, dt=F32):
        return nc.dram_tensor(name, list(shape), dt, kind="Internal").ap()

    x = din("x", [T, D])
    crows = din("crows", [128, 2, 16])
    lbg = din("lbg", [128, 3, 16])
    st0 = din("st0", [2, 8, 128, 128])
    ckT = [din("ck0T", [128, 2, 256]), din("ck1T", [128, 4, 256])]
    cv = [din("cv0", [256, 2, 128]), din("cv1", [256, 4, 128])]
    w_mod = [din("w_mod0", [D, 3 * D]), din("w_mod1", [D, 3 * D])]
    b_mod = [din("b_mod0", [1, 3 * D]), din("b_mod1", [1, 3 * D])]
    norm_g = [din("norm0", [1, D]), din("norm1", [1, D])]
    w_in = [din("w_in0", [D, 7680]), din("w_in1", [D, 5120])]
    w_out = [din("w_out0", [D, D]), din("w_out1", [D, D])]
    onorm_d = din("onorm", [128, 1])
    qn_d = [din("qn0", [128, 1]), din("qn1", [128, 1])]
    kn_d = [din("kn0", [128, 1]), din("kn1", [128, 1])]
    sink_d = din("sink", [1, 16])
    ident_d = din("ident", [128, 128])
    maskF_d = din("maskF", [128, 512])
    mfb_d = din("mfb", [128, 2, 128])
    cm4_d = din("cm4", [128, 4, 128])
    RT_d = din("RT", [128, 128])
    cosT_d = din("cosT", [128, 2048])
    sinT_d = din("sinT", [128, 2048])
    wbias4_d = din("wbias4", [128, 2, 4, 128])

    y = dout("y", [T, D])
    nstate = dout("nstate", [2, 2, 8, 128, 128])
    nk = [dout("nk0", [512, 2, 128]), dout("nk1", [512, 4, 128])]
    nv = [dout("nv0", [512, 2, 128]), dout("nv1", [512, 4, 128])]

    gts = dscr("gts", [4, D])
    oTs = [dscr("oT0", [D, T], BF16), dscr("oT1", [D, T], BF16)]
    y1 = dscr("y1", [T, D])

    c = Ctx(nc)

    uid = [0]

    def sb(es, name, shape, dt):
        uid[0] += 1
        return es.enter_context(nc.sbuf_tensor(f"{name}_{uid[0]}", list(shape), dt))

    def ps(es, name, shape, dt=F32):
        uid[0] += 1
        return es.enter_context(nc.psum_tensor(f"{name}_{uid[0]}", list(shape), dt))

    def mm_group(out, pairs, reads, writes):
        def f():
            n = len(pairs)
            inst = None
            for i, (l, r) in enumerate(pairs):
                inst = nc.tensor.matmul(out, lhsT=l, rhs=r, start=(i == 0), stop=(i == n - 1))
            return inst
        return c.op("pe", f, reads, writes)

    def wview(w, c0, ncols):
        return w[:, c0:c0 + ncols].rearrange("(k p) n -> p k n", p=128)

    with ExitStack() as top:
        identb = sb(top, "identb", [128, 128], BF16)
        identf = sb(top, "identf", [128, 128], F32)
        onesb = sb(top, "onesb", [128, 128], BF16)
        onesf = sb(top, "onesf", [128, 128], F32)
        epsc = sb(top, "epsc", [128, 1], F32)
        hT = sb(top, "hT", [128, 16, T], BF16)

        c.dma("sp", identf[:], ident_d, writes=["identf"])
        c.dma("pool", identb[:], ident_d, writes=["identb"])
        c.op("dve", lambda: nc.vector.memset(onesb[:], 1.0), writes=["onesb"])
        c.op("dve", lambda: nc.vector.memset(onesf[:], 1.0 / 128.0), writes=["onesf"])
        c.op("dve", lambda: nc.vector.memset(epsc[:], EPS), writes=["epsc"])

        def make_stream(es, blocks, ncols_max, nslots=2):
            bufs = [sb(es, f"wbuf{i}", [128, 16, ncols_max], BF16) for i in range(nslots)]
            st = {"issued": 0, "got": 0}

            def issue():
                i = st["issued"]
                if i >= len(blocks):
                    return
                w, c0, ncols = blocks[i]
                s = i % nslots
                c.dma("pool", bufs[s][:, :, 0:ncols], wview(w, c0, ncols), writes=[("wb", s)])
                st["issued"] += 1

            def get():
                i = st["got"]
                while st["issued"] <= i:
                    issue()
                st["got"] += 1
                if st["issued"] <= i + 1:
                    issue()
                s = i % nslots
                return bufs[s], ("wb", s)
            return get

        hkeys = lambda tb: [("hT", tt) for tt in range(tb * 4, tb * 4 + 4)]
        allh = [("hT", tt) for tt in range(NT)]

        def phase_mod_h(L):
            with ExitStack() as es:
                G = sb(es, "G", [128, 2, D], F32)
                SH = sb(es, "SH", [128, 2, D], F32)
                with ExitStack() as es2:
                    getw = make_stream(es2, [(w_mod[L], blk * 512, 512) for blk in range(12)], 512)
                    crs = sb(es2, "crs", [128, 2, 16], F32)
                    scl = sb(es2, "scl", [128, 2, 16], F32)
                    srep = sb(es2, "srep", [128, 2, 16, 128], BF16)
                    bb = [sb(es2, f"bb{i}", [128, 512], F32) for i in range(2)]
                    gb = [sb(es2, f"gb{i}", [128, 512], F32) for i in range(2)]
                    tmp = [sb(es2, f"mtmp{i}", [128, 512], F32) for i in range(2)]
                    pm = [ps(es2, f"pm{i}", [128, 512]) for i in range(2)]
                    c.dma("sp", crs[:], crows, writes=["crs"])
                    c.op("act", lambda: nc.scalar.activation(scl[:], crs[:], AF.Silu), ["crs"], ["scl"])
                    c.op("dve", lambda: nc.vector.tensor_copy(
                        srep[:], scl[:].unsqueeze(3).to_broadcast([128, 2, 16, 128])), ["scl"], ["srep"])
                    for blk in range(12):
                        kind, cb = divmod(blk, 4)
                        b = blk % 2
                        wb, wk = getw()
                        c.dma("sp", bb[b][:], b_mod[L][:, blk * 512:(blk + 1) * 512].partition_broadcast(128),
                              writes=[("bb", b)])
                        if kind == 1:
                            c.dma("sp", gb[b][:], norm_g[L][:, cb * 512:(cb + 1) * 512].partition_broadcast(128),
                                  writes=[("gb", b)])
                        cols = slice(cb * 512, (cb + 1) * 512)
                        for g in range(2):
                            mm_group(pm[g][:], [(srep[:, g, k, :], wb[:, k, 0:512]) for k in range(16)],
                                     [wk, "srep"], [("pm", g)])
                            if kind == 0:
                                c.op("dve", lambda g=g, b=b, cols=cols: nc.vector.tensor_tensor(
                                    out=SH[:, g, cols], in0=pm[g][:], in1=bb[b][:], op=ALU.add),
                                    [("pm", g), ("bb", b)], [("SH", g, cb)])
                            elif kind == 1:
                                c.op("dve", lambda g=g, b=b: nc.vector.tensor_tensor(
                                    out=tmp[g][:], in0=pm[g][:], in1=bb[b][:], op=ALU.add),
                                    [("pm", g), ("bb", b)], [("mtmp", g)])
                                c.op("dve", lambda g=g, b=b, cols=cols: nc.vector.scalar_tensor_tensor(
                                    out=G[:, g, cols], in0=tmp[g][:], scalar=1.0, in1=gb[b][:],
                                    op0=ALU.add, op1=ALU.mult),
                                    [("mtmp", g), ("gb", b)], [("G", g, cb)])
                            else:
                                c.op("dve", lambda g=g, b=b: nc.vector.tensor_tensor(
                                    out=tmp[g][:], in0=pm[g][:], in1=bb[b][:], op=ALU.add),
                                    [("pm", g), ("bb", b)], [("mtmp", g)])
                                c.dma("sp", gts[L * 2 + g:L * 2 + g + 1, cols], tmp[g][0:1, :],
                                      reads=[("mtmp", g)], writes=[("gts", L, g, cb)])
                    c.barrier_all()
                with ExitStack() as es2:
                    NS = 3
                    xt = [sb(es2, f"xt{i}", [128, D], F32) for i in range(NS)]
                    junk = sb(es2, "junk", [128, D], BF16)
                    st = sb(es2, "st", [128, 3 * NS], F32)
                    t1 = [sb(es2, f"t1_{i}", [128, D], F32) for i in range(NS)]
                    hb = [sb(es2, f"hb{i}", [128, D], BF16) for i in range(NS)]
                    pT = [ps(es2, f"pT{i}", [128, 8, 128], BF16) for i in range(2 * NS)]
                    GK = [[("G", g, cb) for cb in range(4)] for g in range(2)]
                    SK = [[("SH", g, cb) for cb in range(4)] for g in range(2)]

                    def h_task(b, tt):
                        g = 0 if tt < 4 else 1
                        rows = slice(tt * 128, (tt + 1) * 128)
                        ssq, std, rstd = st[:, 3 * b:3 * b + 1], st[:, 3 * b + 1:3 * b + 2], st[:, 3 * b + 2:3 * b + 3]
                        if L == 0:
                            c.dma("sp", xt[b][:], x[rows, :], writes=[("xt", b)])
                        else:
                            c.dma("sp", xt[b][:], y1[rows, :], reads=[("y1", tt, cb) for cb in range(4)],
                                  writes=[("xt", b)])
                        yield
                        c.op("act", lambda: nc.scalar.activation(junk[:], xt[b][:], AF.Square, accum_out=ssq),
                             [("xt", b)], ["junk", ("ssq", b)])
                        yield
                        c.op("act", lambda: nc.scalar.activation(std, ssq, AF.Sqrt, scale=1.0 / D, bias=epsc[:]),
                             [("ssq", b), "epsc"], [("std", b)])
                        yield
                        c.op("dve", lambda: nc.vector.reciprocal(rstd, std), [("std", b)], [("rstd", b)])
                        yield
                        c.op("dve", lambda: nc.vector.scalar_tensor_tensor(
                            out=t1[b][:], in0=xt[b][:], scalar=rstd, in1=G[:, g, :], op0=ALU.mult, op1=ALU.mult),
                            [("xt", b), ("rstd", b)] + GK[g], [("t1", b)])
                        yield
                        c.op("pool", lambda: nc.gpsimd.tensor_tensor(
                            out=hb[b][:, 0:1408], in0=t1[b][:, 0:1408], in1=SH[:, g, 0:1408], op=ALU.add),
                            [("t1", b)] + SK[g], [("hb", b, 0)])
                        c.op("dve", lambda: nc.vector.tensor_tensor(
                            out=hb[b][:, 1408:D], in0=t1[b][:, 1408:D], in1=SH[:, g, 1408:D], op=ALU.add),
                            [("t1", b)] + SK[g], [("hb", b, 1)])
                        yield
                        for half in range(2):
                            pp = pT[b * 2 + half]

                            def tr():
                                inst = None
                                for kk in range(8):
                                    k = half * 8 + kk
                                    inst = nc.tensor.transpose(pp[:, kk, :], hb[b][:, k * 128:(k + 1) * 128], identb[:])
                                return inst
                            c.op("pe", tr, [("hb", b, 0), ("hb", b, 1), "identb"], [("pT", b, half)])
                            yield
                            dst = hT[:, half * 8:(half + 1) * 8, tt * 128:(tt + 1) * 128]
                            if half == 0:
                                c.op("act", lambda: nc.scalar.copy(dst, pp[:]), [("pT", b, half)], [("hT", tt, half)])
                            else:
                                c.op("dve", lambda: nc.vector.tensor_copy(dst, pp[:]), [("pT", b, half)],
                                     [("hT", tt, half)])
                            yield

                    run_tasks([(lambda sl, tt=tt: h_task(sl, tt)) for tt in range(NT)], NS)
                    c.barrier_all()
                c.barrier_all()

        def run_tasks(factories, width):
            it = iter(factories)
            active = {}
            free = list(range(width))
            while True:
                while free:
                    f = next(it, None)
                    if f is None:
                        break
                    sl = free.pop(0)
                    active[sl] = f(sl)
                if not active:
                    break
                for sl in sorted(active):
                    try:
                        next(active[sl])
                    except StopIteration:
                        del active[sl]
                        free.append(sl)

        def phase_attn(L):
            nkv = 2 if L == 0 else 4
            G = 1 if L == 0 else 4
            if L == 0:
                kvc0, qc0, orow0 = 5120, 5120 + 512, 8
            else:
                kvc0, qc0, orow0 = 0, 1024, 0
            with ExitStack() as es:
                ablocks = []
                for j_ in range(nkv):
                    ablocks.append((w_in[L], kvc0 + j_ * 256, 256))
                    for h_ in range(4 * j_, 4 * j_ + 4):
                        ablocks.append((w_in[L], qc0 + h_ * 256, 256))
                getw = make_stream(es, ablocks, 256, nslots=3)

                T_sq = [sb(es, f"sq{i}", [128, 512], F32) for i in range(3)]
                T_sd = [sb(es, f"sd{i}", [128, 512], F32) for i in range(3)]
                T_kb = [sb(es, f"kb{i}", [128, 512], BF16) for i in range(3)]
                T_u = [sb(es, f"ru{i}", [128, 512], F32) for i in range(3)]
                cosT = sb(es, "cosT", [128, 2048], F32)
                sinT = sb(es, "sinT", [128, 2048], F32)
                RTb = sb(es, "RTb", [128, 128], BF16)
                gq = sb(es, "gq", [128, 1], F32)
                gk = sb(es, "gk", [128, 1], F32)
                esk = sb(es, "esk", [128, 16], F32)
                wb4 = sb(es, "wb4", [128, 2, 4, 128], BF16)
                KT = sb(es, "KT", [128, T], BF16)
                KcT = sb(es, "KcT", [128, 256], BF16)
                V = sb(es, "V", [128, NT, 128], BF16)
                Vc = sb(es, "Vc", [128, 2, 128], BF16)
                QTg = sb(es, "QTg", [128, G, T], BF16)
                oThg = sb(es, "oThg", [128, G, T], BF16)
                kout = [sb(es, f"kout{i}", [128, 128], F32) for i in range(2)]
                vout = [sb(es, f"vout{i}", [128, 128], F32) for i in range(3)]
                PT = [sb(es, f"PT{i}", [128, 512], BF16) for i in range(4)]
                rl = [sb(es, f"rl{i}", [128, 512], F32) for i in range(2)]
                pj = ps(es, "pj", [128, 512])
                pms = ps(es, "pms", [128, 512])
                pS = [ps(es, f"pS{i}", [128, 512]) for i in range(2)]
                pO = [ps(es, f"pO{i}", [128, 512]) for i in range(2)]
                pL = [ps(es, f"pL{i}", [128, 512]) for i in range(2)]
                pbank = [(pj, "pj"), (pS[0], ("pS", 0)), (pS[1], ("pS", 1))]
                rbank = pbank
                tbank = (pO[0], ("pO", 0))
                msbank = [(pms, "pms"), (pL[0], ("pL", 0)), (pL[1], ("pL", 1))]
                NW = 3

                c.dma("sp", cosT[:], cosT_d, writes=["consts"])
                c.dma("sp", sinT[:], sinT_d, writes=["consts"])
                c.dma("pool", RTb[:], RT_d, writes=["consts"])
                c.dma("sp", gq[:], qn_d[L], writes=["consts"])
                c.dma("sp", gk[:], kn_d[L], writes=["consts"])
                c.dma("pool", wb4[:], wbias4_d, writes=["consts"])
                c.dma("sp", esk[:], sink_d.partition_broadcast(128), writes=["esk0"])
                c.op("act", lambda: nc.scalar.activation(esk[:], esk[:], AF.Exp), ["esk0"], ["esk0", "consts"])

                def fn_task(s, pjt, pjk, gcol, rope_cols, out_bf, outkey, out_f32=None, f32key=None):
                    sq, sd, kb, uu_ = T_sq[s], T_sd[s], T_kb[s], T_u[s]
                    pmt, pmk = msbank[s]
                    c.op("act", lambda: nc.scalar.activation(sq[:], pjt, AF.Square), [pjk], [("sq", s)])
                    yield
                    c.op("pe", lambda: nc.tensor.matmul(pmt[:], lhsT=onesf[:], rhs=sq[:], start=True, stop=True),
                         [("sq", s), "onesf"], [pmk])
                    yield
                    c.op("act", lambda: nc.scalar.activation(sd[:], pmt[:], AF.Ln, bias=epsc[:]),
                         [pmk, "epsc"], [("sd", s)])
                    yield
                    c.op("act", lambda: nc.scalar.activation(sd[:], sd[:], AF.Exp, scale=-0.5), [("sd", s)], [("sd", s)])
                    yield
                    if out_f32 is None:
                        dst, dkey = sq[:], ("sq", s)
                    else:
                        dst, dkey = out_f32, (f32key or outkey + ("f32",))
                    c.op("dve", lambda: nc.vector.scalar_tensor_tensor(
                        out=dst, in0=pjt, scalar=gcol, in1=sd[:], op0=ALU.mult, op1=ALU.mult),
                        [pjk, ("sd", s), "consts"], [dkey])
                    yield
                    if rope_cols is None:
                        c.op("pool", lambda: nc.gpsimd.tensor_copy(out_bf, dst), [dkey], [outkey])
                        yield
                    else:
                        c.op("pool", lambda: nc.gpsimd.tensor_copy(kb[:], dst), [dkey], [("kb", s)])
                        yield
                        prt, prk = rbank[s]
                        c.op("pe", lambda: nc.tensor.matmul(prt[:], lhsT=RTb[:], rhs=kb[:], start=True, stop=True),
                             [("kb", s), "consts"], [prk])
                        yield
                        c.op("pool", lambda: nc.gpsimd.tensor_tensor(out=dst, in0=dst, in1=cosT[:, rope_cols],
                                                                     op=ALU.mult), [dkey, "consts"], [dkey])
                        yield
                        c.op("dve", lambda: nc.vector.tensor_tensor(out=uu_[:], in0=prt[:], in1=sinT[:, rope_cols],
                                                                    op=ALU.mult), [prk, "consts"], [("ru", s)])
                        yield
                        c.op("pool", lambda: nc.gpsimd.tensor_tensor(out=out_bf, in0=dst, in1=uu_[:], op=ALU.add),
                             [dkey, ("ru", s)], [outkey])
                        yield

                def rope_of(tb):
                    return None if tb == 0 else slice((tb - 1) * 512, tb * 512)

                def k_task(sl, wb, wk, j, tb):
                    cols = slice(tb * 512, (tb + 1) * 512)
                    pjt, pjk = pbank[sl]
                    mm_group(pjt[:], [(wb[:, k, 0:128], hT[:, k, cols]) for k in range(16)], [wk], [pjk])
                    yield
                    if tb == 0:
                        kf32 = T_u[sl]
                        yield from fn_task(sl, pjt[:], pjk, gk[:, 0:1], None, KT[:, cols], ("KT", tb), out_f32=kf32[:],
                                           f32key=("ru", sl))
                        for t4 in range(4):
                            b = t4 % 2
                            pvt, pvk = tbank
                            c.op("pe", lambda: nc.tensor.transpose(
                                pvt[:, 0:128], kf32[:, t4 * 128:(t4 + 1) * 128], identf[:]),
                                [("ru", sl), "identf"], [pvk])
                            yield
                            c.op("act", lambda: nc.scalar.copy(kout[b][:], pvt[:, 0:128]), [pvk], [("kout", b)])
                            yield
                            c.dma("sp", nk[L][t4 * 128:(t4 + 1) * 128, j, :], kout[b][:],
                                  reads=[("kout", b)], writes=[("nk", t4, j)])
                    else:
                        yield from fn_task(sl, pjt[:], pjk, gk[:, 0:1], rope_of(tb), KT[:, cols], ("KT", tb))

                def v_task(sl, wb, wk, j, tt):
                    tcols = slice(tt * 128, (tt + 1) * 128)
                    pvt, pvk = pbank[sl]
                    mm_group(pvt[:, 0:128], [(hT[:, k, tcols], wb[:, k, 128:256]) for k in range(16)], [wk], [pvk])
                    yield
                    if tt < 4:
                        b = sl
                        c.op("act", lambda: nc.scalar.copy(vout[b][:], pvt[:, 0:128]), [pvk], [("vout", b)])
                        yield
                        c.op("pool", lambda: nc.gpsimd.tensor_copy(V[:, tt, :], vout[b][:]), [("vout", b)], [("V", tt)])
                        c.dma("sp", nv[L][tt * 128:(tt + 1) * 128, j, :], vout[b][:],
                              reads=[("vout", b)], writes=[("nv", tt, j)])
                        yield
                    else:
                        c.op("act", lambda: nc.scalar.copy(V[:, tt, :], pvt[:, 0:128]), [pvk], [("V", tt)])
                        yield

                def q_task(sl, wb, wk, hh, tb):
                    cols = slice(tb * 512, (tb + 1) * 512)
                    pjt, pjk = pbank[sl]
                    mm_group(pjt[:], [(wb[:, k, 0:128], hT[:, k, cols]) for k in range(16)], [wk], [pjk])
                    yield
                    yield from fn_task(sl, pjt[:], pjk, gq[:, 0:1], rope_of(tb), QTg[:, hh, cols], ("QT", hh, tb))

                def g_task(sl, wb, wk, hh, tb):
                    cols = slice(tb * 512, (tb + 1) * 512)
                    pjt, pjk = pbank[sl]
                    mm_group(pjt[:], [(wb[:, k, 128:256], hT[:, k, cols]) for k in range(16)], [wk], [pjk])
                    yield
                    c.op("act", lambda: nc.scalar.activation(oThg[:, hh, cols], pjt[:], AF.Silu),
                         [pjk], [("oTh", hh, tb)])
                    yield

                sctr = [0, 0]
                sbanks = [[(pS[0], ("pS", 0)), (pS[1], ("pS", 1))], [(pj, "pj"), (pms, "pms")]]

                def attn_block(ob, j, q0, nq, keys, tbq):
                    qcols = slice(q0, q0 + nq)
                    N = G * nq
                    nk_ = len(keys)
                    rhsQ = QTg[:, :, qcols] if G > 1 else QTg[:, 0, qcols]
                    qkeys = [("QT", hh, tbq) for hh in range(G)]
                    slots = []

                    def smm(ki):
                        kind, idx, mi = keys[ki]
                        p = sctr[ob] % 2
                        sctr[ob] += 1
                        slots.append(p)
                        pSt, pSk = sbanks[ob][p]
                        if kind == "l":
                            Kl = KT[:, idx * 128:(idx + 1) * 128]
                            kr = [("KT", idx // 4)]
                        else:
                            Kl = KcT[:, idx * 128:(idx + 1) * 128]
                            kr = ["KcT"]

                        def f():
                            out = pSt[:, :N] if G == 1 else pSt[:, :N].rearrange("p (g q) -> p g q", g=G)
                            inst = nc.tensor.matmul(out, lhsT=Kl, rhs=rhsQ, start=True, stop=(mi is None))
                            if mi is not None:
                                inst = nc.tensor.matmul(out, lhsT=identb[:], rhs=wb4[:, mi, :, 0:nq],
                                                        start=False, stop=True)
                            return inst
                        c.op("pe", f, kr + qkeys + ["consts", "identb"], [pSk])

                    def pv(ki):
                        kind, idx, mi = keys[ki]
                        p = slots[ki]
                        if kind == "l":
                            Vl = V[:, idx, :]
                            kr = [("V", idx)]
                        else:
                            Vl = Vc[:, idx, :]
                            kr = ["Vc"]
                        pSt, pSk = sbanks[ob][p]
                        PTt = PT[ob * 2 + p]
                        c.op("act", lambda: nc.scalar.activation(PTt[:, :N], pSt[:, :N], AF.Exp, scale=SCALE),
                             [pSk], [("PT", ob, p)])

                        def f():
                            nc.tensor.matmul(pO[ob][:, :N], lhsT=Vl, rhs=PTt[:, :N],
                                             start=(ki == 0), stop=(ki == nk_ - 1))
                            return nc.tensor.matmul(pL[ob][:, :N], lhsT=onesb[:], rhs=PTt[:, :N],
                                                    start=(ki == 0), stop=(ki == nk_ - 1))
                        c.op("pe", f, kr + [("PT", ob, p), "onesb"], [("pO", ob), ("pL", ob)])

                    smm(0)
                    yield
                    if nk_ > 1:
                        smm(1)
                        yield
                    for ki in range(nk_):
                        pv(ki)
                        if ki + 2 < nk_:
                            smm(ki + 2)
                        yield
                    r = rl[ob]
                    if L == 1:
                        r3 = r[:, :N].rearrange("p (g q) -> p g q", g=G)
                        l3 = pL[ob][:, :N].rearrange("p (g q) -> p g q", g=G)
                        c.op("dve", lambda: nc.vector.tensor_tensor(
                            out=r3, in0=l3, in1=esk[:, 4 * j:4 * j + 4].unsqueeze(2).to_broadcast([128, G, nq]),
                            op=ALU.add), [("pL", ob), "consts"], [("rl", ob)])
                        c.op("act", lambda: nc.scalar.activation(r[:, :N], r[:, :N], AF.Ln), [("rl", ob)], [("rl", ob)])
                    else:
                        c.op("act", lambda: nc.scalar.activation(r[:, :N], pL[ob][:, :N], AF.Ln), [("pL", ob)], [("rl", ob)])
                    c.op("act", lambda: nc.scalar.activation(r[:, :N], r[:, :N], AF.Exp, scale=-1.0),
                         [("rl", ob)], [("rl", ob)])
                    c.op("dve", lambda: nc.vector.tensor_tensor(out=r[:, :N], in0=pO[ob][:, :N], in1=r[:, :N],
                                                                op=ALU.mult), [("pO", ob), ("rl", ob)], [("rl", ob)])
                    okeys = [("oTh", hh, tbq) for hh in range(G)]
                    if G > 1:
                        o3 = oThg[:, :, qcols]
                        r3 = r[:, :N].rearrange("p (g q) -> p g q", g=G)
                    else:
                        o3 = oThg[:, 0, qcols]
                        r3 = r[:, :N]
                    c.op("pool", lambda: nc.gpsimd.tensor_tensor(out=o3, in0=r3, in1=o3, op=ALU.mult),
                         [("rl", ob)] + okeys, okeys)
                    yield

                for j in range(nkv):
                    wb, wk = getw()
                    c.dma("pool", KcT[:], ckT[L][:, j, :], writes=["KcT"])
                    c.dma("pool", Vc[:], cv[L][:, j, :].rearrange("(t p) d -> p t d", p=128), writes=["Vc"])
                    gens = [(lambda sl, tb=tb: k_task(sl, wb, wk, j, tb)) for tb in range(NTB)] + \
                           [(lambda sl, tt=tt: v_task(sl, wb, wk, j, tt)) for tt in range(NT)]
                    run_tasks(gens, NW)
                    for h0 in range(4 * j, 4 * j + 4, G):
                        gens = []
                        for hh in range(G):
                            wbq, wkq = getw()
                            for tb in range(NTB):
                                gens.append(lambda sl, wbq=wbq, wkq=wkq, hh=hh, tb=tb: g_task(sl, wbq, wkq, hh, tb))
                            for tb in range(NTB):
                                gens.append(lambda sl, wbq=wbq, wkq=wkq, hh=hh, tb=tb: q_task(sl, wbq, wkq, hh, tb))
                            if hh % 2 == 1 or G == 1:
                                run_tasks(gens, NW)
                                gens = []
                        blocks = []
                        if G == 1:
                            for (t0, n) in SEQS[:2]:
                                blocks.append((t0 * 128, 256, [("l", t0, None), ("l", t0 + 1, None)], 0))
                            for tb in range(1, NTB):
                                keys = [("l", kt, None) for kt in range(4, NT)] + [("c", 0, None), ("c", 1, None)]
                                blocks.append((tb * 512, 512, keys, tb))
                        else:
                            for (t0, n) in SEQS[:2]:
                                for tq in range(t0, t0 + n):
                                    blocks.append((tq * 128, 128, [("l", t0, None), ("l", t0 + 1, None)], 0))
                            for i in range(16):
                                keys = []
                                if i > 0:
                                    keys.append(("l", 4 + i - 1, 0))
                                keys.append(("l", 4 + i, None))
                                if i < 15:
                                    keys.append(("l", 4 + i + 1, 1))
                                keys += [("c", 0, None), ("c", 1, None)]
                                blocks.append(((4 + i) * 128, 128, keys, (4 + i) // 4))
                        run_tasks([(lambda sl, blk=blk: attn_block(sl, j, *blk)) for blk in blocks], 2)
                        for hh in range(G):
                            r0 = (orow0 + h0 + hh) * 128
                            c.dma("sp", oTs[L][r0:r0 + 128, :], oThg[:, hh, :],
                                  reads=[("oTh", hh, tb) for tb in range(NTB)], writes=[("oTs", L, orow0 + h0 + hh)])
                c.barrier_all()

        def phase_hgrn():
            with ExitStack() as es:
                getw = make_stream(es, [(w_in[0], h * 640, 640) for h in range(8)], 640)
                maskF = sb(es, "maskF", [128, 512], F32)
                mfb = sb(es, "mfb", [128, 2, 128], F32)
                cm4 = sb(es, "cm4", [128, 4, 128], BF16)
                onc = sb(es, "onc", [128, 1], F32)
                lbe = sb(es, "lbe", [128, 3, 16], F32)
                lbs = sb(es, "lbs", [128, 16], F32)
                lbv = sb(es, "lbv", [128, 16], F32)
                oml = sb(es, "oml", [128, 16], F32)
                noml = sb(es, "noml", [128, 16], F32)
                q32 = [sb(es, f"q32_{i}", [128, 512], F32) for i in range(2)]
                sg = [sb(es, f"sg_{i}", [128, 512], F32) for i in range(2)]
                lg = [sb(es, f"lg_{i}", [128, 512], F32) for i in range(2)]
                k32 = [sb(es, f"k32_{i}", [128, 512], F32) for i in range(2)]
                bF = [sb(es, f"bF_{i}", [128, 512], F32) for i in range(2)]
                totc = [sb(es, f"totc_{i}", [128, 16], F32) for i in range(2)]
                dec = sb(es, "dec", [128, 2, 80], F32)
                qd = [sb(es, f"qd{d}", [128, T], BF16) for d in range(2)]
                ki = [sb(es, f"ki{d}", [128, T], BF16) for d in range(2)]
                keT = [sb(es, f"keT{d}", [128, T], BF16) for d in range(2)]
                sgT = sb(es, "sgT", [128, T], BF16)
                V = sb(es, "Va", [128, NT, 128], BF16)
                OT = sb(es, "OT", [128, T], F32)
                kend = [[sb(es, f"kend{d}{r}", [128, 128], BF16) for r in range(2)] for d in range(2)]
                Vm = [[sb(es, f"Vm{d}{r}", [128, 4, 128], BF16) for r in range(2)] for d in range(2)]
                Am = [[sb(es, f"Am{d}{r}", [128, 128], BF16) for r in range(2)] for d in range(2)]
                S32 = [[sb(es, f"S32_{d}{r}", [128, 128], F32) for r in range(2)] for d in range(2)]
                Sbf = [[sb(es, f"Sbf{d}{p}", [128, 128], BF16) for p in range(4)] for d in range(2)]
                fb = [ps(es, f"hfb{i}", [128, 512]) for i in range(2)]
                bA = [ps(es, f"hbA{d}", [128, 512]) for d in range(2)]
                pO = [ps(es, f"hpO{d}", [128, 512]) for d in range(2)]
                ptr = [ps(es, f"hptr{d}", [128, 8, 128], BF16) for d in range(2)]

                c.dma("sp", maskF[:], maskF_d, writes=["hc"])
                c.dma("sp", mfb[:], mfb_d, writes=["hc"])
                c.dma("pool", cm4[:], cm4_d, writes=["hc"])
                c.dma("sp", onc[:], onorm_d, writes=["hc"])
                c.dma("sp", lbe[:], lbg, writes=["lbe"])
                c.op("act", lambda: nc.scalar.activation(lbe[:], lbe[:], AF.Exp), ["lbe"], ["lbe"])
                c.op("dve", lambda: nc.vector.tensor_tensor(out=lbs[:], in0=lbe[:, 0, :], in1=lbe[:, 1, :], op=ALU.add),
                     ["lbe"], ["lbs"])
                c.op("dve", lambda: nc.vector.tensor_tensor(out=lbs[:], in0=lbs[:], in1=lbe[:, 2, :], op=ALU.add),
                     ["lbe", "lbs"], ["lbs"])
                c.op("dve", lambda: nc.vector.reciprocal(lbs[:], lbs[:]), ["lbs"], ["lbs"])
                c.op("dve", lambda: nc.vector.tensor_tensor(out=lbv[:], in0=lbe[:, 0, :], in1=lbs[:], op=ALU.mult),
                     ["lbe", "lbs"], ["lbv"])
                c.op("dve", lambda: nc.vector.tensor_scalar(out=oml[:], in0=lbv[:], scalar1=-1.0, scalar2=1.0,
                                                            op0=ALU.mult, op1=ALU.add), ["lbv"], ["oml"])
                c.op("dve", lambda: nc.vector.tensor_scalar(out=noml[:], in0=oml[:], scalar1=-1.0, scalar2=None,
                                                            op0=ALU.mult), ["oml"], ["noml", "hc"])

                def bc32(t, ncol=16):
                    return t.unsqueeze(2).to_broadcast([128, ncol, 32])

                def f_task(sl, h, wb, wk, tb, c0, n):
                    cols = slice(c0, c0 + n)
                    nch = n // 32
                    pjt, pjk = fb[sl], ("fb", sl)
                    Q, SG, LG, K, B, TC = q32[sl], sg[sl], lg[sl], k32[sl], bF[sl], totc[sl]
                    kq, ks, kl, kk, kb_, kt = ("q32", sl), ("sg", sl), ("lg", sl), ("k32", sl), ("bF", sl), ("totc", sl)
                    mm_group(pjt[:, :n], [(wb[:, k, 0:128], hT[:, k, cols]) for k in range(16)], [wk], [pjk])
                    yield
                    c.op("act", lambda: nc.scalar.activation(Q[:, :n], pjt[:, :n], AF.Silu), [pjk], [kq])
                    yield
                    for d in range(2):
                        i = d * 8 + h
                        mm_group(pjt[:, :n], [(wb[:, k, 128 * (1 + d):128 * (2 + d)], hT[:, k, cols]) for k in range(16)],
                                 [wk], [pjk])
                        yield
                        c.op("act", lambda: nc.scalar.activation(SG[:, :n], pjt[:, :n], AF.Sigmoid), [pjk], [ks])
                        yield
                        c.op("act", lambda: nc.scalar.activation(LG[:, :n], SG[:, :n], AF.Ln, scale=oml[:, i:i + 1],
                                                                 bias=lbv[:, i:i + 1]), [ks, "hc"], [kl])
                        c.op("dve", lambda: nc.vector.tensor_scalar(
                            out=K[:, :n], in0=SG[:, :n], scalar1=noml[:, i:i + 1], scalar2=oml[:, i:i + 1],
                            op0=ALU.mult, op1=ALU.add), [ks, "hc"], [kk])
                        yield
                        c.op("dve", lambda: nc.vector.tensor_tensor_scan(B[:, :n], maskF[:, :n], LG[:, :n], 0.0, ALU.mult, ALU.add),
                             [kl, "hc"], [kb_])
                        yield
                        tot = B[:, :n].rearrange("p (c t) -> p c t", t=32)[:, :, 31]
                        dslice = dec[:, d, c0 // 32:c0 // 32 + nch]
                        c.op("act", lambda: nc.scalar.activation(dslice, tot, AF.Exp), [kb_], [("dec", d, tb)])
                        if d == 1:
                            c.op("act", lambda: nc.scalar.copy(TC[:, :nch], tot), [kb_], [kt])
                            yield
                            b3 = B[:, :n].rearrange("p (c t) -> p c t", t=32)
                            c.op("dve", lambda: nc.vector.tensor_tensor(out=b3, in0=bc32(TC[:, :nch], nch), in1=b3, op=ALU.subtract),
                                 [kb_, kt], [kb_])
                            yield
                            c.op("pool", lambda: nc.gpsimd.tensor_tensor(out=B[:, :n], in0=B[:, :n], in1=LG[:, :n], op=ALU.add),
                                 [kb_, kl], [kb_])
                        yield
                        c.op("act", lambda: nc.scalar.activation(SG[:, :n], B[:, :n], AF.Exp), [kb_], [ks])
                        c.op("act", lambda: nc.scalar.activation(LG[:, :n], B[:, :n], AF.Exp, scale=-1.0), [kb_], [kl])
                        yield
                        c.op("dve", lambda: nc.vector.tensor_tensor(out=qd[d][:, cols], in0=Q[:, :n], in1=SG[:, :n], op=ALU.mult),
                             [kq, ks], [("qd", d, tb)])
                        yield
                        c.op("dve", lambda: nc.vector.tensor_tensor(out=LG[:, :n], in0=K[:, :n], in1=LG[:, :n], op=ALU.mult),
                             [kk, kl], [kl])
                        yield
                        c.op("pool", lambda: nc.gpsimd.tensor_copy(ki[d][:, cols], LG[:, :n]), [kl], [("ki", d, tb)])
                        k3 = LG[:, :n].rearrange("p (c t) -> p c t", t=32)
                        o3 = keT[d][:, cols].rearrange("p (c t) -> p c t", t=32)
                        c.op("pool", lambda: nc.gpsimd.tensor_tensor(out=o3, in0=k3, in1=bc32(dslice, nch), op=ALU.mult),
                             [kl, ("dec", d, tb)], [("keT", d, tb)])
                        yield
                    mm_group(pjt[:, :n], [(wb[:, k, 512:640], hT[:, k, cols]) for k in range(16)], [wk], [pjk])
                    yield
                    assert h == 0 or (h - 1, max(0, tb - 1)) in post_done, (h, tb)
                    c.op("act", lambda: nc.scalar.activation(sgT[:, cols], pjt[:, :n], AF.Silu), [pjk], [("sgT", tb)])
                    yield
                    for tt in range(c0 // 128, (c0 + n) // 128):
                        tcols = slice(tt * 128, (tt + 1) * 128)
                        mm_group(pjt[:, 0:128], [(hT[:, k, tcols], wb[:, k, 384:512]) for k in range(16)], [wk], [pjk])
                        yield
                        c.op("act", lambda: nc.scalar.copy(V[:, tt, :], pjt[:, 0:128]), [pjk], [("V", tt)])
                        yield

                BLK = [0, 0, 1, 1] + [2 + (t - 4) // 4 for t in range(4, NT)]
                post_done = set()
                ot_written = set()
                vctr = [0, 0]
                sctr = [0, 0]

                def sweep(d, h, si, t0, n):
                    cur = sctr[d] % 2
                    if si < 2:
                        c.op("dve", lambda: nc.vector.memset(S32[d][cur][:], 0.0), [], [("S32", d, cur)])
                    else:
                        c.dma("sp", S32[d][cur][:], st0[d, h], writes=[("S32", d, cur)])
                    sb0 = sctr[d] % 4
                    c.op("act", lambda: nc.scalar.copy(Sbf[d][sb0][:], S32[d][cur][:]),
                         [("S32", d, cur)], [("Sbf", d, sb0)])
                    yield
                    tiles = range(t0, t0 + n) if d == 0 else range(t0 + n - 1, t0 - 1, -1)
                    for tt in tiles:
                        tb = BLK[tt]
                        tcols = slice(tt * 128, (tt + 1) * 128)
                        r = vctr[d] % 2
                        vctr[d] += 1
                        c.op("pe", lambda: nc.tensor.transpose(ptr[d][:, 0, :], keT[d][:, tcols], identb[:]),
                             [("keT", d, tb), "identb"], [("ptr", d)])
                        c.op("pool", lambda: nc.gpsimd.tensor_tensor(
                            out=Vm[d][r][:], in0=V[:, tt, :].unsqueeze(1).to_broadcast([128, 4, 128]), in1=cm4[:],
                            op=ALU.mult), [("V", tt), "hc"], [("Vm", d, r)])
                        yield
                        c.op("act", lambda: nc.scalar.copy(kend[d][r][:], ptr[d][:, 0, :]), [("ptr", d)], [("kend", d, r)])
                        c.op("pe", lambda: nc.tensor.matmul(bA[d][:, 0:128], lhsT=ki[d][:, tcols], rhs=qd[d][:, tcols],
                                                            start=True, stop=True),
                             [("ki", d, tb), ("qd", d, tb)], [("bA", d)])
                        yield
                        c.op("dve", lambda: nc.vector.tensor_tensor(out=Am[d][r][:], in0=bA[d][:, 0:128], in1=mfb[:, d, :],
                                                                    op=ALU.mult), [("bA", d), "hc"], [("Am", d, r)])

                        def umm():
                            inst = None
                            for j in range(4):
                                inst = nc.tensor.matmul(fb[d][:, j * 128:(j + 1) * 128], lhsT=kend[d][r][:],
                                                        rhs=Vm[d][r][:, j, :], start=True, stop=True)
                            return inst
                        c.op("pe", umm, [("kend", d, r), ("Vm", d, r)], [("fb", d)])
                        yield
                        c.op("pe", lambda: nc.tensor.matmul(pO[d][:, 0:128], lhsT=V[:, tt, :], rhs=Am[d][r][:],
                                                            start=True, stop=False),
                             [("V", tt), ("Am", d, r)], [("pO", d)])
                        yield
                        order = range(4) if d == 0 else range(3, -1, -1)
                        for n_, j in enumerate(order):
                            cur = sctr[d] % 2
                            nxt = 1 - cur
                            sbc = sctr[d] % 4
                            sbn = (sctr[d] + 1) % 4
                            ccols = slice(tt * 128 + 32 * j, tt * 128 + 32 * j + 32)
                            gch = tt * 4 + j
                            c.op("pe", lambda: nc.tensor.matmul(
                                pO[d][:, 32 * j:32 * j + 32], lhsT=Sbf[d][sbc][:], rhs=qd[d][:, ccols],
                                start=False, stop=(n_ == 3)),
                                [("Sbf", d, sbc), ("qd", d, tb)], [("pO", d)])
                            c.op("dve", lambda: nc.vector.scalar_tensor_tensor(
                                out=S32[d][nxt][:], in0=S32[d][cur][:], scalar=dec[:, d, gch:gch + 1],
                                in1=fb[d][:, j * 128:(j + 1) * 128], op0=ALU.mult, op1=ALU.add),
                                [("S32", d, cur), ("dec", d, tb), ("fb", d)], [("S32", d, nxt)])
                            yield
                            c.op("act", lambda: nc.scalar.copy(Sbf[d][sbn][:], S32[d][nxt][:]),
                                 [("S32", d, nxt)], [("Sbf", d, sbn)])
                            sctr[d] += 1
                            yield
                        if tt not in ot_written:
                            ot_written.add(tt)
                            c.op("act", lambda: nc.scalar.copy(OT[:, tcols], pO[d][:, 0:128]), [("pO", d)], [("OT", tt)])
                        else:
                            c.op("dve", lambda: nc.vector.tensor_tensor(out=OT[:, tcols], in0=pO[d][:, 0:128],
                                                                        in1=OT[:, tcols], op=ALU.add),
                                 [("pO", d), ("OT", tt)], [("OT", tt)])
                        yield
                    if si < 2:
                        fin = sctr[d] % 2
                        c.dma("sp", nstate[si, d, h], S32[d][fin][:], reads=[("S32", d, fin)],
                              writes=[("nstate", si, d, h)])

                def n_task(sl, h, tb):
                    cols = slice(tb * 512, (tb + 1) * 512)
                    ok = [("OT", tt) for tt in range(tb * 4, tb * 4 + 4)]
                    SQ, SD = q32[sl], sg[sl]
                    kq, ks = ("q32", sl), ("sg", sl)
                    pjt, pjk = fb[sl], ("fb", sl)
                    c.op("act", lambda: nc.scalar.activation(SQ[:], OT[:, cols], AF.Square), ok, [kq])
                    yield
                    c.op("pe", lambda: nc.tensor.matmul(pjt[:], lhsT=onesf[:], rhs=SQ[:], start=True, stop=True),
                         [kq, "onesf"], [pjk])
                    yield
                    c.op("act", lambda: nc.scalar.activation(SD[:], pjt[:], AF.Ln, bias=epsc[:]), [pjk, "epsc"], [ks])
                    yield
                    c.op("act", lambda: nc.scalar.activation(SD[:], SD[:], AF.Exp, scale=-0.5), [ks], [ks])
                    yield
                    c.op("dve", lambda: nc.vector.scalar_tensor_tensor(
                        out=SQ[:], in0=OT[:, cols], scalar=onc[:, 0:1], in1=SD[:], op0=ALU.mult, op1=ALU.mult),
                        ok + [ks, "hc"], [kq])
                    yield
                    ostv = k32[sl][:].bitcast(BF16)[:, 0:512]
                    c.op("pool", lambda: nc.gpsimd.tensor_tensor(out=ostv, in0=SQ[:], in1=sgT[:, cols], op=ALU.mult),
                         [kq] + [("sgT", bb_) for bb_ in sorted(set(BLK[tb * 4:tb * 4 + 4]))], [("k32", sl)])
                    c.dma("sp", oTs[0][h * 128:(h + 1) * 128, cols], ostv, reads=[("k32", sl)],
                          writes=[("oTs", 0, h)])
                    post_done.add((h, tb))
                    yield

                FB = [(2, 512, 512), (3, 1024, 512), (4, 1536, 512), (5, 2048, 512), (0, 0, 256), (1, 256, 256)]
                for h in range(8):
                    wb, wk = getw()
                    tasks = []
                    if h > 0:
                        tasks += [(lambda sl, tb=tb, hp=h - 1: n_task(sl, hp, tb)) for tb in range(NTB)]
                    tasks += [(lambda sl, fb_=fb_: f_task(sl, h, wb, wk, *fb_)) for fb_ in FB]
                    run_tasks(tasks, 2)
                    ot_written.clear()
                    for si, (t0, n) in enumerate(SEQS):
                        run_tasks([(lambda sl, d=d: sweep(d, h, si, t0, n)) for d in range(2)], 2)
                run_tasks([(lambda sl, tb=tb: n_task(sl, 7, tb)) for tb in range(NTB)], 2)
                c.barrier_all()

        def phase_out(L):
            NB = 4
            with ExitStack() as es:
                getw = make_stream(es, [(w_out[L], cb * 512, 512) for cb in range(4)], 512)
                gtb = sb(es, "gtb", [128, 2, D], F32)
                xr = [sb(es, f"xr{i}", [128, 512], F32) for i in range(NB)]
                tm = [sb(es, f"otm{i}", [128, 512], F32) for i in range(NB)]
                po = [ps(es, f"po{i}", [128, 512]) for i in range(2)]
                for g in range(2):
                    c.dma("act", gtb[:, g, :], gts[L * 2 + g:L * 2 + g + 1, :].partition_broadcast(128),
                          reads=[("gts", L, g, cb) for cb in range(4)], writes=[("gtb", g)])
                for k in range(16):
                    c.dma("sp" if k % 2 == 0 else "act", hT[:, k, :], oTs[L][k * 128:(k + 1) * 128, :],
                          reads=[("oTs", L, k)], writes=[("hTk", k)])
                it = 0
                for cb in range(4):
                    ccols = slice(cb * 512, (cb + 1) * 512)
                    wb, wk = getw()
                    for tt in range(NT):
                        b = it % NB
                        pb = it % 2
                        it += 1
                        g = 0 if tt < 4 else 1
                        rows = slice(tt * 128, (tt + 1) * 128)
                        tcols = slice(tt * 128, (tt + 1) * 128)
                        if L == 0:
                            c.dma("sp", xr[b][:], x[rows, ccols], writes=[("xr", b)])
                        else:
                            c.dma("sp", xr[b][:], y1[rows, ccols], reads=[("y1", tt, cb)], writes=[("xr", b)])
                        mm_group(po[pb][:], [(hT[:, k, tcols], wb[:, k, 0:512]) for k in range(16)],
                                 [wk] + [("hTk", k) for k in range(16)], [("po", pb)])
                        c.op("dve", lambda: nc.vector.tensor_tensor(
                            out=tm[b][:], in0=po[pb][:], in1=gtb[:, g, ccols], op=ALU.mult),
                            [("po", pb), ("gtb", g)], [("otm", b)])
                        c.op("pool", lambda: nc.gpsimd.tensor_tensor(out=tm[b][:], in0=tm[b][:], in1=xr[b][:],
                                                                     op=ALU.add),
                             [("otm", b), ("xr", b)], [("otm", b)])
                        if L == 0:
                            c.dma("act", y1[rows, ccols], tm[b][:], reads=[("otm", b)], writes=[("y1", tt, cb)])
                        else:
                            c.dma("act", y[rows, ccols], tm[b][:], reads=[("otm", b)], writes=[("y", tt, cb)])
                c.barrier_all()

        plist = [("mod0", lambda: phase_mod_h(0)), ("hgrn", phase_hgrn), ("attn0", lambda: phase_attn(0)),
                 ("out0", lambda: phase_out(0)), ("mod1", lambda: phase_mod_h(1)), ("attn1", lambda: phase_attn(1)),
                 ("out1", lambda: phase_out(1))]
        for nm, fn in plist:
            if phases is None or nm in phases:
                fn()
        c.finish()
    return nc


def _consts():
    ident = np.eye(128, dtype=np.float32)
    maskF = np.ones((128, 512), np.float32)
    maskF[:, ::32] = 0.0
    s = np.arange(128)[:, None]
    t = np.arange(128)[None, :]
    same = (s // 32) == (t // 32)
    mfb = np.stack([(same & (s <= t)), (same & (s >= t))], axis=1).astype(np.float32)
    cm4 = np.zeros((128, 4, 128), np.float32)
    for j in range(4):
        cm4[32 * j:32 * j + 32, j, :] = 1.0
    R = np.zeros((128, 128), np.float32)
    for m in range(128):
        q = m // 32
        if q in (0, 2):
            R[m, m + 32] = -1.0
        else:
            R[m, m - 32] = 1.0
    RT = np.ascontiguousarray(R.T)
    n_tok = 2048
    row = (np.arange(n_tok) // 64).astype(np.float32)
    col = (np.arange(n_tok) % 64).astype(np.float32)
    inv = (10000.0 ** (-np.arange(32, dtype=np.float32) / 32)).astype(np.float32)
    ar = row[:, None] * inv
    ac = col[:, None] * inv
    ang = np.concatenate([ar, ar, ac, ac], axis=-1).astype(np.float32)
    cosT = np.ascontiguousarray(np.cos(ang).T.astype(np.float32))
    sinT = np.ascontiguousarray(np.sin(ang).T.astype(np.float32))
    b = np.arange(128)[:, None]
    a = np.arange(128)[None, :]
    NEG = -30000.0
    wbias = np.stack([np.where(b >= a, 0.0, NEG), np.where(b <= a, 0.0, NEG)], axis=1).astype(np.float32)
    wbias4 = np.ascontiguousarray(np.broadcast_to(wbias[:, :, None, :], (128, 2, 4, 128))).astype(np.float32)
    return dict(ident=ident, maskF=maskF, mfb=mfb, cm4=cm4, RT=RT, cosT=cosT, sinT=sinT, wbias4=wbias4)


def _perm_w0(w):
    cols = []
    for h in range(8):
        for base in (0, 1024, 2048, 3072, 4096):
            cols.append(np.arange(base + h * 128, base + (h + 1) * 128))
    for j in range(2):
        cols.append(np.arange(6144 + j * 128, 6144 + (j + 1) * 128))
        cols.append(np.arange(6400 + j * 128, 6400 + (j + 1) * 128))
    for h in range(8):
        cols.append(np.arange(5120 + h * 128, 5120 + (h + 1) * 128))
        cols.append(np.arange(6656 + h * 128, 6656 + (h + 1) * 128))
    return np.ascontiguousarray(w[:, np.concatenate(cols)])


def _perm_w1(w):
    cols = []
    for j in range(4):
        cols.append(np.arange(2048 + j * 128, 2048 + (j + 1) * 128))
        cols.append(np.arange(2560 + j * 128, 2560 + (j + 1) * 128))
    for h in range(16):
        cols.append(np.arange(h * 128, (h + 1) * 128))
        cols.append(np.arange(3072 + h * 128, 3072 + (h + 1) * 128))
    return np.ascontiguousarray(w[:, np.concatenate(cols)])


def _prep(x_prompt, x_sample, state_l0_hgrn, cache_l0_k, cache_l0_v, cache_l1_k, cache_l1_v,
           c, c_ctx, lb_gamma,
           l0_norm, l0_w_mod, l0_b_mod, l0_w_in, l0_w_out, l0_a_onorm, l0_b_qnorm, l0_b_knorm,
           l1_norm, l1_w_mod, l1_b_mod, l1_w_in, l1_w_out, l1_c_qnorm, l1_c_knorm, l1_c_sink):
    f = lambda a: np.ascontiguousarray(np.asarray(a, dtype=np.float32))
    x_prompt, x_sample = f(x_prompt), f(x_sample)
    consts = _consts()
    shared = dict(
        w_mod0=f(l0_w_mod), w_mod1=f(l1_w_mod),
        b_mod0=f(l0_b_mod).reshape(1, -1), b_mod1=f(l1_b_mod).reshape(1, -1),
        norm0=f(l0_norm).reshape(1, -1), norm1=f(l1_norm).reshape(1, -1),
        w_in0=_perm_w0(f(l0_w_in)), w_in1=_perm_w1(f(l1_w_in)),
        w_out0=f(l0_w_out), w_out1=f(l1_w_out),
        onorm=f(l0_a_onorm).reshape(128, 1),
        qn0=f(l0_b_qnorm).reshape(128, 1), kn0=f(l0_b_knorm).reshape(128, 1),
        qn1=f(l1_c_qnorm).reshape(128, 1), kn1=f(l1_c_knorm).reshape(128, 1),
        sink=f(l1_c_sink).reshape(1, 16),
        lbg=np.ascontiguousarray(f(lb_gamma).reshape(3, 2, 8, 128).transpose(3, 0, 1, 2).reshape(128, 3, 16)),
        **consts,
    )
    c = f(c)
    c_ctx = f(c_ctx)
    in_maps = []
    for i in range(8):
        m = dict(shared)
        m["x"] = np.ascontiguousarray(np.concatenate(
            [x_prompt[2 * i], x_prompt[2 * i + 1], x_sample[i]], axis=0))
        cr = np.stack([c_ctx, c[i]], axis=0)
        m["crows"] = np.ascontiguousarray(cr.reshape(2, 16, 128).transpose(2, 0, 1))
        m["st0"] = f(state_l0_hgrn[i])
        m["ck0T"] = np.ascontiguousarray(f(cache_l0_k[i]).transpose(2, 1, 0))
        m["cv0"] = f(cache_l0_v[i])
        m["ck1T"] = np.ascontiguousarray(f(cache_l1_k[i]).transpose(2, 1, 0))
        m["cv1"] = f(cache_l1_v[i])
        in_maps.append(m)
    return in_maps


def kernel(**inputs):
    in_maps = _prep(**inputs)
    nc = build_program()
    res = run_bass_kernel_spmd(nc, in_maps, core_ids=list(range(8)))
    r = res.results
    y_prompt = np.stack([r[i // 2]["y"][(i % 2) * 256:(i % 2 + 1) * 256] for i in range(16)], axis=0)
    y_sample = np.stack([r[i]["y"][512:] for i in range(8)], axis=0)
    nstate = np.concatenate([r[i]["nstate"] for i in range(8)], axis=0)
    outs = [y_prompt.astype(np.float32), y_sample.astype(np.float32), nstate.astype(np.float32)]
    for nm in ("nk0", "nv0", "nk1", "nv1"):
        a = np.concatenate([r[i][nm].reshape(2, 256, r[i][nm].shape[1], 128) for i in range(8)], axis=0)
        outs.append(a.astype(np.float32))
    return tuple(outs)
```

```python
import math
import os
from contextlib import ExitStack

import numpy as np
import concourse.bass as bass
import concourse.mybir as mybir
from concourse.bass_utils import run_bass_kernel_spmd

F32 = mybir.dt.float32
BF16 = mybir.dt.bfloat16
AF = mybir.ActivationFunctionType
ALU = mybir.AluOpType

D = 2048
T = 2560
NT = 20
NTB = 5
EPS = 1e-6
SCALE = 1.0 / math.sqrt(128.0)
SEQS = [(0, 2), (2, 2), (4, 16)]


class Ctx:
    NDMA = 32

    def __init__(self, nc):
        self.nc = nc
        self.engs = {"pe": nc.tensor, "dve": nc.vector, "act": nc.scalar,
                     "pool": nc.gpsimd, "sp": nc.sync}
        self.sem = {k: nc.alloc_semaphore(name="s_" + k) for k in self.engs}
        self.cnt = {k: 0 for k in self.engs}
        self.waited = {k: {} for k in self.engs}
        self.last_w = {}
        self.readers = {}
        self.dma_sems = [nc.alloc_semaphore(name=f"s_dma{i}") for i in range(self.NDMA)]
        self.dma_val = [0] * self.NDMA
        self.dma_pool = {"sp": list(range(0, 16)), "pool": list(range(16, 24)), "act": list(range(24, 32))}
        self.dma_rr = {"sp": 0, "pool": 0, "act": 0}

    def _deps(self, reads, writes):
        evs = []
        for r in reads:
            e = self.last_w.get(r)
            if e is not None:
                evs.append(e)
        for w in writes:
            e = self.last_w.get(w)
            if e is not None:
                evs.append(e)
            evs.extend(self.readers.get(w, ()))
        return evs

    def _wait(self, eng, evs, skip_self=False):
        best = {}
        for (name, sem, val) in evs:
            if skip_self and name == eng:
                continue
            if best.get(name, (None, 0))[1] < val:
                best[name] = (sem, val)
        wd = self.waited[eng]
        for name, (sem, val) in best.items():
            if wd.get(name, 0) < val:
                self.engs[eng].wait_ge(sem, val)
                wd[name] = val

    def _commit(self, ev, reads, writes):
        ws = set(writes)
        for r in reads:
            if r in ws:
                continue
            self.readers.setdefault(r, []).append(ev)
        for w in writes:
            self.last_w[w] = ev
            self.readers[w] = []

    EXCL = {"pj", "pms", "prot", "pv", "pS", "pO", "pL", "pm", "pT", "bA", "ptr", "po", "fb"}

    def op(self, eng, fn, reads=(), writes=()):
        reads = list(reads)
        writes = list(writes)
        ex = [r for r in reads if (r if isinstance(r, str) else r[0]) in self.EXCL]
        if ex:
            reads = [r for r in reads if r not in ex]
            writes = writes + [r for r in ex if r not in writes]
        evs = self._deps(reads, writes)
        self._wait(eng, evs, skip_self=(eng == "pe"))
        inst = fn()
        inst.then_inc(self.sem[eng], 1)
        self.cnt[eng] += 1
        ev = (eng, self.sem[eng], self.cnt[eng])
        self._commit(ev, reads, writes)
        return ev

    def dma(self, q, out, in_, reads=(), writes=()):
        reads = list(reads)
        writes = list(writes)
        evs = self._deps(reads, writes)
        lst = self.dma_pool[q]
        k = lst[self.dma_rr[q] % len(lst)]
        self.dma_rr[q] += 1
        name = f"dma{k}"
        if self.dma_val[k] > 0:
            evs.append((name, self.dma_sems[k], self.dma_val[k]))
        self._wait(q, evs)
        self.engs[q].dma_start(out=out, in_=in_).then_inc(self.dma_sems[k], 16)
        self.dma_val[k] += 16
        ev = (name, self.dma_sems[k], self.dma_val[k])
        self._commit(ev, reads, writes)
        return ev

    def all_events(self):
        evs = [(k, self.sem[k], self.cnt[k]) for k in self.engs if self.cnt[k] > 0]
        for i in range(self.NDMA):
            if self.dma_val[i] > 0:
                evs.append((f"dma{i}", self.dma_sems[i], self.dma_val[i]))
        return evs

    def barrier_all(self):
        evs = self.all_events()
        for e in self.engs:
            self._wait(e, evs, skip_self=True)

    def finish(self):
        self._wait("sp", self.all_events(), skip_self=True)


def build_program(phases=None):
    nc = bass.Bass("TRN2", target_bir_lowering=False)

    def din(name, shape, dt=F32):
        return nc.dram_tensor(name, list(shape), dt, kind="ExternalInput").ap()

    def dout(name, shape, dt=F32):
        return nc.dram_tensor(name, list(shape), dt, kind="ExternalOutput").ap()

    def dscr(name, shape, dt=F32):
        return nc.dram_tensor(name, list(shape), dt, kind="Internal").ap()

    x = din("x", [T, D])
    crows = din("crows", [128, 2, 16])
    lbg = din("lbg", [128, 3, 16])
    st0 = din("st0", [2, 8, 128, 128])
    ckT = [din("ck0T", [128, 2, 256]), din("ck1T", [128, 4, 256])]
    cv = [din("cv0", [256, 2, 128]), din("cv1", [256, 4, 128])]
    w_mod = [din("w_mod0", [D, 3 * D]), din("w_mod1", [D, 3 * D])]
    b_mod = [din("b_mod0", [1, 3 * D]), din("b_mod1", [1, 3 * D])]
    norm_g = [din("norm0", [1, D]), din("norm1", [1, D])]
    w_in = [din("w_in0", [D, 7680]), din("w_in1", [D, 5120])]
    w_out = [din("w_out0", [D, D]), din("w_out1", [D, D])]
    onorm_d = din("onorm", [128, 1])
    qn_d = [din("qn0", [128, 1]), din("qn1", [128, 1])]
    kn_d = [din("kn0", [128, 1]), din("kn1", [128, 1])]
    sink_d = din("sink", [1, 16])
    ident_d = din("ident", [128, 128])
    maskF_d = din("maskF", [128, 512])
    mfb_d = din("mfb", [128, 2, 128])
    cm4_d = din("cm4", [128, 4, 128])
    RT_d = din("RT", [128, 128])
    cosT_d = din("cosT", [128, 2048])
    sinT_d = din("sinT", [128, 2048])
    wbias4_d = din("wbias4", [128, 2, 4, 128])

    y = dout("y", [T, D])
    nstate = dout("nstate", [2, 2, 8, 128, 128])
    nk = [dout("nk0", [512, 2, 128]), dout("nk1", [512, 4, 128])]
    nv = [dout("nv0", [512, 2, 128]), dout("nv1", [512, 4, 128])]

    gts = dscr("gts", [4, D])
    oTs = [dscr("oT0", [D, T], BF16), dscr("oT1", [D, T], BF16)]
    y1 = dscr("y1", [T, D])

    c = Ctx(nc)

    uid = [0]

    def sb(es, name, shape, dt):
        uid[0] += 1
        return es.enter_context(nc.sbuf_tensor(f"{name}_{uid[0]}", list(shape), dt))

    def ps(es, name, shape, dt=F32):
        uid[0] += 1
        return es.enter_context(nc.psum_tensor(f"{name}_{uid[0]}", list(shape), dt))

    def mm_group(out, pairs, reads, writes):
        def f():
            n = len(pairs)
            inst = None
            for i, (l, r) in enumerate(pairs):
                inst = nc.tensor.matmul(out, lhsT=l, rhs=r, start=(i == 0), stop=(i == n - 1))
            return inst
        return c.op("pe", f, reads, writes)

    def wview(w, c0, ncols):
        return w[:, c0:c0 + ncols].rearrange("(k p) n -> p k n", p=128)

    with ExitStack() as top:
        identb = sb(top, "identb", [128, 128], BF16)
        identf = sb(top, "identf", [128, 128], F32)
        onesb = sb(top, "onesb", [128, 128], BF16)
        onesf = sb(top, "onesf", [128, 128], F32)
        epsc = sb(top, "epsc", [128, 1], F32)
        hT = sb(top, "hT", [128, 16, T], BF16)

        c.dma("sp", identf[:], ident_d, writes=["identf"])
        c.dma("pool", identb[:], ident_d, writes=["identb"])
        c.op("dve", lambda: nc.vector.memset(onesb[:], 1.0), writes=["onesb"])
        c.op("dve", lambda: nc.vector.memset(onesf[:], 1.0 / 128.0), writes=["onesf"])
        c.op("dve", lambda: nc.vector.memset(epsc[:], EPS), writes=["epsc"])

        def make_stream(es, blocks, ncols_max, nslots=2):
            bufs = [sb(es, f"wbuf{i}", [128, 16, ncols_max], BF16) for i in range(nslots)]
            st = {"issued": 0, "got": 0}

            def issue():
                i = st["issued"]
                if i >= len(blocks):
                    return
                w, c0, ncols = blocks[i]
                s = i % nslots
                c.dma("pool", bufs[s][:, :, 0:ncols], wview(w, c0, ncols), writes=[("wb", s)])
                st["issued"] += 1

            def get():
                i = st["got"]
                while st["issued"] <= i:
                    issue()
                st["got"] += 1
                if st["issued"] <= i + 1:
                    issue()
                s = i % nslots
                return bufs[s], ("wb", s)
            return get

        hkeys = lambda tb: [("hT", tt) for tt in range(tb * 4, tb * 4 + 4)]
        allh = [("hT", tt) for tt in range(NT)]

        def phase_mod_h(L):
            with ExitStack() as es:
                G = sb(es, "G", [128, 2, D], F32)
                SH = sb(es, "SH", [128, 2, D], F32)
                with ExitStack() as es2:
                    getw = make_stream(es2, [(w_mod[L], blk * 512, 512) for blk in range(12)], 512)
                    crs = sb(es2, "crs", [128, 2, 16], F32)
                    scl = sb(es2, "scl", [128, 2, 16], F32)
                    srep = sb(es2, "srep", [128, 2, 16, 128], BF16)
                    bb = [sb(es2, f"bb{i}", [128, 512], F32) for i in range(2)]
                    gb = [sb(es2, f"gb{i}", [128, 512], F32) for i in range(2)]
                    tmp = [sb(es2, f"mtmp{i}", [128, 512], F32) for i in range(2)]
                    pm = [ps(es2, f"pm{i}", [128, 512]) for i in range(2)]
                    c.dma("sp", crs[:], crows, writes=["crs"])
                    c.op("act", lambda: nc.scalar.activation(scl[:], crs[:], AF.Silu), ["crs"], ["scl"])
                    c.op("dve", lambda: nc.vector.tensor_copy(
                        srep[:], scl[:].unsqueeze(3).to_broadcast([128, 2, 16, 128])), ["scl"], ["srep"])
                    for blk in range(12):
                        kind, cb = divmod(blk, 4)
                        b = blk % 2
                        wb, wk = getw()
                        c.dma("sp", bb[b][:], b_mod[L][:, blk * 512:(blk + 1) * 512].partition_broadcast(128),
                              writes=[("bb", b)])
                        if kind == 1:
                            c.dma("sp", gb[b][:], norm_g[L][:, cb * 512:(cb + 1) * 512].partition_broadcast(128),
                                  writes=[("gb", b)])
                        cols = slice(cb * 512, (cb + 1) * 512)
                        for g in range(2):
                            mm_group(pm[g][:], [(srep[:, g, k, :], wb[:, k, 0:512]) for k in range(16)],
                                     [wk, "srep"], [("pm", g)])
                            if kind == 0:
                                c.op("dve", lambda g=g, b=b, cols=cols: nc.vector.tensor_tensor(
                                    out=SH[:, g, cols], in0=pm[g][:], in1=bb[b][:], op=ALU.add),
                                    [("pm", g), ("bb", b)], [("SH", g, cb)])
                            elif kind == 1:
                                c.op("dve", lambda g=g, b=b: nc.vector.tensor_tensor(
                                    out=tmp[g][:], in0=pm[g][:], in1=bb[b][:], op=ALU.add),
                                    [("pm", g), ("bb", b)], [("mtmp", g)])
                                c.op("dve", lambda g=g, b=b, cols=cols: nc.vector.scalar_tensor_tensor(
                                    out=G[:, g, cols], in0=tmp[g][:], scalar=1.0, in1=gb[b][:],
                                    op0=ALU.add, op1=ALU.mult),
                                    [("mtmp", g), ("gb", b)], [("G", g, cb)])
                            else:
                                c.op("dve", lambda g=g, b=b: nc.vector.tensor_tensor(
                                    out=tmp[g][:], in0=pm[g][:], in1=bb[b][:], op=ALU.add),
                                    [("pm", g), ("bb", b)], [("mtmp", g)])
                                c.dma("sp", gts[L * 2 + g:L * 2 + g + 1, cols], tmp[g][0:1, :],
                                      reads=[("mtmp", g)], writes=[("gts", L, g, cb)])
                    c.barrier_all()
                with ExitStack() as es2:
                    NS = 3
                    xt = [sb(es2, f"xt{i}", [128, D], F32) for i in range(NS)]
                    junk = sb(es2, "junk", [128, D], BF16)
                    st = sb(es2, "st", [128, 3 * NS], F32)
                    t1 = [sb(es2, f"t1_{i}", [128, D], F32) for i in range(NS)]
                    hb = [sb(es2, f"hb{i}", [128, D], BF16) for i in range(NS)]
                    pT = [ps(es2, f"pT{i}", [128, 8, 128], BF16) for i in range(2 * NS)]
                    GK = [[("G", g, cb) for cb in range(4)] for g in range(2)]
                    SK = [[("SH", g, cb) for cb in range(4)] for g in range(2)]

                    def h_task(b, tt):
                        g = 0 if tt < 4 else 1
                        rows = slice(tt * 128, (tt + 1) * 128)
                        ssq, std, rstd = st[:, 3 * b:3 * b + 1], st[:, 3 * b + 1:3 * b + 2], st[:, 3 * b + 2:3 * b + 3]
                        if L == 0:
                            c.dma("sp", xt[b][:], x[rows, :], writes=[("xt", b)])
                        else:
                            c.dma("sp", xt[b][:], y1[rows, :], reads=[("y1", tt, cb) for cb in range(4)],
                                  writes=[("xt", b)])
                        yield
                        c.op("act", lambda: nc.scalar.activation(junk[:], xt[b][:], AF.Square, accum_out=ssq),
                             [("xt", b)], ["junk", ("ssq", b)])
                        yield
                        c.op("act", lambda: nc.scalar.activation(std, ssq, AF.Sqrt, scale=1.0 / D, bias=epsc[:]),
                             [("ssq", b), "epsc"], [("std", b)])
                        yield
                        c.op("dve", lambda: nc.vector.reciprocal(rstd, std), [("std", b)], [("rstd", b)])
                        yield
                        c.op("dve", lambda: nc.vector.scalar_tensor_tensor(
                            out=t1[b][:], in0=xt[b][:], scalar=rstd, in1=G[:, g, :], op0=ALU.mult, op1=ALU.mult),
                            [("xt", b), ("rstd", b)] + GK[g], [("t1", b)])
                        yield
                        c.op("pool", lambda: nc.gpsimd.tensor_tensor(
                            out=hb[b][:, 0:1408], in0=t1[b][:, 0:1408], in1=SH[:, g, 0:1408], op=ALU.add),
                            [("t1", b)] + SK[g], [("hb", b, 0)])
                        c.op("dve", lambda: nc.vector.tensor_tensor(
                            out=hb[b][:, 1408:D], in0=t1[b][:, 1408:D], in1=SH[:, g, 1408:D], op=ALU.add),
                            [("t1", b)] + SK[g], [("hb", b, 1)])
                        yield
                        for half in range(2):
                            pp = pT[b * 2 + half]

                            def tr():
                                inst = None
                                for kk in range(8):
                                    k = half * 8 + kk
                                    inst = nc.tensor.transpose(pp[:, kk, :], hb[b][:, k * 128:(k + 1) * 128], identb[:])
                                return inst
                            c.op("pe", tr, [("hb", b, 0), ("hb", b, 1), "identb"], [("pT", b, half)])
                            yield
                            dst = hT[:, half * 8:(half + 1) * 8, tt * 128:(tt + 1) * 128]
                            if half == 0:
                                c.op("act", lambda: nc.scalar.copy(dst, pp[:]), [("pT", b, half)], [("hT", tt, half)])
                            else:
                                c.op("dve", lambda: nc.vector.tensor_copy(dst, pp[:]), [("pT", b, half)],
                                     [("hT", tt, half)])
                            yield

                    run_tasks([(lambda sl, tt=tt: h_task(sl, tt)) for tt in range(NT)], NS)
                    c.barrier_all()
                c.barrier_all()

        def run_tasks(factories, width):
            it = iter(factories)
            active = {}
            free = list(range(width))
            while True:
                while free:
                    f = next(it, None)
                    if f is None:
                        break
                    sl = free.pop(0)
                    active[sl] = f(sl)
                if not active:
                    break
                for sl in sorted(active):
                    try:
                        next(active[sl])
                    except StopIteration:
                        del active[sl]
                        free.append(sl)

        def phase_attn(L):
            nkv = 2 if L == 0 else 4
            G = 1 if L == 0 else 4
            if L == 0:
                kvc0, qc0, orow0 = 5120, 5120 + 512, 8
            else:
                kvc0, qc0, orow0 = 0, 1024, 0
            with ExitStack() as es:
                ablocks = []
                for j_ in range(nkv):
                    ablocks.append((w_in[L], kvc0 + j_ * 256, 256))
                    for h_ in range(4 * j_, 4 * j_ + 4):
                        ablocks.append((w_in[L], qc0 + h_ * 256, 256))
                getw = make_stream(es, ablocks, 256, nslots=3)

                T_sq = [sb(es, f"sq{i}", [128, 512], F32) for i in range(3)]
                T_sd = [sb(es, f"sd{i}", [128, 512], F32) for i in range(3)]
                T_kb = [sb(es, f"kb{i}", [128, 512], BF16) for i in range(3)]
                T_u = [sb(es, f"ru{i}", [128, 512], F32) for i in range(3)]
                cosT = sb(es, "cosT", [128, 2048], F32)
                sinT = sb(es, "sinT", [128, 2048], F32)
                RTb = sb(es, "RTb", [128, 128], BF16)
                gq = sb(es, "gq", [128, 1], F32)
                gk = sb(es, "gk", [128, 1], F32)
                esk = sb(es, "esk", [128, 16], F32)
                wb4 = sb(es, "wb4", [128, 2, 4, 128], BF16)
                KT = sb(es, "KT", [128, T], BF16)
                KcT = sb(es, "KcT", [128, 256], BF16)
                V = sb(es, "V", [128, NT, 128], BF16)
                Vc = sb(es, "Vc", [128, 2, 128], BF16)
                QTg = sb(es, "QTg", [128, G, T], BF16)
                oThg = sb(es, "oThg", [128, G, T], BF16)
                kout = [sb(es, f"kout{i}", [128, 128], F32) for i in range(2)]
                vout = [sb(es, f"vout{i}", [128, 128], F32) for i in range(3)]
                PT = [sb(es, f"PT{i}", [128, 512], BF16) for i in range(4)]
                rl = [sb(es, f"rl{i}", [128, 512], F32) for i in range(2)]
                pj = ps(es, "pj", [128, 512])
                pms = ps(es, "pms", [128, 512])
                pS = [ps(es, f"pS{i}", [128, 512]) for i in range(2)]
                pO = [ps(es, f"pO{i}", [128, 512]) for i in range(2)]
                pL = [ps(es, f"pL{i}", [128, 512]) for i in range(2)]
                pbank = [(pj, "pj"), (pS[0], ("pS", 0)), (pS[1], ("pS", 1))]
                rbank = pbank
                tbank = (pO[0], ("pO", 0))
                msbank = [(pms, "pms"), (pL[0], ("pL", 0)), (pL[1], ("pL", 1))]
                NW = 3

                c.dma("sp", cosT[:], cosT_d, writes=["consts"])
                c.dma("sp", sinT[:], sinT_d, writes=["consts"])
                c.dma("pool", RTb[:], RT_d, writes=["consts"])
                c.dma("sp", gq[:], qn_d[L], writes=["consts"])
                c.dma("sp", gk[:], kn_d[L], writes=["consts"])
                c.dma("pool", wb4[:], wbias4_d, writes=["consts"])
                c.dma("sp", esk[:], sink_d.partition_broadcast(128), writes=["esk0"])
                c.op("act", lambda: nc.scalar.activation(esk[:], esk[:], AF.Exp), ["esk0"], ["esk0", "consts"])

                def fn_task(s, pjt, pjk, gcol, rope_cols, out_bf, outkey, out_f32=None, f32key=None):
                    sq, sd, kb, uu_ = T_sq[s], T_sd[s], T_kb[s], T_u[s]
                    pmt, pmk = msbank[s]
                    c.op("act", lambda: nc.scalar.activation(sq[:], pjt, AF.Square), [pjk], [("sq", s)])
                    yield
                    c.op("pe", lambda: nc.tensor.matmul(pmt[:], lhsT=onesf[:], rhs=sq[:], start=True, stop=True),
                         [("sq", s), "onesf"], [pmk])
                    yield
                    c.op("act", lambda: nc.scalar.activation(sd[:], pmt[:], AF.Ln, bias=epsc[:]),
                         [pmk, "epsc"], [("sd", s)])
                    yield
                    c.op("act", lambda: nc.scalar.activation(sd[:], sd[:], AF.Exp, scale=-0.5), [("sd", s)], [("sd", s)])
                    yield
                    if out_f32 is None:
                        dst, dkey = sq[:], ("sq", s)
                    else:
                        dst, dkey = out_f32, (f32key or outkey + ("f32",))
                    c.op("dve", lambda: nc.vector.scalar_tensor_tensor(
                        out=dst, in0=pjt, scalar=gcol, in1=sd[:], op0=ALU.mult, op1=ALU.mult),
                        [pjk, ("sd", s), "consts"], [dkey])
                    yield
                    if rope_cols is None:
                        c.op("pool", lambda: nc.gpsimd.tensor_copy(out_bf, dst), [dkey], [outkey])
                        yield
                    else:
                        c.op("pool", lambda: nc.gpsimd.tensor_copy(kb[:], dst), [dkey], [("kb", s)])
                        yield
                        prt, prk = rbank[s]
                        c.op("pe", lambda: nc.tensor.matmul(prt[:], lhsT=RTb[:], rhs=kb[:], start=True, stop=True),
                             [("kb", s), "consts"], [prk])
                        yield
                        c.op("pool", lambda: nc.gpsimd.tensor_tensor(out=dst, in0=dst, in1=cosT[:, rope_cols],
                                                                     op=ALU.mult), [dkey, "consts"], [dkey])
                        yield
                        c.op("dve", lambda: nc.vector.tensor_tensor(out=uu_[:], in0=prt[:], in1=sinT[:, rope_cols],
                                                                    op=ALU.mult), [prk, "consts"], [("ru", s)])
                        yield
                        c.op("pool", lambda: nc.gpsimd.tensor_tensor(out=out_bf, in0=dst, in1=uu_[:], op=ALU.add),
                             [dkey, ("ru", s)], [outkey])
                        yield

                def rope_of(tb):
                    return None if tb == 0 else slice((tb - 1) * 512, tb * 512)

                def k_task(sl, wb, wk, j, tb):
                    cols = slice(tb * 512, (tb + 1) * 512)
                    pjt, pjk = pbank[sl]
                    mm_group(pjt[:], [(wb[:, k, 0:128], hT[:, k, cols]) for k in range(16)], [wk], [pjk])
                    yield
                    if tb == 0:
                        kf32 = T_u[sl]
                        yield from fn_task(sl, pjt[:], pjk, gk[:, 0:1], None, KT[:, cols], ("KT", tb), out_f32=kf32[:],
                                           f32key=("ru", sl))
                        for t4 in range(4):
                            b = t4 % 2
                            pvt, pvk = tbank
                            c.op("pe", lambda: nc.tensor.transpose(
                                pvt[:, 0:128], kf32[:, t4 * 128:(t4 + 1) * 128], identf[:]),
                                [("ru", sl), "identf"], [pvk])
                            yield
                            c.op("act", lambda: nc.scalar.copy(kout[b][:], pvt[:, 0:128]), [pvk], [("kout", b)])
                            yield
                            c.dma("sp", nk[L][t4 * 128:(t4 + 1) * 128, j, :], kout[b][:],
                                  reads=[("kout", b)], writes=[("nk", t4, j)])
                    else:
                        yield from fn_task(sl, pjt[:], pjk, gk[:, 0:1], rope_of(tb), KT[:, cols], ("KT", tb))

                def v_task(sl, wb, wk, j, tt):
                    tcols = slice(tt * 128, (tt + 1) * 128)
                    pvt, pvk = pbank[sl]
                    mm_group(pvt[:, 0:128], [(hT[:, k, tcols], wb[:, k, 128:256]) for k in range(16)], [wk], [pvk])
                    yield
                    if tt < 4:
                        b = sl
                        c.op("act", lambda: nc.scalar.copy(vout[b][:], pvt[:, 0:128]), [pvk], [("vout", b)])
                        yield
                        c.op("pool", lambda: nc.gpsimd.tensor_copy(V[:, tt, :], vout[b][:]), [("vout", b)], [("V", tt)])
                        c.dma("sp", nv[L][tt * 128:(tt + 1) * 128, j, :], vout[b][:],
                              reads=[("vout", b)], writes=[("nv", tt, j)])
                        yield
                    else:
                        c.op("act", lambda: nc.scalar.copy(V[:, tt, :], pvt[:, 0:128]), [pvk], [("V", tt)])
                        yield

                def q_task(sl, wb, wk, hh, tb):
                    cols = slice(tb * 512, (tb + 1) * 512)
                    pjt, pjk = pbank[sl]
                    mm_group(pjt[:], [(wb[:, k, 0:128], hT[:, k, cols]) for k in range(16)], [wk], [pjk])
                    yield
                    yield from fn_task(sl, pjt[:], pjk, gq[:, 0:1], rope_of(tb), QTg[:, hh, cols], ("QT", hh, tb))

                def g_task(sl, wb, wk, hh, tb):
                    cols = slice(tb * 512, (tb + 1) * 512)
                    pjt, pjk = pbank[sl]
                    mm_group(pjt[:], [(wb[:, k, 128:256], hT[:, k, cols]) for k in range(16)], [wk], [pjk])
                    yield
                    c.op("act", lambda: nc.scalar.activation(oThg[:, hh, cols], pjt[:], AF.Silu),
                         [pjk], [("oTh", hh, tb)])
                    yield

                sctr = [0, 0]
                sbanks = [[(pS[0], ("pS", 0)), (pS[1], ("pS", 1))], [(pj, "pj"), (pms, "pms")]]

                def attn_block(ob, j, q0, nq, keys, tbq):
                    qcols = slice(q0, q0 + nq)
                    N = G * nq
                    nk_ = len(keys)
                    rhsQ = QTg[:, :, qcols] if G > 1 else QTg[:, 0, qcols]
                    qkeys = [("QT", hh, tbq) for hh in range(G)]
                    slots = []

                    def smm(ki):
                        kind, idx, mi = keys[ki]
                        p = sctr[ob] % 2
                        sctr[ob] += 1
                        slots.append(p)
                        pSt, pSk = sbanks[ob][p]
                        if kind == "l":
                            Kl = KT[:, idx * 128:(idx + 1) * 128]
                            kr = [("KT", idx // 4)]
                        else:
                            Kl = KcT[:, idx * 128:(idx + 1) * 128]
                            kr = ["KcT"]

                        def f():
                            out = pSt[:, :N] if G == 1 else pSt[:, :N].rearrange("p (g q) -> p g q", g=G)
                            inst = nc.tensor.matmul(out, lhsT=Kl, rhs=rhsQ, start=True, stop=(mi is None))
                            if mi is not None:
                                inst = nc.tensor.matmul(out, lhsT=identb[:], rhs=wb4[:, mi, :, 0:nq],
                                                        start=False, stop=True)
                            return inst
                        c.op("pe", f, kr + qkeys + ["consts", "identb"], [pSk])

                    def pv(ki):
                        kind, idx, mi = keys[ki]
                        p = slots[ki]
                        if kind == "l":
                            Vl = V[:, idx, :]
                            kr = [("V", idx)]
                        else:
                            Vl = Vc[:, idx, :]
                            kr = ["Vc"]
                        pSt, pSk = sbanks[ob][p]
                        PTt = PT[ob * 2 + p]
                        c.op("act", lambda: nc.scalar.activation(PTt[:, :N], pSt[:, :N], AF.Exp, scale=SCALE),
                             [pSk], [("PT", ob, p)])

                        def f():
                            nc.tensor.matmul(pO[ob][:, :N], lhsT=Vl, rhs=PTt[:, :N],
                                             start=(ki == 0), stop=(ki == nk_ - 1))
                            return nc.tensor.matmul(pL[ob][:, :N], lhsT=onesb[:], rhs=PTt[:, :N],
                                                    start=(ki == 0), stop=(ki == nk_ - 1))
                        c.op("pe", f, kr + [("PT", ob, p), "onesb"], [("pO", ob), ("pL", ob)])

                    smm(0)
                    yield
                    if nk_ > 1:
                        smm(1)
                        yield
                    for ki in range(nk_):
                        pv(ki)
                        if ki + 2 < nk_:
                            smm(ki + 2)
                        yield
                    r = rl[ob]
                    if L == 1:
                        r3 = r[:, :N].rearrange("p (g q) -> p g q", g=G)
                        l3 = pL[ob][:, :N].rearrange("p (g q) -> p g q", g=G)
                        c.op("dve", lambda: nc.vector.tensor_tensor(
                            out=r3, in0=l3, in1=esk[:, 4 * j:4 * j + 4].unsqueeze(2).to_broadcast([128, G, nq]),
                            op=ALU.add), [("pL", ob), "consts"], [("rl", ob)])
                        c.op("act", lambda: nc.scalar.activation(r[:, :N], r[:, :N], AF.Ln), [("rl", ob)], [("rl", ob)])
                    else:
                        c.op("act", lambda: nc.scalar.activation(r[:, :N], pL[ob][:, :N], AF.Ln), [("pL", ob)], [("rl", ob)])
                    c.op("act", lambda: nc.scalar.activation(r[:, :N], r[:, :N], AF.Exp, scale=-1.0),
                         [("rl", ob)], [("rl", ob)])
                    c.op("dve", lambda: nc.vector.tensor_tensor(out=r[:, :N], in0=pO[ob][:, :N], in1=r[:, :N],
                                                                op=ALU.mult), [("pO", ob), ("rl", ob)], [("rl", ob)])
                    okeys = [("oTh", hh, tbq) for hh in range(G)]
                    if G > 1:
                        o3 = oThg[:, :, qcols]
                        r3 = r[:, :N].rearrange("p (g q) -> p g q", g=G)
                    else:
                        o3 = oThg[:, 0, qcols]
                        r3 = r[:, :N]
                    c.op("pool", lambda: nc.gpsimd.tensor_tensor(out=o3, in0=r3, in1=o3, op=ALU.mult),
                         [("rl", ob)] + okeys, okeys)
                    yield

                for j in range(nkv):
                    wb, wk = getw()
                    c.dma("pool", KcT[:], ckT[L][:, j, :], writes=["KcT"])
                    c.dma("pool", Vc[:], cv[L][:, j, :].rearrange("(t p) d -> p t d", p=128), writes=["Vc"])
                    gens = [(lambda sl, tb=tb: k_task(sl, wb, wk, j, tb)) for tb in range(NTB)] + \
                           [(lambda sl, tt=tt: v_task(sl, wb, wk, j, tt)) for tt in range(NT)]
                    run_tasks(gens, NW)
                    for h0 in range(4 * j, 4 * j + 4, G):
                        gens = []
                        for hh in range(G):
                            wbq, wkq = getw()
                            for tb in range(NTB):
                                gens.append(lambda sl, wbq=wbq, wkq=wkq, hh=hh, tb=tb: g_task(sl, wbq, wkq, hh, tb))
                            for tb in range(NTB):
                                gens.append(lambda sl, wbq=wbq, wkq=wkq, hh=hh, tb=tb: q_task(sl, wbq, wkq, hh, tb))
                            if hh % 2 == 1 or G == 1:
                                run_tasks(gens, NW)
                                gens = []
                        blocks = []
                        if G == 1:
                            for (t0, n) in SEQS[:2]:
                                blocks.append((t0 * 128, 256, [("l", t0, None), ("l", t0 + 1, None)], 0))
                            for tb in range(1, NTB):
                                keys = [("l", kt, None) for kt in range(4, NT)] + [("c", 0, None), ("c", 1, None)]
                                blocks.append((tb * 512, 512, keys, tb))
                        else:
                            for (t0, n) in SEQS[:2]:
                                for tq in range(t0, t0 + n):
                                    blocks.append((tq * 128, 128, [("l", t0, None), ("l", t0 + 1, None)], 0))
                            for i in range(16):
                                keys = []
                                if i > 0:
                                    keys.append(("l", 4 + i - 1, 0))
                                keys.append(("l", 4 + i, None))
                                if i < 15:
                                    keys.append(("l", 4 + i + 1, 1))
                                keys += [("c", 0, None), ("c", 1, None)]
                                blocks.append(((4 + i) * 128, 128, keys, (4 + i) // 4))
                        run_tasks([(lambda sl, blk=blk: attn_block(sl, j, *blk)) for blk in blocks], 2)
                        for hh in range(G):
                            r0 = (orow0 + h0 + hh) * 128
                            c.dma("sp", oTs[L][r0:r0 + 128, :], oThg[:, hh, :],
                                  reads=[("oTh", hh, tb) for tb in range(NTB)], writes=[("oTs", L, orow0 + h0 + hh)])
                c.barrier_all()

        def phase_hgrn():
            with ExitStack() as es:
                getw = make_stream(es, [(w_in[0], h * 640, 640) for h in range(8)], 640)
                maskF = sb(es, "maskF", [128, 512], F32)
                mfb = sb(es, "mfb", [128, 2, 128], F32)
                cm4 = sb(es, "cm4", [128, 4, 128], BF16)
                onc = sb(es, "onc", [128, 1], F32)
                lbe = sb(es, "lbe", [128, 3, 16], F32)
                lbs = sb(es, "lbs", [128, 16], F32)
                lbv = sb(es, "lbv", [128, 16], F32)
                oml = sb(es, "oml", [128, 16], F32)
                noml = sb(es, "noml", [128, 16], F32)
                q32 = [sb(es, f"q32_{i}", [128, 512], F32) for i in range(2)]
                sg = [sb(es, f"sg_{i}", [128, 512], F32) for i in range(2)]
                lg = [sb(es, f"lg_{i}", [128, 512], F32) for i in range(2)]
                k32 = [sb(es, f"k32_{i}", [128, 512], F32) for i in range(2)]
                bF = [sb(es, f"bF_{i}", [128, 512], F32) for i in range(2)]
                totc = [sb(es, f"totc_{i}", [128, 16], F32) for i in range(2)]
                dec = sb(es, "dec", [128, 2, 80], F32)
                qd = [sb(es, f"qd{d}", [128, T], BF16) for d in range(2)]
                ki = [sb(es, f"ki{d}", [128, T], BF16) for d in range(2)]
                keT = [sb(es, f"keT{d}", [128, T], BF16) for d in range(2)]
                sgT = sb(es, "sgT", [128, T], BF16)
                V = sb(es, "Va", [128, NT, 128], BF16)
                OT = sb(es, "OT", [128, T], F32)
                kend = [[sb(es, f"kend{d}{r}", [128, 128], BF16) for r in range(2)] for d in range(2)]
                Vm = [[sb(es, f"Vm{d}{r}", [128, 4, 128], BF16) for r in range(2)] for d in range(2)]
                Am = [[sb(es, f"Am{d}{r}", [128, 128], BF16) for r in range(2)] for d in range(2)]
                S32 = [[sb(es, f"S32_{d}{r}", [128, 128], F32) for r in range(2)] for d in range(2)]
                Sbf = [[sb(es, f"Sbf{d}{p}", [128, 128], BF16) for p in range(4)] for d in range(2)]
                fb = [ps(es, f"hfb{i}", [128, 512]) for i in range(2)]
                bA = [ps(es, f"hbA{d}", [128, 512]) for d in range(2)]
                pO = [ps(es, f"hpO{d}", [128, 512]) for d in range(2)]
                ptr = [ps(es, f"hptr{d}", [128, 8, 128], BF16) for d in range(2)]

                c.dma("sp", maskF[:], maskF_d, writes=["hc"])
                c.dma("sp", mfb[:], mfb_d, writes=["hc"])
                c.dma("pool", cm4[:], cm4_d, writes=["hc"])
                c.dma("sp", onc[:], onorm_d, writes=["hc"])
                c.dma("sp", lbe[:], lbg, writes=["lbe"])
                c.op("act", lambda: nc.scalar.activation(lbe[:], lbe[:], AF.Exp), ["lbe"], ["lbe"])
                c.op("dve", lambda: nc.vector.tensor_tensor(out=lbs[:], in0=lbe[:, 0, :], in1=lbe[:, 1, :], op=ALU.add),
                     ["lbe"], ["lbs"])
                c.op("dve", lambda: nc.vector.tensor_tensor(out=lbs[:], in0=lbs[:], in1=lbe[:, 2, :], op=ALU.add),
                     ["lbe", "lbs"], ["lbs"])
                c.op("dve", lambda: nc.vector.reciprocal(lbs[:], lbs[:]), ["lbs"], ["lbs"])
                c.op("dve", lambda: nc.vector.tensor_tensor(out=lbv[:], in0=lbe[:, 0, :], in1=lbs[:], op=ALU.mult),
                     ["lbe", "lbs"], ["lbv"])
                c.op("dve", lambda: nc.vector.tensor_scalar(out=oml[:], in0=lbv[:], scalar1=-1.0, scalar2=1.0,
                                                            op0=ALU.mult, op1=ALU.add), ["lbv"], ["oml"])
                c.op("dve", lambda: nc.vector.tensor_scalar(out=noml[:], in0=oml[:], scalar1=-1.0, scalar2=None,
                                                            op0=ALU.mult), ["oml"], ["noml", "hc"])

                def bc32(t, ncol=16):
                    return t.unsqueeze(2).to_broadcast([128, ncol, 32])

                def f_task(sl, h, wb, wk, tb, c0, n):
                    cols = slice(c0, c0 + n)
                    nch = n // 32
                    rot = [(fb[sl], ("fb", sl)), (bA[sl], ("bA", sl)), (pO[sl], ("pO", sl))]
                    rctr = [0]

                    def nextb():
                        r_ = rot[rctr[0] % 3]
                        rctr[0] += 1
                        return r_
                    pjt, pjk = nextb()
                    Q, SG, LG, K, B, TC = q32[sl], sg[sl], lg[sl], k32[sl], bF[sl], totc[sl]
                    kq, ks, kl, kk, kb_, kt = ("q32", sl), ("sg", sl), ("lg", sl), ("k32", sl), ("bF", sl), ("totc", sl)
                    mm_group(pjt[:, :n], [(wb[:, k, 0:128], hT[:, k, cols]) for k in range(16)], [wk], [pjk])
                    yield
                    c.op("act", lambda: nc.scalar.activation(Q[:, :n], pjt[:, :n], AF.Silu), [pjk], [kq])
                    yield
                    for d in range(2):
                        i = d * 8 + h
                        pjt, pjk = nextb()
                        mm_group(pjt[:, :n], [(wb[:, k, 128 * (1 + d):128 * (2 + d)], hT[:, k, cols]) for k in range(16)],
                                 [wk], [pjk])
                        yield
                        c.op("act", lambda: nc.scalar.activation(SG[:, :n], pjt[:, :n], AF.Sigmoid), [pjk], [ks])
                        yield
                        c.op("act", lambda: nc.scalar.activation(LG[:, :n], SG[:, :n], AF.Ln, scale=oml[:, i:i + 1],
                                                                 bias=lbv[:, i:i + 1]), [ks, "hc"], [kl])
                        c.op("dve", lambda: nc.vector.tensor_scalar(
                            out=K[:, :n], in0=SG[:, :n], scalar1=noml[:, i:i + 1], scalar2=oml[:, i:i + 1],
                            op0=ALU.mult, op1=ALU.add), [ks, "hc"], [kk])
                        yield
                        c.op("dve", lambda: nc.vector.tensor_tensor_scan(B[:, :n], maskF[:, :n], LG[:, :n], 0.0, ALU.mult, ALU.add),
                             [kl, "hc"], [kb_])
                        yield
                        tot = B[:, :n].rearrange("p (c t) -> p c t", t=32)[:, :, 31]
                        dslice = dec[:, d, c0 // 32:c0 // 32 + nch]
                        c.op("act", lambda: nc.scalar.activation(dslice, tot, AF.Exp), [kb_], [("dec", d, tb)])
                        if d == 1:
                            c.op("act", lambda: nc.scalar.copy(TC[:, :nch], tot), [kb_], [kt])
                            yield
                            b3 = B[:, :n].rearrange("p (c t) -> p c t", t=32)
                            c.op("dve", lambda: nc.vector.tensor_tensor(out=b3, in0=bc32(TC[:, :nch], nch), in1=b3, op=ALU.subtract),
                                 [kb_, kt], [kb_])
                            yield
                            c.op("pool", lambda: nc.gpsimd.tensor_tensor(out=B[:, :n], in0=B[:, :n], in1=LG[:, :n], op=ALU.add),
                                 [kb_, kl], [kb_])
                        yield
                        c.op("act", lambda: nc.scalar.activation(SG[:, :n], B[:, :n], AF.Exp), [kb_], [ks])
                        c.op("act", lambda: nc.scalar.activation(LG[:, :n], B[:, :n], AF.Exp, scale=-1.0), [kb_], [kl])
                        yield
                        c.op("dve", lambda: nc.vector.tensor_tensor(out=qd[d][:, cols], in0=Q[:, :n], in1=SG[:, :n], op=ALU.mult),
                             [kq, ks], [("qd", d, tb)])
                        yield
                        c.op("dve", lambda: nc.vector.tensor_tensor(out=LG[:, :n], in0=K[:, :n], in1=LG[:, :n], op=ALU.mult),
                             [kk, kl], [kl])
                        yield
                        c.op("pool", lambda: nc.gpsimd.tensor_copy(ki[d][:, cols], LG[:, :n]), [kl], [("ki", d, tb)])
                        k3 = LG[:, :n].rearrange("p (c t) -> p c t", t=32)
                        o3 = keT[d][:, cols].rearrange("p (c t) -> p c t", t=32)
                        c.op("pool", lambda: nc.gpsimd.tensor_tensor(out=o3, in0=k3, in1=bc32(dslice, nch), op=ALU.mult),
                             [kl, ("dec", d, tb)], [("keT", d, tb)])
                        yield
                    pjt, pjk = nextb()
                    mm_group(pjt[:, :n], [(wb[:, k, 512:640], hT[:, k, cols]) for k in range(16)], [wk], [pjk])
                    yield
                    c.op("act", lambda: nc.scalar.activation(sgT[:, cols], pjt[:, :n], AF.Silu), [pjk], [("sgT", tb)])
                    yield
                    for tt in range(c0 // 128, (c0 + n) // 128):
                        tcols = slice(tt * 128, (tt + 1) * 128)
                        pjt, pjk = nextb()
                        mm_group(pjt[:, 0:128], [(hT[:, k, tcols], wb[:, k, 384:512]) for k in range(16)], [wk], [pjk])
                        yield
                        c.op("act", lambda: nc.scalar.copy(V[:, tt, :], pjt[:, 0:128]), [pjk], [("V", tt)])
                        yield

                BLK = [0, 0, 1, 1] + [2 + (t - 4) // 4 for t in range(4, NT)]
                ot_written = set()
                vctr = [0, 0]
                sctr = [0, 0]

                def sweep(d, h, si, t0, n):
                    cur = sctr[d] % 2
                    if si < 2:
                        c.op("dve", lambda: nc.vector.memset(S32[d][cur][:], 0.0), [], [("S32", d, cur)])
                    else:
                        c.dma("sp", S32[d][cur][:], st0[d, h], writes=[("S32", d, cur)])
                    sb0 = sctr[d] % 4
                    c.op("act", lambda: nc.scalar.copy(Sbf[d][sb0][:], S32[d][cur][:]),
                         [("S32", d, cur)], [("Sbf", d, sb0)])
                    yield
                    tiles = range(t0, t0 + n) if d == 0 else range(t0 + n - 1, t0 - 1, -1)
                    for tt in tiles:
                        tb = BLK[tt]
                        tcols = slice(tt * 128, (tt + 1) * 128)
                        r = vctr[d] % 2
                        vctr[d] += 1
                        c.op("pe", lambda: nc.tensor.transpose(ptr[d][:, 0, :], keT[d][:, tcols], identb[:]),
                             [("keT", d, tb), "identb"], [("ptr", d)])
                        c.op("pool", lambda: nc.gpsimd.tensor_tensor(
                            out=Vm[d][r][:], in0=V[:, tt, :].unsqueeze(1).to_broadcast([128, 4, 128]), in1=cm4[:],
                            op=ALU.mult), [("V", tt), "hc"], [("Vm", d, r)])
                        yield
                        c.op("act", lambda: nc.scalar.copy(kend[d][r][:], ptr[d][:, 0, :]), [("ptr", d)], [("kend", d, r)])
                        c.op("pe", lambda: nc.tensor.matmul(bA[d][:, 0:128], lhsT=ki[d][:, tcols], rhs=qd[d][:, tcols],
                                                            start=True, stop=True),
                             [("ki", d, tb), ("qd", d, tb)], [("bA", d)])
                        yield
                        c.op("dve", lambda: nc.vector.tensor_tensor(out=Am[d][r][:], in0=bA[d][:, 0:128], in1=mfb[:, d, :],
                                                                    op=ALU.mult), [("bA", d), "hc"], [("Am", d, r)])

                        def umm():
                            inst = None
                            for j in range(4):
                                inst = nc.tensor.matmul(fb[d][:, j * 128:(j + 1) * 128], lhsT=kend[d][r][:],
                                                        rhs=Vm[d][r][:, j, :], start=True, stop=True)
                            return inst
                        c.op("pe", umm, [("kend", d, r), ("Vm", d, r)], [("fb", d)])
                        yield
                        c.op("pe", lambda: nc.tensor.matmul(pO[d][:, 0:128], lhsT=V[:, tt, :], rhs=Am[d][r][:],
                                                            start=True, stop=False),
                             [("V", tt), ("Am", d, r)], [("pO", d)])
                        yield
                        order = range(4) if d == 0 else range(3, -1, -1)
                        for n_, j in enumerate(order):
                            cur = sctr[d] % 2
                            nxt = 1 - cur
                            sbc = sctr[d] % 4
                            sbn = (sctr[d] + 1) % 4
                            ccols = slice(tt * 128 + 32 * j, tt * 128 + 32 * j + 32)
                            gch = tt * 4 + j
                            c.op("pe", lambda: nc.tensor.matmul(
                                pO[d][:, 32 * j:32 * j + 32], lhsT=Sbf[d][sbc][:], rhs=qd[d][:, ccols],
                                start=False, stop=(n_ == 3)),
                                [("Sbf", d, sbc), ("qd", d, tb)], [("pO", d)])
                            c.op("dve", lambda: nc.vector.scalar_tensor_tensor(
                                out=S32[d][nxt][:], in0=S32[d][cur][:], scalar=dec[:, d, gch:gch + 1],
                                in1=fb[d][:, j * 128:(j + 1) * 128], op0=ALU.mult, op1=ALU.add),
                                [("S32", d, cur), ("dec", d, tb), ("fb", d)], [("S32", d, nxt)])
                            yield
                            c.op("act", lambda: nc.scalar.copy(Sbf[d][sbn][:], S32[d][nxt][:]),
                                 [("S32", d, nxt)], [("Sbf", d, sbn)])
                            sctr[d] += 1
                            yield
                        if tt not in ot_written:
                            ot_written.add(tt)
                            c.op("act", lambda: nc.scalar.copy(OT[:, tcols], pO[d][:, 0:128]), [("pO", d)], [("OT", tt)])
                        else:
                            c.op("dve", lambda: nc.vector.tensor_tensor(out=OT[:, tcols], in0=pO[d][:, 0:128],
                                                                        in1=OT[:, tcols], op=ALU.add),
                                 [("pO", d), ("OT", tt)], [("OT", tt)])
                        yield
                    if si < 2:
                        fin = sctr[d] % 2
                        c.dma("sp", nstate[si, d, h], S32[d][fin][:], reads=[("S32", d, fin)],
                              writes=[("nstate", si, d, h)])

                def n_task(sl, h, tb):
                    cols = slice(tb * 512, (tb + 1) * 512)
                    ok = [("OT", tt) for tt in range(tb * 4, tb * 4 + 4)]
                    SQ, SD = q32[sl], sg[sl]
                    kq, ks = ("q32", sl), ("sg", sl)
                    pjt, pjk = fb[sl], ("fb", sl)
                    c.op("act", lambda: nc.scalar.activation(SQ[:], OT[:, cols], AF.Square), ok, [kq])
                    yield
                    c.op("pe", lambda: nc.tensor.matmul(pjt[:], lhsT=onesf[:], rhs=SQ[:], start=True, stop=True),
                         [kq, "onesf"], [pjk])
                    yield
                    c.op("act", lambda: nc.scalar.activation(SD[:], pjt[:], AF.Ln, bias=epsc[:]), [pjk, "epsc"], [ks])
                    yield
                    c.op("act", lambda: nc.scalar.activation(SD[:], SD[:], AF.Exp, scale=-0.5), [ks], [ks])
                    yield
                    c.op("dve", lambda: nc.vector.scalar_tensor_tensor(
                        out=SQ[:], in0=OT[:, cols], scalar=onc[:, 0:1], in1=SD[:], op0=ALU.mult, op1=ALU.mult),
                        ok + [ks, "hc"], [kq])
                    yield
                    ostv = k32[sl][:].bitcast(BF16)[:, 0:512]
                    c.op("pool", lambda: nc.gpsimd.tensor_tensor(out=ostv, in0=SQ[:], in1=sgT[:, cols], op=ALU.mult),
                         [kq] + [("sgT", bb_) for bb_ in sorted(set(BLK[tb * 4:tb * 4 + 4]))], [("k32", sl)])
                    c.dma("sp", oTs[0][h * 128:(h + 1) * 128, cols], ostv, reads=[("k32", sl)],
                          writes=[("oTs", 0, h)])
                    yield

                HG = os.environ.get("MK_HG", "")
                for h in range(8):
                    wb, wk = getw()
                    if "nof" not in HG:
                        FB = [(2, 512, 512), (3, 1024, 512), (4, 1536, 512), (5, 2048, 512), (0, 0, 256), (1, 256, 256)]
                        run_tasks([(lambda sl, fb_=fb_: f_task(sl, h, wb, wk, *fb_)) for fb_ in FB], 2)
                    ot_written.clear()
                    if "noscan" not in HG:
                        for si, (t0, n) in enumerate(SEQS):
                            run_tasks([(lambda sl, d=d: sweep(d, h, si, t0, n)) for d in range(2)], 2)
                    if "nopost" not in HG:
                        run_tasks([(lambda sl, tb=tb: n_task(sl, h, tb)) for tb in range(NTB)], 2)
                c.barrier_all()

        def phase_out(L):
            NB = 4
            with ExitStack() as es:
                getw = make_stream(es, [(w_out[L], cb * 512, 512) for cb in range(4)], 512)
                gtb = sb(es, "gtb", [128, 2, D], F32)
                xr = [sb(es, f"xr{i}", [128, 512], F32) for i in range(NB)]
                tm = [sb(es, f"otm{i}", [128, 512], F32) for i in range(NB)]
                po = [ps(es, f"po{i}", [128, 512]) for i in range(2)]
                for g in range(2):
                    c.dma("act", gtb[:, g, :], gts[L * 2 + g:L * 2 + g + 1, :].partition_broadcast(128),
                          reads=[("gts", L, g, cb) for cb in range(4)], writes=[("gtb", g)])
                for k in range(16):
                    c.dma("sp" if k % 2 == 0 else "act", hT[:, k, :], oTs[L][k * 128:(k + 1) * 128, :],
                          reads=[("oTs", L, k)], writes=[("hTk", k)])
                it = 0
                for cb in range(4):
                    ccols = slice(cb * 512, (cb + 1) * 512)
                    wb, wk = getw()
                    for tt in range(NT):
                        b = it % NB
                        pb = it % 2
                        it += 1
                        g = 0 if tt < 4 else 1
                        rows = slice(tt * 128, (tt + 1) * 128)
                        tcols = slice(tt * 128, (tt + 1) * 128)
                        if L == 0:
                            c.dma("sp", xr[b][:], x[rows, ccols], writes=[("xr", b)])
                        else:
                            c.dma("sp", xr[b][:], y1[rows, ccols], reads=[("y1", tt, cb)], writes=[("xr", b)])
                        mm_group(po[pb][:], [(hT[:, k, tcols], wb[:, k, 0:512]) for k in range(16)],
                                 [wk] + [("hTk", k) for k in range(16)], [("po", pb)])
                        c.op("dve", lambda: nc.vector.tensor_tensor(
                            out=tm[b][:], in0=po[pb][:], in1=gtb[:, g, ccols], op=ALU.mult),
                            [("po", pb), ("gtb", g)], [("otm", b)])
                        c.op("pool", lambda: nc.gpsimd.tensor_tensor(out=tm[b][:], in0=tm[b][:], in1=xr[b][:],
                                                                     op=ALU.add),
                             [("otm", b), ("xr", b)], [("otm", b)])
                        if L == 0:
                            c.dma("act", y1[rows, ccols], tm[b][:], reads=[("otm", b)], writes=[("y1", tt, cb)])
                        else:
                            c.dma("act", y[rows, ccols], tm[b][:], reads=[("otm", b)], writes=[("y", tt, cb)])
                c.barrier_all()

        plist = [("mod0", lambda: phase_mod_h(0)), ("hgrn", phase_hgrn), ("attn0", lambda: phase_attn(0)),
                 ("out0", lambda: phase_out(0)), ("mod1", lambda: phase_mod_h(1)), ("attn1", lambda: phase_attn(1)),
                 ("out1", lambda: phase_out(1))]
        for nm, fn in plist:
            if phases is None or nm in phases:
                fn()
        c.finish()
    return nc


def _consts():
    ident = np.eye(128, dtype=np.float32)
    maskF = np.ones((128, 512), np.float32)
    maskF[:, ::32] = 0.0
    s = np.arange(128)[:, None]
    t = np.arange(128)[None, :]
    same = (s // 32) == (t // 32)
    mfb = np.stack([(same & (s <= t)), (same & (s >= t))], axis=1).astype(np.float32)
    cm4 = np.zeros((128, 4, 128), np.float32)
    for j in range(4):
        cm4[32 * j:32 * j + 32, j, :] = 1.0
    R = np.zeros((128, 128), np.float32)
    for m in range(128):
        q = m // 32
        if q in (0, 2):
            R[m, m + 32] = -1.0
        else:
            R[m, m - 32] = 1.0
    RT = np.ascontiguousarray(R.T)
    n_tok = 2048
    row = (np.arange(n_tok) // 64).astype(np.float32)
    col = (np.arange(n_tok) % 64).astype(np.float32)
    inv = (10000.0 ** (-np.arange(32, dtype=np.float32) / 32)).astype(np.float32)
    ar = row[:, None] * inv
    ac = col[:, None] * inv
    ang = np.concatenate([ar, ar, ac, ac], axis=-1).astype(np.float32)
    cosT = np.ascontiguousarray(np.cos(ang).T.astype(np.float32))
    sinT = np.ascontiguousarray(np.sin(ang).T.astype(np.float32))
    b = np.arange(128)[:, None]
    a = np.arange(128)[None, :]
    NEG = -30000.0
    wbias = np.stack([np.where(b >= a, 0.0, NEG), np.where(b <= a, 0.0, NEG)], axis=1).astype(np.float32)
    wbias4 = np.ascontiguousarray(np.broadcast_to(wbias[:, :, None, :], (128, 2, 4, 128))).astype(np.float32)
    return dict(ident=ident, maskF=maskF, mfb=mfb, cm4=cm4, RT=RT, cosT=cosT, sinT=sinT, wbias4=wbias4)


def _perm_w0(w):
    cols = []
    for h in range(8):
        for base in (0, 1024, 2048, 3072, 4096):
            cols.append(np.arange(base + h * 128, base + (h + 1) * 128))
    for j in range(2):
        cols.append(np.arange(6144 + j * 128, 6144 + (j + 1) * 128))
        cols.append(np.arange(6400 + j * 128, 6400 + (j + 1) * 128))
    for h in range(8):
        cols.append(np.arange(5120 + h * 128, 5120 + (h + 1) * 128))
        cols.append(np.arange(6656 + h * 128, 6656 + (h + 1) * 128))
    return np.ascontiguousarray(w[:, np.concatenate(cols)])


def _perm_w1(w):
    cols = []
    for j in range(4):
        cols.append(np.arange(2048 + j * 128, 2048 + (j + 1) * 128))
        cols.append(np.arange(2560 + j * 128, 2560 + (j + 1) * 128))
    for h in range(16):
        cols.append(np.arange(h * 128, (h + 1) * 128))
        cols.append(np.arange(3072 + h * 128, 3072 + (h + 1) * 128))
    return np.ascontiguousarray(w[:, np.concatenate(cols)])


def _prep(x_prompt, x_sample, state_l0_hgrn, cache_l0_k, cache_l0_v, cache_l1_k, cache_l1_v,
           c, c_ctx, lb_gamma,
           l0_norm, l0_w_mod, l0_b_mod, l0_w_in, l0_w_out, l0_a_onorm, l0_b_qnorm, l0_b_knorm,
           l1_norm, l1_w_mod, l1_b_mod, l1_w_in, l1_w_out, l1_c_qnorm, l1_c_knorm, l1_c_sink):
    f = lambda a: np.ascontiguousarray(np.asarray(a, dtype=np.float32))
    x_prompt, x_sample = f(x_prompt), f(x_sample)
    consts = _consts()
    shared = dict(
        w_mod0=f(l0_w_mod), w_mod1=f(l1_w_mod),
        b_mod0=f(l0_b_mod).reshape(1, -1), b_mod1=f(l1_b_mod).reshape(1, -1),
        norm0=f(l0_norm).reshape(1, -1), norm1=f(l1_norm).reshape(1, -1),
        w_in0=_perm_w0(f(l0_w_in)), w_in1=_perm_w1(f(l1_w_in)),
        w_out0=f(l0_w_out), w_out1=f(l1_w_out),
        onorm=f(l0_a_onorm).reshape(128, 1),
        qn0=f(l0_b_qnorm).reshape(128, 1), kn0=f(l0_b_knorm).reshape(128, 1),
        qn1=f(l1_c_qnorm).reshape(128, 1), kn1=f(l1_c_knorm).reshape(128, 1),
        sink=f(l1_c_sink).reshape(1, 16),
        lbg=np.ascontiguousarray(f(lb_gamma).reshape(3, 2, 8, 128).transpose(3, 0, 1, 2).reshape(128, 3, 16)),
        **consts,
    )
    c = f(c)
    c_ctx = f(c_ctx)
    in_maps = []
    for i in range(8):
        m = dict(shared)
        m["x"] = np.ascontiguousarray(np.concatenate(
            [x_prompt[2 * i], x_prompt[2 * i + 1], x_sample[i]], axis=0))
        cr = np.stack([c_ctx, c[i]], axis=0)
        m["crows"] = np.ascontiguousarray(cr.reshape(2, 16, 128).transpose(2, 0, 1))
        m["st0"] = f(state_l0_hgrn[i])
        m["ck0T"] = np.ascontiguousarray(f(cache_l0_k[i]).transpose(2, 1, 0))
        m["cv0"] = f(cache_l0_v[i])
        m["ck1T"] = np.ascontiguousarray(f(cache_l1_k[i]).transpose(2, 1, 0))
        m["cv1"] = f(cache_l1_v[i])
        in_maps.append(m)
    return in_maps


def kernel(**inputs):
    in_maps = _prep(**inputs)
    nc = build_program()
    res = run_bass_kernel_spmd(nc, in_maps, core_ids=list(range(8)))
    r = res.results
    y_prompt = np.stack([r[i // 2]["y"][(i % 2) * 256:(i % 2 + 1) * 256] for i in range(16)], axis=0)
    y_sample = np.stack([r[i]["y"][512:] for i in range(8)], axis=0)
    nstate = np.concatenate([r[i]["nstate"] for i in range(8)], axis=0)
    outs = [y_prompt.astype(np.float32), y_sample.astype(np.float32), nstate.astype(np.float32)]
    for nm in ("nk0", "nv0", "nk1", "nv1"):
        a = np.concatenate([r[i][nm].reshape(2, 256, r[i][nm].shape[1], 128) for i in range(8)], axis=0)
        outs.append(a.astype(np.float32))
    return tuple(outs)
```

```python
import math
import os
from contextlib import ExitStack

import numpy as np
import concourse.bass as bass
import concourse.mybir as mybir
from concourse.bass_utils import run_bass_kernel_spmd

F32 = mybir.dt.float32
BF16 = mybir.dt.bfloat16
AF = mybir.ActivationFunctionType
ALU = mybir.AluOpType

D = 2048
T = 2560
NT = 20
NTB = 5
EPS = 1e-6
SCALE = 1.0 / math.sqrt(128.0)
SEQS = [(0, 2), (2, 2), (4, 16)]


class Ctx:
    NDMA = 32

    def __init__(self, nc):
        self.nc = nc
        self.engs = {"pe": nc.tensor, "dve": nc.vector, "act": nc.scalar,
                     "pool": nc.gpsimd, "sp": nc.sync}
        self.sem = {k: nc.alloc_semaphore(name="s_" + k) for k in self.engs}
        self.cnt = {k: 0 for k in self.engs}
        self.waited = {k: {} for k in self.engs}
        self.last_w = {}
        self.readers = {}
        self.dma_sems = [nc.alloc_semaphore(name=f"s_dma{i}") for i in range(self.NDMA)]
        self.dma_val = [0] * self.NDMA
        self.dma_pool = {"sp": list(range(0, 16)), "pool": list(range(16, 24)), "act": list(range(24, 32))}
        self.dma_rr = {"sp": 0, "pool": 0, "act": 0}

    def _deps(self, reads, writes):
        evs = []
        for r in reads:
            e = self.last_w.get(r)
            if e is not None:
                evs.append(e)
        for w in writes:
            e = self.last_w.get(w)
            if e is not None:
                evs.append(e)
            evs.extend(self.readers.get(w, ()))
        return evs

    def _wait(self, eng, evs, skip_self=False):
        best = {}
        for (name, sem, val) in evs:
            if skip_self and name == eng:
                continue
            if best.get(name, (None, 0))[1] < val:
                best[name] = (sem, val)
        wd = self.waited[eng]
        for name, (sem, val) in best.items():
            if wd.get(name, 0) < val:
                self.engs[eng].wait_ge(sem, val)
                wd[name] = val

    def _commit(self, ev, reads, writes):
        ws = set(writes)
        for r in reads:
            if r in ws:
                continue
            self.readers.setdefault(r, []).append(ev)
        for w in writes:
            self.last_w[w] = ev
            self.readers[w] = []

    EXCL = {"pj", "pms", "prot", "pv", "pS", "pO", "pL", "pm", "pT", "bA", "ptr", "po", "fb"}

    def op(self, eng, fn, reads=(), writes=()):
        reads = list(reads)
        writes = list(writes)
        ex = [r for r in reads if (r if isinstance(r, str) else r[0]) in self.EXCL]
        if ex:
            reads = [r for r in reads if r not in ex]
            writes = writes + [r for r in ex if r not in writes]
        evs = self._deps(reads, writes)
        self._wait(eng, evs, skip_self=(eng == "pe"))
        inst = fn()
        inst.then_inc(self.sem[eng], 1)
        self.cnt[eng] += 1
        ev = (eng, self.sem[eng], self.cnt[eng])
        self._commit(ev, reads, writes)
        return ev

    def dma(self, q, out, in_, reads=(), writes=()):
        reads = list(reads)
        writes = list(writes)
        evs = self._deps(reads, writes)
        lst = self.dma_pool[q]
        k = lst[self.dma_rr[q] % len(lst)]
        self.dma_rr[q] += 1
        name = f"dma{k}"
        if self.dma_val[k] > 0:
            evs.append((name, self.dma_sems[k], self.dma_val[k]))
        self._wait(q, evs)
        self.engs[q].dma_start(out=out, in_=in_).then_inc(self.dma_sems[k], 16)
        self.dma_val[k] += 16
        ev = (name, self.dma_sems[k], self.dma_val[k])
        self._commit(ev, reads, writes)
        return ev

    def all_events(self):
        evs = [(k, self.sem[k], self.cnt[k]) for k in self.engs if self.cnt[k] > 0]
        for i in range(self.NDMA):
            if self.dma_val[i] > 0:
                evs.append((f"dma{i}", self.dma_sems[i], self.dma_val[i]))
        return evs

    def barrier_all(self):
        evs = self.all_events()
        for e in self.engs:
            self._wait(e, evs, skip_self=True)

    def finish(self):
        self._wait("sp", self.all_events(), skip_self=True)


def build_program(phases=None):
    nc = bass.Bass("TRN2", target_bir_lowering=False)

    def din(name, shape, dt=F32):
        return nc.dram_tensor(name, list(shape), dt, kind="ExternalInput").ap()

    def dout(name, shape, dt=F32):
        return nc.dram_tensor(name, list(shape), dt, kind="ExternalOutput").ap()

    def dscr(name, shape, dt=F32):
        return nc.dram_tensor(name, list(shape), dt, kind="Internal").ap()

    x = din("x", [T, D])
    crows = din("crows", [128, 2, 16])
    lbg = din("lbg", [128, 3, 16])
    st0 = din("st0", [2, 8, 128, 128])
    ckT = [din("ck0T", [128, 2, 256]), din("ck1T", [128, 4, 256])]
    cv = [din("cv0", [256, 2, 128]), din("cv1", [256, 4, 128])]
    w_mod = [din("w_mod0", [D, 3 * D]), din("w_mod1", [D, 3 * D])]
    b_mod = [din("b_mod0", [1, 3 * D]), din("b_mod1", [1, 3 * D])]
    norm_g = [din("norm0", [1, D]), din("norm1", [1, D])]
    w_in = [din("w_in0", [D, 7680]), din("w_in1", [D, 5120])]
    w_out = [din("w_out0", [D, D]), din("w_out1", [D, D])]
    onorm_d = din("onorm", [128, 1])
    qn_d = [din("qn0", [128, 1]), din("qn1", [128, 1])]
    kn_d = [din("kn0", [128, 1]), din("kn1", [128, 1])]
    sink_d = din("sink", [1, 16])
    ident_d = din("ident", [128, 128])
    maskF_d = din("maskF", [128, 512])
    mfb_d = din("mfb", [128, 2, 128])
    cm4_d = din("cm4", [128, 4, 128])
    RT_d = din("RT", [128, 128])
    cosT_d = din("cosT", [128, 2048])
    sinT_d = din("sinT", [128, 2048])
    wbias4_d = din("wbias4", [128, 2, 4, 128])

    y = dout("y", [T, D])
    nstate = dout("nstate", [2, 2, 8, 128, 128])
    nk = [dout("nk0", [512, 2, 128]), dout("nk1", [512, 4, 128])]
    nv = [dout("nv0", [512, 2, 128]), dout("nv1", [512, 4, 128])]

    gts = dscr("gts", [4, D])
    oTs = [dscr("oT0", [D, T], BF16), dscr("oT1", [D, T], BF16)]
    y1 = dscr("y1", [T, D])

    c = Ctx(nc)

    uid = [0]

    def sb(es, name, shape, dt):
        uid[0] += 1
        return es.enter_context(nc.sbuf_tensor(f"{name}_{uid[0]}", list(shape), dt))

    def ps(es, name, shape, dt=F32):
        uid[0] += 1
        return es.enter_context(nc.psum_tensor(f"{name}_{uid[0]}", list(shape), dt))

    def mm_group(out, pairs, reads, writes):
        def f():
            n = len(pairs)
            inst = None
            for i, (l, r) in enumerate(pairs):
                inst = nc.tensor.matmul(out, lhsT=l, rhs=r, start=(i == 0), stop=(i == n - 1))
            return inst
        return c.op("pe", f, reads, writes)

    def wview(w, c0, ncols):
        return w[:, c0:c0 + ncols].rearrange("(k p) n -> p k n", p=128)

    with ExitStack() as top:
        identb = sb(top, "identb", [128, 128], BF16)
        identf = sb(top, "identf", [128, 128], F32)
        onesb = sb(top, "onesb", [128, 128], BF16)
        onesf = sb(top, "onesf", [128, 128], F32)
        epsc = sb(top, "epsc", [128, 1], F32)
        hT = sb(top, "hT", [128, 16, T], BF16)

        c.dma("sp", identf[:], ident_d, writes=["identf"])
        c.dma("pool", identb[:], ident_d, writes=["identb"])
        c.op("dve", lambda: nc.vector.memset(onesb[:], 1.0), writes=["onesb"])
        c.op("dve", lambda: nc.vector.memset(onesf[:], 1.0 / 128.0), writes=["onesf"])
        c.op("dve", lambda: nc.vector.memset(epsc[:], EPS), writes=["epsc"])

        def make_stream(es, blocks, ncols_max, nslots=2):
            bufs = [sb(es, f"wbuf{i}", [128, 16, ncols_max], BF16) for i in range(nslots)]
            st = {"issued": 0, "got": 0}

            def issue():
                i = st["issued"]
                if i >= len(blocks):
                    return
                w, c0, ncols = blocks[i]
                s = i % nslots
                c.dma("pool", bufs[s][:, :, 0:ncols], wview(w, c0, ncols), writes=[("wb", s)])
                st["issued"] += 1

            def get():
                i = st["got"]
                while st["issued"] <= i:
                    issue()
                st["got"] += 1
                if st["issued"] <= i + 1:
                    issue()
                s = i % nslots
                return bufs[s], ("wb", s)
            return get

        hkeys = lambda tb: [("hT", tt) for tt in range(tb * 4, tb * 4 + 4)]
        allh = [("hT", tt) for tt in range(NT)]

        def phase_mod_h(L):
            with ExitStack() as es:
                G = sb(es, "G", [128, 2, D], F32)
                SH = sb(es, "SH", [128, 2, D], F32)
                with ExitStack() as es2:
                    getw = make_stream(es2, [(w_mod[L], blk * 512, 512) for blk in range(12)], 512)
                    crs = sb(es2, "crs", [128, 2, 16], F32)
                    scl = sb(es2, "scl", [128, 2, 16], F32)
                    srep = sb(es2, "srep", [128, 2, 16, 128], BF16)
                    bb = [sb(es2, f"bb{i}", [128, 512], F32) for i in range(2)]
                    gb = [sb(es2, f"gb{i}", [128, 512], F32) for i in range(2)]
                    tmp = [sb(es2, f"mtmp{i}", [128, 512], F32) for i in range(2)]
                    pm = [ps(es2, f"pm{i}", [128, 512]) for i in range(2)]
                    c.dma("sp", crs[:], crows, writes=["crs"])
                    c.op("act", lambda: nc.scalar.activation(scl[:], crs[:], AF.Silu), ["crs"], ["scl"])
                    c.op("dve", lambda: nc.vector.tensor_copy(
                        srep[:], scl[:].unsqueeze(3).to_broadcast([128, 2, 16, 128])), ["scl"], ["srep"])
                    for blk in range(12):
                        kind, cb = divmod(blk, 4)
                        b = blk % 2
                        wb, wk = getw()
                        c.dma("sp", bb[b][:], b_mod[L][:, blk * 512:(blk + 1) * 512].partition_broadcast(128),
                              writes=[("bb", b)])
                        if kind == 1:
                            c.dma("sp", gb[b][:], norm_g[L][:, cb * 512:(cb + 1) * 512].partition_broadcast(128),
                                  writes=[("gb", b)])
                        cols = slice(cb * 512, (cb + 1) * 512)
                        for g in range(2):
                            mm_group(pm[g][:], [(srep[:, g, k, :], wb[:, k, 0:512]) for k in range(16)],
                                     [wk, "srep"], [("pm", g)])
                            if kind == 0:
                                c.op("dve", lambda g=g, b=b, cols=cols: nc.vector.tensor_tensor(
                                    out=SH[:, g, cols], in0=pm[g][:], in1=bb[b][:], op=ALU.add),
                                    [("pm", g), ("bb", b)], [("SH", g, cb)])
                            elif kind == 1:
                                c.op("dve", lambda g=g, b=b: nc.vector.tensor_tensor(
                                    out=tmp[g][:], in0=pm[g][:], in1=bb[b][:], op=ALU.add),
                                    [("pm", g), ("bb", b)], [("mtmp", g)])
                                c.op("dve", lambda g=g, b=b, cols=cols: nc.vector.scalar_tensor_tensor(
                                    out=G[:, g, cols], in0=tmp[g][:], scalar=1.0, in1=gb[b][:],
                                    op0=ALU.add, op1=ALU.mult),
                                    [("mtmp", g), ("gb", b)], [("G", g, cb)])
                            else:
                                c.op("dve", lambda g=g, b=b: nc.vector.tensor_tensor(
                                    out=tmp[g][:], in0=pm[g][:], in1=bb[b][:], op=ALU.add),
                                    [("pm", g), ("bb", b)], [("mtmp", g)])
                                c.dma("sp", gts[L * 2 + g:L * 2 + g + 1, cols], tmp[g][0:1, :],
                                      reads=[("mtmp", g)], writes=[("gts", L, g, cb)])
                    c.barrier_all()
                with ExitStack() as es2:
                    NS = 3
                    xt = [sb(es2, f"xt{i}", [128, D], F32) for i in range(NS)]
                    junk = sb(es2, "junk", [128, D], BF16)
                    st = sb(es2, "st", [128, 3 * NS], F32)
                    t1 = [sb(es2, f"t1_{i}", [128, D], F32) for i in range(NS)]
                    hb = [sb(es2, f"hb{i}", [128, D], BF16) for i in range(NS)]
                    pT = [ps(es2, f"pT{i}", [128, 8, 128], BF16) for i in range(2 * NS)]
                    GK = [[("G", g, cb) for cb in range(4)] for g in range(2)]
                    SK = [[("SH", g, cb) for cb in range(4)] for g in range(2)]

                    def h_task(b, tt):
                        g = 0 if tt < 4 else 1
                        rows = slice(tt * 128, (tt + 1) * 128)
                        ssq, std, rstd = st[:, 3 * b:3 * b + 1], st[:, 3 * b + 1:3 * b + 2], st[:, 3 * b + 2:3 * b + 3]
                        if L == 0:
                            c.dma("sp", xt[b][:], x[rows, :], writes=[("xt", b)])
                        else:
                            c.dma("sp", xt[b][:], y1[rows, :], reads=[("y1", tt, cb) for cb in range(4)],
                                  writes=[("xt", b)])
                        yield
                        c.op("act", lambda: nc.scalar.activation(junk[:], xt[b][:], AF.Square, accum_out=ssq),
                             [("xt", b)], ["junk", ("ssq", b)])
                        yield
                        c.op("act", lambda: nc.scalar.activation(std, ssq, AF.Sqrt, scale=1.0 / D, bias=epsc[:]),
                             [("ssq", b), "epsc"], [("std", b)])
                        yield
                        c.op("dve", lambda: nc.vector.reciprocal(rstd, std), [("std", b)], [("rstd", b)])
                        yield
                        c.op("dve", lambda: nc.vector.scalar_tensor_tensor(
                            out=t1[b][:], in0=xt[b][:], scalar=rstd, in1=G[:, g, :], op0=ALU.mult, op1=ALU.mult),
                            [("xt", b), ("rstd", b)] + GK[g], [("t1", b)])
                        yield
                        c.op("pool", lambda: nc.gpsimd.tensor_tensor(
                            out=hb[b][:, 0:1408], in0=t1[b][:, 0:1408], in1=SH[:, g, 0:1408], op=ALU.add),
                            [("t1", b)] + SK[g], [("hb", b, 0)])
                        c.op("dve", lambda: nc.vector.tensor_tensor(
                            out=hb[b][:, 1408:D], in0=t1[b][:, 1408:D], in1=SH[:, g, 1408:D], op=ALU.add),
                            [("t1", b)] + SK[g], [("hb", b, 1)])
                        yield
                        for half in range(2):
                            pp = pT[b * 2 + half]

                            def tr():
                                inst = None
                                for kk in range(8):
                                    k = half * 8 + kk
                                    inst = nc.tensor.transpose(pp[:, kk, :], hb[b][:, k * 128:(k + 1) * 128], identb[:])
                                return inst
                            c.op("pe", tr, [("hb", b, 0), ("hb", b, 1), "identb"], [("pT", b, half)])
                            yield
                            dst = hT[:, half * 8:(half + 1) * 8, tt * 128:(tt + 1) * 128]
                            if half == 0:
                                c.op("act", lambda: nc.scalar.copy(dst, pp[:]), [("pT", b, half)], [("hT", tt, half)])
                            else:
                                c.op("dve", lambda: nc.vector.tensor_copy(dst, pp[:]), [("pT", b, half)],
                                     [("hT", tt, half)])
                            yield

                    run_tasks([(lambda sl, tt=tt: h_task(sl, tt)) for tt in range(NT)], NS)
                    c.barrier_all()
                c.barrier_all()

        def run_tasks(factories, width):
            it = iter(factories)
            active = {}
            free = list(range(width))
            while True:
                while free:
                    f = next(it, None)
                    if f is None:
                        break
                    sl = free.pop(0)
                    active[sl] = f(sl)
                if not active:
                    break
                for sl in sorted(active):
                    try:
                        next(active[sl])
                    except StopIteration:
                        del active[sl]
                        free.append(sl)

        def phase_attn(L):
            nkv = 2 if L == 0 else 4
            G = 1 if L == 0 else 4
            if L == 0:
                kvc0, qc0, orow0 = 5120, 5120 + 512, 8
            else:
                kvc0, qc0, orow0 = 0, 1024, 0
            with ExitStack() as es:
                ablocks = []
                for j_ in range(nkv):
                    ablocks.append((w_in[L], kvc0 + j_ * 256, 256))
                    for h_ in range(4 * j_, 4 * j_ + 4):
                        ablocks.append((w_in[L], qc0 + h_ * 256, 256))
                getw = make_stream(es, ablocks, 256, nslots=3)

                T_sq = [sb(es, f"sq{i}", [128, 512], F32) for i in range(3)]
                T_sd = [sb(es, f"sd{i}", [128, 512], F32) for i in range(3)]
                T_kb = [sb(es, f"kb{i}", [128, 512], BF16) for i in range(3)]
                T_u = [sb(es, f"ru{i}", [128, 512], F32) for i in range(3)]
                cosT = sb(es, "cosT", [128, 2048], F32)
                sinT = sb(es, "sinT", [128, 2048], F32)
                RTb = sb(es, "RTb", [128, 128], BF16)
                gq = sb(es, "gq", [128, 1], F32)
                gk = sb(es, "gk", [128, 1], F32)
                esk = sb(es, "esk", [128, 16], F32)
                wb4 = sb(es, "wb4", [128, 2, 4, 128], BF16)
                KT = sb(es, "KT", [128, T], BF16)
                KcT = sb(es, "KcT", [128, 256], BF16)
                V = sb(es, "V", [128, NT, 128], BF16)
                Vc = sb(es, "Vc", [128, 2, 128], BF16)
                QTg = sb(es, "QTg", [128, G, T], BF16)
                oThg = sb(es, "oThg", [128, G, T], BF16)
                kout = [sb(es, f"kout{i}", [128, 128], F32) for i in range(2)]
                vout = [sb(es, f"vout{i}", [128, 128], F32) for i in range(3)]
                PT = [sb(es, f"PT{i}", [128, 512], BF16) for i in range(4)]
                rl = [sb(es, f"rl{i}", [128, 512], F32) for i in range(2)]
                pj = ps(es, "pj", [128, 512])
                pms = ps(es, "pms", [128, 512])
                pS = [ps(es, f"pS{i}", [128, 512]) for i in range(2)]
                pO = [ps(es, f"pO{i}", [128, 512]) for i in range(2)]
                pL = [ps(es, f"pL{i}", [128, 512]) for i in range(2)]
                pbank = [(pj, "pj"), (pS[0], ("pS", 0)), (pS[1], ("pS", 1))]
                rbank = pbank
                tbank = (pO[0], ("pO", 0))
                msbank = [(pms, "pms"), (pL[0], ("pL", 0)), (pL[1], ("pL", 1))]
                NW = 3

                c.dma("sp", cosT[:], cosT_d, writes=["consts"])
                c.dma("sp", sinT[:], sinT_d, writes=["consts"])
                c.dma("pool", RTb[:], RT_d, writes=["consts"])
                c.dma("sp", gq[:], qn_d[L], writes=["consts"])
                c.dma("sp", gk[:], kn_d[L], writes=["consts"])
                c.dma("pool", wb4[:], wbias4_d, writes=["consts"])
                c.dma("sp", esk[:], sink_d.partition_broadcast(128), writes=["esk0"])
                c.op("act", lambda: nc.scalar.activation(esk[:], esk[:], AF.Exp), ["esk0"], ["esk0", "consts"])

                def fn_task(s, pjt, pjk, gcol, rope_cols, out_bf, outkey, out_f32=None, f32key=None):
                    sq, sd, kb, uu_ = T_sq[s], T_sd[s], T_kb[s], T_u[s]
                    pmt, pmk = msbank[s]
                    c.op("act", lambda: nc.scalar.activation(sq[:], pjt, AF.Square), [pjk], [("sq", s)])
                    yield
                    c.op("pe", lambda: nc.tensor.matmul(pmt[:], lhsT=onesf[:], rhs=sq[:], start=True, stop=True),
                         [("sq", s), "onesf"], [pmk])
                    yield
                    c.op("act", lambda: nc.scalar.activation(sd[:], pmt[:], AF.Ln, bias=epsc[:]),
                         [pmk, "epsc"], [("sd", s)])
                    yield
                    c.op("act", lambda: nc.scalar.activation(sd[:], sd[:], AF.Exp, scale=-0.5), [("sd", s)], [("sd", s)])
                    yield
                    if out_f32 is None:
                        dst, dkey = sq[:], ("sq", s)
                    else:
                        dst, dkey = out_f32, (f32key or outkey + ("f32",))
                    c.op("dve", lambda: nc.vector.scalar_tensor_tensor(
                        out=dst, in0=pjt, scalar=gcol, in1=sd[:], op0=ALU.mult, op1=ALU.mult),
                        [pjk, ("sd", s), "consts"], [dkey])
                    yield
                    if rope_cols is None:
                        c.op("pool", lambda: nc.gpsimd.tensor_copy(out_bf, dst), [dkey], [outkey])
                        yield
                    else:
                        c.op("pool", lambda: nc.gpsimd.tensor_copy(kb[:], dst), [dkey], [("kb", s)])
                        yield
                        prt, prk = rbank[s]
                        c.op("pe", lambda: nc.tensor.matmul(prt[:], lhsT=RTb[:], rhs=kb[:], start=True, stop=True),
                             [("kb", s), "consts"], [prk])
                        yield
                        c.op("pool", lambda: nc.gpsimd.tensor_tensor(out=dst, in0=dst, in1=cosT[:, rope_cols],
                                                                     op=ALU.mult), [dkey, "consts"], [dkey])
                        yield
                        c.op("dve", lambda: nc.vector.tensor_tensor(out=uu_[:], in0=prt[:], in1=sinT[:, rope_cols],
                                                                    op=ALU.mult), [prk, "consts"], [("ru", s)])
                        yield
                        c.op("pool", lambda: nc.gpsimd.tensor_tensor(out=out_bf, in0=dst, in1=uu_[:], op=ALU.add),
                             [dkey, ("ru", s)], [outkey])
                        yield

                def rope_of(tb):
                    return None if tb == 0 else slice((tb - 1) * 512, tb * 512)

                def k_task(sl, wb, wk, j, tb):
                    cols = slice(tb * 512, (tb + 1) * 512)
                    pjt, pjk = pbank[sl]
                    mm_group(pjt[:], [(wb[:, k, 0:128], hT[:, k, cols]) for k in range(16)], [wk], [pjk])
                    yield
                    if tb == 0:
                        kf32 = T_u[sl]
                        yield from fn_task(sl, pjt[:], pjk, gk[:, 0:1], None, KT[:, cols], ("KT", tb), out_f32=kf32[:],
                                           f32key=("ru", sl))
                        for t4 in range(4):
                            b = t4 % 2
                            pvt, pvk = tbank
                            c.op("pe", lambda: nc.tensor.transpose(
                                pvt[:, 0:128], kf32[:, t4 * 128:(t4 + 1) * 128], identf[:]),
                                [("ru", sl), "identf"], [pvk])
                            yield
                            c.op("act", lambda: nc.scalar.copy(kout[b][:], pvt[:, 0:128]), [pvk], [("kout", b)])
                            yield
                            c.dma("sp", nk[L][t4 * 128:(t4 + 1) * 128, j, :], kout[b][:],
                                  reads=[("kout", b)], writes=[("nk", t4, j)])
                    else:
                        yield from fn_task(sl, pjt[:], pjk, gk[:, 0:1], rope_of(tb), KT[:, cols], ("KT", tb))

                def v_task(sl, wb, wk, j, tt):
                    tcols = slice(tt * 128, (tt + 1) * 128)
                    pvt, pvk = pbank[sl]
                    mm_group(pvt[:, 0:128], [(hT[:, k, tcols], wb[:, k, 128:256]) for k in range(16)], [wk], [pvk])
                    yield
                    if tt < 4:
                        b = sl
                        c.op("act", lambda: nc.scalar.copy(vout[b][:], pvt[:, 0:128]), [pvk], [("vout", b)])
                        yield
                        c.op("pool", lambda: nc.gpsimd.tensor_copy(V[:, tt, :], vout[b][:]), [("vout", b)], [("V", tt)])
                        c.dma("sp", nv[L][tt * 128:(tt + 1) * 128, j, :], vout[b][:],
                              reads=[("vout", b)], writes=[("nv", tt, j)])
                        yield
                    else:
                        c.op("act", lambda: nc.scalar.copy(V[:, tt, :], pvt[:, 0:128]), [pvk], [("V", tt)])
                        yield

                def q_task(sl, wb, wk, hh, tb):
                    cols = slice(tb * 512, (tb + 1) * 512)
                    pjt, pjk = pbank[sl]
                    mm_group(pjt[:], [(wb[:, k, 0:128], hT[:, k, cols]) for k in range(16)], [wk], [pjk])
                    yield
                    yield from fn_task(sl, pjt[:], pjk, gq[:, 0:1], rope_of(tb), QTg[:, hh, cols], ("QT", hh, tb))

                def g_task(sl, wb, wk, hh, tb):
                    cols = slice(tb * 512, (tb + 1) * 512)
                    pjt, pjk = pbank[sl]
                    mm_group(pjt[:], [(wb[:, k, 128:256], hT[:, k, cols]) for k in range(16)], [wk], [pjk])
                    yield
                    c.op("act", lambda: nc.scalar.activation(oThg[:, hh, cols], pjt[:], AF.Silu),
                         [pjk], [("oTh", hh, tb)])
                    yield

                sctr = [0, 0]
                sbanks = [[(pS[0], ("pS", 0)), (pS[1], ("pS", 1))], [(pj, "pj"), (pms, "pms")]]

                def attn_block(ob, j, q0, nq, keys, tbq):
                    qcols = slice(q0, q0 + nq)
                    N = G * nq
                    nk_ = len(keys)
                    rhsQ = QTg[:, :, qcols] if G > 1 else QTg[:, 0, qcols]
                    qkeys = [("QT", hh, tbq) for hh in range(G)]
                    slots = []

                    def smm(ki):
                        kind, idx, mi = keys[ki]
                        p = sctr[ob] % 2
                        sctr[ob] += 1
                        slots.append(p)
                        pSt, pSk = sbanks[ob][p]
                        if kind == "l":
                            Kl = KT[:, idx * 128:(idx + 1) * 128]
                            kr = [("KT", idx // 4)]
                        else:
                            Kl = KcT[:, idx * 128:(idx + 1) * 128]
                            kr = ["KcT"]

                        def f():
                            out = pSt[:, :N] if G == 1 else pSt[:, :N].rearrange("p (g q) -> p g q", g=G)
                            inst = nc.tensor.matmul(out, lhsT=Kl, rhs=rhsQ, start=True, stop=(mi is None))
                            if mi is not None:
                                inst = nc.tensor.matmul(out, lhsT=identb[:], rhs=wb4[:, mi, :, 0:nq],
                                                        start=False, stop=True)
                            return inst
                        c.op("pe", f, kr + qkeys + ["consts", "identb"], [pSk])

                    def pv(ki):
                        kind, idx, mi = keys[ki]
                        p = slots[ki]
                        if kind == "l":
                            Vl = V[:, idx, :]
                            kr = [("V", idx)]
                        else:
                            Vl = Vc[:, idx, :]
                            kr = ["Vc"]
                        pSt, pSk = sbanks[ob][p]
                        PTt = PT[ob * 2 + p]
                        c.op("act", lambda: nc.scalar.activation(PTt[:, :N], pSt[:, :N], AF.Exp, scale=SCALE),
                             [pSk], [("PT", ob, p)])

                        def f():
                            nc.tensor.matmul(pO[ob][:, :N], lhsT=Vl, rhs=PTt[:, :N],
                                             start=(ki == 0), stop=(ki == nk_ - 1))
                            return nc.tensor.matmul(pL[ob][:, :N], lhsT=onesb[:], rhs=PTt[:, :N],
                                                    start=(ki == 0), stop=(ki == nk_ - 1))
                        c.op("pe", f, kr + [("PT", ob, p), "onesb"], [("pO", ob), ("pL", ob)])

                    smm(0)
                    yield
                    if nk_ > 1:
                        smm(1)
                        yield
                    for ki in range(nk_):
                        pv(ki)
                        if ki + 2 < nk_:
                            smm(ki + 2)
                        yield
                    r = rl[ob]
                    if L == 1:
                        r3 = r[:, :N].rearrange("p (g q) -> p g q", g=G)
                        l3 = pL[ob][:, :N].rearrange("p (g q) -> p g q", g=G)
                        c.op("dve", lambda: nc.vector.tensor_tensor(
                            out=r3, in0=l3, in1=esk[:, 4 * j:4 * j + 4].unsqueeze(2).to_broadcast([128, G, nq]),
                            op=ALU.add), [("pL", ob), "consts"], [("rl", ob)])
                        c.op("act", lambda: nc.scalar.activation(r[:, :N], r[:, :N], AF.Ln), [("rl", ob)], [("rl", ob)])
                    else:
                        c.op("act", lambda: nc.scalar.activation(r[:, :N], pL[ob][:, :N], AF.Ln), [("pL", ob)], [("rl", ob)])
                    c.op("act", lambda: nc.scalar.activation(r[:, :N], r[:, :N], AF.Exp, scale=-1.0),
                         [("rl", ob)], [("rl", ob)])
                    c.op("dve", lambda: nc.vector.tensor_tensor(out=r[:, :N], in0=pO[ob][:, :N], in1=r[:, :N],
                                                                op=ALU.mult), [("pO", ob), ("rl", ob)], [("rl", ob)])
                    okeys = [("oTh", hh, tbq) for hh in range(G)]
                    if G > 1:
                        o3 = oThg[:, :, qcols]
                        r3 = r[:, :N].rearrange("p (g q) -> p g q", g=G)
                    else:
                        o3 = oThg[:, 0, qcols]
                        r3 = r[:, :N]
                    c.op("pool", lambda: nc.gpsimd.tensor_tensor(out=o3, in0=r3, in1=o3, op=ALU.mult),
                         [("rl", ob)] + okeys, okeys)
                    yield

                for j in range(nkv):
                    wb, wk = getw()
                    c.dma("pool", KcT[:], ckT[L][:, j, :], writes=["KcT"])
                    c.dma("pool", Vc[:], cv[L][:, j, :].rearrange("(t p) d -> p t d", p=128), writes=["Vc"])
                    gens = [(lambda sl, tb=tb: k_task(sl, wb, wk, j, tb)) for tb in range(NTB)] + \
                           [(lambda sl, tt=tt: v_task(sl, wb, wk, j, tt)) for tt in range(NT)]
                    run_tasks(gens, NW)
                    for h0 in range(4 * j, 4 * j + 4, G):
                        gens = []
                        for hh in range(G):
                            wbq, wkq = getw()
                            for tb in range(NTB):
                                gens.append(lambda sl, wbq=wbq, wkq=wkq, hh=hh, tb=tb: g_task(sl, wbq, wkq, hh, tb))
                            for tb in range(NTB):
                                gens.append(lambda sl, wbq=wbq, wkq=wkq, hh=hh, tb=tb: q_task(sl, wbq, wkq, hh, tb))
                            if hh % 2 == 1 or G == 1:
                                run_tasks(gens, NW)
                                gens = []
                        blocks = []
                        if G == 1:
                            for (t0, n) in SEQS[:2]:
                                blocks.append((t0 * 128, 256, [("l", t0, None), ("l", t0 + 1, None)], 0))
                            for tb in range(1, NTB):
                                keys = [("l", kt, None) for kt in range(4, NT)] + [("c", 0, None), ("c", 1, None)]
                                blocks.append((tb * 512, 512, keys, tb))
                        else:
                            for (t0, n) in SEQS[:2]:
                                for tq in range(t0, t0 + n):
                                    blocks.append((tq * 128, 128, [("l", t0, None), ("l", t0 + 1, None)], 0))
                            for i in range(16):
                                keys = []
                                if i > 0:
                                    keys.append(("l", 4 + i - 1, 0))
                                keys.append(("l", 4 + i, None))
                                if i < 15:
                                    keys.append(("l", 4 + i + 1, 1))
                                keys += [("c", 0, None), ("c", 1, None)]
                                blocks.append(((4 + i) * 128, 128, keys, (4 + i) // 4))
                        run_tasks([(lambda sl, blk=blk: attn_block(sl, j, *blk)) for blk in blocks], 2)
                        for hh in range(G):
                            r0 = (orow0 + h0 + hh) * 128
                            c.dma("sp", oTs[L][r0:r0 + 128, :], oThg[:, hh, :],
                                  reads=[("oTh", hh, tb) for tb in range(NTB)], writes=[("oTs", L, orow0 + h0 + hh)])
                c.barrier_all()

        def phase_hgrn():
            with ExitStack() as es:
                getw = make_stream(es, [(w_in[0], h * 640, 640) for h in range(8)], 640)
                maskF = sb(es, "maskF", [128, 512], F32)
                mfb = sb(es, "mfb", [128, 2, 128], F32)
                cm4 = sb(es, "cm4", [128, 4, 128], BF16)
                onc = sb(es, "onc", [128, 1], F32)
                lbe = sb(es, "lbe", [128, 3, 16], F32)
                lbs = sb(es, "lbs", [128, 16], F32)
                lbv = sb(es, "lbv", [128, 16], F32)
                oml = sb(es, "oml", [128, 16], F32)
                noml = sb(es, "noml", [128, 16], F32)
                q32 = [sb(es, f"q32_{i}", [128, 512], F32) for i in range(2)]
                sg = [sb(es, f"sg_{i}", [128, 512], F32) for i in range(2)]
                lg = [sb(es, f"lg_{i}", [128, 512], F32) for i in range(2)]
                k32 = [sb(es, f"k32_{i}", [128, 512], F32) for i in range(2)]
                bF = [sb(es, f"bF_{i}", [128, 512], F32) for i in range(2)]
                totc = [sb(es, f"totc_{i}", [128, 16], F32) for i in range(2)]
                dec = sb(es, "dec", [128, 2, 80], F32)
                qd = [sb(es, f"qd{d}", [128, T], BF16) for d in range(2)]
                ki = [sb(es, f"ki{d}", [128, T], BF16) for d in range(2)]
                keT = [sb(es, f"keT{d}", [128, T], BF16) for d in range(2)]
                sgT = sb(es, "sgT", [128, T], BF16)
                V = sb(es, "Va", [128, NT, 128], BF16)
                OT = sb(es, "OT", [128, T], F32)
                kend = [[sb(es, f"kend{d}{r}", [128, 128], BF16) for r in range(2)] for d in range(2)]
                Vm = [[sb(es, f"Vm{d}{r}", [128, 4, 128], BF16) for r in range(2)] for d in range(2)]
                Am = [[sb(es, f"Am{d}{r}", [128, 128], BF16) for r in range(2)] for d in range(2)]
                S32 = [[sb(es, f"S32_{d}{r}", [128, 128], F32) for r in range(2)] for d in range(2)]
                Sbf = [[sb(es, f"Sbf{d}{p}", [128, 128], BF16) for p in range(4)] for d in range(2)]
                fb = [ps(es, f"hfb{i}", [128, 512]) for i in range(2)]
                bA = [ps(es, f"hbA{d}", [128, 512]) for d in range(2)]
                pO = [ps(es, f"hpO{d}", [128, 512]) for d in range(2)]
                ptr = [ps(es, f"hptr{d}", [128, 8, 128], BF16) for d in range(2)]

                c.dma("sp", maskF[:], maskF_d, writes=["hc"])
                c.dma("sp", mfb[:], mfb_d, writes=["hc"])
                c.dma("pool", cm4[:], cm4_d, writes=["hc"])
                c.dma("sp", onc[:], onorm_d, writes=["hc"])
                c.dma("sp", lbe[:], lbg, writes=["lbe"])
                c.op("act", lambda: nc.scalar.activation(lbe[:], lbe[:], AF.Exp), ["lbe"], ["lbe"])
                c.op("dve", lambda: nc.vector.tensor_tensor(out=lbs[:], in0=lbe[:, 0, :], in1=lbe[:, 1, :], op=ALU.add),
                     ["lbe"], ["lbs"])
                c.op("dve", lambda: nc.vector.tensor_tensor(out=lbs[:], in0=lbs[:], in1=lbe[:, 2, :], op=ALU.add),
                     ["lbe", "lbs"], ["lbs"])
                c.op("dve", lambda: nc.vector.reciprocal(lbs[:], lbs[:]), ["lbs"], ["lbs"])
                c.op("dve", lambda: nc.vector.tensor_tensor(out=lbv[:], in0=lbe[:, 0, :], in1=lbs[:], op=ALU.mult),
                     ["lbe", "lbs"], ["lbv"])
                c.op("dve", lambda: nc.vector.tensor_scalar(out=oml[:], in0=lbv[:], scalar1=-1.0, scalar2=1.0,
                                                            op0=ALU.mult, op1=ALU.add), ["lbv"], ["oml"])
                c.op("dve", lambda: nc.vector.tensor_scalar(out=noml[:], in0=oml[:], scalar1=-1.0, scalar2=None,
                                                            op0=ALU.mult), ["oml"], ["noml", "hc"])

                def bc32(t, ncol=16):
                    return t.unsqueeze(2).to_broadcast([128, ncol, 32])

                def f_task(sl, h, wb, wk, tb, c0, n):
                    cols = slice(c0, c0 + n)
                    nch = n // 32
                    ptrf = ptr[sl][:].rearrange("p a b -> p (a b)").bitcast(F32)
                    rot = [(fb[sl], ("fb", sl)), (bA[sl], ("bA", sl)), (pO[sl], ("pO", sl)), (ptrf, ("ptr", sl))]
                    rctr = [0]

                    def nextb():
                        r_ = rot[rctr[0] % 4]
                        rctr[0] += 1
                        return r_
                    pjt, pjk = nextb()
                    Q, SG, LG, K, B, TC = q32[sl], sg[sl], lg[sl], k32[sl], bF[sl], totc[sl]
                    kq, ks, kl, kk, kb_, kt = ("q32", sl), ("sg", sl), ("lg", sl), ("k32", sl), ("bF", sl), ("totc", sl)
                    mm_group(pjt[:, :n], [(wb[:, k, 0:128], hT[:, k, cols]) for k in range(16)], [wk], [pjk])
                    yield
                    c.op("act", lambda: nc.scalar.activation(Q[:, :n], pjt[:, :n], AF.Silu), [pjk], [kq])
                    yield
                    for d in range(2):
                        i = d * 8 + h
                        pjt, pjk = nextb()
                        mm_group(pjt[:, :n], [(wb[:, k, 128 * (1 + d):128 * (2 + d)], hT[:, k, cols]) for k in range(16)],
                                 [wk], [pjk])
                        yield
                        c.op("act", lambda: nc.scalar.activation(SG[:, :n], pjt[:, :n], AF.Sigmoid), [pjk], [ks])
                        yield
                        c.op("act", lambda: nc.scalar.activation(LG[:, :n], SG[:, :n], AF.Ln, scale=oml[:, i:i + 1],
                                                                 bias=lbv[:, i:i + 1]), [ks, "hc"], [kl])
                        c.op("dve", lambda: nc.vector.tensor_scalar(
                            out=K[:, :n], in0=SG[:, :n], scalar1=noml[:, i:i + 1], scalar2=oml[:, i:i + 1],
                            op0=ALU.mult, op1=ALU.add), [ks, "hc"], [kk])
                        yield
                        c.op("dve", lambda: nc.vector.tensor_tensor_scan(B[:, :n], maskF[:, :n], LG[:, :n], 0.0, ALU.mult, ALU.add),
                             [kl, "hc"], [kb_])
                        yield
                        tot = B[:, :n].rearrange("p (c t) -> p c t", t=32)[:, :, 31]
                        dslice = dec[:, d, c0 // 32:c0 // 32 + nch]
                        c.op("act", lambda: nc.scalar.activation(dslice, tot, AF.Exp), [kb_], [("dec", d, tb)])
                        if d == 1:
                            c.op("act", lambda: nc.scalar.copy(TC[:, :nch], tot), [kb_], [kt])
                            yield
                            b3 = B[:, :n].rearrange("p (c t) -> p c t", t=32)
                            c.op("dve", lambda: nc.vector.tensor_tensor(out=b3, in0=bc32(TC[:, :nch], nch), in1=b3, op=ALU.subtract),
                                 [kb_, kt], [kb_])
                            yield
                            c.op("pool", lambda: nc.gpsimd.tensor_tensor(out=B[:, :n], in0=B[:, :n], in1=LG[:, :n], op=ALU.add),
                                 [kb_, kl], [kb_])
                        yield
                        c.op("act", lambda: nc.scalar.activation(SG[:, :n], B[:, :n], AF.Exp), [kb_], [ks])
                        c.op("act", lambda: nc.scalar.activation(LG[:, :n], B[:, :n], AF.Exp, scale=-1.0), [kb_], [kl])
                        yield
                        c.op("dve", lambda: nc.vector.tensor_tensor(out=qd[d][:, cols], in0=Q[:, :n], in1=SG[:, :n], op=ALU.mult),
                             [kq, ks], [("qd", d, tb)])
                        yield
                        c.op("dve", lambda: nc.vector.tensor_tensor(out=LG[:, :n], in0=K[:, :n], in1=LG[:, :n], op=ALU.mult),
                             [kk, kl], [kl])
                        yield
                        c.op("pool", lambda: nc.gpsimd.tensor_copy(ki[d][:, cols], LG[:, :n]), [kl], [("ki", d, tb)])
                        k3 = LG[:, :n].rearrange("p (c t) -> p c t", t=32)
                        o3 = keT[d][:, cols].rearrange("p (c t) -> p c t", t=32)
                        c.op("pool", lambda: nc.gpsimd.tensor_tensor(out=o3, in0=k3, in1=bc32(dslice, nch), op=ALU.mult),
                             [kl, ("dec", d, tb)], [("keT", d, tb)])
                        yield
                    pjt, pjk = nextb()
                    mm_group(pjt[:, :n], [(wb[:, k, 512:640], hT[:, k, cols]) for k in range(16)], [wk], [pjk])
                    yield
                    c.op("act", lambda: nc.scalar.activation(sgT[:, cols], pjt[:, :n], AF.Silu), [pjk], [("sgT", tb)])
                    yield
                    for tt in range(c0 // 128, (c0 + n) // 128):
                        tcols = slice(tt * 128, (tt + 1) * 128)
                        pjt, pjk = nextb()
                        mm_group(pjt[:, 0:128], [(hT[:, k, tcols], wb[:, k, 384:512]) for k in range(16)], [wk], [pjk])
                        yield
                        c.op("act", lambda: nc.scalar.copy(V[:, tt, :], pjt[:, 0:128]), [pjk], [("V", tt)])
                        yield

                BLK = [0, 0, 1, 1] + [2 + (t - 4) // 4 for t in range(4, NT)]
                ot_written = set()
                vctr = [0, 0]
                sctr = [0, 0]

                def sweep(d, h, si, t0, n):
                    cur = sctr[d] % 2
                    if si < 2:
                        c.op("dve", lambda: nc.vector.memset(S32[d][cur][:], 0.0), [], [("S32", d, cur)])
                    else:
                        c.dma("sp", S32[d][cur][:], st0[d, h], writes=[("S32", d, cur)])
                    sb0 = sctr[d] % 4
                    c.op("act", lambda: nc.scalar.copy(Sbf[d][sb0][:], S32[d][cur][:]),
                         [("S32", d, cur)], [("Sbf", d, sb0)])
                    yield
                    tiles = range(t0, t0 + n) if d == 0 else range(t0 + n - 1, t0 - 1, -1)
                    for tt in tiles:
                        tb = BLK[tt]
                        tcols = slice(tt * 128, (tt + 1) * 128)
                        r = vctr[d] % 2
                        vctr[d] += 1
                        c.op("pe", lambda: nc.tensor.transpose(ptr[d][:, 0, :], keT[d][:, tcols], identb[:]),
                             [("keT", d, tb), "identb"], [("ptr", d)])
                        c.op("pool", lambda: nc.gpsimd.tensor_tensor(
                            out=Vm[d][r][:], in0=V[:, tt, :].unsqueeze(1).to_broadcast([128, 4, 128]), in1=cm4[:],
                            op=ALU.mult), [("V", tt), "hc"], [("Vm", d, r)])
                        yield
                        c.op("act", lambda: nc.scalar.copy(kend[d][r][:], ptr[d][:, 0, :]), [("ptr", d)], [("kend", d, r)])
                        c.op("pe", lambda: nc.tensor.matmul(bA[d][:, 0:128], lhsT=ki[d][:, tcols], rhs=qd[d][:, tcols],
                                                            start=True, stop=True),
                             [("ki", d, tb), ("qd", d, tb)], [("bA", d)])
                        yield
                        c.op("dve", lambda: nc.vector.tensor_tensor(out=Am[d][r][:], in0=bA[d][:, 0:128], in1=mfb[:, d, :],
                                                                    op=ALU.mult), [("bA", d), "hc"], [("Am", d, r)])

                        def umm():
                            inst = None
                            for j in range(4):
                                inst = nc.tensor.matmul(fb[d][:, j * 128:(j + 1) * 128], lhsT=kend[d][r][:],
                                                        rhs=Vm[d][r][:, j, :], start=True, stop=True)
                            return inst
                        c.op("pe", umm, [("kend", d, r), ("Vm", d, r)], [("fb", d)])
                        yield
                        c.op("pe", lambda: nc.tensor.matmul(pO[d][:, 0:128], lhsT=V[:, tt, :], rhs=Am[d][r][:],
                                                            start=True, stop=False),
                             [("V", tt), ("Am", d, r)], [("pO", d)])
                        yield
                        order = range(4) if d == 0 else range(3, -1, -1)
                        for n_, j in enumerate(order):
                            cur = sctr[d] % 2
                            nxt = 1 - cur
                            sbc = sctr[d] % 4
                            sbn = (sctr[d] + 1) % 4
                            ccols = slice(tt * 128 + 32 * j, tt * 128 + 32 * j + 32)
                            gch = tt * 4 + j
                            c.op("pe", lambda: nc.tensor.matmul(
                                pO[d][:, 32 * j:32 * j + 32], lhsT=Sbf[d][sbc][:], rhs=qd[d][:, ccols],
                                start=False, stop=(n_ == 3)),
                                [("Sbf", d, sbc), ("qd", d, tb)], [("pO", d)])
                            c.op("dve", lambda: nc.vector.scalar_tensor_tensor(
                                out=S32[d][nxt][:], in0=S32[d][cur][:], scalar=dec[:, d, gch:gch + 1],
                                in1=fb[d][:, j * 128:(j + 1) * 128], op0=ALU.mult, op1=ALU.add),
                                [("S32", d, cur), ("dec", d, tb), ("fb", d)], [("S32", d, nxt)])
                            yield
                            c.op("act", lambda: nc.scalar.copy(Sbf[d][sbn][:], S32[d][nxt][:]),
                                 [("S32", d, nxt)], [("Sbf", d, sbn)])
                            sctr[d] += 1
                            yield
                        if tt not in ot_written:
                            ot_written.add(tt)
                            c.op("act", lambda: nc.scalar.copy(OT[:, tcols], pO[d][:, 0:128]), [("pO", d)], [("OT", tt)])
                        else:
                            c.op("dve", lambda: nc.vector.tensor_tensor(out=OT[:, tcols], in0=pO[d][:, 0:128],
                                                                        in1=OT[:, tcols], op=ALU.add),
                                 [("pO", d), ("OT", tt)], [("OT", tt)])
                        yield
                    if si < 2:
                        fin = sctr[d] % 2
                        c.dma("sp", nstate[si, d, h], S32[d][fin][:], reads=[("S32", d, fin)],
                              writes=[("nstate", si, d, h)])

                def n_task(sl, h, tb):
                    cols = slice(tb * 512, (tb + 1) * 512)
                    ok = [("OT", tt) for tt in range(tb * 4, tb * 4 + 4)]
                    SQ, SD = q32[sl], sg[sl]
                    kq, ks = ("q32", sl), ("sg", sl)
                    pjt, pjk = fb[sl], ("fb", sl)
                    c.op("act", lambda: nc.scalar.activation(SQ[:], OT[:, cols], AF.Square), ok, [kq])
                    yield
                    c.op("pe", lambda: nc.tensor.matmul(pjt[:], lhsT=onesf[:], rhs=SQ[:], start=True, stop=True),
                         [kq, "onesf"], [pjk])
                    yield
                    c.op("act", lambda: nc.scalar.activation(SD[:], pjt[:], AF.Ln, bias=epsc[:]), [pjk, "epsc"], [ks])
                    yield
                    c.op("act", lambda: nc.scalar.activation(SD[:], SD[:], AF.Exp, scale=-0.5), [ks], [ks])
                    yield
                    c.op("dve", lambda: nc.vector.scalar_tensor_tensor(
                        out=SQ[:], in0=OT[:, cols], scalar=onc[:, 0:1], in1=SD[:], op0=ALU.mult, op1=ALU.mult),
                        ok + [ks, "hc"], [kq])
                    yield
                    ostv = k32[sl][:].bitcast(BF16)[:, 0:512]
                    c.op("pool", lambda: nc.gpsimd.tensor_tensor(out=ostv, in0=SQ[:], in1=sgT[:, cols], op=ALU.mult),
                         [kq] + [("sgT", bb_) for bb_ in sorted(set(BLK[tb * 4:tb * 4 + 4]))], [("k32", sl)])
                    c.dma("sp", oTs[0][h * 128:(h + 1) * 128, cols], ostv, reads=[("k32", sl)],
                          writes=[("oTs", 0, h)])
                    yield

                HG = os.environ.get("MK_HG", "")
                for h in range(8):
                    wb, wk = getw()
                    if "nof" not in HG:
                        FB = [(2, 512, 512), (3, 1024, 512), (4, 1536, 512), (5, 2048, 512), (0, 0, 256), (1, 256, 256)]
                        run_tasks([(lambda sl, fb_=fb_: f_task(sl, h, wb, wk, *fb_)) for fb_ in FB], 2)
                    ot_written.clear()
                    if "noscan" not in HG:
                        for si, (t0, n) in enumerate(SEQS):
                            run_tasks([(lambda sl, d=d: sweep(d, h, si, t0, n)) for d in range(2)], 2)
                    if "nopost" not in HG:
                        run_tasks([(lambda sl, tb=tb: n_task(sl, h, tb)) for tb in range(NTB)], 2)
                c.barrier_all()

        def phase_out(L):
            NB = 4
            with ExitStack() as es:
                getw = make_stream(es, [(w_out[L], cb * 512, 512) for cb in range(4)], 512)
                gtb = sb(es, "gtb", [128, 2, D], F32)
                xr = [sb(es, f"xr{i}", [128, 512], F32) for i in range(NB)]
                tm = [sb(es, f"otm{i}", [128, 512], F32) for i in range(NB)]
                po = [ps(es, f"po{i}", [128, 512]) for i in range(2)]
                for g in range(2):
                    c.dma("act", gtb[:, g, :], gts[L * 2 + g:L * 2 + g + 1, :].partition_broadcast(128),
                          reads=[("gts", L, g, cb) for cb in range(4)], writes=[("gtb", g)])
                for k in range(16):
                    c.dma("sp" if k % 2 == 0 else "act", hT[:, k, :], oTs[L][k * 128:(k + 1) * 128, :],
                          reads=[("oTs", L, k)], writes=[("hTk", k)])
                it = 0
                for cb in range(4):
                    ccols = slice(cb * 512, (cb + 1) * 512)
                    wb, wk = getw()
                    for tt in range(NT):
                        b = it % NB
                        pb = it % 2
                        it += 1
                        g = 0 if tt < 4 else 1
                        rows = slice(tt * 128, (tt + 1) * 128)
                        tcols = slice(tt * 128, (tt + 1) * 128)
                        if L == 0:
                            c.dma("sp", xr[b][:], x[rows, ccols], writes=[("xr", b)])
                        else:
                            c.dma("sp", xr[b][:], y1[rows, ccols], reads=[("y1", tt, cb)], writes=[("xr", b)])
                        mm_group(po[pb][:], [(hT[:, k, tcols], wb[:, k, 0:512]) for k in range(16)],
                                 [wk] + [("hTk", k) for k in range(16)], [("po", pb)])
                        c.op("dve", lambda: nc.vector.tensor_tensor(
                            out=tm[b][:], in0=po[pb][:], in1=gtb[:, g, ccols], op=ALU.mult),
                            [("po", pb), ("gtb", g)], [("otm", b)])
                        c.op("pool", lambda: nc.gpsimd.tensor_tensor(out=tm[b][:], in0=tm[b][:], in1=xr[b][:],
                                                                     op=ALU.add),
                             [("otm", b), ("xr", b)], [("otm", b)])
                        if L == 0:
                            c.dma("act", y1[rows, ccols], tm[b][:], reads=[("otm", b)], writes=[("y1", tt, cb)])
                        else:
                            c.dma("act", y[rows, ccols], tm[b][:], reads=[("otm", b)], writes=[("y", tt, cb)])
                c.barrier_all()

        plist = [("mod0", lambda: phase_mod_h(0)), ("hgrn", phase_hgrn), ("attn0", lambda: phase_attn(0)),
                 ("out0", lambda: phase_out(0)), ("mod1", lambda: phase_mod_h(1)), ("attn1", lambda: phase_attn(1)),
                 ("out1", lambda: phase_out(1))]
        for nm, fn in plist:
            if phases is None or nm in phases:
                fn()
        c.finish()
    return nc


def _consts():
    ident = np.eye(128, dtype=np.float32)
    maskF = np.ones((128, 512), np.float32)
    maskF[:, ::32] = 0.0
    s = np.arange(128)[:, None]
    t = np.arange(128)[None, :]
    same = (s // 32) == (t // 32)
    mfb = np.stack([(same & (s <= t)), (same & (s >= t))], axis=1).astype(np.float32)
    cm4 = np.zeros((128, 4, 128), np.float32)
    for j in range(4):
        cm4[32 * j:32 * j + 32, j, :] = 1.0
    R = np.zeros((128, 128), np.float32)
    for m in range(128):
        q = m // 32
        if q in (0, 2):
            R[m, m + 32] = -1.0
        else:
            R[m, m - 32] = 1.0
    RT = np.ascontiguousarray(R.T)
    n_tok = 2048
    row = (np.arange(n_tok) // 64).astype(np.float32)
    col = (np.arange(n_tok) % 64).astype(np.float32)
    inv = (10000.0 ** (-np.arange(32, dtype=np.float32) / 32)).astype(np.float32)
    ar = row[:, None] * inv
    ac = col[:, None] * inv
    ang = np.concatenate([ar, ar, ac, ac], axis=-1).astype(np.float32)
    cosT = np.ascontiguousarray(np.cos(ang).T.astype(np.float32))
    sinT = np.ascontiguousarray(np.sin(ang).T.astype(np.float32))
    b = np.arange(128)[:, None]
    a = np.arange(128)[None, :]
    NEG = -30000.0
    wbias = np.stack([np.where(b >= a, 0.0, NEG), np.where(b <= a, 0.0, NEG)], axis=1).astype(np.float32)
    wbias4 = np.ascontiguousarray(np.broadcast_to(wbias[:, :, None, :], (128, 2, 4, 128))).astype(np.float32)
    return dict(ident=ident, maskF=maskF, mfb=mfb, cm4=cm4, RT=RT, cosT=cosT, sinT=sinT, wbias4=wbias4)


def _perm_w0(w):
    cols = []
    for h in range(8):
        for base in (0, 1024, 2048, 3072, 4096):
            cols.append(np.arange(base + h * 128, base + (h + 1) * 128))
    for j in range(2):
        cols.append(np.arange(6144 + j * 128, 6144 + (j + 1) * 128))
        cols.append(np.arange(6400 + j * 128, 6400 + (j + 1) * 128))
    for h in range(8):
        cols.append(np.arange(5120 + h * 128, 5120 + (h + 1) * 128))
        cols.append(np.arange(6656 + h * 128, 6656 + (h + 1) * 128))
    return np.ascontiguousarray(w[:, np.concatenate(cols)])


def _perm_w1(w):
    cols = []
    for j in range(4):
        cols.append(np.arange(2048 + j * 128, 2048 + (j + 1) * 128))
        cols.append(np.arange(2560 + j * 128, 2560 + (j + 1) * 128))
    for h in range(16):
        cols.append(np.arange(h * 128, (h + 1) * 128))
        cols.append(np.arange(3072 + h * 128, 3072 + (h + 1) * 128))
    return np.ascontiguousarray(w[:, np.concatenate(cols)])


def _prep(x_prompt, x_sample, state_l0_hgrn, cache_l0_k, cache_l0_v, cache_l1_k, cache_l1_v,
           c, c_ctx, lb_gamma,
           l0_norm, l0_w_mod, l0_b_mod, l0_w_in, l0_w_out, l0_a_onorm, l0_b_qnorm, l0_b_knorm,
           l1_norm, l1_w_mod, l1_b_mod, l1_w_in, l1_w_out, l1_c_qnorm, l1_c_knorm, l1_c_sink):
    f = lambda a: np.ascontiguousarray(np.asarray(a, dtype=np.float32))
    x_prompt, x_sample = f(x_prompt), f(x_sample)
    consts = _consts()
    shared = dict(
        w_mod0=f(l0_w_mod), w_mod1=f(l1_w_mod),
        b_mod0=f(l0_b_mod).reshape(1, -1), b_mod1=f(l1_b_mod).reshape(1, -1),
        norm0=f(l0_norm).reshape(1, -1), norm1=f(l1_norm).reshape(1, -1),
        w_in0=_perm_w0(f(l0_w_in)), w_in1=_perm_w1(f(l1_w_in)),
        w_out0=f(l0_w_out), w_out1=f(l1_w_out),
        onorm=f(l0_a_onorm).reshape(128, 1),
        qn0=f(l0_b_qnorm).reshape(128, 1), kn0=f(l0_b_knorm).reshape(128, 1),
        qn1=f(l1_c_qnorm).reshape(128, 1), kn1=f(l1_c_knorm).reshape(128, 1),
        sink=f(l1_c_sink).reshape(1, 16),
        lbg=np.ascontiguousarray(f(lb_gamma).reshape(3, 2, 8, 128).transpose(3, 0, 1, 2).reshape(128, 3, 16)),
        **consts,
    )
    c = f(c)
    c_ctx = f(c_ctx)
    in_maps = []
    for i in range(8):
        m = dict(shared)
        m["x"] = np.ascontiguousarray(np.concatenate(
            [x_prompt[2 * i], x_prompt[2 * i + 1], x_sample[i]], axis=0))
        cr = np.stack([c_ctx, c[i]], axis=0)
        m["crows"] = np.ascontiguousarray(cr.reshape(2, 16, 128).transpose(2, 0, 1))
        m["st0"] = f(state_l0_hgrn[i])
        m["ck0T"] = np.ascontiguousarray(f(cache_l0_k[i]).transpose(2, 1, 0))
        m["cv0"] = f(cache_l0_v[i])
        m["ck1T"] = np.ascontiguousarray(f(cache_l1_k[i]).transpose(2, 1, 0))
        m["cv1"] = f(cache_l1_v[i])
        in_maps.append(m)
    return in_maps


def kernel(**inputs):
    in_maps = _prep(**inputs)
    nc = build_program()
    res = run_bass_kernel_spmd(nc, in_maps, core_ids=list(range(8)))
    r = res.results
    y_prompt = np.stack([r[i // 2]["y"][(i % 2) * 256:(i % 2 + 1) * 256] for i in range(16)], axis=0)
    y_sample = np.stack([r[i]["y"][512:] for i in range(8)], axis=0)
    nstate = np.concatenate([r[i]["nstate"] for i in range(8)], axis=0)
    outs = [y_prompt.astype(np.float32), y_sample.astype(np.float32), nstate.astype(np.float32)]
    for nm in ("nk0", "nv0", "nk1", "nv1"):
        a = np.concatenate([r[i][nm].reshape(2, 256, r[i][nm].shape[1], 128) for i in range(8)], axis=0)
        outs.append(a.astype(np.float32))
    return tuple(outs)
```
